# Optimizing a Trainium2 kernel written in Bass

```python
import math
import jax, jax.numpy as jnp
from jax import lax
import numpy as np

D_MODEL = 1024
BATCH = 8
SEQ = 2048
DEPTH = 2

CHUNK = 64
N_META = 16
Q_BLOCK = 128
ROPE_THETA = 500000.0
EPS = 1e-6
N_BRANCH = 3

SB_HEADS = D_MODEL // 128
SB_HEAD_DIM = 64
SB_WIDTH = SB_HEADS * SB_HEAD_DIM

MLA_HEADS = D_MODEL // 128
MLA_NOPE = 64
MLA_ROPE = 32
MLA_V = 64
MLA_Q_RANK = 3 * D_MODEL // 8
MLA_KV_RANK = D_MODEL // 4
MLA_WIDTH = MLA_HEADS * MLA_V

DIFF_HEADS = D_MODEL // 256
DIFF_HEAD_DIM = 64
DIFF_V_DIM = 2 * DIFF_HEAD_DIM
DIFF_WIDTH = DIFF_HEADS * DIFF_V_DIM
ROT_DIM = DIFF_HEAD_DIM // 4

IN_SIZES = (
    SB_WIDTH, SB_WIDTH, SB_WIDTH, SB_WIDTH,
    MLA_Q_RANK, MLA_KV_RANK, MLA_ROPE, MLA_WIDTH,
    2 * DIFF_HEADS * DIFF_HEAD_DIM, 2 * DIFF_HEADS * DIFF_HEAD_DIM,
    DIFF_HEADS * DIFF_V_DIM, DIFF_WIDTH,
    N_BRANCH * D_MODEL,
)
IN_COLS = sum(IN_SIZES)

kernel_name = "hybrid_stickbreak_mla_diffattn_gated_merge"


def _rmsnorm(x, g):
    x32 = x.astype(jnp.float32)
    y = x32 * lax.rsqrt(jnp.mean(x32 * x32, axis=-1, keepdims=True) + EPS)
    return y.astype(x.dtype) * g


def _rope_tables(n_pos, dim):
    inv_freq = ROPE_THETA ** (-jnp.arange(0, dim, 2, dtype=jnp.float32) / dim)
    ang = jnp.arange(n_pos, dtype=jnp.float32)[:, None] * inv_freq[None, :]
    return jnp.cos(ang), jnp.sin(ang)


def _rope(x, cos, sin):
    half = x.shape[-1] // 2
    x32 = x.astype(jnp.float32)
    x1, x2 = x32[..., :half], x32[..., half:]
    out = jnp.concatenate([x1 * cos - x2 * sin, x2 * cos + x1 * sin], axis=-1)
    return out.astype(x.dtype)


def _partial_rope(x, cos, sin):
    return jnp.concatenate([_rope(x[..., :ROT_DIM], cos, sin), x[..., ROT_DIM:]], axis=-1)


def _heads(t, n_heads):
    b, l, w = t.shape
    return t.reshape(b, l, n_heads, w // n_heads).transpose(0, 2, 1, 3)


def _merge_heads(t):
    b, h, l, d = t.shape
    return t.transpose(0, 2, 1, 3).reshape(b, l, h * d)


def _sweep(block_fn, n_blocks):
    return jnp.concatenate([block_fn(i) for i in range(n_blocks)], axis=2)


def _chunk_mask(chunk_ids, q0, k_end):
    qc = chunk_ids[q0:q0 + Q_BLOCK]
    kc = chunk_ids[:k_end]
    return kc[None, :] <= qc[:, None]


def _masked_softmax(s, mask):
    return jax.nn.softmax(jnp.where(mask, s, -jnp.inf), axis=-1)


def _layer(x, layer_idx, pos, chunk_ids, cos_d, sin_d, cos_m, sin_m,
           norm_g, w_in, b_gate, mla_cq_g, mla_ckv_g, mla_w_uq, mla_w_ukv,
           diff_lambda, diff_norm_g, w_o_sb, w_o_mla, w_o_diff, w_out):
    B, L, D = x.shape
    n_blocks = L // Q_BLOCK
    h = _rmsnorm(x, norm_g)
    proj = h @ w_in
    offsets = []
    acc = 0
    for s in IN_SIZES[:-1]:
        acc += s
        offsets.append(acc)
    (sb_q, sb_k, sb_v, sb_z, mla_cq, mla_ckv, mla_kr, mla_z,
     d_q, d_k, d_v, d_z, gate_logits) = jnp.split(proj, offsets, axis=-1)

    q_a, k_a, v_a = _heads(sb_q, SB_HEADS), _heads(sb_k, SB_HEADS), _heads(sb_v, SB_HEADS)
    scale_a = 1.0 / math.sqrt(SB_HEAD_DIM)

    def sb_block(i):
        q0 = i * Q_BLOCK
        k_end = q0 + Q_BLOCK
        z = jnp.einsum('bhqd,bhkd->bhqk', q_a[:, :, q0:k_end], k_a[:, :, :k_end]).astype(jnp.float32) * scale_a
        valid = pos[:k_end][None, :] < pos[q0:k_end][:, None]
        log_keep = jnp.where(valid, -jax.nn.softplus(z), 0.0)
        after = lax.cumsum(log_keep, axis=3, reverse=True) - log_keep
        w = jnp.where(valid, jnp.exp(jax.nn.log_sigmoid(z) + after), 0.0)
        return jnp.einsum('bhqk,bhkd->bhqd', w.astype(v_a.dtype), v_a[:, :, :k_end])

    o_sb = _merge_heads(_sweep(sb_block, n_blocks))

    q_b = _heads(_rmsnorm(mla_cq, mla_cq_g) @ mla_w_uq, MLA_HEADS)
    q_b = jnp.concatenate([q_b[..., :MLA_NOPE], _rope(q_b[..., MLA_NOPE:], cos_m, sin_m)], axis=-1)
    kv_b = _heads(_rmsnorm(mla_ckv, mla_ckv_g) @ mla_w_ukv, MLA_HEADS)
    k_rope = _rope(mla_kr, cos_m, sin_m)
    k_b = jnp.concatenate([kv_b[..., :MLA_NOPE],
                           jnp.broadcast_to(k_rope[:, None], (B, MLA_HEADS, L, MLA_ROPE))], axis=-1)
    v_b = kv_b[..., MLA_NOPE:]
    scale_b = 1.0 / math.sqrt(MLA_NOPE + MLA_ROPE)

    def mla_block(i):
        q0 = i * Q_BLOCK
        k_end = min(q0 + Q_BLOCK + CHUNK, L)
        s = jnp.einsum('bhqd,bhkd->bhqk', q_b[:, :, q0:q0 + Q_BLOCK], k_b[:, :, :k_end]).astype(jnp.float32) * scale_b
        p = _masked_softmax(s, _chunk_mask(chunk_ids, q0, k_end))
        return jnp.einsum('bhqk,bhkd->bhqd', p.astype(v_b.dtype), v_b[:, :, :k_end])

    o_mla = _merge_heads(_sweep(mla_block, n_blocks))

    q_c = _partial_rope(d_q.reshape(B, L, DIFF_HEADS, 2, DIFF_HEAD_DIM).transpose(0, 2, 3, 1, 4), cos_d, sin_d)
    k_c = _partial_rope(d_k.reshape(B, L, DIFF_HEADS, 2, DIFF_HEAD_DIM).transpose(0, 2, 3, 1, 4), cos_d, sin_d)
    v_c = _heads(d_v, DIFF_HEADS)
    lam_init = 0.8 - 0.6 * math.exp(-0.3 * layer_idx)
    lam32 = diff_lambda.astype(jnp.float32)
    lam = (jnp.exp(jnp.sum(lam32[0] * lam32[1])) - jnp.exp(jnp.sum(lam32[2] * lam32[3])) + lam_init)
    scale_c = 1.0 / math.sqrt(DIFF_HEAD_DIM)

    def diff_block(i):
        q0 = i * Q_BLOCK
        k_end = min(q0 + Q_BLOCK + CHUNK, L)
        s = jnp.einsum('bhmqd,bhmkd->bhmqk', q_c[:, :, :, q0:q0 + Q_BLOCK], k_c[:, :, :, :k_end]).astype(jnp.float32) * scale_c
        p = _masked_softmax(s, _chunk_mask(chunk_ids, q0, k_end))
        w = p[:, :, 0] - lam * p[:, :, 1]
        return jnp.einsum('bhqk,bhkd->bhqd', w.astype(v_c.dtype), v_c[:, :, :k_end])

    o_c = _sweep(diff_block, n_blocks)
    o_diff = _merge_heads(_rmsnorm(o_c, diff_norm_g) * (1.0 - lam_init))

    y_sb = (o_sb * jax.nn.silu(sb_z)) @ w_o_sb
    y_mla = (o_mla * jax.nn.silu(mla_z)) @ w_o_mla
    y_diff = (o_diff * jax.nn.silu(d_z)) @ w_o_diff
    g = jax.nn.sigmoid(gate_logits + b_gate).reshape(B, L, N_BRANCH, D)
    merged = g[:, :, 0] * y_sb + g[:, :, 1] * y_mla + g[:, :, 2] * y_diff
    return x + merged @ w_out


def setup_inputs(seed: int = 0) -> dict:
    key = jax.random.key(seed)
    ks = jax.random.split(key, 18)

    def nrm(k, shape, fan_in):
        return jax.random.normal(k, shape, jnp.float32) * fan_in ** -0.5

    def gain(k, shape):
        return 1.0 + 0.05 * jax.random.normal(k, shape, jnp.float32)

    return {
        "x": jax.random.normal(ks[0], (BATCH, SEQ, D_MODEL), jnp.float32),
        "meta_tokens": jax.random.normal(ks[1], (N_META, D_MODEL), jnp.float32),
        "norm_g": gain(ks[2], (DEPTH, D_MODEL)),
        "w_in": nrm(ks[3], (DEPTH, D_MODEL, IN_COLS), D_MODEL),
        "b_gate": 0.01 * jax.random.normal(ks[4], (DEPTH, N_BRANCH * D_MODEL), jnp.float32),
        "mla_cq_g": gain(ks[5], (DEPTH, MLA_Q_RANK)),
        "mla_ckv_g": gain(ks[6], (DEPTH, MLA_KV_RANK)),
        "mla_w_uq": nrm(ks[7], (DEPTH, MLA_Q_RANK, MLA_HEADS * (MLA_NOPE + MLA_ROPE)), MLA_Q_RANK),
        "mla_w_ukv": nrm(ks[8], (DEPTH, MLA_KV_RANK, MLA_HEADS * (MLA_NOPE + MLA_V)), MLA_KV_RANK),
        "diff_lambda": 0.1 * jax.random.normal(ks[9], (DEPTH, 4, DIFF_HEAD_DIM), jnp.float32),
        "diff_norm_g": gain(ks[10], (DEPTH, DIFF_V_DIM)),
        "w_o_sb": nrm(ks[11], (DEPTH, SB_WIDTH, D_MODEL), SB_WIDTH),
        "w_o_mla": nrm(ks[12], (DEPTH, MLA_WIDTH, D_MODEL), MLA_WIDTH),
        "w_o_diff": nrm(ks[13], (DEPTH, DIFF_WIDTH, D_MODEL), DIFF_WIDTH),
        "w_out": nrm(ks[14], (DEPTH, D_MODEL, D_MODEL), D_MODEL),
        "final_g": gain(ks[15], (D_MODEL,)),
    }


def reference(x, meta_tokens, norm_g, w_in, b_gate, mla_cq_g, mla_ckv_g, mla_w_uq, mla_w_ukv,
              diff_lambda, diff_norm_g, w_o_sb, w_o_mla, w_o_diff, w_out, final_g):
    B, S, D = x.shape
    L = S + N_META
    L_pad = -(-L // Q_BLOCK) * Q_BLOCK
    meta = jnp.broadcast_to(meta_tokens.astype(x.dtype)[None], (B, N_META, D))
    h = jnp.concatenate([meta, x, jnp.zeros((B, L_pad - L, D), x.dtype)], axis=1)
    pos = jnp.arange(L_pad)
    chunk_ids = jnp.where(pos < N_META, 0, (pos - N_META) // CHUNK + 1)
    cos_d, sin_d = _rope_tables(L_pad, ROT_DIM)
    cos_m, sin_m = _rope_tables(L_pad, MLA_ROPE)
    for l in range(DEPTH):
        h = _layer(h, l, pos, chunk_ids, cos_d, sin_d, cos_m, sin_m,
                   norm_g[l], w_in[l], b_gate[l], mla_cq_g[l], mla_ckv_g[l], mla_w_uq[l], mla_w_ukv[l],
                   diff_lambda[l], diff_norm_g[l], w_o_sb[l], w_o_mla[l], w_o_diff[l], w_out[l])
    h = _rmsnorm(h, final_g)
    return h[:, N_META:N_META + S]
```

```python
import math
import numpy as np
import ml_dtypes
from contextlib import ExitStack
import concourse.bass as bass
import concourse.mybir as mybir
from concourse.bass_utils import run_bass_kernel_spmd

F32 = mybir.dt.float32
BF16 = mybir.dt.bfloat16
AF = mybir.ActivationFunctionType
ALU = mybir.AluOpType

D = 1024
S = 2048
NMETA = 16
NT = 17
L = NT * 128
KC = 8
EPS = 1e-6
THETA = 500000.0
CHUNKS = [(0, 512), (512, 512), (1024, 512), (1536, 512), (2048, 128)]


class Buf:
    __slots__ = ("name", "t", "w", "r", "dsem", "dcnt")

    def __init__(self, name, t=None):
        self.name = name
        self.t = t
        self.w = []
        self.r = []
        self.dsem = None
        self.dcnt = 0

    def __getitem__(self, idx):
        return self.t[idx]


class Prog:
    ENGS = ("tensor", "vector", "scalar", "gpsimd", "sync")

    def __init__(self, nc, stack):
        self.nc = nc
        self.stack = stack
        self.sems = {}
        self.cnt = {e: 0 for e in self.ENGS}
        self.seen = {e: {} for e in self.ENGS}
        self.q = {e: [] for e in self.ENGS}
        for e in self.ENGS:
            self._sem("E_" + e)
        self.nbuf = 0

    def _sem(self, key):
        if key not in self.sems:
            self.sems[key] = self.stack.enter_context(self.nc.semaphore(key))
        return key

    def sb(self, st, name, shape, dt):
        self.uid = getattr(self, "uid", 0) + 1
        name = "%s_%d" % (name, self.uid)
        t = st.enter_context(self.nc.sbuf_tensor(name, list(shape), dt))
        return Buf(name, t)

    def ps(self, st, name, shape, dt=F32):
        t = st.enter_context(self.nc.psum_tensor(name, list(shape), dt))
        return Buf(name, t)

    def _waits(self, eng, reads, writes):
        own = "E_" + eng
        need = {}
        for b in reads:
            for (k, v) in b.w:
                if k == own and (eng == "tensor" or v > self.cnt[eng]):
                    continue
                if need.get(k, 0) < v:
                    need[k] = v
        for b in writes:
            for (k, v) in b.w:
                if k == own and (eng == "tensor" or v > self.cnt[eng]):
                    continue
                if need.get(k, 0) < v:
                    need[k] = v
            for (k, v) in b.r:
                if k == own and (eng == "tensor" or v > self.cnt[eng]):
                    continue
                if need.get(k, 0) < v:
                    need[k] = v
        out = []
        seen = self.seen[eng]
        for k, v in need.items():
            if seen.get(k, 0) < v:
                seen[k] = v
                out.append((k, v))
        return out

    def op(self, eng, fn, reads=(), writes=(), inc=True):
        waits = self._waits(eng, reads, writes)
        key = "E_" + eng
        val = self.cnt[eng] + 1
        if inc:
            self.cnt[eng] = val
        ev = (key, val)
        for b in writes:
            b.w = [ev]
            b.r = []
        for b in reads:
            b.r = [e for e in b.r if e[0] != key] + [ev]
        self.q[eng].append((waits, fn, [(key, 1)] if inc else []))

    def dma(self, eng, fn, reads=(), writes=(), sem_buf=None):
        waits = self._waits(eng, reads, writes)
        sb = sem_buf or (writes[0] if writes else reads[0])
        if sb.dsem is None:
            sb.dsem = self._sem("D_%d" % self.nbuf)
            self.nbuf += 1
        sb.dcnt += 16
        ev = (sb.dsem, sb.dcnt)
        for b in writes:
            b.w = [e for e in b.w if e[0] != sb.dsem and e[0].startswith("D_")] + [ev]
            b.r = []
        for b in reads:
            b.r = [e for e in b.r if e[0] != sb.dsem] + [ev]
        self.q[eng].append((waits, fn, [(sb.dsem, 16)]))

    def wait_all(self, eng, bufs):
        waits = self._waits(eng, (), bufs)
        self.q[eng].append((waits, None, []))

    def emit(self):
        nc = self.nc
        qs = self.q
        self.q = {e: [] for e in self.ENGS}
        sems = self.sems
        with nc.Block() as block:
            def mk(ename):
                items = qs[ename]

                def body(e):
                    for waits, fn, incs in items:
                        for (k, v) in waits:
                            e.wait_ge(sems[k], v)
                        if fn is None:
                            continue
                        ins = fn(e)
                        for (k, n) in incs:
                            ins.then_inc(sems[k], n)
                return body
            block.tensor(mk("tensor"))
            block.vector(mk("vector"))
            block.scalar(mk("scalar"))
            block.gpsimd(mk("gpsimd"))
            block.sync(mk("sync"))


def _rope_tables():
    pos = np.arange(L, dtype=np.float32)
    C = np.ones((128, L), np.float32)
    Sg = np.zeros((128, L), np.float32)
    inv_d = (np.float32(THETA) ** (-np.arange(0, 16, 2, dtype=np.float32) / np.float32(16))).astype(np.float32)
    ang_d = (pos[:, None] * inv_d[None, :]).astype(np.float32)
    cd, sd = np.cos(ang_d).astype(np.float32), np.sin(ang_d).astype(np.float32)
    for base in (0, 64):
        for r in range(16):
            C[base + r] = cd[:, r % 8]
            Sg[base + r] = -sd[:, r % 8] if r < 8 else sd[:, r % 8]
    inv_m = (np.float32(THETA) ** (-np.arange(0, 32, 2, dtype=np.float32) / np.float32(32))).astype(np.float32)
    ang_m = (pos[:, None] * inv_m[None, :]).astype(np.float32)
    cm, sm = np.cos(ang_m).astype(np.float32), np.sin(ang_m).astype(np.float32)
    for r in range(32):
        C[32 + r] = cm[:, r % 16]
        Sg[32 + r] = -sm[:, r % 16] if r < 16 else sm[:, r % 16]
    return C, Sg


def _masks():
    a = np.arange(128)
    tri = (a[None, :] < a[:, None]).astype(np.float32)
    cq = (a + 48) // 64
    m0 = (cq[:, None] <= cq[None, :])
    m1 = ((a[:, None] < 16) & (a[None, :] >= 80))
    m01 = np.concatenate([m0, m1], axis=1).astype(np.float32).astype(ml_dtypes.bfloat16)
    ident = np.eye(128, dtype=np.float32).astype(ml_dtypes.bfloat16)
    return tri, m01, ident


O_SBQ, O_SBK, O_SBV, O_SBZ = 0, 512, 1024, 1536
O_CQ, O_CKV, O_KR, O_MZ = 2048, 2432, 2688, 2720
O_DQ, O_DK, O_DV, O_DZ = 3232, 3744, 4256, 4768
O_G = 5280


def _host_layouts(inp):
    w_in = inp["w_in"]
    swap64 = np.concatenate([np.arange(8, 16), np.arange(0, 8), np.arange(16, 64)])
    idx_d = np.concatenate([m * 64 + swap64 for m in range(8)])
    kr_sw = np.concatenate([np.arange(16, 32), np.arange(0, 16)])
    w_x = np.concatenate([w_in[:, :, O_DQ + idx_d], w_in[:, :, O_DK + idx_d], w_in[:, :, O_KR + kr_sw]], axis=2)
    uq = inp["mla_w_uq"]
    ia, ib = [], []
    for h in range(8):
        b = 96 * h
        ia += list(range(b, b + 32)) + list(range(b + 64, b + 96)) + list(range(b + 32, b + 64))
        ib += list(range(b, b + 32)) + list(range(b + 80, b + 96)) + list(range(b + 64, b + 80)) + list(range(b + 32, b + 64))
    uqa = uq[:, :, np.array(ia)]
    uqb = uq[:, :, np.array(ib)]
    ukv = inp["mla_w_ukv"]
    ikn, iv = [], []
    for h in range(8):
        b = 128 * h
        ikn += list(range(b, b + 32)) + list(range(b, b + 32)) + list(range(b + 32, b + 64))
        iv += list(range(b + 64, b + 128))
    ukn = ukv[:, :, np.array(ikn)]
    ukvv = ukv[:, :, np.array(iv)]
    bg = inp["b_gate"].reshape(2, 3, 8, 128).transpose(0, 1, 3, 2)
    return {
        "w_x": np.ascontiguousarray(w_x),
        "uqa": np.ascontiguousarray(uqa), "uqb": np.ascontiguousarray(uqb),
        "ukn": np.ascontiguousarray(ukn), "ukvv": np.ascontiguousarray(ukvv),
        "bg": np.ascontiguousarray(bg),
    }


def build_nc(nlayers=2, final_norm=True, branches=(0, 1, 2)):
    nc = bass.Bass("TRN2", target_bir_lowering=False)

    def din(name, shape, dt=F32):
        return nc.dram_tensor(name, list(shape), dt, kind="ExternalInput").ap()

    h0 = din("h0", [L, D])
    norm_g = din("norm_g", [2, D])
    w_in = din("w_in", [2, D, 8352])
    w_x = din("w_x", [2, D, 1056])
    bg = din("bg", [2, 3, 128, 8])
    cq_g = din("mla_cq_g", [2, 384])
    ckv_g = din("mla_ckv_g", [2, 256])
    uqa = din("uqa", [2, 384, 768])
    uqb = din("uqb", [2, 384, 768])
    ukn = din("ukn", [2, 256, 768])
    ukvv = din("ukvv", [2, 256, 512])
    dlam = din("diff_lambda", [2, 256])
    dng = din("diff_norm_g", [2, 128])
    w_o = [din("w_o_sb", [2, 512, D]), din("w_o_mla", [2, 512, D]), din("w_o_diff", [2, 512, D])]
    w_out = din("w_out", [2, D, D])
    final_g = din("final_g", [1, D])
    c_ropec = din("c_ropec", [128, L])
    c_ropes = din("c_ropes", [128, L])
    c_tri = din("c_tri", [128, 128])
    c_m01 = din("c_m01", [128, 256], BF16)
    c_ident = din("c_ident", [128, 128], BF16)
    y = nc.dram_tensor("y", [S, D], F32, kind="ExternalOutput").ap()

    with ExitStack() as top:
        P = Prog(nc, top)
        X = P.sb(top, "X", [128, NT, D], F32)
        hT = P.sb(top, "hT", [128, KC, L], BF16)
        og = P.sb(top, "og", [128, NT, 512], BF16)
        ropec = P.sb(top, "ropec", [128, L], F32)
        ropes = P.sb(top, "ropes", [128, L], F32)
        tri = P.sb(top, "tri", [128, 128], F32)
        m01 = P.sb(top, "m01", [128, 256], BF16)
        ident = P.sb(top, "ident", [128, 128], BF16)
        fb = [P.ps(top, "fb%d" % i, [128, 512], F32) for i in range(6)]
        tb = [P.ps(top, "tb%d" % i, [128, 1024], BF16) for i in range(2)]

        def MM(out, lhsT, rhs, start, stop, reads, writes, inc=True):
            P.op("tensor", lambda e: e.matmul(out, lhsT=lhsT, rhs=rhs, start=start, stop=stop), reads, writes, inc)

        def TR(out, in_, reads, writes, inc=True):
            P.op("tensor", lambda e: e.transpose(out=out, in_=in_, identity=ident[:]), list(reads) + [ident], writes, inc)

        def ACT(out, in_, func, reads, writes, bias=None, scale=None, accum_out=None):
            kw = {}
            if bias is not None:
                kw["bias"] = bias
            if scale is not None:
                kw["scale"] = scale
            if accum_out is not None:
                kw["accum_out"] = accum_out
            P.op("scalar", lambda e: e.activation(out=out, in_=in_, func=func, **kw), reads, writes)

        def TT(eng, out, in0, in1, op, reads, writes):
            P.op(eng, lambda e: e.tensor_tensor(out=out, in0=in0, in1=in1, op=op), reads, writes)

        def TS(eng, out, in0, s1, s2, op0, op1, reads, writes):
            if op1 is None:
                P.op(eng, lambda e: e.tensor_scalar(out=out, in0=in0, scalar1=s1, scalar2=None, op0=op0), reads, writes)
            else:
                P.op(eng, lambda e: e.tensor_scalar(out=out, in0=in0, scalar1=s1, scalar2=s2, op0=op0, op1=op1), reads, writes)

        def STT(eng, out, in0, scalar, in1, op0, op1, reads, writes):
            P.op(eng, lambda e: e.scalar_tensor_tensor(out=out, in0=in0, scalar=scalar, in1=in1, op0=op0, op1=op1), reads, writes)

        def RSTD(out, ss_ap, n, reads, writes, mult=1.0):
            ACT(out, ss_ap, AF.Ln, reads, writes, bias=EPS, scale=1.0 / n)
            ACT(out, out, AF.Exp, writes, writes, bias=(math.log(mult) if mult != 1.0 else None), scale=-0.5)

        def CP(eng, out, in_, reads, writes):
            if eng == "scalar":
                P.op(eng, lambda e: e.copy(out=out, in_=in_), reads, writes)
            else:
                P.op(eng, lambda e: e.tensor_copy(out=out, in_=in_), reads, writes)

        def MEMSET(eng, ap, val, writes):
            P.op(eng, lambda e: e.memset(ap, val), (), writes)

        def DMA(eng, out, in_, reads=(), writes=()):
            P.dma(eng, lambda e: e.dma_start(out=out, in_=in_), reads, writes)

        def load_w(buf, dram2d, k_chunks, c0, c1, dst_c0=0):
            v = dram2d.rearrange("(k p) c -> p k c", p=128)
            for k in range(k_chunks):
                DMA("gpsimd", buf[:, k, dst_c0:dst_c0 + (c1 - c0)], v[:, k, c0:c1], writes=[buf])

        def bcast_load(buf, row_ap, n):
            DMA("sync", buf[:], row_ap.to_broadcast([128, n]), writes=[buf])

        fctr = [0]

        def next_f(lo=0, hi=4):
            b = fb[lo + fctr[0] % (hi - lo)]
            fctr[0] += 1
            return b

        DMA("sync", ropec[:], c_ropec[:, :], writes=[ropec])
        DMA("sync", ropes[:], c_ropes[:, :], writes=[ropes])
        DMA("sync", tri[:], c_tri[:, :], writes=[tri])
        DMA("sync", m01[:], c_m01[:, :], writes=[m01])
        DMA("sync", ident[:], c_ident[:, :], writes=[ident])
        h0v = h0.rearrange("(t p) d -> p t d", p=128)
        for i in range(NT):
            DMA("sync", X[:, i, :], h0v[:, i, :], writes=[X])
        P.emit()

        def phase_norm(l):
            with ExitStack() as st:
                grep = P.sb(st, "grep", [128, D], F32)
                junk = P.sb(st, "junk", [128, D], BF16)
                ss = P.sb(st, "ss", [128, NT], F32)
                rs = P.sb(st, "rs", [128, NT], F32)
                hn = [P.sb(st, "hn%d" % j, [128, D], BF16) for j in range(2)]
                bcast_load(grep, norm_g[l:l + 1, :], D)
                MEMSET("vector", ss[:], 0.0, [ss])
                for i in range(NT):
                    ACT(junk[:], X[:, i, :], AF.Square, [X], [junk, ss], accum_out=ss[:, i:i + 1])
                    RSTD(rs[:, i:i + 1], ss[:, i:i + 1], D, [ss], [rs])
                    h = hn[i % 2]
                    STT("vector", h[:], X[:, i, :], rs[:, i:i + 1], grep[:], ALU.mult, ALU.mult, [X, rs, grep], [h])
                    t = tb[i % 2]
                    for k in range(KC):
                        TR(t[:, k * 128:(k + 1) * 128], h[:, k * 128:(k + 1) * 128], [h], [t], inc=(k == KC - 1))
                    CP("scalar", hT[:, :, i * 128:(i + 1) * 128], t[:].rearrange("p (k c) -> p k c", k=KC), [t], [hT])
                P.emit()

        def proj_tok(i, W, c0, n, kchunks=KC, src=None, src_k0=0):
            src = src or hT
            p = next_f()
            for k in range(kchunks):
                MM(p[:, 0:n], src[:, src_k0 + k, i * 128:(i + 1) * 128], W[:, k, c0:c0 + n], k == 0, k == kchunks - 1,
                   [src, W], [p], inc=(k == kchunks - 1))
            return p

        def proj_feat(p, c0, n, W, wc0, M, kchunks=KC, src=None, src_k0=0):
            src = src or hT
            for k in range(kchunks):
                MM(p[0:M, 0:n], W[:, k, wc0:wc0 + M], src[:, src_k0 + k, c0:c0 + n], k == 0, k == kchunks - 1,
                   [src, W], [p], inc=(k == kchunks - 1))

        def epilogue(l, b, zoff):
            with ExitStack() as st:
                Wz = P.sb(st, "Wz", [128, KC, 512], BF16)
                Wo = P.sb(st, "Wo", [128, 4, D], BF16)
                Wg = P.sb(st, "Wg", [128, KC, D], BF16)
                Wout = P.sb(st, "Wout", [128, KC, D], BF16)
                bgt = P.sb(st, "bgt", [128, 8], F32)
                G = [P.sb(st, "G%d" % j, [128, 512], BF16) for j in range(2)]
                ogg = [P.sb(st, "ogg%d" % j, [128, 512], BF16) for j in range(2)]
                oggT = P.sb(st, "oggT", [128, 4, 512], BF16)
                sg = [P.sb(st, "sg%d" % j, [128, 512], F32) for j in range(2)]
                mT = P.sb(st, "mT", [128, KC, 512], BF16)
                load_w(Wz, w_in[l], KC, zoff, zoff + 512)
                load_w(Wo, w_o[b][l], 4, 0, D)
                load_w(Wg, w_in[l], KC, O_G + b * D, O_G + (b + 1) * D)
                load_w(Wout, w_out[l], KC, 0, D)
                DMA("sync", bgt[:], bg[l, b, :, :], writes=[bgt])
                cnt = 0
                for (c0, n) in CHUNKS:
                    tiles = list(range(c0 // 128, (c0 + n) // 128))
                    for j, i in enumerate(tiles):
                        pz = proj_tok(i, Wz, 0, 512)
                        g_, o_ = G[cnt % 2], ogg[cnt % 2]
                        ACT(g_[:], pz[:, :], AF.Silu, [pz], [g_])
                        TT("gpsimd", o_[:], og[:, i, :], g_[:], ALU.mult, [og, g_], [o_])
                        t = tb[cnt % 2]
                        for c in range(4):
                            TR(t[:, c * 128:(c + 1) * 128], o_[:, c * 128:(c + 1) * 128], [o_], [t], inc=(c == 3))
                        CP("vector", oggT[:, :, j * 128:(j + 1) * 128], t[:, 0:512].rearrange("p (c q) -> p c q", c=4), [t], [oggT])
                        cnt += 1
                    for oc in range(8):
                        pg = fb[4 + oc % 2]
                        proj_feat(pg, c0, n, Wg, oc * 128, 128)
                        py = next_f()
                        for c in range(4):
                            MM(py[:, 0:n], Wo[:, c, oc * 128:(oc + 1) * 128], oggT[:, c, 0:n], c == 0, c == 3, [Wo, oggT], [py], inc=(c == 3))
                        s_ = sg[oc % 2]
                        ACT(s_[:, 0:n], pg[:, 0:n], AF.Sigmoid, [pg, bgt], [s_], bias=bgt[:, oc:oc + 1])
                        TT("vector", mT[:, oc, 0:n], s_[:, 0:n], py[:, 0:n], ALU.mult, [s_, py], [mT])
                    for j, i in enumerate(tiles):
                        for half in range(2):
                            po = next_f()
                            for k in range(KC):
                                MM(po[:, :], mT[:, k, j * 128:(j + 1) * 128], Wout[:, k, half * 512:(half + 1) * 512], k == 0, k == KC - 1,
                                   [mT, Wout], [po], inc=(k == KC - 1))
                            TT("vector", X[:, i, half * 512:(half + 1) * 512], X[:, i, half * 512:(half + 1) * 512], po[:, :], ALU.add, [X, po], [X])
                P.emit()

        def branch_sb(l):
            with ExitStack() as bst:
                V = P.sb(bst, "Vsb", [128, NT, 512], BF16)
                with ExitStack() as st:
                    Wv = P.sb(st, "Wv", [128, KC, 512], BF16)
                    load_w(Wv, w_in[l], KC, O_SBV, O_SBV + 512)
                    for i in range(NT):
                        p = proj_tok(i, Wv, 0, 512)
                        CP("scalar" if i % 2 else "vector", V[:, i, :], p[:, :], [p], [V])
                    P.emit()
                with ExitStack() as st:
                    Wqk = P.sb(st, "Wqk", [128, KC, 1024], BF16)
                    qT = P.sb(st, "qT", [128, L], BF16)
                    kT = P.sb(st, "kT", [128, L], BF16)
                    NS = 3
                    ez = [P.sb(st, "ez%d" % j, [128, 512], F32) for j in range(NS)]
                    sp = [P.sb(st, "sp%d" % j, [128, 512], F32) for j in range(NS)]
                    Cb = [P.sb(st, "Cb%d" % j, [128, 512], F32) for j in range(NS)]
                    wb = [P.sb(st, "wb%d" % j, [128, 512], BF16) for j in range(NS)]
                    wT = [P.sb(st, "wT%d" % j, [128, 512], BF16) for j in range(NS)]
                    cn = [P.sb(st, "cn%d" % j, [128, 1], F32) for j in range(NS)]
                    load_w(Wqk, w_in[l], KC, O_SBQ, O_SBQ + 1024)
                    for pr in range(4):
                        for (c0, n) in CHUNKS:
                            p = next_f(4, 6)
                            proj_feat(p, c0, n, Wqk, pr * 128, 128)
                            CP("scalar", qT[:, c0:c0 + n], p[:, 0:n], [p], [qT])
                            p = next_f(4, 6)
                            proj_feat(p, c0, n, Wqk, 512 + pr * 128, 128)
                            CP("vector", kT[:, c0:c0 + n], p[:, 0:n], [p], [kT])
                        items = []
                        for hh in range(2):
                            for i in range(NT):
                                nk = (i + 1) * 128
                                chs = [(k0, min(512, nk - k0)) for k0 in range(0, nk, 512)][::-1]
                                for ci, (k0, n) in enumerate(chs):
                                    items.append((hh, i, ci, k0, n, ci == len(chs) - 1))
                        N = len(items)
                        obank = {}
                        ocnt = [0]

                        def s1(j):
                            hh, i, ci, k0, n, last = items[j]
                            r0 = 64 * hh
                            z = fb[j % 2]
                            MM(z[:, 0:n], qT[r0:r0 + 64, i * 128:(i + 1) * 128], kT[r0:r0 + 64, k0:k0 + n], True, True, [qT, kT], [z])
                            s = j % NS
                            ACT(ez[s][:, 0:n], z[:, 0:n], AF.Exp, [z], [ez[s]], scale=0.125)
                            if ci == 0:
                                TT("gpsimd", ez[s][:, n - 128:n], ez[s][:, n - 128:n], tri[:], ALU.mult, [ez[s], tri], [ez[s]])
                            ACT(sp[s][:, 0:n], ez[s][:, 0:n], AF.Ln, [ez[s]], [sp[s]], bias=1.0)

                        def s2(j):
                            hh, i, ci, k0, n, last = items[j]
                            s = j % NS
                            P.op("vector", lambda e: e.tensor_tensor_scan(out=Cb[s][:, 0:n], data0=sp[s][:, 0:n], data1=sp[s][:, 0:n],
                                                                           initial=0.0, op0=ALU.add, op1=ALU.max), [sp[s]], [Cb[s]])
                            if ci == 0:
                                CP("vector", cn[s][:], Cb[s][:, n - 1:n], [Cb[s]], [cn[s]])
                            else:
                                sp_ = (j - 1) % NS
                                TT("vector", cn[s][:], cn[sp_][:], Cb[s][:, n - 1:n], ALU.add, [cn[sp_], Cb[s]], [cn[s]])
                            STT("vector", sp[s][:, 0:n], Cb[s][:, 0:n], cn[s][:, 0:1], sp[s][:, 0:n], ALU.subtract, ALU.subtract,
                                [Cb[s], cn[s], sp[s]], [sp[s]])

                        def s3(j):
                            hh, i, ci, k0, n, last = items[j]
                            h = 2 * pr + hh
                            s = j % NS
                            ACT(Cb[s][:, 0:n], sp[s][:, 0:n], AF.Exp, [sp[s]], [Cb[s]])
                            TT("gpsimd", wb[s][:, 0:n], ez[s][:, 0:n], Cb[s][:, 0:n], ALU.mult, [ez[s], Cb[s]], [wb[s]])
                            t = tb[j % 2]
                            nb = n // 128
                            for jb in range(nb):
                                TR(t[:, jb * 128:(jb + 1) * 128], wb[s][:, jb * 128:(jb + 1) * 128], [wb[s]], [t], inc=(jb == nb - 1))
                            CP("vector", wT[s][:, 0:n], t[:, 0:n], [t], [wT[s]])
                            if ci == 0:
                                obank[(hh, i)] = fb[2 + ocnt[0] % 2]
                                ocnt[0] += 1
                            O = obank[(hh, i)]
                            for jb in range(nb):
                                kb = k0 // 128 + jb
                                MM(O[:, 0:64], wT[s][:, jb * 128:(jb + 1) * 128], V[:, kb, h * 64:(h + 1) * 64],
                                   ci == 0 and jb == 0, last and jb == nb - 1, [wT[s], V], [O], inc=(jb == nb - 1))
                            if last:
                                CP("scalar", og[:, i, h * 64:(h + 1) * 64], O[:, 0:64], [O], [og])

                        for step in range(N + 2):
                            if step < N:
                                s1(step)
                            if 0 <= step - 1 < N:
                                s2(step - 1)
                            if 0 <= step - 2 < N:
                                s3(step - 2)
                    P.emit()
            epilogue(l, 0, O_SBZ)

        def softmax_attn(st, units, dv1, scale, finalize):
            NS = 3
            PT = [P.sb(st, "PT%d" % j, [128, 512], BF16) for j in range(NS)]
            items = []
            for i in range(NT):
                kbs = list(range(0, min(i + 2, NT)))
                groups = [kbs[a:a + 4] for a in range(0, len(kbs), 4)]
                for u in range(len(units)):
                    for gi, g in enumerate(groups):
                        items.append((i, u, g, gi == 0, gi == len(groups) - 1))
            N = len(items)
            nu = len(units)

            def s1(j):
                i, u, g, first, last = items[j]
                QTb, KTb, r0, nr, vfn = units[u]
                z = fb[j % 2]
                for a, kb in enumerate(g):
                    MM(z[:, a * 128:(a + 1) * 128], KTb[r0:r0 + nr, kb * 128:(kb + 1) * 128], QTb[r0:r0 + nr, i * 128:(i + 1) * 128],
                       True, True, [QTb, KTb], [z], inc=(a == len(g) - 1))

            def s2(j):
                i, u, g, first, last = items[j]
                z = fb[j % 2]
                s = j % NS
                n = len(g) * 128
                ACT(PT[s][:, 0:n], z[:, 0:n], AF.Exp, [z], [PT[s]], scale=scale)
                for a, kb in enumerate(g):
                    if kb == i:
                        TT("gpsimd", PT[s][:, a * 128:(a + 1) * 128], PT[s][:, a * 128:(a + 1) * 128], m01[:, 0:128], ALU.mult, [PT[s], m01], [PT[s]])
                    elif kb == i + 1:
                        TT("gpsimd", PT[s][:, a * 128:(a + 1) * 128], PT[s][:, a * 128:(a + 1) * 128], m01[:, 128:256], ALU.mult, [PT[s], m01], [PT[s]])

            def s3(j):
                i, u, g, first, last = items[j]
                QTb, KTb, r0, nr, vfn = units[u]
                s = j % NS
                O = fb[2 + u] if nu > 1 else fb[2 + i % 2]
                for a, kb in enumerate(g):
                    MM(O[:, 0:dv1], PT[s][:, a * 128:(a + 1) * 128], vfn(kb), first and a == 0, last and a == len(g) - 1,
                       [PT[s]], [O], inc=(a == len(g) - 1))
                if last and u == nu - 1:
                    finalize(i, [fb[2 + uu] for uu in range(nu)] if nu > 1 else [O])

            for step in range(N + 2):
                if step < N:
                    s1(step)
                if 0 <= step - 1 < N:
                    s2(step - 1)
                if 0 <= step - 2 < N:
                    s3(step - 2)

        def branch_mla(l):
            with ExitStack() as bst:
                cnT = P.sb(bst, "cnT", [128, 5, L], BF16)
                Va = P.sb(bst, "Va", [128, NT, 8, 68], BF16)
                KT = P.sb(bst, "KTm", [128, L], BF16)
                with ExitStack() as st:
                    W = P.sb(st, "Wm", [128, KC, 704], BF16)
                    Wv = P.sb(st, "Wukvv", [128, 2, 512], BF16)
                    gq = P.sb(st, "gq", [128, 640], F32)
                    junk = P.sb(st, "junkm", [128, 384], BF16)
                    ss = P.sb(st, "ssm", [128, 2 * NT], F32)
                    rs = P.sb(st, "rsm", [128, 2 * NT], F32)
                    cb = [P.sb(st, "cb%d" % j, [128, 640], BF16) for j in range(2)]
                    t1 = P.sb(st, "t1m", [128, 512], F32)
                    t2 = P.sb(st, "t2m", [128, 512], F32)
                    load_w(W, w_in[l], KC, O_CQ, O_CQ + 672)
                    load_w(W, w_x[l], KC, 1024, 1056, dst_c0=672)
                    load_w(Wv, ukvv[l], 2, 0, 512)
                    DMA("sync", gq[:, 0:384], cq_g[l:l + 1, :].to_broadcast([128, 384]), writes=[gq])
                    DMA("sync", gq[:, 384:640], ckv_g[l:l + 1, :].to_broadcast([128, 256]), writes=[gq])
                    MEMSET("vector", ss[:], 0.0, [ss])
                    MEMSET("gpsimd", Va[:].rearrange("p a b c -> p (a b c)"), 1.0, [Va])
                    for i in range(NT):
                        c = cb[i % 2]
                        for part, (wc0, n, dc0) in enumerate(((0, 384, 0), (384, 256, 384))):
                            p = proj_tok(i, W, wc0, n)
                            col = 2 * i + part
                            ACT(junk[:, 0:n], p[:, 0:n], AF.Square, [p], [junk, ss], accum_out=ss[:, col:col + 1])
                            RSTD(rs[:, col:col + 1], ss[:, col:col + 1], n, [ss], [rs])
                            STT("vector", c[:, dc0:dc0 + n], p[:, 0:n], rs[:, col:col + 1], gq[:, dc0:dc0 + n], ALU.mult, ALU.mult, [p, rs, gq], [c])
                        t = tb[i % 2]
                        for k in range(5):
                            TR(t[:, k * 128:(k + 1) * 128], c[:, k * 128:(k + 1) * 128], [c], [t], inc=(k == 4))
                        CP("scalar", cnT[:, :, i * 128:(i + 1) * 128], t[:, 0:640].rearrange("p (k c) -> p k c", k=5), [t], [cnT])
                    for (c0, n) in CHUNKS:
                        pa = fb[4]
                        pb = fb[5]
                        proj_feat(pa, c0, n, W, 608, 64)
                        proj_feat(pb, c0, n, W, 640, 64)
                        TT("vector", t1[32:64, 0:n], pa[32:64, 0:n], ropec[32:64, c0:c0 + n], ALU.mult, [pa, ropec], [t1])
                        TT("vector", t2[32:64, 0:n], pb[32:64, 0:n], ropes[32:64, c0:c0 + n], ALU.mult, [pb, ropes], [t2])
                        TT("vector", KT[32:64, c0:c0 + n], t1[32:64, 0:n], t2[32:64, 0:n], ALU.add, [t1, t2], [KT])
                    for i in range(NT):
                        p = proj_tok(i, Wv, 0, 512, kchunks=2, src=cnT, src_k0=3)
                        CP("scalar" if i % 2 else "vector", Va[:, i, :, 0:64], p[:, :].rearrange("p (h d) -> p h d", h=8), [p], [Va])
                    P.emit()
                with ExitStack() as st:
                    QT = P.sb(st, "QTm", [128, L], BF16)
                    Wa = P.sb(st, "Wuqa", [128, 3, 768], BF16)
                    Wb = P.sb(st, "Wuqb", [128, 3, 768], BF16)
                    Wk = P.sb(st, "Wukn", [128, 2, 768], BF16)
                    t1 = P.sb(st, "t1q", [128, 512], F32)
                    t2 = P.sb(st, "t2q", [128, 512], F32)
                    rcp = P.sb(st, "rcp", [128, 1], F32)
                    load_w(Wa, uqa[l], 3, 0, 768)
                    load_w(Wb, uqb[l], 3, 0, 768)
                    load_w(Wk, ukn[l], 2, 0, 768)
                    for h in range(8):
                        for (c0, n) in CHUNKS:
                            pa, pb = fb[4], fb[5]
                            proj_feat(pa, c0, n, Wa, h * 96, 96, kchunks=3, src=cnT)
                            proj_feat(pb, c0, n, Wb, h * 96, 96, kchunks=3, src=cnT)
                            CP("scalar", QT[0:32, c0:c0 + n], pa[0:32, 0:n], [pa], [QT])
                            CP("scalar", QT[64:96, c0:c0 + n], pa[64:96, 0:n], [pa], [QT])
                            TT("vector", t1[32:64, 0:n], pa[32:64, 0:n], ropec[32:64, c0:c0 + n], ALU.mult, [pa, ropec], [t1])
                            TT("vector", t2[32:64, 0:n], pb[32:64, 0:n], ropes[32:64, c0:c0 + n], ALU.mult, [pb, ropes], [t2])
                            TT("vector", QT[32:64, c0:c0 + n], t1[32:64, 0:n], t2[32:64, 0:n], ALU.add, [t1, t2], [QT])
                            pk = fb[4]
                            proj_feat(pk, c0, n, Wk, h * 96, 96, kchunks=2, src=cnT, src_k0=3)
                            CP("scalar", KT[0:32, c0:c0 + n], pk[0:32, 0:n], [pk], [KT])
                            CP("vector", KT[64:96, c0:c0 + n], pk[64:96, 0:n], [pk], [KT])

                        def fin(i, Os, h=h):
                            O = Os[0]
                            P.op("vector", lambda e: e.reciprocal(out=rcp[:], in_=O[:, 64:65]), [O], [rcp])
                            TS("vector", og[:, i, h * 64:(h + 1) * 64], O[:, 0:64], rcp[:, 0:1], None, ALU.mult, None, [O, rcp], [og])

                        with ExitStack() as st2:
                            softmax_attn(st2, [(QT, KT, 0, 96, (lambda kb, h=h: Va[:, kb, h, 0:65]))], 65, 1.0 / math.sqrt(96.0), fin)
                            P.emit()
            epilogue(l, 1, O_MZ)

        def branch_diff(l):
            lam_init = 0.8 - 0.6 * math.exp(-0.3 * l)
            with ExitStack() as bst:
                Vd = P.sb(bst, "Vd", [128, NT, 4, 132], BF16)
                lam = P.sb(bst, "lam", [128, 1], F32)
                gd = P.sb(bst, "gd", [128, 128], F32)
                with ExitStack() as st:
                    Wv = P.sb(st, "Wdv", [128, KC, 512], BF16)
                    dl = P.sb(st, "dl", [128, 256], F32)
                    pr_ = P.sb(st, "prd", [128, 128], F32)
                    sm = P.sb(st, "smd", [128, 2], F32)
                    load_w(Wv, w_in[l], KC, O_DV, O_DV + 512)
                    DMA("sync", dl[:], dlam[l:l + 1, :].to_broadcast([128, 256]), writes=[dl])
                    DMA("sync", gd[:], dng[l:l + 1, :].to_broadcast([128, 128]), writes=[gd])
                    dl3 = dl[:].rearrange("p (a b) -> p a b", a=2)
                    TT("vector", pr_[:].rearrange("p (a b) -> p a b", a=2), dl3[:, :, 0:64], dl3[:, :, 64:128], ALU.mult, [dl], [pr_])
                    P.op("vector", lambda e: e.reduce_sum(out=sm[:], in_=pr_[:].rearrange("p (a b) -> p a b", a=2), axis=mybir.AxisListType.X), [pr_], [sm])
                    ACT(sm[:], sm[:], AF.Exp, [sm], [sm])
                    TT("vector", lam[:], sm[:, 0:1], sm[:, 1:2], ALU.subtract, [sm], [lam])
                    TS("vector", lam[:], lam[:], lam_init, None, ALU.add, None, [lam], [lam])
                    MEMSET("gpsimd", Vd[:].rearrange("p a b c -> p (a b c)"), 1.0, [Vd])
                    for i in range(NT):
                        p = proj_tok(i, Wv, 0, 512)
                        CP("scalar" if i % 2 else "vector", Vd[:, i, :, 0:128], p[:, :].rearrange("p (h d) -> p h d", h=4), [p], [Vd])
                    P.emit()
                with ExitStack() as st:
                    Wq = P.sb(st, "Wdq", [128, KC, 512], BF16)
                    QT = P.sb(st, "QTd", [128, L], BF16)
                    KT = P.sb(st, "KTd", [128, L], BF16)
                    t1 = P.sb(st, "t1d", [128, 512], F32)
                    t2 = P.sb(st, "t2d", [128, 512], F32)
                    rc = P.sb(st, "rcd", [128, 2], F32)
                    tm = P.sb(st, "tmd", [128, 128], F32)
                    oc_ = P.sb(st, "ocd", [128, 128], F32)
                    jk = P.sb(st, "jkd", [128, 128], BF16)
                    ssd = P.sb(st, "ssd", [128, 1], F32)
                    rsd = P.sb(st, "rsd", [128, 1], F32)
                    import os as _os
                    _stop = int(_os.environ.get('DIFF_STOP', '9'))
                    for h in range(4 if _stop >= 4 else (1 if _stop >= 2 else 0)):
                        load_w(Wq, w_in[l], KC, O_DQ + h * 128, O_DQ + (h + 1) * 128, dst_c0=0)
                        load_w(Wq, w_x[l], KC, h * 128, (h + 1) * 128, dst_c0=128)
                        load_w(Wq, w_in[l], KC, O_DK + h * 128, O_DK + (h + 1) * 128, dst_c0=256)
                        load_w(Wq, w_x[l], KC, 512 + h * 128, 512 + (h + 1) * 128, dst_c0=384)
                        for (dst, wc) in ((QT, 0), (KT, 256)):
                            for (c0, n) in CHUNKS:
                                pa, pb = fb[4], fb[5]
                                proj_feat(pa, c0, n, Wq, wc, 128)
                                proj_feat(pb, c0, n, Wq, wc + 128, 128)
                                CP("scalar", dst[:, c0:c0 + n], pa[:, 0:n], [pa], [dst])
                                for r0 in (0, 64):
                                    TT("vector", t1[r0:r0 + 32, 0:n], pa[r0:r0 + 32, 0:n], ropec[r0:r0 + 32, c0:c0 + n], ALU.mult, [pa, ropec, dst], [t1])
                                    TT("vector", t2[r0:r0 + 32, 0:n], pb[r0:r0 + 32, 0:n], ropes[r0:r0 + 32, c0:c0 + n], ALU.mult, [pb, ropes, dst], [t2])
                                    TT("vector", dst[r0:r0 + 32, c0:c0 + n], t1[r0:r0 + 32, 0:n], t2[r0:r0 + 32, 0:n], ALU.add, [t1, t2], [dst])

                        def fin(i, Os, h=h):
                            O0, O1 = Os
                            P.op("vector", lambda e: e.reciprocal(out=rc[:, 0:1], in_=O0[:, 128:129]), [O0], [rc])
                            P.op("vector", lambda e: e.reciprocal(out=rc[:, 1:2], in_=O1[:, 128:129]), [O1], [rc])
                            TT("vector", rc[:, 1:2], rc[:, 1:2], lam[:, 0:1], ALU.mult, [rc, lam], [rc])
                            TS("vector", tm[:], O1[:, 0:128], rc[:, 1:2], None, ALU.mult, None, [O1, rc], [tm])
                            STT("vector", oc_[:], O0[:, 0:128], rc[:, 0:1], tm[:], ALU.mult, ALU.subtract, [O0, rc, tm], [oc_])
                            MEMSET("vector", ssd[:], 0.0, [ssd])
                            ACT(jk[:], oc_[:], AF.Square, [oc_], [jk, ssd], accum_out=ssd[:, 0:1])
                            RSTD(rsd[:], ssd[:], 128, [ssd], [rsd], mult=1.0 - lam_init)
                            STT("vector", og[:, i, h * 128:(h + 1) * 128], oc_[:], rsd[:, 0:1], gd[:], ALU.mult, ALU.mult, [oc_, rsd, gd], [og])

                        if _stop < 3:
                            P.emit()
                            continue
                        with ExitStack() as st2:
                            softmax_attn(st2, [(QT, KT, 0, 64, (lambda kb, h=h: Vd[:, kb, h, 0:129])),
                                               (QT, KT, 64, 64, (lambda kb, h=h: Vd[:, kb, h, 0:129]))], 129, 0.125, fin)
                            P.emit()
            epilogue(l, 2, O_DZ)

        for l in range(nlayers):
            phase_norm(l)
            if 0 in branches:
                branch_sb(l)
            if 1 in branches:
                branch_mla(l)
            if 2 in branches:
                branch_diff(l)

        with ExitStack() as st:
            grep = P.sb(st, "grepf", [128, D], F32)
            junk = P.sb(st, "junkf", [128, D], BF16)
            ss = P.sb(st, "ssf", [128, NT], F32)
            rs = P.sb(st, "rsf", [128, NT], F32)
            yo = [P.sb(st, "yo%d" % j, [128, D], F32) for j in range(2)]
            bcast_load(grep, final_g[0:1, :], D)
            MEMSET("vector", ss[:], 0.0, [ss])
            for i in range(NT):
                o = yo[i % 2]
                if final_norm:
                    ACT(junk[:], X[:, i, :], AF.Square, [X], [junk, ss], accum_out=ss[:, i:i + 1])
                    RSTD(rs[:, i:i + 1], ss[:, i:i + 1], D, [ss], [rs])
                    STT("vector", o[:], X[:, i, :], rs[:, i:i + 1], grep[:], ALU.mult, ALU.mult, [X, rs, grep], [o])
                else:
                    CP("vector", o[:], X[:, i, :], [X], [o])
                p_lo = NMETA if i == 0 else 0
                p_hi = NMETA if i == NT - 1 else 128
                s0 = 128 * i - NMETA + p_lo
                DMA("sync", y[s0:s0 + (p_hi - p_lo), :], o[p_lo:p_hi, :], reads=[o])
            P.wait_all("sync", yo)
            P.emit()
    return nc


_CACHE = {}


def _consts():
    if "c" not in _CACHE:
        C, Sg = _rope_tables()
        tri, m01, ident = _masks()
        _CACHE["c"] = {"c_ropec": C, "c_ropes": Sg, "c_tri": tri, "c_m01": m01, "c_ident": ident}
    return _CACHE["c"]


def make_in_maps(inp):
    x = np.asarray(inp["x"], np.float32)
    B = x.shape[0]
    meta = np.asarray(inp["meta_tokens"], np.float32)
    lay = _host_layouts({k: np.asarray(v) for k, v in inp.items()})
    shared = dict(_consts())
    shared.update(lay)
    for k in ("norm_g", "w_in", "mla_cq_g", "mla_ckv_g", "diff_norm_g", "w_o_sb", "w_o_mla", "w_o_diff", "w_out"):
        shared[k] = np.ascontiguousarray(np.asarray(inp[k], np.float32))
    shared["diff_lambda"] = np.ascontiguousarray(np.asarray(inp["diff_lambda"], np.float32).reshape(2, 256))
    shared["final_g"] = np.ascontiguousarray(np.asarray(inp["final_g"], np.float32).reshape(1, D))
    maps = []
    for b in range(B):
        h0 = np.concatenate([meta, x[b], np.zeros((L - NMETA - S, D), np.float32)], axis=0)
        m = dict(shared)
        m["h0"] = np.ascontiguousarray(h0)
        maps.append(m)
    return maps


def kernel(**inputs):
    maps = make_in_maps(inputs)
    if "nc" not in _CACHE:
        _CACHE["nc"] = build_nc()
    res = run_bass_kernel_spmd(_CACHE["nc"], maps, core_ids=list(range(len(maps))))
    return np.stack([np.asarray(r["y"], np.float32) for r in res.results], axis=0)
```

```python
import math
import numpy as np
import ml_dtypes
from contextlib import ExitStack
import concourse.bass as bass
import concourse.mybir as mybir
from concourse.bass_utils import run_bass_kernel_spmd

F32 = mybir.dt.float32
BF16 = mybir.dt.bfloat16
AF = mybir.ActivationFunctionType
ALU = mybir.AluOpType

D = 1024
S = 2048
NMETA = 16
NT = 17
L = NT * 128
KC = 8
EPS = 1e-6
THETA = 500000.0
CHUNKS = [(0, 512), (512, 512), (1024, 512), (1536, 512), (2048, 128)]


class Buf:
    __slots__ = ("name", "t", "w", "r", "dsem", "dcnt")

    def __init__(self, name, t=None):
        self.name = name
        self.t = t
        self.w = []
        self.r = []
        self.dsem = None
        self.dcnt = 0

    def __getitem__(self, idx):
        return self.t[idx]


class Prog:
    ENGS = ("tensor", "vector", "scalar", "gpsimd", "sync")

    def __init__(self, nc, stack):
        self.nc = nc
        self.stack = stack
        self.sems = {}
        self.cnt = {e: 0 for e in self.ENGS}
        self.seen = {e: {} for e in self.ENGS}
        self.q = {e: [] for e in self.ENGS}
        for e in self.ENGS:
            self._sem("E_" + e)
        self.nbuf = 0

    def _sem(self, key):
        if key not in self.sems:
            self.sems[key] = self.stack.enter_context(self.nc.semaphore(key))
        return key

    def sb(self, st, name, shape, dt):
        self.uid = getattr(self, "uid", 0) + 1
        name = "%s_%d" % (name, self.uid)
        t = st.enter_context(self.nc.sbuf_tensor(name, list(shape), dt))
        return Buf(name, t)

    def ps(self, st, name, shape, dt=F32):
        t = st.enter_context(self.nc.psum_tensor(name, list(shape), dt))
        return Buf(name, t)

    def _waits(self, eng, reads, writes):
        own = "E_" + eng
        need = {}
        for b in reads:
            for (k, v) in b.w:
                if k == own and (eng == "tensor" or v > self.cnt[eng]):
                    continue
                if need.get(k, 0) < v:
                    need[k] = v
        for b in writes:
            for (k, v) in b.w:
                if k == own and (eng == "tensor" or v > self.cnt[eng]):
                    continue
                if need.get(k, 0) < v:
                    need[k] = v
            for (k, v) in b.r:
                if k == own and (eng == "tensor" or v > self.cnt[eng]):
                    continue
                if need.get(k, 0) < v:
                    need[k] = v
        out = []
        seen = self.seen[eng]
        for k, v in need.items():
            if seen.get(k, 0) < v:
                seen[k] = v
                out.append((k, v))
        return out

    def op(self, eng, fn, reads=(), writes=(), inc=True):
        waits = self._waits(eng, reads, writes)
        key = "E_" + eng
        val = self.cnt[eng] + 1
        if inc:
            self.cnt[eng] = val
        ev = (key, val)
        for b in writes:
            b.w = [ev]
            b.r = []
        for b in reads:
            b.r = [e for e in b.r if e[0] != key] + [ev]
        self.q[eng].append((waits, fn, [(key, 1)] if inc else []))

    def dma(self, eng, fn, reads=(), writes=(), sem_buf=None):
        waits = self._waits(eng, reads, writes)
        sb = sem_buf or (writes[0] if writes else reads[0])
        if sb.dsem is None:
            sb.dsem = self._sem("D_%d" % self.nbuf)
            self.nbuf += 1
        sb.dcnt += 16
        ev = (sb.dsem, sb.dcnt)
        for b in writes:
            b.w = [e for e in b.w if e[0] != sb.dsem and e[0].startswith("D_")] + [ev]
            b.r = []
        for b in reads:
            b.r = [e for e in b.r if e[0] != sb.dsem] + [ev]
        self.q[eng].append((waits, fn, [(sb.dsem, 16)]))

    def wait_all(self, eng, bufs):
        waits = self._waits(eng, (), bufs)
        self.q[eng].append((waits, None, []))

    def emit(self):
        nc = self.nc
        qs = self.q
        self.q = {e: [] for e in self.ENGS}
        sems = self.sems
        with nc.Block() as block:
            def mk(ename):
                items = qs[ename]

                def body(e):
                    for waits, fn, incs in items:
                        for (k, v) in waits:
                            e.wait_ge(sems[k], v)
                        if fn is None:
                            continue
                        ins = fn(e)
                        for (k, n) in incs:
                            ins.then_inc(sems[k], n)
                return body
            block.tensor(mk("tensor"))
            block.vector(mk("vector"))
            block.scalar(mk("scalar"))
            block.gpsimd(mk("gpsimd"))
            block.sync(mk("sync"))


def _rope_tables():
    pos = np.arange(L, dtype=np.float32)
    C = np.ones((128, L), np.float32)
    Sg = np.zeros((128, L), np.float32)
    inv_d = (np.float32(THETA) ** (-np.arange(0, 16, 2, dtype=np.float32) / np.float32(16))).astype(np.float32)
    ang_d = (pos[:, None] * inv_d[None, :]).astype(np.float32)
    cd, sd = np.cos(ang_d).astype(np.float32), np.sin(ang_d).astype(np.float32)
    for base in (0, 64):
        for r in range(16):
            C[base + r] = cd[:, r % 8]
            Sg[base + r] = -sd[:, r % 8] if r < 8 else sd[:, r % 8]
    inv_m = (np.float32(THETA) ** (-np.arange(0, 32, 2, dtype=np.float32) / np.float32(32))).astype(np.float32)
    ang_m = (pos[:, None] * inv_m[None, :]).astype(np.float32)
    cm, sm = np.cos(ang_m).astype(np.float32), np.sin(ang_m).astype(np.float32)
    for r in range(32):
        C[32 + r] = cm[:, r % 16]
        Sg[32 + r] = -sm[:, r % 16] if r < 16 else sm[:, r % 16]
    return C, Sg


def _masks():
    a = np.arange(128)
    tri = (a[None, :] < a[:, None]).astype(np.float32)
    cq = (a + 48) // 64
    m0 = (cq[:, None] <= cq[None, :])
    m1 = ((a[:, None] < 16) & (a[None, :] >= 80))
    m01 = np.concatenate([m0, m1], axis=1).astype(np.float32).astype(ml_dtypes.bfloat16)
    ident = np.eye(128, dtype=np.float32).astype(ml_dtypes.bfloat16)
    return tri, m01, ident


O_SBQ, O_SBK, O_SBV, O_SBZ = 0, 512, 1024, 1536
O_CQ, O_CKV, O_KR, O_MZ = 2048, 2432, 2688, 2720
O_DQ, O_DK, O_DV, O_DZ = 3232, 3744, 4256, 4768
O_G = 5280


def _host_layouts(inp):
    w_in = inp["w_in"]
    swap64 = np.concatenate([np.arange(8, 16), np.arange(0, 8), np.arange(16, 64)])
    idx_d = np.concatenate([m * 64 + swap64 for m in range(8)])
    kr_sw = np.concatenate([np.arange(16, 32), np.arange(0, 16)])
    w_x = np.concatenate([w_in[:, :, O_DQ + idx_d], w_in[:, :, O_DK + idx_d], w_in[:, :, O_KR + kr_sw]], axis=2)
    uq = inp["mla_w_uq"]
    ia, ib = [], []
    for h in range(8):
        b = 96 * h
        ia += list(range(b, b + 32)) + list(range(b + 64, b + 96)) + list(range(b + 32, b + 64))
        ib += list(range(b, b + 32)) + list(range(b + 80, b + 96)) + list(range(b + 64, b + 80)) + list(range(b + 32, b + 64))
    uqa = uq[:, :, np.array(ia)]
    uqb = uq[:, :, np.array(ib)]
    ukv = inp["mla_w_ukv"]
    ikn, iv = [], []
    for h in range(8):
        b = 128 * h
        ikn += list(range(b, b + 32)) + list(range(b, b + 32)) + list(range(b + 32, b + 64))
        iv += list(range(b + 64, b + 128))
    ukn = ukv[:, :, np.array(ikn)]
    ukvv = ukv[:, :, np.array(iv)]
    bg = inp["b_gate"].reshape(2, 3, 8, 128).transpose(0, 1, 3, 2)
    return {
        "w_x": np.ascontiguousarray(w_x),
        "uqa": np.ascontiguousarray(uqa), "uqb": np.ascontiguousarray(uqb),
        "ukn": np.ascontiguousarray(ukn), "ukvv": np.ascontiguousarray(ukvv),
        "bg": np.ascontiguousarray(bg),
    }


def build_nc(nlayers=2, final_norm=True, branches=(0, 1, 2)):
    nc = bass.Bass("TRN2", target_bir_lowering=False)

    def din(name, shape, dt=F32):
        return nc.dram_tensor(name, list(shape), dt, kind="ExternalInput").ap()

    h0 = din("h0", [L, D])
    norm_g = din("norm_g", [2, D])
    w_in = din("w_in", [2, D, 8352])
    w_x = din("w_x", [2, D, 1056])
    bg = din("bg", [2, 3, 128, 8])
    cq_g = din("mla_cq_g", [2, 384])
    ckv_g = din("mla_ckv_g", [2, 256])
    uqa = din("uqa", [2, 384, 768])
    uqb = din("uqb", [2, 384, 768])
    ukn = din("ukn", [2, 256, 768])
    ukvv = din("ukvv", [2, 256, 512])
    dlam = din("diff_lambda", [2, 256])
    dng = din("diff_norm_g", [2, 128])
    w_o = [din("w_o_sb", [2, 512, D]), din("w_o_mla", [2, 512, D]), din("w_o_diff", [2, 512, D])]
    w_out = din("w_out", [2, D, D])
    final_g = din("final_g", [1, D])
    c_ropec = din("c_ropec", [128, L])
    c_ropes = din("c_ropes", [128, L])
    c_tri = din("c_tri", [128, 128])
    c_m01 = din("c_m01", [128, 256], BF16)
    c_ident = din("c_ident", [128, 128], BF16)
    y = nc.dram_tensor("y", [S, D], F32, kind="ExternalOutput").ap()

    with ExitStack() as top:
        P = Prog(nc, top)
        X = P.sb(top, "X", [128, NT, D], F32)
        hT = P.sb(top, "hT", [128, KC, L], BF16)
        og = P.sb(top, "og", [128, NT, 512], BF16)
        ropec = P.sb(top, "ropec", [128, L], F32)
        ropes = P.sb(top, "ropes", [128, L], F32)
        tri = P.sb(top, "tri", [128, 128], F32)
        m01 = P.sb(top, "m01", [128, 256], BF16)
        ident = P.sb(top, "ident", [128, 128], BF16)
        fb = [P.ps(top, "fb%d" % i, [128, 512], F32) for i in range(6)]
        tb = [P.ps(top, "tb%d" % i, [128, 1024], BF16) for i in range(2)]

        def MM(out, lhsT, rhs, start, stop, reads, writes, inc=True):
            P.op("tensor", lambda e: e.matmul(out, lhsT=lhsT, rhs=rhs, start=start, stop=stop), reads, writes, inc)

        def TR(out, in_, reads, writes, inc=True):
            P.op("tensor", lambda e: e.transpose(out=out, in_=in_, identity=ident[:]), list(reads) + [ident], writes, inc)

        def ACT(out, in_, func, reads, writes, bias=None, scale=None, accum_out=None):
            kw = {}
            if bias is not None:
                kw["bias"] = bias
            if scale is not None:
                kw["scale"] = scale
            if accum_out is not None:
                kw["accum_out"] = accum_out
            P.op("scalar", lambda e: e.activation(out=out, in_=in_, func=func, **kw), reads, writes)

        def TT(eng, out, in0, in1, op, reads, writes):
            P.op(eng, lambda e: e.tensor_tensor(out=out, in0=in0, in1=in1, op=op), reads, writes)

        def TS(eng, out, in0, s1, s2, op0, op1, reads, writes):
            if op1 is None:
                P.op(eng, lambda e: e.tensor_scalar(out=out, in0=in0, scalar1=s1, scalar2=None, op0=op0), reads, writes)
            else:
                P.op(eng, lambda e: e.tensor_scalar(out=out, in0=in0, scalar1=s1, scalar2=s2, op0=op0, op1=op1), reads, writes)

        def STT(eng, out, in0, scalar, in1, op0, op1, reads, writes):
            P.op(eng, lambda e: e.scalar_tensor_tensor(out=out, in0=in0, scalar=scalar, in1=in1, op0=op0, op1=op1), reads, writes)

        def RSTD(out, ss_ap, n, reads, writes, mult=1.0):
            ACT(out, ss_ap, AF.Ln, reads, writes, bias=EPS, scale=1.0 / n)
            ACT(out, out, AF.Exp, writes, writes, bias=(math.log(mult) if mult != 1.0 else None), scale=-0.5)

        def CP(eng, out, in_, reads, writes):
            if eng == "scalar":
                P.op(eng, lambda e: e.copy(out=out, in_=in_), reads, writes)
            else:
                P.op(eng, lambda e: e.tensor_copy(out=out, in_=in_), reads, writes)

        def MEMSET(eng, ap, val, writes):
            P.op(eng, lambda e: e.memset(ap, val), (), writes)

        def DMA(eng, out, in_, reads=(), writes=()):
            P.dma(eng, lambda e: e.dma_start(out=out, in_=in_), reads, writes)

        def load_w(buf, dram2d, k_chunks, c0, c1, dst_c0=0):
            v = dram2d.rearrange("(k p) c -> p k c", p=128)
            DMA("gpsimd", buf[:, 0:k_chunks, dst_c0:dst_c0 + (c1 - c0)], v[:, :, c0:c1], writes=[buf])

        def bcast_load(buf, row_ap, n):
            DMA("sync", buf[:], row_ap.to_broadcast([128, n]), writes=[buf])

        fctr = [0]

        def next_f(lo=0, hi=4):
            b = fb[lo + fctr[0] % (hi - lo)]
            fctr[0] += 1
            return b

        DMA("sync", ropec[:], c_ropec[:, :], writes=[ropec])
        DMA("sync", ropes[:], c_ropes[:, :], writes=[ropes])
        DMA("sync", tri[:], c_tri[:, :], writes=[tri])
        DMA("sync", m01[:], c_m01[:, :], writes=[m01])
        DMA("sync", ident[:], c_ident[:, :], writes=[ident])
        h0v = h0.rearrange("(t p) d -> p t d", p=128)
        for i in range(NT):
            DMA("sync", X[:, i, :], h0v[:, i, :], writes=[X])
        P.emit()

        def phase_norm(l):
            with ExitStack() as st:
                grep = P.sb(st, "grep", [128, D], F32)
                junk = P.sb(st, "junk", [128, D], BF16)
                ss = P.sb(st, "ss", [128, NT], F32)
                rs = P.sb(st, "rs", [128, NT], F32)
                hn = [P.sb(st, "hn%d" % j, [128, D], BF16) for j in range(2)]
                bcast_load(grep, norm_g[l:l + 1, :], D)
                MEMSET("vector", ss[:], 0.0, [ss])
                for i in range(NT):
                    ACT(junk[:], X[:, i, :], AF.Square, [X], [junk, ss], accum_out=ss[:, i:i + 1])
                    RSTD(rs[:, i:i + 1], ss[:, i:i + 1], D, [ss], [rs])
                    h = hn[i % 2]
                    STT("vector", h[:], X[:, i, :], rs[:, i:i + 1], grep[:], ALU.mult, ALU.mult, [X, rs, grep], [h])
                    t = tb[i % 2]
                    for k in range(KC):
                        TR(t[:, k * 128:(k + 1) * 128], h[:, k * 128:(k + 1) * 128], [h], [t], inc=(k == KC - 1))
                    CP("scalar", hT[:, :, i * 128:(i + 1) * 128], t[:].rearrange("p (k c) -> p k c", k=KC), [t], [hT])
                P.emit()

        def proj_tok(i, W, c0, n, kchunks=KC, src=None, src_k0=0):
            src = src or hT
            p = next_f()
            for k in range(kchunks):
                MM(p[:, 0:n], src[:, src_k0 + k, i * 128:(i + 1) * 128], W[:, k, c0:c0 + n], k == 0, k == kchunks - 1,
                   [src, W], [p], inc=(k == kchunks - 1))
            return p

        def proj_feat(p, c0, n, W, wc0, M, kchunks=KC, src=None, src_k0=0):
            src = src or hT
            for k in range(kchunks):
                MM(p[0:M, 0:n], W[:, k, wc0:wc0 + M], src[:, src_k0 + k, c0:c0 + n], k == 0, k == kchunks - 1,
                   [src, W], [p], inc=(k == kchunks - 1))

        def epilogue(l, b, zoff):
            with ExitStack() as st:
                Wz = P.sb(st, "Wz", [128, KC, 512], BF16)
                Wo = P.sb(st, "Wo", [128, 4, D], BF16)
                Wg = P.sb(st, "Wg", [128, KC, D], BF16)
                Wout = P.sb(st, "Wout", [128, KC, D], BF16)
                bgt = P.sb(st, "bgt", [128, 8], F32)
                G = [P.sb(st, "G%d" % j, [128, 512], BF16) for j in range(2)]
                ogg = [P.sb(st, "ogg%d" % j, [128, 512], BF16) for j in range(4)]
                oggT = P.sb(st, "oggT", [128, 4, 512], BF16)
                sg = [P.sb(st, "sg%d" % j, [128, 512], F32) for j in range(2)]
                mT = P.sb(st, "mT", [128, KC, 512], BF16)
                load_w(Wz, w_in[l], KC, zoff, zoff + 512)
                load_w(Wo, w_o[b][l], 4, 0, D)
                load_w(Wg, w_in[l], KC, O_G + b * D, O_G + (b + 1) * D)
                load_w(Wout, w_out[l], KC, 0, D)
                DMA("sync", bgt[:], bg[l, b, :, :], writes=[bgt])
                cnt = 0
                for (c0, n) in CHUNKS:
                    tiles = list(range(c0 // 128, (c0 + n) // 128))
                    pzs = [proj_tok(i, Wz, 0, 512) for i in tiles]
                    for oc in range(2):
                        proj_feat(fb[4 + oc], c0, n, Wg, oc * 128, 128)
                    for j, i in enumerate(tiles):
                        pz = pzs[j]
                        g_, o_ = G[cnt % 2], ogg[cnt % 4]
                        ACT(g_[:], pz[:, :], AF.Silu, [pz], [g_])
                        TT("vector", o_[:], og[:, i, :], g_[:], ALU.mult, [og, g_], [o_])
                        cnt += 1
                        t = tb[j % 2]
                        for c in range(4):
                            TR(t[:, c * 128:(c + 1) * 128], o_[:, c * 128:(c + 1) * 128], [o_], [t], inc=(c == 3))
                        CP("vector", oggT[:, :, j * 128:(j + 1) * 128], t[:, 0:512].rearrange("p (c q) -> p c q", c=4), [t], [oggT])
                    for oc in range(8):
                        pg = fb[4 + oc % 2]
                        if oc >= 2:
                            proj_feat(pg, c0, n, Wg, oc * 128, 128)
                        py = next_f()
                        for c in range(4):
                            MM(py[:, 0:n], Wo[:, c, oc * 128:(oc + 1) * 128], oggT[:, c, 0:n], c == 0, c == 3, [Wo, oggT], [py], inc=(c == 3))
                        s_ = sg[oc % 2]
                        ACT(s_[:, 0:n], pg[:, 0:n], AF.Sigmoid, [pg, bgt], [s_], bias=bgt[:, oc:oc + 1])
                        TT("vector", mT[:, oc, 0:n], s_[:, 0:n], py[:, 0:n], ALU.mult, [s_, py], [mT])
                    for j, i in enumerate(tiles):
                        for half in range(2):
                            po = next_f()
                            for k in range(KC):
                                MM(po[:, :], mT[:, k, j * 128:(j + 1) * 128], Wout[:, k, half * 512:(half + 1) * 512], k == 0, k == KC - 1,
                                   [mT, Wout], [po], inc=(k == KC - 1))
                            TT("vector", X[:, i, half * 512:(half + 1) * 512], X[:, i, half * 512:(half + 1) * 512], po[:, :], ALU.add, [X, po], [X])
                P.emit()

        def branch_sb(l):
            with ExitStack() as bst:
                V = P.sb(bst, "Vsb", [128, NT, 512], BF16)
                with ExitStack() as st:
                    Wv = P.sb(st, "Wv", [128, KC, 512], BF16)
                    load_w(Wv, w_in[l], KC, O_SBV, O_SBV + 512)
                    for i in range(NT):
                        p = proj_tok(i, Wv, 0, 512)
                        CP("scalar" if i % 2 else "vector", V[:, i, :], p[:, :], [p], [V])
                    P.emit()
                with ExitStack() as st:
                    Wqk = P.sb(st, "Wqk", [128, KC, 256], BF16)
                    qT = P.sb(st, "qT", [128, L], BF16)
                    kT = P.sb(st, "kT", [128, L], BF16)
                    NE, NSP, NC = 5, 4, 4
                    ez = [P.sb(st, "ez%d" % j, [128, 512], F32) for j in range(NE)]
                    sp = [P.sb(st, "sp%d" % j, [128, 512], F32) for j in range(NSP)]
                    Cb = [P.sb(st, "Cb%d" % j, [128, 512], F32) for j in range(NC)]
                    wb = [P.sb(st, "wb%d" % j, [128, 512], BF16) for j in range(2)]
                    wT = [P.sb(st, "wT%d" % j, [128, 512], BF16) for j in range(2)]
                    cn = [P.sb(st, "cn%d" % j, [128, 1], F32) for j in range(3)]
                    for pr in range(4):
                        load_w(Wqk, w_in[l], KC, O_SBQ + pr * 128, O_SBQ + (pr + 1) * 128, dst_c0=0)
                        load_w(Wqk, w_in[l], KC, O_SBK + pr * 128, O_SBK + (pr + 1) * 128, dst_c0=128)
                        for (c0, n) in CHUNKS:
                            p = next_f(4, 6)
                            proj_feat(p, c0, n, Wqk, 0, 128)
                            CP("scalar", qT[:, c0:c0 + n], p[:, 0:n], [p], [qT])
                            p = next_f(4, 6)
                            proj_feat(p, c0, n, Wqk, 128, 128)
                            CP("vector", kT[:, c0:c0 + n], p[:, 0:n], [p], [kT])
                        items = []
                        for hh in range(2):
                            for i in range(NT):
                                nk = (i + 1) * 128
                                chs = [(k0, min(512, nk - k0)) for k0 in range(0, nk, 512)][::-1]
                                for ci, (k0, n) in enumerate(chs):
                                    items.append((hh, i, ci, k0, n, ci == len(chs) - 1))
                        N = len(items)
                        obank = {}
                        ocnt = [0]

                        def st_mm(j):
                            hh, i, ci, k0, n, last = items[j]
                            r0 = 64 * hh
                            z = fb[j % 2]
                            MM(z[:, 0:n], qT[r0:r0 + 64, i * 128:(i + 1) * 128], kT[r0:r0 + 64, k0:k0 + n], True, True, [qT, kT], [z])

                        def st_expz(j):
                            hh, i, ci, k0, n, last = items[j]
                            e_ = ez[j % NE]
                            ACT(e_[:, 0:n], fb[j % 2][:, 0:n], AF.Exp, [fb[j % 2]], [e_], scale=0.125)
                            if ci == 0:
                                TT("gpsimd", e_[:, n - 128:n], e_[:, n - 128:n], tri[:], ALU.mult, [e_, tri], [e_])

                        def st_ln(j):
                            hh, i, ci, k0, n, last = items[j]
                            ACT(sp[j % NSP][:, 0:n], ez[j % NE][:, 0:n], AF.Ln, [ez[j % NE]], [sp[j % NSP]], bias=1.0)

                        def st_scan(j):
                            hh, i, ci, k0, n, last = items[j]
                            s_, c_ = sp[j % NSP], Cb[j % NC]
                            P.op("vector", lambda e: e.tensor_tensor_scan(out=c_[:, 0:n], data0=s_[:, 0:n], data1=s_[:, 0:n],
                                                                           initial=0.0, op0=ALU.add, op1=ALU.max), [s_], [c_])

                        def st_cn(j):
                            hh, i, ci, k0, n, last = items[j]
                            s_, c_ = sp[j % NSP], Cb[j % NC]
                            if ci == 0:
                                CP("vector", cn[j % 3][:], c_[:, n - 1:n], [c_], [cn[j % 3]])
                            else:
                                TT("vector", cn[j % 3][:], cn[(j - 1) % 3][:], c_[:, n - 1:n], ALU.add, [cn[(j - 1) % 3], c_], [cn[j % 3]])

                        def st_stt(j):
                            hh, i, ci, k0, n, last = items[j]
                            s_, c_ = sp[j % NSP], Cb[j % NC]
                            STT("vector", s_[:, 0:n], c_[:, 0:n], cn[j % 3][:, 0:1], s_[:, 0:n], ALU.subtract, ALU.subtract,
                                [c_, cn[j % 3], s_], [s_])

                        def st_expt(j):
                            hh, i, ci, k0, n, last = items[j]
                            ACT(Cb[j % NC][:, 0:n], sp[j % NSP][:, 0:n], AF.Exp, [sp[j % NSP]], [Cb[j % NC]])

                        def st_mult(j):
                            hh, i, ci, k0, n, last = items[j]
                            TT("gpsimd", wb[j % 2][:, 0:n], ez[j % NE][:, 0:n], Cb[j % NC][:, 0:n], ALU.mult, [ez[j % NE], Cb[j % NC]], [wb[j % 2]])

                        def st_tr(j):
                            hh, i, ci, k0, n, last = items[j]
                            t = tb[j % 2]
                            nb = n // 128
                            for jb in range(nb):
                                TR(t[:, jb * 128:(jb + 1) * 128], wb[j % 2][:, jb * 128:(jb + 1) * 128], [wb[j % 2]], [t], inc=(jb == nb - 1))

                        def st_evac(j):
                            hh, i, ci, k0, n, last = items[j]
                            CP("scalar", wT[j % 2][:, 0:n], tb[j % 2][:, 0:n], [tb[j % 2]], [wT[j % 2]])

                        def st_pv(j):
                            hh, i, ci, k0, n, last = items[j]
                            h = 2 * pr + hh
                            nb = n // 128
                            if ci == 0:
                                obank[(hh, i)] = fb[2 + ocnt[0] % 2]
                                ocnt[0] += 1
                            O = obank[(hh, i)]
                            for jb in range(nb):
                                kb = k0 // 128 + jb
                                MM(O[:, 0:64], wT[j % 2][:, jb * 128:(jb + 1) * 128], V[:, kb, h * 64:(h + 1) * 64],
                                   ci == 0 and jb == 0, last and jb == nb - 1, [wT[j % 2], V], [O], inc=(jb == nb - 1))
                            if last:
                                CP("vector", og[:, i, h * 64:(h + 1) * 64], O[:, 0:64], [O], [og])

                        sched = [(st_mm, 0), (st_expz, 1), (st_expt, 4), (st_ln, 1), (st_evac, 7), (st_cn, 3), (st_scan, 2), (st_stt, 3),
                                 (st_mult, 5), (st_tr, 6), (st_pv, 8)]
                        for step in range(N + 8):
                            for fn, off in sched:
                                if 0 <= step - off < N:
                                    fn(step - off)
                    P.emit()
            epilogue(l, 0, O_SBZ)

        def softmax_attn(st, units, dv1, scale, finalize):
            NS = 3
            PT = [P.sb(st, "PT%d" % j, [128, 512], BF16) for j in range(NS)]
            items = []
            for i in range(NT):
                kbs = list(range(0, min(i + 2, NT)))
                groups = [kbs[a:a + 4] for a in range(0, len(kbs), 4)]
                for u in range(len(units)):
                    for gi, g in enumerate(groups):
                        items.append((i, u, g, gi == 0, gi == len(groups) - 1))
            N = len(items)
            nu = len(units)

            def s1(j):
                i, u, g, first, last = items[j]
                QTb, KTb, r0, nr, vfn = units[u]
                z = fb[j % 2]
                for a, kb in enumerate(g):
                    MM(z[:, a * 128:(a + 1) * 128], KTb[r0:r0 + nr, kb * 128:(kb + 1) * 128], QTb[r0:r0 + nr, i * 128:(i + 1) * 128],
                       True, True, [QTb, KTb], [z], inc=(a == len(g) - 1))

            def s2(j):
                i, u, g, first, last = items[j]
                z = fb[j % 2]
                s = j % NS
                n = len(g) * 128
                ACT(PT[s][:, 0:n], z[:, 0:n], AF.Exp, [z], [PT[s]], scale=scale)
                for a, kb in enumerate(g):
                    if kb == i:
                        TT("gpsimd", PT[s][:, a * 128:(a + 1) * 128], PT[s][:, a * 128:(a + 1) * 128], m01[:, 0:128], ALU.mult, [PT[s], m01], [PT[s]])
                    elif kb == i + 1:
                        TT("gpsimd", PT[s][:, a * 128:(a + 1) * 128], PT[s][:, a * 128:(a + 1) * 128], m01[:, 128:256], ALU.mult, [PT[s], m01], [PT[s]])

            def s3(j):
                i, u, g, first, last = items[j]
                QTb, KTb, r0, nr, vfn = units[u]
                s = j % NS
                O = fb[2 + u] if nu > 1 else fb[2 + i % 2]
                for a, kb in enumerate(g):
                    MM(O[:, 0:dv1], PT[s][:, a * 128:(a + 1) * 128], vfn(kb), first and a == 0, last and a == len(g) - 1,
                       [PT[s]], [O], inc=(a == len(g) - 1))
                if last and u == nu - 1:
                    finalize(i, [fb[2 + uu] for uu in range(nu)] if nu > 1 else [O])

            for step in range(N + 2):
                if step < N:
                    s1(step)
                if 0 <= step - 1 < N:
                    s2(step - 1)
                if 0 <= step - 2 < N:
                    s3(step - 2)

        def branch_mla(l):
            with ExitStack() as bst:
                cnT = P.sb(bst, "cnT", [128, 5, L], BF16)
                Va = P.sb(bst, "Va", [128, NT, 8, 68], BF16)
                KT = P.sb(bst, "KTm", [128, L], BF16)
                with ExitStack() as st:
                    W = P.sb(st, "Wm", [128, KC, 704], BF16)
                    Wv = P.sb(st, "Wukvv", [128, 2, 512], BF16)
                    gq = P.sb(st, "gq", [128, 640], F32)
                    junk = P.sb(st, "junkm", [128, 384], BF16)
                    ss = P.sb(st, "ssm", [128, 2 * NT], F32)
                    rs = P.sb(st, "rsm", [128, 2 * NT], F32)
                    cb = [P.sb(st, "cb%d" % j, [128, 640], BF16) for j in range(2)]
                    t1 = P.sb(st, "t1m", [128, 512], F32)
                    t2 = P.sb(st, "t2m", [128, 512], F32)
                    load_w(W, w_in[l], KC, O_CQ, O_CQ + 672)
                    load_w(W, w_x[l], KC, 1024, 1056, dst_c0=672)
                    load_w(Wv, ukvv[l], 2, 0, 512)
                    DMA("sync", gq[:, 0:384], cq_g[l:l + 1, :].to_broadcast([128, 384]), writes=[gq])
                    DMA("sync", gq[:, 384:640], ckv_g[l:l + 1, :].to_broadcast([128, 256]), writes=[gq])
                    MEMSET("vector", ss[:], 0.0, [ss])
                    MEMSET("gpsimd", Va[:].rearrange("p a b c -> p (a b c)"), 1.0, [Va])
                    for i in range(NT):
                        c = cb[i % 2]
                        for part, (wc0, n, dc0) in enumerate(((0, 384, 0), (384, 256, 384))):
                            p = proj_tok(i, W, wc0, n)
                            col = 2 * i + part
                            ACT(junk[:, 0:n], p[:, 0:n], AF.Square, [p], [junk, ss], accum_out=ss[:, col:col + 1])
                            RSTD(rs[:, col:col + 1], ss[:, col:col + 1], n, [ss], [rs])
                            STT("vector", c[:, dc0:dc0 + n], p[:, 0:n], rs[:, col:col + 1], gq[:, dc0:dc0 + n], ALU.mult, ALU.mult, [p, rs, gq], [c])
                        t = tb[i % 2]
                        for k in range(5):
                            TR(t[:, k * 128:(k + 1) * 128], c[:, k * 128:(k + 1) * 128], [c], [t], inc=(k == 4))
                        CP("scalar", cnT[:, :, i * 128:(i + 1) * 128], t[:, 0:640].rearrange("p (k c) -> p k c", k=5), [t], [cnT])
                    for (c0, n) in CHUNKS:
                        pa = fb[4]
                        pb = fb[5]
                        proj_feat(pa, c0, n, W, 608, 64)
                        proj_feat(pb, c0, n, W, 640, 64)
                        TT("vector", t1[32:64, 0:n], pa[32:64, 0:n], ropec[32:64, c0:c0 + n], ALU.mult, [pa, ropec], [t1])
                        TT("vector", t2[32:64, 0:n], pb[32:64, 0:n], ropes[32:64, c0:c0 + n], ALU.mult, [pb, ropes], [t2])
                        TT("vector", KT[32:64, c0:c0 + n], t1[32:64, 0:n], t2[32:64, 0:n], ALU.add, [t1, t2], [KT])
                    for i in range(NT):
                        p = proj_tok(i, Wv, 0, 512, kchunks=2, src=cnT, src_k0=3)
                        CP("scalar" if i % 2 else "vector", Va[:, i, :, 0:64], p[:, :].rearrange("p (h d) -> p h d", h=8), [p], [Va])
                    P.emit()
                with ExitStack() as st:
                    QT = P.sb(st, "QTm", [128, L], BF16)
                    Wa = P.sb(st, "Wuqa", [128, 3, 768], BF16)
                    Wb = P.sb(st, "Wuqb", [128, 3, 768], BF16)
                    Wk = P.sb(st, "Wukn", [128, 2, 768], BF16)
                    t1 = P.sb(st, "t1q", [128, 512], F32)
                    t2 = P.sb(st, "t2q", [128, 512], F32)
                    rcp = P.sb(st, "rcp", [128, 1], F32)
                    load_w(Wa, uqa[l], 3, 0, 768)
                    load_w(Wb, uqb[l], 3, 0, 768)
                    load_w(Wk, ukn[l], 2, 0, 768)
                    for h in range(8):
                        for (c0, n) in CHUNKS:
                            pa, pb = fb[4], fb[5]
                            proj_feat(pa, c0, n, Wa, h * 96, 96, kchunks=3, src=cnT)
                            proj_feat(pb, c0, n, Wb, h * 96, 96, kchunks=3, src=cnT)
                            CP("scalar", QT[0:32, c0:c0 + n], pa[0:32, 0:n], [pa], [QT])
                            CP("scalar", QT[64:96, c0:c0 + n], pa[64:96, 0:n], [pa], [QT])
                            TT("vector", t1[32:64, 0:n], pa[32:64, 0:n], ropec[32:64, c0:c0 + n], ALU.mult, [pa, ropec], [t1])
                            TT("vector", t2[32:64, 0:n], pb[32:64, 0:n], ropes[32:64, c0:c0 + n], ALU.mult, [pb, ropes], [t2])
                            TT("vector", QT[32:64, c0:c0 + n], t1[32:64, 0:n], t2[32:64, 0:n], ALU.add, [t1, t2], [QT])
                            pk = fb[4]
                            proj_feat(pk, c0, n, Wk, h * 96, 96, kchunks=2, src=cnT, src_k0=3)
                            CP("scalar", KT[0:32, c0:c0 + n], pk[0:32, 0:n], [pk], [KT])
                            CP("vector", KT[64:96, c0:c0 + n], pk[64:96, 0:n], [pk], [KT])

                        def fin(i, Os, h=h):
                            O = Os[0]
                            P.op("vector", lambda e: e.reciprocal(out=rcp[:], in_=O[:, 64:65]), [O], [rcp])
                            TS("vector", og[:, i, h * 64:(h + 1) * 64], O[:, 0:64], rcp[:, 0:1], None, ALU.mult, None, [O, rcp], [og])

                        with ExitStack() as st2:
                            softmax_attn(st2, [(QT, KT, 0, 96, (lambda kb, h=h: Va[:, kb, h, 0:65]))], 65, 1.0 / math.sqrt(96.0), fin)
                            P.emit()
            epilogue(l, 1, O_MZ)

        def branch_diff(l):
            lam_init = 0.8 - 0.6 * math.exp(-0.3 * l)
            with ExitStack() as bst:
                Vd = P.sb(bst, "Vd", [128, NT, 4, 132], BF16)
                lam = P.sb(bst, "lam", [128, 1], F32)
                gd = P.sb(bst, "gd", [128, 128], F32)
                with ExitStack() as st:
                    Wv = P.sb(st, "Wdv", [128, KC, 512], BF16)
                    dl = P.sb(st, "dl", [128, 256], F32)
                    pr_ = P.sb(st, "prd", [128, 128], F32)
                    sm = P.sb(st, "smd", [128, 2], F32)
                    load_w(Wv, w_in[l], KC, O_DV, O_DV + 512)
                    DMA("sync", dl[:], dlam[l:l + 1, :].to_broadcast([128, 256]), writes=[dl])
                    DMA("sync", gd[:], dng[l:l + 1, :].to_broadcast([128, 128]), writes=[gd])
                    dl3 = dl[:].rearrange("p (a b) -> p a b", a=2)
                    TT("vector", pr_[:].rearrange("p (a b) -> p a b", a=2), dl3[:, :, 0:64], dl3[:, :, 64:128], ALU.mult, [dl], [pr_])
                    P.op("vector", lambda e: e.reduce_sum(out=sm[:], in_=pr_[:].rearrange("p (a b) -> p a b", a=2), axis=mybir.AxisListType.X), [pr_], [sm])
                    ACT(sm[:], sm[:], AF.Exp, [sm], [sm])
                    TT("vector", lam[:], sm[:, 0:1], sm[:, 1:2], ALU.subtract, [sm], [lam])
                    TS("vector", lam[:], lam[:], lam_init, None, ALU.add, None, [lam], [lam])
                    MEMSET("gpsimd", Vd[:].rearrange("p a b c -> p (a b c)"), 1.0, [Vd])
                    for i in range(NT):
                        p = proj_tok(i, Wv, 0, 512)
                        CP("scalar" if i % 2 else "vector", Vd[:, i, :, 0:128], p[:, :].rearrange("p (h d) -> p h d", h=4), [p], [Vd])
                    P.emit()
                with ExitStack() as st:
                    Wq = P.sb(st, "Wdq", [128, KC, 512], BF16)
                    QT = P.sb(st, "QTd", [128, L], BF16)
                    KT = P.sb(st, "KTd", [128, L], BF16)
                    t1 = P.sb(st, "t1d", [128, 512], F32)
                    t2 = P.sb(st, "t2d", [128, 512], F32)
                    rc = P.sb(st, "rcd", [128, 2], F32)
                    tm = P.sb(st, "tmd", [128, 128], F32)
                    oc_ = P.sb(st, "ocd", [128, 128], F32)
                    jk = P.sb(st, "jkd", [128, 128], BF16)
                    ssd = P.sb(st, "ssd", [128, 1], F32)
                    rsd = P.sb(st, "rsd", [128, 1], F32)
                    import os as _os
                    _stop = int(_os.environ.get('DIFF_STOP', '9'))
                    for h in range(4 if _stop >= 4 else (1 if _stop >= 2 else 0)):
                        load_w(Wq, w_in[l], KC, O_DQ + h * 128, O_DQ + (h + 1) * 128, dst_c0=0)
                        load_w(Wq, w_x[l], KC, h * 128, (h + 1) * 128, dst_c0=128)
                        load_w(Wq, w_in[l], KC, O_DK + h * 128, O_DK + (h + 1) * 128, dst_c0=256)
                        load_w(Wq, w_x[l], KC, 512 + h * 128, 512 + (h + 1) * 128, dst_c0=384)
                        for (dst, wc) in ((QT, 0), (KT, 256)):
                            for (c0, n) in CHUNKS:
                                pa, pb = fb[4], fb[5]
                                proj_feat(pa, c0, n, Wq, wc, 128)
                                proj_feat(pb, c0, n, Wq, wc + 128, 128)
                                CP("scalar", dst[:, c0:c0 + n], pa[:, 0:n], [pa], [dst])
                                for r0 in (0, 64):
                                    TT("vector", t1[r0:r0 + 32, 0:n], pa[r0:r0 + 32, 0:n], ropec[r0:r0 + 32, c0:c0 + n], ALU.mult, [pa, ropec, dst], [t1])
                                    TT("vector", t2[r0:r0 + 32, 0:n], pb[r0:r0 + 32, 0:n], ropes[r0:r0 + 32, c0:c0 + n], ALU.mult, [pb, ropes, dst], [t2])
                                    TT("vector", dst[r0:r0 + 32, c0:c0 + n], t1[r0:r0 + 32, 0:n], t2[r0:r0 + 32, 0:n], ALU.add, [t1, t2], [dst])

                        def fin(i, Os, h=h):
                            O0, O1 = Os
                            P.op("vector", lambda e: e.reciprocal(out=rc[:, 0:1], in_=O0[:, 128:129]), [O0], [rc])
                            P.op("vector", lambda e: e.reciprocal(out=rc[:, 1:2], in_=O1[:, 128:129]), [O1], [rc])
                            TT("vector", rc[:, 1:2], rc[:, 1:2], lam[:, 0:1], ALU.mult, [rc, lam], [rc])
                            TS("vector", tm[:], O1[:, 0:128], rc[:, 1:2], None, ALU.mult, None, [O1, rc], [tm])
                            STT("vector", oc_[:], O0[:, 0:128], rc[:, 0:1], tm[:], ALU.mult, ALU.subtract, [O0, rc, tm], [oc_])
                            MEMSET("vector", ssd[:], 0.0, [ssd])
                            ACT(jk[:], oc_[:], AF.Square, [oc_], [jk, ssd], accum_out=ssd[:, 0:1])
                            RSTD(rsd[:], ssd[:], 128, [ssd], [rsd], mult=1.0 - lam_init)
                            STT("vector", og[:, i, h * 128:(h + 1) * 128], oc_[:], rsd[:, 0:1], gd[:], ALU.mult, ALU.mult, [oc_, rsd, gd], [og])

                        if _stop < 3:
                            P.emit()
                            continue
                        with ExitStack() as st2:
                            softmax_attn(st2, [(QT, KT, 0, 64, (lambda kb, h=h: Vd[:, kb, h, 0:129])),
                                               (QT, KT, 64, 64, (lambda kb, h=h: Vd[:, kb, h, 0:129]))], 129, 0.125, fin)
                            P.emit()
            epilogue(l, 2, O_DZ)

        import os as _os2
        _epi_only = _os2.environ.get("EPI_ONLY")
        for l in range(nlayers):
            phase_norm(l)
            if _epi_only:
                for _ in range(int(_epi_only)):
                    epilogue(l, 0, O_SBZ)
                continue
            if 0 in branches:
                branch_sb(l)
            if 1 in branches:
                branch_mla(l)
            if 2 in branches:
                branch_diff(l)

        with ExitStack() as st:
            grep = P.sb(st, "grepf", [128, D], F32)
            junk = P.sb(st, "junkf", [128, D], BF16)
            ss = P.sb(st, "ssf", [128, NT], F32)
            rs = P.sb(st, "rsf", [128, NT], F32)
            yo = [P.sb(st, "yo%d" % j, [128, D], F32) for j in range(2)]
            bcast_load(grep, final_g[0:1, :], D)
            MEMSET("vector", ss[:], 0.0, [ss])
            for i in range(NT):
                o = yo[i % 2]
                if final_norm:
                    ACT(junk[:], X[:, i, :], AF.Square, [X], [junk, ss], accum_out=ss[:, i:i + 1])
                    RSTD(rs[:, i:i + 1], ss[:, i:i + 1], D, [ss], [rs])
                    STT("vector", o[:], X[:, i, :], rs[:, i:i + 1], grep[:], ALU.mult, ALU.mult, [X, rs, grep], [o])
                else:
                    CP("vector", o[:], X[:, i, :], [X], [o])
                p_lo = NMETA if i == 0 else 0
                p_hi = NMETA if i == NT - 1 else 128
                s0 = 128 * i - NMETA + p_lo
                DMA("sync", y[s0:s0 + (p_hi - p_lo), :], o[p_lo:p_hi, :], reads=[o])
            P.wait_all("sync", yo)
            P.emit()
    return nc


_CACHE = {}


def _consts():
    if "c" not in _CACHE:
        C, Sg = _rope_tables()
        tri, m01, ident = _masks()
        _CACHE["c"] = {"c_ropec": C, "c_ropes": Sg, "c_tri": tri, "c_m01": m01, "c_ident": ident}
    return _CACHE["c"]


def make_in_maps(inp):
    x = np.asarray(inp["x"], np.float32)
    B = x.shape[0]
    meta = np.asarray(inp["meta_tokens"], np.float32)
    lay = _host_layouts({k: np.asarray(v) for k, v in inp.items()})
    shared = dict(_consts())
    shared.update(lay)
    for k in ("norm_g", "w_in", "mla_cq_g", "mla_ckv_g", "diff_norm_g", "w_o_sb", "w_o_mla", "w_o_diff", "w_out"):
        shared[k] = np.ascontiguousarray(np.asarray(inp[k], np.float32))
    shared["diff_lambda"] = np.ascontiguousarray(np.asarray(inp["diff_lambda"], np.float32).reshape(2, 256))
    shared["final_g"] = np.ascontiguousarray(np.asarray(inp["final_g"], np.float32).reshape(1, D))
    maps = []
    for b in range(B):
        h0 = np.concatenate([meta, x[b], np.zeros((L - NMETA - S, D), np.float32)], axis=0)
        m = dict(shared)
        m["h0"] = np.ascontiguousarray(h0)
        maps.append(m)
    return maps


def kernel(**inputs):
    maps = make_in_maps(inputs)
    if "nc" not in _CACHE:
        _CACHE["nc"] = build_nc()
    res = run_bass_kernel_spmd(_CACHE["nc"], maps, core_ids=list(range(len(maps))))
    return np.stack([np.asarray(r["y"], np.float32) for r in res.results], axis=0)
```

```python
import math
import numpy as np
import ml_dtypes
from contextlib import ExitStack
import concourse.bass as bass
import concourse.mybir as mybir
from concourse.bass_utils import run_bass_kernel_spmd

F32 = mybir.dt.float32
BF16 = mybir.dt.bfloat16
AF = mybir.ActivationFunctionType
ALU = mybir.AluOpType

D = 1024
S = 2048
NMETA = 16
NT = 17
L = NT * 128
KC = 8
EPS = 1e-6
THETA = 500000.0
CHUNKS = [(0, 512), (512, 512), (1024, 512), (1536, 512), (2048, 128)]


class Buf:
    __slots__ = ("name", "t", "w", "r", "dsem", "dcnt")

    def __init__(self, name, t=None):
        self.name = name
        self.t = t
        self.w = []
        self.r = []
        self.dsem = None
        self.dcnt = 0

    def __getitem__(self, idx):
        return self.t[idx]


class Prog:
    ENGS = ("tensor", "vector", "scalar", "gpsimd", "sync")

    def __init__(self, nc, stack):
        self.nc = nc
        self.stack = stack
        self.sems = {}
        self.cnt = {e: 0 for e in self.ENGS}
        self.seen = {e: {} for e in self.ENGS}
        self.q = {e: [] for e in self.ENGS}
        for e in self.ENGS:
            self._sem("E_" + e)
        self.nbuf = 0

    def _sem(self, key):
        if key not in self.sems:
            self.sems[key] = self.stack.enter_context(self.nc.semaphore(key))
        return key

    def sb(self, st, name, shape, dt):
        self.uid = getattr(self, "uid", 0) + 1
        name = "%s_%d" % (name, self.uid)
        t = st.enter_context(self.nc.sbuf_tensor(name, list(shape), dt))
        return Buf(name, t)

    def ps(self, st, name, shape, dt=F32):
        t = st.enter_context(self.nc.psum_tensor(name, list(shape), dt))
        return Buf(name, t)

    def _waits(self, eng, reads, writes):
        own = "E_" + eng
        need = {}
        for b in reads:
            for (k, v) in b.w:
                if k == own and (eng == "tensor" or v > self.cnt[eng]):
                    continue
                if need.get(k, 0) < v:
                    need[k] = v
        for b in writes:
            for (k, v) in b.w:
                if k == own and (eng == "tensor" or v > self.cnt[eng]):
                    continue
                if need.get(k, 0) < v:
                    need[k] = v
            for (k, v) in b.r:
                if k == own and (eng == "tensor" or v > self.cnt[eng]):
                    continue
                if need.get(k, 0) < v:
                    need[k] = v
        out = []
        seen = self.seen[eng]
        for k, v in need.items():
            if seen.get(k, 0) < v:
                seen[k] = v
                out.append((k, v))
        return out

    def op(self, eng, fn, reads=(), writes=(), inc=True):
        waits = self._waits(eng, reads, writes)
        key = "E_" + eng
        val = self.cnt[eng] + 1
        if inc:
            self.cnt[eng] = val
        ev = (key, val)
        for b in writes:
            b.w = [ev]
            b.r = []
        for b in reads:
            b.r = [e for e in b.r if e[0] != key] + [ev]
        self.q[eng].append((waits, fn, [(key, 1)] if inc else []))

    def dma(self, eng, fn, reads=(), writes=(), sem_buf=None):
        waits = self._waits(eng, reads, writes)
        sb = sem_buf or (writes[0] if writes else reads[0])
        if sb.dsem is None:
            sb.dsem = self._sem("D_%d" % self.nbuf)
            self.nbuf += 1
        sb.dcnt += 16
        ev = (sb.dsem, sb.dcnt)
        for b in writes:
            b.w = [e for e in b.w if e[0] != sb.dsem and e[0].startswith("D_")] + [ev]
            b.r = []
        for b in reads:
            b.r = [e for e in b.r if e[0] != sb.dsem] + [ev]
        self.q[eng].append((waits, fn, [(sb.dsem, 16)]))

    def wait_all(self, eng, bufs):
        waits = self._waits(eng, (), bufs)
        self.q[eng].append((waits, None, []))

    def emit(self):
        nc = self.nc
        qs = self.q
        self.q = {e: [] for e in self.ENGS}
        sems = self.sems
        with nc.Block() as block:
            def mk(ename):
                items = qs[ename]

                def body(e):
                    for waits, fn, incs in items:
                        for (k, v) in waits:
                            e.wait_ge(sems[k], v)
                        if fn is None:
                            continue
                        ins = fn(e)
                        for (k, n) in incs:
                            ins.then_inc(sems[k], n)
                return body
            block.tensor(mk("tensor"))
            block.vector(mk("vector"))
            block.scalar(mk("scalar"))
            block.gpsimd(mk("gpsimd"))
            block.sync(mk("sync"))


def _rope_tables():
    pos = np.arange(L, dtype=np.float32)
    C = np.ones((128, L), np.float32)
    Sg = np.zeros((128, L), np.float32)
    inv_d = (np.float32(THETA) ** (-np.arange(0, 16, 2, dtype=np.float32) / np.float32(16))).astype(np.float32)
    ang_d = (pos[:, None] * inv_d[None, :]).astype(np.float32)
    cd, sd = np.cos(ang_d).astype(np.float32), np.sin(ang_d).astype(np.float32)
    for base in (0, 64):
        for r in range(16):
            C[base + r] = cd[:, r % 8]
            Sg[base + r] = -sd[:, r % 8] if r < 8 else sd[:, r % 8]
    inv_m = (np.float32(THETA) ** (-np.arange(0, 32, 2, dtype=np.float32) / np.float32(32))).astype(np.float32)
    ang_m = (pos[:, None] * inv_m[None, :]).astype(np.float32)
    cm, sm = np.cos(ang_m).astype(np.float32), np.sin(ang_m).astype(np.float32)
    for r in range(32):
        C[32 + r] = cm[:, r % 16]
        Sg[32 + r] = -sm[:, r % 16] if r < 16 else sm[:, r % 16]
    return C, Sg


def _masks():
    a = np.arange(128)
    tri = (a[None, :] < a[:, None]).astype(np.float32)
    cq = (a + 48) // 64
    m0 = (cq[:, None] <= cq[None, :])
    m1 = ((a[:, None] < 16) & (a[None, :] >= 80))
    m01 = np.concatenate([m0, m1], axis=1).astype(np.float32).astype(ml_dtypes.bfloat16)
    ident = np.eye(128, dtype=np.float32).astype(ml_dtypes.bfloat16)
    return tri, m01, ident


O_SBQ, O_SBK, O_SBV, O_SBZ = 0, 512, 1024, 1536
O_CQ, O_CKV, O_KR, O_MZ = 2048, 2432, 2688, 2720
O_DQ, O_DK, O_DV, O_DZ = 3232, 3744, 4256, 4768
O_G = 5280


def _host_layouts(inp):
    w_in = inp["w_in"]
    swap64 = np.concatenate([np.arange(8, 16), np.arange(0, 8), np.arange(16, 64)])
    idx_d = np.concatenate([m * 64 + swap64 for m in range(8)])
    kr_sw = np.concatenate([np.arange(16, 32), np.arange(0, 16)])
    w_x = np.concatenate([w_in[:, :, O_DQ + idx_d], w_in[:, :, O_DK + idx_d], w_in[:, :, O_KR + kr_sw]], axis=2)
    uq = inp["mla_w_uq"]
    ia, ib = [], []
    for h in range(8):
        b = 96 * h
        ia += list(range(b, b + 32)) + list(range(b + 64, b + 96)) + list(range(b + 32, b + 64))
        ib += list(range(b, b + 32)) + list(range(b + 80, b + 96)) + list(range(b + 64, b + 80)) + list(range(b + 32, b + 64))
    uqa = uq[:, :, np.array(ia)]
    uqb = uq[:, :, np.array(ib)]
    ukv = inp["mla_w_ukv"]
    ikn, iv = [], []
    for h in range(8):
        b = 128 * h
        ikn += list(range(b, b + 32)) + list(range(b, b + 32)) + list(range(b + 32, b + 64))
        iv += list(range(b + 64, b + 128))
    ukn = ukv[:, :, np.array(ikn)]
    ukvv = ukv[:, :, np.array(iv)]
    bg = inp["b_gate"].reshape(2, 3, 8, 128).transpose(0, 1, 3, 2)
    return {
        "w_x": np.ascontiguousarray(w_x),
        "uqa": np.ascontiguousarray(uqa), "uqb": np.ascontiguousarray(uqb),
        "ukn": np.ascontiguousarray(ukn), "ukvv": np.ascontiguousarray(ukvv),
        "bg": np.ascontiguousarray(bg),
    }


def build_nc(nlayers=2, final_norm=True, branches=(0, 1, 2)):
    nc = bass.Bass("TRN2", target_bir_lowering=False)

    def din(name, shape, dt=F32):
        return nc.dram_tensor(name, list(shape), dt, kind="ExternalInput").ap()

    h0 = din("h0", [L, D])
    norm_g = din("norm_g", [2, D])
    w_in = din("w_in", [2, D, 8352])
    w_x = din("w_x", [2, D, 1056])
    bg = din("bg", [2, 3, 128, 8])
    cq_g = din("mla_cq_g", [2, 384])
    ckv_g = din("mla_ckv_g", [2, 256])
    uqa = din("uqa", [2, 384, 768])
    uqb = din("uqb", [2, 384, 768])
    ukn = din("ukn", [2, 256, 768])
    ukvv = din("ukvv", [2, 256, 512])
    dlam = din("diff_lambda", [2, 256])
    dng = din("diff_norm_g", [2, 128])
    w_o = [din("w_o_sb", [2, 512, D]), din("w_o_mla", [2, 512, D]), din("w_o_diff", [2, 512, D])]
    w_out = din("w_out", [2, D, D])
    final_g = din("final_g", [1, D])
    c_ropec = din("c_ropec", [128, L])
    c_ropes = din("c_ropes", [128, L])
    c_tri = din("c_tri", [128, 128])
    c_m01 = din("c_m01", [128, 256], BF16)
    c_ident = din("c_ident", [128, 128], BF16)
    y = nc.dram_tensor("y", [S, D], F32, kind="ExternalOutput").ap()

    with ExitStack() as top:
        P = Prog(nc, top)
        X = P.sb(top, "X", [128, NT, D], F32)
        hT = P.sb(top, "hT", [128, KC, L], BF16)
        og = P.sb(top, "og", [128, NT, 512], BF16)
        ropec = P.sb(top, "ropec", [128, L], F32)
        ropes = P.sb(top, "ropes", [128, L], F32)
        tri = P.sb(top, "tri", [128, 128], F32)
        m01 = P.sb(top, "m01", [128, 256], BF16)
        ident = P.sb(top, "ident", [128, 128], BF16)
        z2 = [P.ps(top, "z2_%d" % i, [128, 1024], F32) for i in range(2)]
        fb = [Buf("fb0", z2[0][:, 0:512]), Buf("fb1", z2[0][:, 512:1024]), Buf("fb2", z2[1][:, 0:512]), Buf("fb3", z2[1][:, 512:1024])]
        fb += [P.ps(top, "fb%d" % i, [128, 512], F32) for i in range(4, 8)]
        tb = [Buf("tb%d" % i, fb[6 + i][:].bitcast(BF16)) for i in range(2)]
        for i in range(2):
            tb[i].w, tb[i].r = fb[6 + i].w, fb[6 + i].r

        def MM(out, lhsT, rhs, start, stop, reads, writes, inc=True):
            P.op("tensor", lambda e: e.matmul(out, lhsT=lhsT, rhs=rhs, start=start, stop=stop), reads, writes, inc)

        def TR(out, in_, reads, writes, inc=True):
            P.op("tensor", lambda e: e.transpose(out=out, in_=in_, identity=ident[:]), list(reads) + [ident], writes, inc)

        def ACT(out, in_, func, reads, writes, bias=None, scale=None, accum_out=None):
            kw = {}
            if bias is not None:
                kw["bias"] = bias
            if scale is not None:
                kw["scale"] = scale
            if accum_out is not None:
                kw["accum_out"] = accum_out
            P.op("scalar", lambda e: e.activation(out=out, in_=in_, func=func, **kw), reads, writes)

        def TT(eng, out, in0, in1, op, reads, writes):
            P.op(eng, lambda e: e.tensor_tensor(out=out, in0=in0, in1=in1, op=op), reads, writes)

        def TS(eng, out, in0, s1, s2, op0, op1, reads, writes):
            if op1 is None:
                P.op(eng, lambda e: e.tensor_scalar(out=out, in0=in0, scalar1=s1, scalar2=None, op0=op0), reads, writes)
            else:
                P.op(eng, lambda e: e.tensor_scalar(out=out, in0=in0, scalar1=s1, scalar2=s2, op0=op0, op1=op1), reads, writes)

        def STT(eng, out, in0, scalar, in1, op0, op1, reads, writes):
            P.op(eng, lambda e: e.scalar_tensor_tensor(out=out, in0=in0, scalar=scalar, in1=in1, op0=op0, op1=op1), reads, writes)

        def RSTD(out, ss_ap, n, reads, writes, mult=1.0):
            ACT(out, ss_ap, AF.Ln, reads, writes, bias=EPS, scale=1.0 / n)
            ACT(out, out, AF.Exp, writes, writes, bias=(math.log(mult) if mult != 1.0 else None), scale=-0.5)

        def CP(eng, out, in_, reads, writes):
            if eng == "scalar":
                P.op(eng, lambda e: e.copy(out=out, in_=in_), reads, writes)
            else:
                P.op(eng, lambda e: e.tensor_copy(out=out, in_=in_), reads, writes)

        def MEMSET(eng, ap, val, writes):
            P.op(eng, lambda e: e.memset(ap, val), (), writes)

        def DMA(eng, out, in_, reads=(), writes=()):
            P.dma(eng, lambda e: e.dma_start(out=out, in_=in_), reads, writes)

        def load_w(buf, dram2d, k_chunks, c0, c1, dst_c0=0):
            v = dram2d.rearrange("(k p) c -> p k c", p=128)
            DMA("gpsimd", buf[:, 0:k_chunks, dst_c0:dst_c0 + (c1 - c0)], v[:, :, c0:c1], writes=[buf])

        def bcast_load(buf, row_ap, n):
            DMA("sync", buf[:], row_ap.to_broadcast([128, n]), writes=[buf])

        fctr = [0]

        def next_f(lo=0, hi=4):
            b = fb[lo + fctr[0] % (hi - lo)]
            fctr[0] += 1
            return b

        DMA("sync", ropec[:], c_ropec[:, :], writes=[ropec])
        DMA("sync", ropes[:], c_ropes[:, :], writes=[ropes])
        DMA("sync", tri[:], c_tri[:, :], writes=[tri])
        DMA("sync", m01[:], c_m01[:, :], writes=[m01])
        DMA("sync", ident[:], c_ident[:, :], writes=[ident])
        h0v = h0.rearrange("(t p) d -> p t d", p=128)
        for i in range(NT):
            DMA("sync", X[:, i, :], h0v[:, i, :], writes=[X])
        P.emit()

        def phase_norm(l):
            with ExitStack() as st:
                grep = P.sb(st, "grep", [128, D], F32)
                junk = P.sb(st, "junk", [128, D], BF16)
                ss = P.sb(st, "ss", [128, NT], F32)
                rs = P.sb(st, "rs", [128, NT], F32)
                hn = [P.sb(st, "hn%d" % j, [128, D], BF16) for j in range(2)]
                bcast_load(grep, norm_g[l:l + 1, :], D)
                MEMSET("vector", ss[:], 0.0, [ss])
                for i in range(NT):
                    ACT(junk[:], X[:, i, :], AF.Square, [X], [junk, ss], accum_out=ss[:, i:i + 1])
                    RSTD(rs[:, i:i + 1], ss[:, i:i + 1], D, [ss], [rs])
                    h = hn[i % 2]
                    STT("vector", h[:], X[:, i, :], rs[:, i:i + 1], grep[:], ALU.mult, ALU.mult, [X, rs, grep], [h])
                    t = tb[i % 2]
                    for k in range(KC):
                        TR(t[:, k * 128:(k + 1) * 128], h[:, k * 128:(k + 1) * 128], [h], [t], inc=(k == KC - 1))
                    CP("scalar", hT[:, :, i * 128:(i + 1) * 128], t[:].rearrange("p (k c) -> p k c", k=KC), [t], [hT])
                P.emit()

        def proj_tok(i, W, c0, n, kchunks=KC, src=None, src_k0=0):
            src = src or hT
            p = next_f()
            for k in range(kchunks):
                MM(p[:, 0:n], src[:, src_k0 + k, i * 128:(i + 1) * 128], W[:, k, c0:c0 + n], k == 0, k == kchunks - 1,
                   [src, W], [p], inc=(k == kchunks - 1))
            return p

        def proj_feat(p, c0, n, W, wc0, M, kchunks=KC, src=None, src_k0=0):
            src = src or hT
            for k in range(kchunks):
                MM(p[0:M, 0:n], W[:, k, wc0:wc0 + M], src[:, src_k0 + k, c0:c0 + n], k == 0, k == kchunks - 1,
                   [src, W], [p], inc=(k == kchunks - 1))

        def epilogue(l, b, zoff):
            with ExitStack() as st:
                Wz = P.sb(st, "Wz", [128, KC, 512], BF16)
                Wo = P.sb(st, "Wo", [128, 4, D], BF16)
                Wg = P.sb(st, "Wg", [128, KC, D], BF16)
                Wout = P.sb(st, "Wout", [128, KC, D], BF16)
                bgt = P.sb(st, "bgt", [128, 8], F32)
                G = [P.sb(st, "G%d" % j, [128, 512], BF16) for j in range(2)]
                ogg = [P.sb(st, "ogg%d" % j, [128, 512], BF16) for j in range(4)]
                oggT = P.sb(st, "oggT", [128, 4, 512], BF16)
                sg = [P.sb(st, "sg%d" % j, [128, 512], F32) for j in range(2)]
                mT = P.sb(st, "mT", [128, KC, 512], BF16)
                load_w(Wz, w_in[l], KC, zoff, zoff + 512)
                load_w(Wo, w_o[b][l], 4, 0, D)
                load_w(Wg, w_in[l], KC, O_G + b * D, O_G + (b + 1) * D)
                load_w(Wout, w_out[l], KC, 0, D)
                DMA("sync", bgt[:], bg[l, b, :, :], writes=[bgt])
                cnt = 0
                for (c0, n) in CHUNKS:
                    tiles = list(range(c0 // 128, (c0 + n) // 128))
                    pzs = [proj_tok(i, Wz, 0, 512) for i in tiles]
                    for oc in range(2):
                        proj_feat(fb[4 + oc], c0, n, Wg, oc * 128, 128)
                    for j, i in enumerate(tiles):
                        pz = pzs[j]
                        g_, o_ = G[cnt % 2], ogg[cnt % 4]
                        ACT(g_[:], pz[:, :], AF.Silu, [pz], [g_])
                        TT("vector", o_[:], og[:, i, :], g_[:], ALU.mult, [og, g_], [o_])
                        cnt += 1
                        t = tb[j % 2]
                        for c in range(4):
                            TR(t[:, c * 128:(c + 1) * 128], o_[:, c * 128:(c + 1) * 128], [o_], [t], inc=(c == 3))
                        CP("vector", oggT[:, :, j * 128:(j + 1) * 128], t[:, 0:512].rearrange("p (c q) -> p c q", c=4), [t], [oggT])
                    for oc in range(8):
                        pg = fb[4 + oc % 2]
                        if oc >= 2:
                            proj_feat(pg, c0, n, Wg, oc * 128, 128)
                        py = next_f()
                        for c in range(4):
                            MM(py[:, 0:n], Wo[:, c, oc * 128:(oc + 1) * 128], oggT[:, c, 0:n], c == 0, c == 3, [Wo, oggT], [py], inc=(c == 3))
                        s_ = sg[oc % 2]
                        ACT(s_[:, 0:n], pg[:, 0:n], AF.Sigmoid, [pg, bgt], [s_], bias=bgt[:, oc:oc + 1])
                        TT("vector", mT[:, oc, 0:n], s_[:, 0:n], py[:, 0:n], ALU.mult, [s_, py], [mT])
                    for j, i in enumerate(tiles):
                        for half in range(2):
                            po = next_f()
                            for k in range(KC):
                                MM(po[:, :], mT[:, k, j * 128:(j + 1) * 128], Wout[:, k, half * 512:(half + 1) * 512], k == 0, k == KC - 1,
                                   [mT, Wout], [po], inc=(k == KC - 1))
                            TT("vector", X[:, i, half * 512:(half + 1) * 512], X[:, i, half * 512:(half + 1) * 512], po[:, :], ALU.add, [X, po], [X])
                P.emit()

        def branch_sb(l):
            with ExitStack() as bst:
                V = P.sb(bst, "Vsb", [128, NT, 512], BF16)
                with ExitStack() as st:
                    Wv = P.sb(st, "Wv", [128, KC, 512], BF16)
                    load_w(Wv, w_in[l], KC, O_SBV, O_SBV + 512)
                    for i in range(NT):
                        p = proj_tok(i, Wv, 0, 512)
                        CP("scalar" if i % 2 else "vector", V[:, i, :], p[:, :], [p], [V])
                    P.emit()
                with ExitStack() as st:
                    Wqk = P.sb(st, "Wqk", [128, KC, 256], BF16)
                    qT = P.sb(st, "qT", [128, L], BF16)
                    kT = P.sb(st, "kT", [128, L], BF16)
                    NE, NSP, NC = 5, 4, 4
                    ez = [P.sb(st, "ez%d" % j, [128, 512], F32) for j in range(NE)]
                    sp = [P.sb(st, "sp%d" % j, [128, 512], F32) for j in range(NSP)]
                    Cb = [P.sb(st, "Cb%d" % j, [128, 512], F32) for j in range(NC)]
                    wb = [P.sb(st, "wb%d" % j, [128, 512], BF16) for j in range(2)]
                    wT = [P.sb(st, "wT%d" % j, [128, 512], BF16) for j in range(2)]
                    cn = [P.sb(st, "cn%d" % j, [128, 1], F32) for j in range(3)]
                    for pr in range(4):
                        load_w(Wqk, w_in[l], KC, O_SBQ + pr * 128, O_SBQ + (pr + 1) * 128, dst_c0=0)
                        load_w(Wqk, w_in[l], KC, O_SBK + pr * 128, O_SBK + (pr + 1) * 128, dst_c0=128)
                        for (c0, n) in CHUNKS:
                            p = next_f(4, 6)
                            proj_feat(p, c0, n, Wqk, 0, 128)
                            CP("scalar", qT[:, c0:c0 + n], p[:, 0:n], [p], [qT])
                            p = next_f(4, 6)
                            proj_feat(p, c0, n, Wqk, 128, 128)
                            CP("vector", kT[:, c0:c0 + n], p[:, 0:n], [p], [kT])
                        items = []
                        for hh in range(2):
                            for i in range(NT):
                                nk = (i + 1) * 128
                                chs = [(k0, min(512, nk - k0)) for k0 in range(0, nk, 512)][::-1]
                                for ci, (k0, n) in enumerate(chs):
                                    items.append((hh, i, ci, k0, n, ci == len(chs) - 1))
                        N = len(items)
                        obank = {}
                        ocnt = [0]

                        def st_mm(j):
                            hh, i, ci, k0, n, last = items[j]
                            r0 = 64 * hh
                            z = fb[j % 2]
                            MM(z[:, 0:n], qT[r0:r0 + 64, i * 128:(i + 1) * 128], kT[r0:r0 + 64, k0:k0 + n], True, True, [qT, kT], [z])

                        def st_expz(j):
                            hh, i, ci, k0, n, last = items[j]
                            e_ = ez[j % NE]
                            ACT(e_[:, 0:n], fb[j % 2][:, 0:n], AF.Exp, [fb[j % 2]], [e_], scale=0.125)
                            if ci == 0:
                                TT("gpsimd", e_[:, n - 128:n], e_[:, n - 128:n], tri[:], ALU.mult, [e_, tri], [e_])

                        def st_ln(j):
                            hh, i, ci, k0, n, last = items[j]
                            ACT(sp[j % NSP][:, 0:n], ez[j % NE][:, 0:n], AF.Ln, [ez[j % NE]], [sp[j % NSP]], bias=1.0)

                        def st_scan(j):
                            hh, i, ci, k0, n, last = items[j]
                            s_, c_ = sp[j % NSP], Cb[j % NC]
                            P.op("vector", lambda e: e.tensor_tensor_scan(out=c_[:, 0:n], data0=s_[:, 0:n], data1=s_[:, 0:n],
                                                                           initial=0.0, op0=ALU.add, op1=ALU.max), [s_], [c_])

                        def st_cn(j):
                            hh, i, ci, k0, n, last = items[j]
                            s_, c_ = sp[j % NSP], Cb[j % NC]
                            if ci == 0:
                                CP("vector", cn[j % 3][:], c_[:, n - 1:n], [c_], [cn[j % 3]])
                            else:
                                TT("vector", cn[j % 3][:], cn[(j - 1) % 3][:], c_[:, n - 1:n], ALU.add, [cn[(j - 1) % 3], c_], [cn[j % 3]])

                        def st_stt(j):
                            hh, i, ci, k0, n, last = items[j]
                            s_, c_ = sp[j % NSP], Cb[j % NC]
                            STT("vector", s_[:, 0:n], c_[:, 0:n], cn[j % 3][:, 0:1], s_[:, 0:n], ALU.subtract, ALU.subtract,
                                [c_, cn[j % 3], s_], [s_])

                        def st_expt(j):
                            hh, i, ci, k0, n, last = items[j]
                            ACT(Cb[j % NC][:, 0:n], sp[j % NSP][:, 0:n], AF.Exp, [sp[j % NSP]], [Cb[j % NC]])

                        def st_mult(j):
                            hh, i, ci, k0, n, last = items[j]
                            TT("gpsimd", wb[j % 2][:, 0:n], ez[j % NE][:, 0:n], Cb[j % NC][:, 0:n], ALU.mult, [ez[j % NE], Cb[j % NC]], [wb[j % 2]])

                        def st_tr(j):
                            hh, i, ci, k0, n, last = items[j]
                            t = tb[j % 2]
                            nb = n // 128
                            for jb in range(nb):
                                TR(t[:, jb * 128:(jb + 1) * 128], wb[j % 2][:, jb * 128:(jb + 1) * 128], [wb[j % 2]], [t], inc=(jb == nb - 1))

                        def st_evac(j):
                            hh, i, ci, k0, n, last = items[j]
                            CP("scalar", wT[j % 2][:, 0:n], tb[j % 2][:, 0:n], [tb[j % 2]], [wT[j % 2]])

                        def st_pv(j):
                            hh, i, ci, k0, n, last = items[j]
                            h = 2 * pr + hh
                            nb = n // 128
                            if ci == 0:
                                obank[(hh, i)] = fb[2 + ocnt[0] % 2]
                                ocnt[0] += 1
                            O = obank[(hh, i)]
                            for jb in range(nb):
                                kb = k0 // 128 + jb
                                MM(O[:, 0:64], wT[j % 2][:, jb * 128:(jb + 1) * 128], V[:, kb, h * 64:(h + 1) * 64],
                                   ci == 0 and jb == 0, last and jb == nb - 1, [wT[j % 2], V], [O], inc=(jb == nb - 1))
                            if last:
                                CP("vector", og[:, i, h * 64:(h + 1) * 64], O[:, 0:64], [O], [og])

                        sched = [(st_mm, 0), (st_expz, 1), (st_expt, 4), (st_ln, 1), (st_evac, 7), (st_cn, 3), (st_scan, 2), (st_stt, 3),
                                 (st_mult, 5), (st_tr, 6), (st_pv, 8)]
                        for step in range(N + 8):
                            for fn, off in sched:
                                if 0 <= step - off < N:
                                    fn(step - off)
                    P.emit()
            epilogue(l, 0, O_SBZ)

        def softmax_attn(st, units, dv1, scale, finalize):
            NS = 3
            import os as _os3
            if _os3.environ.get("SKIP_ATTN"):
                return
            PT = [P.sb(st, "PT%d" % j, [128, 512], BF16) for j in range(NS)]
            items = []
            for i in range(NT):
                kbs = list(range(0, min(i + 2, NT)))
                groups = [kbs[a:a + 4] for a in range(0, len(kbs), 4)]
                for u in range(len(units)):
                    for gi, g in enumerate(groups):
                        items.append((i, u, g, gi == 0, gi == len(groups) - 1))
            N = len(items)
            nu = len(units)

            def s1(j):
                i, u, g, first, last = items[j]
                QTb, KTb, r0, nr, vfn = units[u]
                z = fb[j % 2]
                for a, kb in enumerate(g):
                    MM(z[:, a * 128:(a + 1) * 128], KTb[r0:r0 + nr, kb * 128:(kb + 1) * 128], QTb[r0:r0 + nr, i * 128:(i + 1) * 128],
                       True, True, [QTb, KTb], [z], inc=(a == len(g) - 1))

            def s2(j):
                i, u, g, first, last = items[j]
                z = fb[j % 2]
                s = j % NS
                n = len(g) * 128
                ACT(PT[s][:, 0:n], z[:, 0:n], AF.Exp, [z], [PT[s]], scale=scale)
                for a, kb in enumerate(g):
                    if kb == i:
                        TT("gpsimd", PT[s][:, a * 128:(a + 1) * 128], PT[s][:, a * 128:(a + 1) * 128], m01[:, 0:128], ALU.mult, [PT[s], m01], [PT[s]])
                    elif kb == i + 1:
                        TT("gpsimd", PT[s][:, a * 128:(a + 1) * 128], PT[s][:, a * 128:(a + 1) * 128], m01[:, 128:256], ALU.mult, [PT[s], m01], [PT[s]])

            def s3(j):
                i, u, g, first, last = items[j]
                QTb, KTb, r0, nr, vfn = units[u]
                s = j % NS
                O = fb[2 + u] if nu > 1 else fb[2 + i % 2]
                for a, kb in enumerate(g):
                    MM(O[:, 0:dv1], PT[s][:, a * 128:(a + 1) * 128], vfn(kb), first and a == 0, last and a == len(g) - 1,
                       [PT[s]], [O], inc=(a == len(g) - 1))
                if last and u == nu - 1:
                    finalize(i, [fb[2 + uu] for uu in range(nu)] if nu > 1 else [O])

            for step in range(N + 2):
                if step < N:
                    s1(step)
                if 0 <= step - 1 < N:
                    s2(step - 1)
                if 0 <= step - 2 < N:
                    s3(step - 2)

        def diff_attn(st, QTb, KTb, vfn, finalize):
            import os as _os4
            if _os4.environ.get("SKIP_ATTN"):
                return
            NS = 3
            PT = [P.sb(st, "PTd%d" % j, [128, 1024], BF16) for j in range(NS)]
            items = []
            for i in range(NT):
                kbs = list(range(0, min(i + 2, NT)))
                groups = [kbs[a:a + 4] for a in range(0, len(kbs), 4)]
                for gi, g in enumerate(groups):
                    items.append((i, g, gi == 0, gi == len(groups) - 1))
            N = len(items)

            def s1(j):
                i, g, first, last = items[j]
                zt = z2[j % 2]
                for a, kb in enumerate(g):
                    for u in range(2):
                        r0 = 64 * u
                        MM(zt[:, u * 512 + a * 128:u * 512 + (a + 1) * 128], KTb[r0:r0 + 64, kb * 128:(kb + 1) * 128],
                           QTb[r0:r0 + 64, i * 128:(i + 1) * 128], True, True, [QTb, KTb], [zt], inc=(a == len(g) - 1 and u == 1))

            def s2(j):
                i, g, first, last = items[j]
                zt = z2[j % 2]
                p_ = PT[j % NS]
                n = len(g) * 128
                ACT(p_[:].rearrange("p (u c) -> p u c", u=2)[:, :, 0:n], zt[:].rearrange("p (u c) -> p u c", u=2)[:, :, 0:n],
                    AF.Exp, [zt], [p_], scale=0.125)
                for a, kb in enumerate(g):
                    if kb == i or kb == i + 1:
                        m = m01[:, 0:128] if kb == i else m01[:, 128:256]
                        for u in range(2):
                            sl = p_[:, u * 512 + a * 128:u * 512 + (a + 1) * 128]
                            TT("gpsimd", sl, sl, m, ALU.mult, [p_, m01], [p_])

            def s3(j):
                i, g, first, last = items[j]
                p_ = PT[j % NS]
                Os = [fb[4 + 2 * (i % 2)], fb[5 + 2 * (i % 2)]]
                for a, kb in enumerate(g):
                    for u in range(2):
                        MM(Os[u][:, 0:129], p_[:, u * 512 + a * 128:u * 512 + (a + 1) * 128], vfn(kb),
                           first and a == 0, last and a == len(g) - 1, [p_], [Os[u]], inc=(a == len(g) - 1))
                if last:
                    finalize(i, Os)

            for step in range(N + 2):
                if step < N:
                    s1(step)
                if 0 <= step - 1 < N:
                    s2(step - 1)
                if 0 <= step - 2 < N:
                    s3(step - 2)

        def branch_mla(l):
            with ExitStack() as bst:
                cnT = P.sb(bst, "cnT", [128, 5, L], BF16)
                Va = P.sb(bst, "Va", [128, NT, 8, 68], BF16)
                KT = P.sb(bst, "KTm", [128, L], BF16)
                with ExitStack() as st:
                    W = P.sb(st, "Wm", [128, KC, 704], BF16)
                    Wv = P.sb(st, "Wukvv", [128, 2, 512], BF16)
                    gq = P.sb(st, "gq", [128, 640], F32)
                    junk = P.sb(st, "junkm", [128, 384], BF16)
                    ss = P.sb(st, "ssm", [128, 2 * NT], F32)
                    rs = P.sb(st, "rsm", [128, 2 * NT], F32)
                    cb = [P.sb(st, "cb%d" % j, [128, 640], BF16) for j in range(2)]
                    t1 = P.sb(st, "t1m", [128, 512], F32)
                    t2 = P.sb(st, "t2m", [128, 512], F32)
                    load_w(W, w_in[l], KC, O_CQ, O_CQ + 672)
                    load_w(W, w_x[l], KC, 1024, 1056, dst_c0=672)
                    load_w(Wv, ukvv[l], 2, 0, 512)
                    DMA("sync", gq[:, 0:384], cq_g[l:l + 1, :].to_broadcast([128, 384]), writes=[gq])
                    DMA("sync", gq[:, 384:640], ckv_g[l:l + 1, :].to_broadcast([128, 256]), writes=[gq])
                    MEMSET("vector", ss[:], 0.0, [ss])
                    MEMSET("gpsimd", Va[:].rearrange("p a b c -> p (a b c)"), 1.0, [Va])
                    for i in range(NT):
                        c = cb[i % 2]
                        for part, (wc0, n, dc0) in enumerate(((0, 384, 0), (384, 256, 384))):
                            p = proj_tok(i, W, wc0, n)
                            col = 2 * i + part
                            ACT(junk[:, 0:n], p[:, 0:n], AF.Square, [p], [junk, ss], accum_out=ss[:, col:col + 1])
                            RSTD(rs[:, col:col + 1], ss[:, col:col + 1], n, [ss], [rs])
                            STT("vector", c[:, dc0:dc0 + n], p[:, 0:n], rs[:, col:col + 1], gq[:, dc0:dc0 + n], ALU.mult, ALU.mult, [p, rs, gq], [c])
                        t = tb[i % 2]
                        for k in range(5):
                            TR(t[:, k * 128:(k + 1) * 128], c[:, k * 128:(k + 1) * 128], [c], [t], inc=(k == 4))
                        CP("scalar", cnT[:, :, i * 128:(i + 1) * 128], t[:, 0:640].rearrange("p (k c) -> p k c", k=5), [t], [cnT])
                    for (c0, n) in CHUNKS:
                        pa = fb[4]
                        pb = fb[5]
                        proj_feat(pa, c0, n, W, 608, 64)
                        proj_feat(pb, c0, n, W, 640, 64)
                        TT("vector", t1[32:64, 0:n], pa[32:64, 0:n], ropec[32:64, c0:c0 + n], ALU.mult, [pa, ropec], [t1])
                        TT("vector", t2[32:64, 0:n], pb[32:64, 0:n], ropes[32:64, c0:c0 + n], ALU.mult, [pb, ropes], [t2])
                        TT("vector", KT[32:64, c0:c0 + n], t1[32:64, 0:n], t2[32:64, 0:n], ALU.add, [t1, t2], [KT])
                    for i in range(NT):
                        p = proj_tok(i, Wv, 0, 512, kchunks=2, src=cnT, src_k0=3)
                        CP("scalar" if i % 2 else "vector", Va[:, i, :, 0:64], p[:, :].rearrange("p (h d) -> p h d", h=8), [p], [Va])
                    P.emit()
                with ExitStack() as st:
                    QT = P.sb(st, "QTm", [128, L], BF16)
                    Wa = P.sb(st, "Wuqa", [128, 3, 768], BF16)
                    Wb = P.sb(st, "Wuqb", [128, 3, 768], BF16)
                    Wk = P.sb(st, "Wukn", [128, 2, 768], BF16)
                    t1 = P.sb(st, "t1q", [128, 512], F32)
                    t2 = P.sb(st, "t2q", [128, 512], F32)
                    rcp = P.sb(st, "rcp", [128, 1], F32)
                    load_w(Wa, uqa[l], 3, 0, 768)
                    load_w(Wb, uqb[l], 3, 0, 768)
                    load_w(Wk, ukn[l], 2, 0, 768)
                    for h in range(8):
                        for cidx, (c0, n) in enumerate(CHUNKS):
                            pa, pb = fb[4 + 2 * (cidx % 2)], fb[5 + 2 * (cidx % 2)]
                            proj_feat(pa, c0, n, Wa, h * 96, 96, kchunks=3, src=cnT)
                            proj_feat(pb, c0, n, Wb, h * 96, 96, kchunks=3, src=cnT)
                            CP("scalar", QT[0:32, c0:c0 + n], pa[0:32, 0:n], [pa], [QT])
                            CP("scalar", QT[64:96, c0:c0 + n], pa[64:96, 0:n], [pa], [QT])
                            TT("vector", t1[32:64, 0:n], pa[32:64, 0:n], ropec[32:64, c0:c0 + n], ALU.mult, [pa, ropec], [t1])
                            TT("vector", t2[32:64, 0:n], pb[32:64, 0:n], ropes[32:64, c0:c0 + n], ALU.mult, [pb, ropes], [t2])
                            TT("vector", QT[32:64, c0:c0 + n], t1[32:64, 0:n], t2[32:64, 0:n], ALU.add, [t1, t2], [QT])
                            pk = pb
                            proj_feat(pk, c0, n, Wk, h * 96, 96, kchunks=2, src=cnT, src_k0=3)
                            CP("scalar", KT[0:32, c0:c0 + n], pk[0:32, 0:n], [pk], [KT])
                            CP("vector", KT[64:96, c0:c0 + n], pk[64:96, 0:n], [pk], [KT])

                        def fin(i, Os, h=h):
                            O = Os[0]
                            P.op("vector", lambda e: e.reciprocal(out=rcp[:], in_=O[:, 64:65]), [O], [rcp])
                            TS("vector", og[:, i, h * 64:(h + 1) * 64], O[:, 0:64], rcp[:, 0:1], None, ALU.mult, None, [O, rcp], [og])

                        with ExitStack() as st2:
                            softmax_attn(st2, [(QT, KT, 0, 96, (lambda kb, h=h: Va[:, kb, h, 0:65]))], 65, 1.0 / math.sqrt(96.0), fin)
                            P.emit()
            epilogue(l, 1, O_MZ)

        def branch_diff(l):
            lam_init = 0.8 - 0.6 * math.exp(-0.3 * l)
            with ExitStack() as bst:
                Vd = P.sb(bst, "Vd", [128, NT, 4, 132], BF16)
                lam = P.sb(bst, "lam", [128, 1], F32)
                gd = P.sb(bst, "gd", [128, 128], F32)
                with ExitStack() as st:
                    Wv = P.sb(st, "Wdv", [128, KC, 512], BF16)
                    dl = P.sb(st, "dl", [128, 256], F32)
                    pr_ = P.sb(st, "prd", [128, 128], F32)
                    sm = P.sb(st, "smd", [128, 2], F32)
                    load_w(Wv, w_in[l], KC, O_DV, O_DV + 512)
                    DMA("sync", dl[:], dlam[l:l + 1, :].to_broadcast([128, 256]), writes=[dl])
                    DMA("sync", gd[:], dng[l:l + 1, :].to_broadcast([128, 128]), writes=[gd])
                    dl3 = dl[:].rearrange("p (a b) -> p a b", a=2)
                    TT("vector", pr_[:].rearrange("p (a b) -> p a b", a=2), dl3[:, :, 0:64], dl3[:, :, 64:128], ALU.mult, [dl], [pr_])
                    P.op("vector", lambda e: e.reduce_sum(out=sm[:], in_=pr_[:].rearrange("p (a b) -> p a b", a=2), axis=mybir.AxisListType.X), [pr_], [sm])
                    ACT(sm[:], sm[:], AF.Exp, [sm], [sm])
                    TT("vector", lam[:], sm[:, 0:1], sm[:, 1:2], ALU.subtract, [sm], [lam])
                    TS("vector", lam[:], lam[:], lam_init, None, ALU.add, None, [lam], [lam])
                    MEMSET("gpsimd", Vd[:].rearrange("p a b c -> p (a b c)"), 1.0, [Vd])
                    for i in range(NT):
                        p = proj_tok(i, Wv, 0, 512)
                        CP("scalar" if i % 2 else "vector", Vd[:, i, :, 0:128], p[:, :].rearrange("p (h d) -> p h d", h=4), [p], [Vd])
                    P.emit()
                with ExitStack() as st:
                    Wq = P.sb(st, "Wdq", [128, KC, 512], BF16)
                    QT = P.sb(st, "QTd", [128, L], BF16)
                    KT = P.sb(st, "KTd", [128, L], BF16)
                    t1 = P.sb(st, "t1d", [128, 512], F32)
                    t2 = P.sb(st, "t2d", [128, 512], F32)
                    rc = P.sb(st, "rcd", [128, 2], F32)
                    tm = P.sb(st, "tmd", [128, 128], F32)
                    oc_ = P.sb(st, "ocd", [128, 128], F32)
                    jk = P.sb(st, "jkd", [128, 128], BF16)
                    ssd = P.sb(st, "ssd", [128, 1], F32)
                    rsd = P.sb(st, "rsd", [128, 1], F32)
                    for h in range(4):
                        load_w(Wq, w_in[l], KC, O_DQ + h * 128, O_DQ + (h + 1) * 128, dst_c0=0)
                        load_w(Wq, w_x[l], KC, h * 128, (h + 1) * 128, dst_c0=128)
                        load_w(Wq, w_in[l], KC, O_DK + h * 128, O_DK + (h + 1) * 128, dst_c0=256)
                        load_w(Wq, w_x[l], KC, 512 + h * 128, 512 + (h + 1) * 128, dst_c0=384)
                        cc = 0
                        for (dst, wc) in ((QT, 0), (KT, 256)):
                            for (c0, n) in CHUNKS:
                                pa, pb = fb[2 * (cc % 2)], fb[1 + 2 * (cc % 2)]
                                cc += 1
                                proj_feat(pa, c0, n, Wq, wc, 128)
                                proj_feat(pb, c0, n, Wq, wc + 128, 128)
                                TT("vector", t1[:, 0:n], pa[:, 0:n], ropec[:, c0:c0 + n], ALU.mult, [pa, ropec], [t1])
                                TT("vector", t2[:, 0:n], pb[:, 0:n], ropes[:, c0:c0 + n], ALU.mult, [pb, ropes], [t2])
                                TT("vector", dst[:, c0:c0 + n], t1[:, 0:n], t2[:, 0:n], ALU.add, [t1, t2], [dst])
                                CP("scalar", dst[32:64, c0:c0 + n], pa[32:64, 0:n], [pa], [dst])
                        P.emit()

                        def fin(i, Os, h=h):
                            O0, O1 = Os
                            P.op("vector", lambda e: e.reciprocal(out=rc[:, 0:1], in_=O0[:, 128:129]), [O0], [rc])
                            P.op("vector", lambda e: e.reciprocal(out=rc[:, 1:2], in_=O1[:, 128:129]), [O1], [rc])
                            TT("vector", rc[:, 1:2], rc[:, 1:2], lam[:, 0:1], ALU.mult, [rc, lam], [rc])
                            TS("vector", tm[:], O1[:, 0:128], rc[:, 1:2], None, ALU.mult, None, [O1, rc], [tm])
                            STT("vector", oc_[:], O0[:, 0:128], rc[:, 0:1], tm[:], ALU.mult, ALU.subtract, [O0, rc, tm], [oc_])
                            MEMSET("vector", ssd[:], 0.0, [ssd])
                            ACT(jk[:], oc_[:], AF.Square, [oc_], [jk, ssd], accum_out=ssd[:, 0:1])
                            RSTD(rsd[:], ssd[:], 128, [ssd], [rsd], mult=1.0 - lam_init)
                            STT("vector", og[:, i, h * 128:(h + 1) * 128], oc_[:], rsd[:, 0:1], gd[:], ALU.mult, ALU.mult, [oc_, rsd, gd], [og])

                        with ExitStack() as st2:
                            diff_attn(st2, QT, KT, (lambda kb, h=h: Vd[:, kb, h, 0:129]), fin)
                            P.emit()
            epilogue(l, 2, O_DZ)

        import os as _os2
        _epi_only = _os2.environ.get("EPI_ONLY")
        for l in range(nlayers):
            phase_norm(l)
            if _epi_only:
                for _ in range(int(_epi_only)):
                    epilogue(l, 0, O_SBZ)
                continue
            if 0 in branches:
                branch_sb(l)
            if 1 in branches:
                branch_mla(l)
            if 2 in branches:
                branch_diff(l)

        with ExitStack() as st:
            grep = P.sb(st, "grepf", [128, D], F32)
            junk = P.sb(st, "junkf", [128, D], BF16)
            ss = P.sb(st, "ssf", [128, NT], F32)
            rs = P.sb(st, "rsf", [128, NT], F32)
            yo = [P.sb(st, "yo%d" % j, [128, D], F32) for j in range(2)]
            bcast_load(grep, final_g[0:1, :], D)
            MEMSET("vector", ss[:], 0.0, [ss])
            for i in range(NT):
                o = yo[i % 2]
                if final_norm:
                    ACT(junk[:], X[:, i, :], AF.Square, [X], [junk, ss], accum_out=ss[:, i:i + 1])
                    RSTD(rs[:, i:i + 1], ss[:, i:i + 1], D, [ss], [rs])
                    STT("vector", o[:], X[:, i, :], rs[:, i:i + 1], grep[:], ALU.mult, ALU.mult, [X, rs, grep], [o])
                else:
                    CP("vector", o[:], X[:, i, :], [X], [o])
                p_lo = NMETA if i == 0 else 0
                p_hi = NMETA if i == NT - 1 else 128
                s0 = 128 * i - NMETA + p_lo
                DMA("sync", y[s0:s0 + (p_hi - p_lo), :], o[p_lo:p_hi, :], reads=[o])
            P.wait_all("sync", yo)
            P.emit()
    return nc


_CACHE = {}


def _consts():
    if "c" not in _CACHE:
        C, Sg = _rope_tables()
        tri, m01, ident = _masks()
        _CACHE["c"] = {"c_ropec": C, "c_ropes": Sg, "c_tri": tri, "c_m01": m01, "c_ident": ident}
    return _CACHE["c"]


def make_in_maps(inp):
    x = np.asarray(inp["x"], np.float32)
    B = x.shape[0]
    meta = np.asarray(inp["meta_tokens"], np.float32)
    lay = _host_layouts({k: np.asarray(v) for k, v in inp.items()})
    shared = dict(_consts())
    shared.update(lay)
    for k in ("norm_g", "w_in", "mla_cq_g", "mla_ckv_g", "diff_norm_g", "w_o_sb", "w_o_mla", "w_o_diff", "w_out"):
        shared[k] = np.ascontiguousarray(np.asarray(inp[k], np.float32))
    shared["diff_lambda"] = np.ascontiguousarray(np.asarray(inp["diff_lambda"], np.float32).reshape(2, 256))
    shared["final_g"] = np.ascontiguousarray(np.asarray(inp["final_g"], np.float32).reshape(1, D))
    maps = []
    for b in range(B):
        h0 = np.concatenate([meta, x[b], np.zeros((L - NMETA - S, D), np.float32)], axis=0)
        m = dict(shared)
        m["h0"] = np.ascontiguousarray(h0)
        maps.append(m)
    return maps


def kernel(**inputs):
    maps = make_in_maps(inputs)
    if "nc" not in _CACHE:
        _CACHE["nc"] = build_nc()
    res = run_bass_kernel_spmd(_CACHE["nc"], maps, core_ids=list(range(len(maps))))
    return np.stack([np.asarray(r["y"], np.float32) for r in res.results], axis=0)
```

```python
import math
import numpy as np
import ml_dtypes
from contextlib import ExitStack
import concourse.bass as bass
import concourse.mybir as mybir
from concourse.bass_utils import run_bass_kernel_spmd

F32 = mybir.dt.float32
BF16 = mybir.dt.bfloat16
AF = mybir.ActivationFunctionType
ALU = mybir.AluOpType

D = 1024
S = 2048
NMETA = 16
NT = 17
L = NT * 128
KC = 8
EPS = 1e-6
THETA = 500000.0
CHUNKS = [(0, 512), (512, 512), (1024, 512), (1536, 512), (2048, 128)]


class Buf:
    __slots__ = ("name", "t", "w", "r", "dsem", "dcnt")

    def __init__(self, name, t=None):
        self.name = name
        self.t = t
        self.w = []
        self.r = []
        self.dsem = None
        self.dcnt = 0

    def __getitem__(self, idx):
        return self.t[idx]


class Prog:
    ENGS = ("tensor", "vector", "scalar", "gpsimd", "sync")

    def __init__(self, nc, stack):
        self.nc = nc
        self.stack = stack
        self.sems = {}
        self.cnt = {e: 0 for e in self.ENGS}
        self.seen = {e: {} for e in self.ENGS}
        self.q = {e: [] for e in self.ENGS}
        self.snaps = {}
        for e in self.ENGS:
            self._sem("E_" + e)
        self.nbuf = 0

    def _sem(self, key):
        if key not in self.sems:
            self.sems[key] = self.stack.enter_context(self.nc.semaphore(key))
        return key

    def sb(self, st, name, shape, dt):
        self.uid = getattr(self, "uid", 0) + 1
        name = "%s_%d" % (name, self.uid)
        t = st.enter_context(self.nc.sbuf_tensor(name, list(shape), dt))
        return Buf(name, t)

    def ps(self, st, name, shape, dt=F32):
        t = st.enter_context(self.nc.psum_tensor(name, list(shape), dt))
        return Buf(name, t)

    def _waits(self, eng, reads, writes):
        own = "E_" + eng
        need = {}
        for b in reads:
            for (k, v) in b.w:
                if k == own and (eng == "tensor" or v > self.cnt[eng]):
                    continue
                if need.get(k, 0) < v:
                    need[k] = v
        for b in writes:
            for (k, v) in b.w:
                if k == own and (eng == "tensor" or v > self.cnt[eng]):
                    continue
                if need.get(k, 0) < v:
                    need[k] = v
            for (k, v) in b.r:
                if k == own and (eng == "tensor" or v > self.cnt[eng]):
                    continue
                if need.get(k, 0) < v:
                    need[k] = v
        out = []
        seen = self.seen[eng]
        snaps = self.snaps
        for k, v in sorted(need.items(), key=lambda kv: 0 if kv[0].startswith("E_") else 1):
            if seen.get(k, 0) < v:
                seen[k] = v
                out.append((k, v))
                sn = snaps.get((k, v))
                if sn:
                    for k2, v2 in sn.items():
                        if seen.get(k2, 0) < v2:
                            seen[k2] = v2
        return out

    def op(self, eng, fn, reads=(), writes=(), inc=True):
        waits = self._waits(eng, reads, writes)
        key = "E_" + eng
        val = self.cnt[eng] + 1
        if inc:
            self.cnt[eng] = val
        ev = (key, val)
        if inc:
            self.snaps[ev] = dict(self.seen[eng])
        for b in writes:
            b.w = [ev]
            b.r = []
        for b in reads:
            b.r = [e for e in b.r if e[0] != key] + [ev]
        self.q[eng].append((waits, fn, [(key, 1)] if inc else []))

    def dma(self, eng, fn, reads=(), writes=(), sem_buf=None):
        waits = self._waits(eng, reads, writes)
        sb = sem_buf or (writes[0] if writes else reads[0])
        if sb.dsem is None:
            sb.dsem = self._sem("D_%d" % self.nbuf)
            self.nbuf += 1
        sb.dcnt += 16
        ev = (sb.dsem, sb.dcnt)
        self.snaps[ev] = dict(self.seen[eng])
        for b in writes:
            b.w = [e for e in b.w if e[0] != sb.dsem and e[0].startswith("D_")] + [ev]
            b.r = []
        for b in reads:
            b.r = [e for e in b.r if e[0] != sb.dsem] + [ev]
        self.q[eng].append((waits, fn, [(sb.dsem, 16)]))

    def wait_all(self, eng, bufs):
        waits = self._waits(eng, (), bufs)
        self.q[eng].append((waits, None, []))

    def emit(self):
        nc = self.nc
        qs = self.q
        self.q = {e: [] for e in self.ENGS}
        sems = self.sems
        with nc.Block() as block:
            def mk(ename):
                items = qs[ename]

                def body(e):
                    for waits, fn, incs in items:
                        if fn is None:
                            for (k, v) in waits:
                                e.wait_ge(sems[k], v)
                            continue
                        for (k, v) in waits[1:]:
                            e.wait_ge(sems[k], v)
                        ins = fn(e)
                        if waits:
                            ins._wait_ge(sems[waits[0][0]], waits[0][1])
                        for (k, n) in incs:
                            ins.then_inc(sems[k], n)
                return body
            block.tensor(mk("tensor"))
            block.vector(mk("vector"))
            block.scalar(mk("scalar"))
            block.gpsimd(mk("gpsimd"))
            block.sync(mk("sync"))


def _rope_tables():
    pos = np.arange(L, dtype=np.float32)
    C = np.ones((128, L), np.float32)
    Sg = np.zeros((128, L), np.float32)
    inv_d = (np.float32(THETA) ** (-np.arange(0, 16, 2, dtype=np.float32) / np.float32(16))).astype(np.float32)
    ang_d = (pos[:, None] * inv_d[None, :]).astype(np.float32)
    cd, sd = np.cos(ang_d).astype(np.float32), np.sin(ang_d).astype(np.float32)
    for base in (0, 64):
        for r in range(16):
            C[base + r] = cd[:, r % 8]
            Sg[base + r] = -sd[:, r % 8] if r < 8 else sd[:, r % 8]
    inv_m = (np.float32(THETA) ** (-np.arange(0, 32, 2, dtype=np.float32) / np.float32(32))).astype(np.float32)
    ang_m = (pos[:, None] * inv_m[None, :]).astype(np.float32)
    cm, sm = np.cos(ang_m).astype(np.float32), np.sin(ang_m).astype(np.float32)
    for r in range(32):
        C[32 + r] = cm[:, r % 16]
        Sg[32 + r] = -sm[:, r % 16] if r < 16 else sm[:, r % 16]
    return C, Sg


def _masks():
    a = np.arange(128)
    tri = (a[None, :] < a[:, None]).astype(np.float32)
    cq = (a + 48) // 64
    m0 = (cq[:, None] <= cq[None, :])
    m1 = ((a[:, None] < 16) & (a[None, :] >= 80))
    m01 = np.concatenate([m0, m1], axis=1).astype(np.float32).astype(ml_dtypes.bfloat16)
    ident = np.eye(128, dtype=np.float32).astype(ml_dtypes.bfloat16)
    return tri, m01, ident


O_SBQ, O_SBK, O_SBV, O_SBZ = 0, 512, 1024, 1536
O_CQ, O_CKV, O_KR, O_MZ = 2048, 2432, 2688, 2720
O_DQ, O_DK, O_DV, O_DZ = 3232, 3744, 4256, 4768
O_G = 5280


def _host_layouts(inp):
    w_in = inp["w_in"]
    swap64 = np.concatenate([np.arange(8, 16), np.arange(0, 8), np.arange(16, 64)])
    idx_d = np.concatenate([m * 64 + swap64 for m in range(8)])
    kr_sw = np.concatenate([np.arange(16, 32), np.arange(0, 16)])
    w_x = np.concatenate([w_in[:, :, O_DQ + idx_d], w_in[:, :, O_DK + idx_d], w_in[:, :, O_KR + kr_sw]], axis=2)
    uq = inp["mla_w_uq"]
    ia, ib = [], []
    for h in range(8):
        b = 96 * h
        ia += list(range(b, b + 32)) + list(range(b + 64, b + 96)) + list(range(b + 32, b + 64))
        ib += list(range(b, b + 32)) + list(range(b + 80, b + 96)) + list(range(b + 64, b + 80)) + list(range(b + 32, b + 64))
    uqa = uq[:, :, np.array(ia)]
    uqb = uq[:, :, np.array(ib)]
    ukv = inp["mla_w_ukv"]
    ikn, iv = [], []
    for h in range(8):
        b = 128 * h
        ikn += list(range(b, b + 32)) + list(range(b, b + 32)) + list(range(b + 32, b + 64))
        iv += list(range(b + 64, b + 128))
    ukn = ukv[:, :, np.array(ikn)]
    ukvv = ukv[:, :, np.array(iv)]
    bg = inp["b_gate"].reshape(2, 3, 8, 128).transpose(0, 1, 3, 2)
    return {
        "w_x": np.ascontiguousarray(w_x),
        "uqa": np.ascontiguousarray(uqa), "uqb": np.ascontiguousarray(uqb),
        "ukn": np.ascontiguousarray(ukn), "ukvv": np.ascontiguousarray(ukvv),
        "bg": np.ascontiguousarray(bg),
    }


def build_nc(nlayers=2, final_norm=True, branches=(0, 1, 2)):
    nc = bass.Bass("TRN2", target_bir_lowering=False)

    def din(name, shape, dt=F32):
        return nc.dram_tensor(name, list(shape), dt, kind="ExternalInput").ap()

    h0 = din("h0", [L, D])
    norm_g = din("norm_g", [2, D])
    w_in = din("w_in", [2, D, 8352])
    w_x = din("w_x", [2, D, 1056])
    bg = din("bg", [2, 3, 128, 8])
    cq_g = din("mla_cq_g", [2, 384])
    ckv_g = din("mla_ckv_g", [2, 256])
    uqa = din("uqa", [2, 384, 768])
    uqb = din("uqb", [2, 384, 768])
    ukn = din("ukn", [2, 256, 768])
    ukvv = din("ukvv", [2, 256, 512])
    dlam = din("diff_lambda", [2, 256])
    dng = din("diff_norm_g", [2, 128])
    w_o = [din("w_o_sb", [2, 512, D]), din("w_o_mla", [2, 512, D]), din("w_o_diff", [2, 512, D])]
    w_out = din("w_out", [2, D, D])
    final_g = din("final_g", [1, D])
    c_ropec = din("c_ropec", [128, L])
    c_ropes = din("c_ropes", [128, L])
    c_tri = din("c_tri", [128, 128])
    c_m01 = din("c_m01", [128, 256], BF16)
    c_ident = din("c_ident", [128, 128], BF16)
    y = nc.dram_tensor("y", [S, D], F32, kind="ExternalOutput").ap()

    with ExitStack() as top:
        P = Prog(nc, top)
        X = P.sb(top, "X", [128, NT, D], F32)
        hT = P.sb(top, "hT", [128, KC, L], BF16)
        og = P.sb(top, "og", [128, NT, 512], BF16)
        ropec = P.sb(top, "ropec", [128, L], F32)
        ropes = P.sb(top, "ropes", [128, L], F32)
        tri = P.sb(top, "tri", [128, 128], F32)
        m01 = P.sb(top, "m01", [128, 256], BF16)
        ident = P.sb(top, "ident", [128, 128], BF16)
        z2 = [P.ps(top, "z2_%d" % i, [128, 1024], F32) for i in range(2)]
        fb = [Buf("fb0", z2[0][:, 0:512]), Buf("fb1", z2[0][:, 512:1024]), Buf("fb2", z2[1][:, 0:512]), Buf("fb3", z2[1][:, 512:1024])]
        fb += [P.ps(top, "fb%d" % i, [128, 512], F32) for i in range(4, 8)]
        tb = [Buf("tb%d" % i, fb[6 + i][:].bitcast(BF16)) for i in range(2)]
        for i in range(2):
            tb[i].w, tb[i].r = fb[6 + i].w, fb[6 + i].r

        def MM(out, lhsT, rhs, start, stop, reads, writes, inc=True):
            P.op("tensor", lambda e: e.matmul(out, lhsT=lhsT, rhs=rhs, start=start, stop=stop), reads, writes, inc)

        def TR(out, in_, reads, writes, inc=True):
            P.op("tensor", lambda e: e.transpose(out=out, in_=in_, identity=ident[:]), list(reads) + [ident], writes, inc)

        def ACT(out, in_, func, reads, writes, bias=None, scale=None, accum_out=None):
            kw = {}
            if bias is not None:
                kw["bias"] = bias
            if scale is not None:
                kw["scale"] = scale
            if accum_out is not None:
                kw["accum_out"] = accum_out
            P.op("scalar", lambda e: e.activation(out=out, in_=in_, func=func, **kw), reads, writes)

        def TT(eng, out, in0, in1, op, reads, writes):
            P.op(eng, lambda e: e.tensor_tensor(out=out, in0=in0, in1=in1, op=op), reads, writes)

        def TS(eng, out, in0, s1, s2, op0, op1, reads, writes):
            if op1 is None:
                P.op(eng, lambda e: e.tensor_scalar(out=out, in0=in0, scalar1=s1, scalar2=None, op0=op0), reads, writes)
            else:
                P.op(eng, lambda e: e.tensor_scalar(out=out, in0=in0, scalar1=s1, scalar2=s2, op0=op0, op1=op1), reads, writes)

        def STT(eng, out, in0, scalar, in1, op0, op1, reads, writes):
            P.op(eng, lambda e: e.scalar_tensor_tensor(out=out, in0=in0, scalar=scalar, in1=in1, op0=op0, op1=op1), reads, writes)

        def RSTD(out, ss_ap, n, reads, writes, mult=1.0):
            ACT(out, ss_ap, AF.Ln, reads, writes, bias=EPS, scale=1.0 / n)
            ACT(out, out, AF.Exp, writes, writes, bias=(math.log(mult) if mult != 1.0 else None), scale=-0.5)

        def CP(eng, out, in_, reads, writes):
            if eng == "scalar":
                P.op(eng, lambda e: e.copy(out=out, in_=in_), reads, writes)
            else:
                P.op(eng, lambda e: e.tensor_copy(out=out, in_=in_), reads, writes)

        def MEMSET(eng, ap, val, writes):
            P.op(eng, lambda e: e.memset(ap, val), (), writes)

        def DMA(eng, out, in_, reads=(), writes=()):
            P.dma(eng, lambda e: e.dma_start(out=out, in_=in_), reads, writes)

        def load_w(buf, dram2d, k_chunks, c0, c1, dst_c0=0):
            v = dram2d.rearrange("(k p) c -> p k c", p=128)
            DMA("gpsimd", buf[:, 0:k_chunks, dst_c0:dst_c0 + (c1 - c0)], v[:, :, c0:c1], writes=[buf])

        def bcast_load(buf, row_ap, n):
            DMA("sync", buf[:], row_ap.to_broadcast([128, n]), writes=[buf])

        fctr = [0]

        def next_f(lo=0, hi=4):
            b = fb[lo + fctr[0] % (hi - lo)]
            fctr[0] += 1
            return b

        DMA("sync", ropec[:], c_ropec[:, :], writes=[ropec])
        DMA("sync", ropes[:], c_ropes[:, :], writes=[ropes])
        DMA("sync", tri[:], c_tri[:, :], writes=[tri])
        DMA("sync", m01[:], c_m01[:, :], writes=[m01])
        DMA("sync", ident[:], c_ident[:, :], writes=[ident])
        h0v = h0.rearrange("(t p) d -> p t d", p=128)
        for i in range(NT):
            DMA("sync", X[:, i, :], h0v[:, i, :], writes=[X])
        P.emit()

        def phase_norm(l):
            with ExitStack() as st:
                grep = P.sb(st, "grep", [128, D], F32)
                junk = P.sb(st, "junk", [128, D], BF16)
                ss = P.sb(st, "ss", [128, NT], F32)
                rs = P.sb(st, "rs", [128, NT], F32)
                hn = [P.sb(st, "hn%d" % j, [128, D], BF16) for j in range(2)]
                bcast_load(grep, norm_g[l:l + 1, :], D)
                MEMSET("vector", ss[:], 0.0, [ss])
                for i in range(NT):
                    ACT(junk[:], X[:, i, :], AF.Square, [X], [junk, ss], accum_out=ss[:, i:i + 1])
                    RSTD(rs[:, i:i + 1], ss[:, i:i + 1], D, [ss], [rs])
                    h = hn[i % 2]
                    STT("vector", h[:], X[:, i, :], rs[:, i:i + 1], grep[:], ALU.mult, ALU.mult, [X, rs, grep], [h])
                    t = tb[i % 2]
                    for k in range(KC):
                        TR(t[:, k * 128:(k + 1) * 128], h[:, k * 128:(k + 1) * 128], [h], [t], inc=(k == KC - 1))
                    CP("scalar", hT[:, :, i * 128:(i + 1) * 128], t[:].rearrange("p (k c) -> p k c", k=KC), [t], [hT])
                P.emit()

        def proj_tok(i, W, c0, n, kchunks=KC, src=None, src_k0=0):
            src = src or hT
            p = next_f()
            for k in range(kchunks):
                MM(p[:, 0:n], src[:, src_k0 + k, i * 128:(i + 1) * 128], W[:, k, c0:c0 + n], k == 0, k == kchunks - 1,
                   [src, W], [p], inc=(k == kchunks - 1))
            return p

        def proj_feat(p, c0, n, W, wc0, M, kchunks=KC, src=None, src_k0=0):
            src = src or hT
            for k in range(kchunks):
                MM(p[0:M, 0:n], W[:, k, wc0:wc0 + M], src[:, src_k0 + k, c0:c0 + n], k == 0, k == kchunks - 1,
                   [src, W], [p], inc=(k == kchunks - 1))

        def epilogue(l, b, zoff):
            with ExitStack() as st:
                Wz = P.sb(st, "Wz", [128, KC, 512], BF16)
                Wo = P.sb(st, "Wo", [128, 4, D], BF16)
                Wg = P.sb(st, "Wg", [128, KC, D], BF16)
                Wout = P.sb(st, "Wout", [128, KC, D], BF16)
                bgt = P.sb(st, "bgt", [128, 8], F32)
                G = [P.sb(st, "G%d" % j, [128, 512], BF16) for j in range(2)]
                ogg = [P.sb(st, "ogg%d" % j, [128, 512], BF16) for j in range(4)]
                oggT = P.sb(st, "oggT", [128, 4, 512], BF16)
                sg = [P.sb(st, "sg%d" % j, [128, 512], F32) for j in range(2)]
                mT = P.sb(st, "mT", [128, KC, 512], BF16)
                load_w(Wz, w_in[l], KC, zoff, zoff + 512)
                load_w(Wo, w_o[b][l], 4, 0, D)
                load_w(Wg, w_in[l], KC, O_G + b * D, O_G + (b + 1) * D)
                load_w(Wout, w_out[l], KC, 0, D)
                DMA("sync", bgt[:], bg[l, b, :, :], writes=[bgt])
                cnt = 0
                for (c0, n) in CHUNKS:
                    tiles = list(range(c0 // 128, (c0 + n) // 128))
                    pzs = [proj_tok(i, Wz, 0, 512) for i in tiles]
                    for oc in range(2):
                        proj_feat(fb[4 + oc], c0, n, Wg, oc * 128, 128)
                    for j, i in enumerate(tiles):
                        pz = pzs[j]
                        g_, o_ = G[cnt % 2], ogg[cnt % 4]
                        ACT(g_[:], pz[:, :], AF.Silu, [pz], [g_])
                        TT("vector", o_[:], og[:, i, :], g_[:], ALU.mult, [og, g_], [o_])
                        cnt += 1
                        t = tb[j % 2]
                        for c in range(4):
                            TR(t[:, c * 128:(c + 1) * 128], o_[:, c * 128:(c + 1) * 128], [o_], [t], inc=(c == 3))
                        CP("vector", oggT[:, :, j * 128:(j + 1) * 128], t[:, 0:512].rearrange("p (c q) -> p c q", c=4), [t], [oggT])
                    for oc in range(8):
                        pg = fb[4 + oc % 2]
                        if oc >= 2:
                            proj_feat(pg, c0, n, Wg, oc * 128, 128)
                        py = next_f()
                        for c in range(4):
                            MM(py[:, 0:n], Wo[:, c, oc * 128:(oc + 1) * 128], oggT[:, c, 0:n], c == 0, c == 3, [Wo, oggT], [py], inc=(c == 3))
                        s_ = sg[oc % 2]
                        ACT(s_[:, 0:n], pg[:, 0:n], AF.Sigmoid, [pg, bgt], [s_], bias=bgt[:, oc:oc + 1])
                        TT("vector", mT[:, oc, 0:n], s_[:, 0:n], py[:, 0:n], ALU.mult, [s_, py], [mT])
                    for j, i in enumerate(tiles):
                        for half in range(2):
                            po = next_f()
                            for k in range(KC):
                                MM(po[:, :], mT[:, k, j * 128:(j + 1) * 128], Wout[:, k, half * 512:(half + 1) * 512], k == 0, k == KC - 1,
                                   [mT, Wout], [po], inc=(k == KC - 1))
                            TT("vector", X[:, i, half * 512:(half + 1) * 512], X[:, i, half * 512:(half + 1) * 512], po[:, :], ALU.add, [X, po], [X])
                P.emit()

        def branch_sb(l):
            with ExitStack() as bst:
                V = P.sb(bst, "Vsb", [128, NT, 512], BF16)
                with ExitStack() as st:
                    Wv = P.sb(st, "Wv", [128, KC, 512], BF16)
                    load_w(Wv, w_in[l], KC, O_SBV, O_SBV + 512)
                    for i in range(NT):
                        p = proj_tok(i, Wv, 0, 512)
                        CP("scalar" if i % 2 else "vector", V[:, i, :], p[:, :], [p], [V])
                    P.emit()
                with ExitStack() as st:
                    Wqk = P.sb(st, "Wqk", [128, KC, 256], BF16)
                    qT = P.sb(st, "qT", [128, L], BF16)
                    kT = P.sb(st, "kT", [128, L], BF16)
                    NE, NSP, NC = 5, 4, 4
                    ez = [P.sb(st, "ez%d" % j, [128, 512], F32) for j in range(NE)]
                    sp = [P.sb(st, "sp%d" % j, [128, 512], F32) for j in range(NSP)]
                    Cb = [P.sb(st, "Cb%d" % j, [128, 512], F32) for j in range(NC)]
                    wb = [P.sb(st, "wb%d" % j, [128, 512], BF16) for j in range(2)]
                    wT = [P.sb(st, "wT%d" % j, [128, 512], BF16) for j in range(2)]
                    cn = [P.sb(st, "cn%d" % j, [128, 1], F32) for j in range(3)]
                    for pr in range(4):
                        load_w(Wqk, w_in[l], KC, O_SBQ + pr * 128, O_SBQ + (pr + 1) * 128, dst_c0=0)
                        load_w(Wqk, w_in[l], KC, O_SBK + pr * 128, O_SBK + (pr + 1) * 128, dst_c0=128)
                        for (c0, n) in CHUNKS:
                            p = next_f(4, 6)
                            proj_feat(p, c0, n, Wqk, 0, 128)
                            CP("scalar", qT[:, c0:c0 + n], p[:, 0:n], [p], [qT])
                            p = next_f(4, 6)
                            proj_feat(p, c0, n, Wqk, 128, 128)
                            CP("vector", kT[:, c0:c0 + n], p[:, 0:n], [p], [kT])
                        items = []
                        for hh in range(2):
                            for i in range(NT):
                                nk = (i + 1) * 128
                                chs = [(k0, min(512, nk - k0)) for k0 in range(0, nk, 512)][::-1]
                                for ci, (k0, n) in enumerate(chs):
                                    items.append((hh, i, ci, k0, n, ci == len(chs) - 1))
                        N = len(items)
                        obank = {}
                        ocnt = [0]

                        def st_mm(j):
                            hh, i, ci, k0, n, last = items[j]
                            r0 = 64 * hh
                            z = fb[j % 2]
                            MM(z[:, 0:n], qT[r0:r0 + 64, i * 128:(i + 1) * 128], kT[r0:r0 + 64, k0:k0 + n], True, True, [qT, kT], [z])

                        def st_expz(j):
                            hh, i, ci, k0, n, last = items[j]
                            e_ = ez[j % NE]
                            ACT(e_[:, 0:n], fb[j % 2][:, 0:n], AF.Exp, [fb[j % 2]], [e_], scale=0.125)
                            if ci == 0:
                                TT("gpsimd", e_[:, n - 128:n], e_[:, n - 128:n], tri[:], ALU.mult, [e_, tri], [e_])

                        def st_ln(j):
                            hh, i, ci, k0, n, last = items[j]
                            ACT(sp[j % NSP][:, 0:n], ez[j % NE][:, 0:n], AF.Ln, [ez[j % NE]], [sp[j % NSP]], bias=1.0)

                        def st_scan(j):
                            hh, i, ci, k0, n, last = items[j]
                            s_, c_ = sp[j % NSP], Cb[j % NC]
                            P.op("vector", lambda e: e.tensor_tensor_scan(out=c_[:, 0:n], data0=s_[:, 0:n], data1=s_[:, 0:n],
                                                                           initial=0.0, op0=ALU.add, op1=ALU.max), [s_], [c_])

                        def st_cn(j):
                            hh, i, ci, k0, n, last = items[j]
                            s_, c_ = sp[j % NSP], Cb[j % NC]
                            if ci == 0:
                                CP("vector", cn[j % 3][:], c_[:, n - 1:n], [c_], [cn[j % 3]])
                            else:
                                TT("vector", cn[j % 3][:], cn[(j - 1) % 3][:], c_[:, n - 1:n], ALU.add, [cn[(j - 1) % 3], c_], [cn[j % 3]])

                        def st_stt(j):
                            hh, i, ci, k0, n, last = items[j]
                            s_, c_ = sp[j % NSP], Cb[j % NC]
                            STT("vector", s_[:, 0:n], c_[:, 0:n], cn[j % 3][:, 0:1], s_[:, 0:n], ALU.subtract, ALU.subtract,
                                [c_, cn[j % 3], s_], [s_])

                        def st_expt(j):
                            hh, i, ci, k0, n, last = items[j]
                            ACT(Cb[j % NC][:, 0:n], sp[j % NSP][:, 0:n], AF.Exp, [sp[j % NSP]], [Cb[j % NC]])

                        def st_mult(j):
                            hh, i, ci, k0, n, last = items[j]
                            TT("gpsimd", wb[j % 2][:, 0:n], ez[j % NE][:, 0:n], Cb[j % NC][:, 0:n], ALU.mult, [ez[j % NE], Cb[j % NC]], [wb[j % 2]])

                        def st_tr(j):
                            hh, i, ci, k0, n, last = items[j]
                            t = tb[j % 2]
                            nb = n // 128
                            for jb in range(nb):
                                TR(t[:, jb * 128:(jb + 1) * 128], wb[j % 2][:, jb * 128:(jb + 1) * 128], [wb[j % 2]], [t], inc=(jb == nb - 1))

                        def st_evac(j):
                            hh, i, ci, k0, n, last = items[j]
                            CP("scalar", wT[j % 2][:, 0:n], tb[j % 2][:, 0:n], [tb[j % 2]], [wT[j % 2]])

                        def st_pv(j):
                            hh, i, ci, k0, n, last = items[j]
                            h = 2 * pr + hh
                            nb = n // 128
                            if ci == 0:
                                obank[(hh, i)] = fb[2 + ocnt[0] % 2]
                                ocnt[0] += 1
                            O = obank[(hh, i)]
                            for jb in range(nb):
                                kb = k0 // 128 + jb
                                MM(O[:, 0:64], wT[j % 2][:, jb * 128:(jb + 1) * 128], V[:, kb, h * 64:(h + 1) * 64],
                                   ci == 0 and jb == 0, last and jb == nb - 1, [wT[j % 2], V], [O], inc=(jb == nb - 1))
                            if last:
                                CP("vector", og[:, i, h * 64:(h + 1) * 64], O[:, 0:64], [O], [og])

                        sched = [(st_mm, 0), (st_expz, 1), (st_expt, 4), (st_ln, 1), (st_evac, 7), (st_cn, 3), (st_scan, 2), (st_stt, 3),
                                 (st_mult, 5), (st_tr, 6), (st_pv, 8)]
                        for step in range(N + 8):
                            for fn, off in sched:
                                if 0 <= step - off < N:
                                    fn(step - off)
                    P.emit()
            epilogue(l, 0, O_SBZ)

        def softmax_attn(st, units, dv1, scale, finalize):
            NS = 3
            import os as _os3
            if _os3.environ.get("SKIP_ATTN"):
                return
            PT = [P.sb(st, "PT%d" % j, [128, 512], BF16) for j in range(NS)]
            items = []
            for i in range(NT):
                kbs = list(range(0, min(i + 2, NT)))
                groups = [kbs[a:a + 4] for a in range(0, len(kbs), 4)]
                for u in range(len(units)):
                    for gi, g in enumerate(groups):
                        items.append((i, u, g, gi == 0, gi == len(groups) - 1))
            N = len(items)
            nu = len(units)

            def s1(j):
                i, u, g, first, last = items[j]
                QTb, KTb, r0, nr, vfn = units[u]
                z = fb[j % 2]
                for a, kb in enumerate(g):
                    MM(z[:, a * 128:(a + 1) * 128], KTb[r0:r0 + nr, kb * 128:(kb + 1) * 128], QTb[r0:r0 + nr, i * 128:(i + 1) * 128],
                       True, True, [QTb, KTb], [z], inc=(a == len(g) - 1))

            def s2(j):
                i, u, g, first, last = items[j]
                z = fb[j % 2]
                s = j % NS
                n = len(g) * 128
                ACT(PT[s][:, 0:n], z[:, 0:n], AF.Exp, [z], [PT[s]], scale=scale)
                for a, kb in enumerate(g):
                    if kb == i:
                        TT("gpsimd", PT[s][:, a * 128:(a + 1) * 128], PT[s][:, a * 128:(a + 1) * 128], m01[:, 0:128], ALU.mult, [PT[s], m01], [PT[s]])
                    elif kb == i + 1:
                        TT("gpsimd", PT[s][:, a * 128:(a + 1) * 128], PT[s][:, a * 128:(a + 1) * 128], m01[:, 128:256], ALU.mult, [PT[s], m01], [PT[s]])

            def s3(j):
                i, u, g, first, last = items[j]
                QTb, KTb, r0, nr, vfn = units[u]
                s = j % NS
                O = fb[2 + u] if nu > 1 else fb[2 + i % 2]
                for a, kb in enumerate(g):
                    MM(O[:, 0:dv1], PT[s][:, a * 128:(a + 1) * 128], vfn(kb), first and a == 0, last and a == len(g) - 1,
                       [PT[s]], [O], inc=(a == len(g) - 1))
                if last and u == nu - 1:
                    finalize(i, [fb[2 + uu] for uu in range(nu)] if nu > 1 else [O])

            for step in range(N + 2):
                if step < N:
                    s1(step)
                if 0 <= step - 1 < N:
                    s2(step - 1)
                if 0 <= step - 2 < N:
                    s3(step - 2)

        def diff_attn(st, QTb, KTb, vfn, finalize):
            import os as _os4
            if _os4.environ.get("SKIP_ATTN"):
                return
            NS = 3
            PT = [P.sb(st, "PTd%d" % j, [128, 1024], BF16) for j in range(NS)]
            items = []
            for i in range(NT):
                kbs = list(range(0, min(i + 2, NT)))
                groups = [kbs[a:a + 4] for a in range(0, len(kbs), 4)]
                for gi, g in enumerate(groups):
                    items.append((i, g, gi == 0, gi == len(groups) - 1))
            N = len(items)

            def s1(j):
                i, g, first, last = items[j]
                zt = z2[j % 2]
                for a, kb in enumerate(g):
                    for u in range(2):
                        r0 = 64 * u
                        MM(zt[:, u * 512 + a * 128:u * 512 + (a + 1) * 128], KTb[r0:r0 + 64, kb * 128:(kb + 1) * 128],
                           QTb[r0:r0 + 64, i * 128:(i + 1) * 128], True, True, [QTb, KTb], [zt], inc=(a == len(g) - 1 and u == 1))

            def s2(j):
                i, g, first, last = items[j]
                zt = z2[j % 2]
                p_ = PT[j % NS]
                n = len(g) * 128
                ACT(p_[:].rearrange("p (u c) -> p u c", u=2)[:, :, 0:n], zt[:].rearrange("p (u c) -> p u c", u=2)[:, :, 0:n],
                    AF.Exp, [zt], [p_], scale=0.125)
                for a, kb in enumerate(g):
                    if kb == i or kb == i + 1:
                        m = m01[:, 0:128] if kb == i else m01[:, 128:256]
                        for u in range(2):
                            sl = p_[:, u * 512 + a * 128:u * 512 + (a + 1) * 128]
                            TT("gpsimd", sl, sl, m, ALU.mult, [p_, m01], [p_])

            def s3(j):
                i, g, first, last = items[j]
                p_ = PT[j % NS]
                Os = [fb[4 + 2 * (i % 2)], fb[5 + 2 * (i % 2)]]
                for a, kb in enumerate(g):
                    for u in range(2):
                        MM(Os[u][:, 0:129], p_[:, u * 512 + a * 128:u * 512 + (a + 1) * 128], vfn(kb),
                           first and a == 0, last and a == len(g) - 1, [p_], [Os[u]], inc=(a == len(g) - 1))
                if last:
                    finalize(i, Os)

            for step in range(N + 2):
                if step < N:
                    s1(step)
                if 0 <= step - 1 < N:
                    s2(step - 1)
                if 0 <= step - 2 < N:
                    s3(step - 2)

        def branch_mla(l):
            with ExitStack() as bst:
                cnT = P.sb(bst, "cnT", [128, 5, L], BF16)
                Va = P.sb(bst, "Va", [128, NT, 8, 68], BF16)
                KT = P.sb(bst, "KTm", [128, L], BF16)
                with ExitStack() as st:
                    W = P.sb(st, "Wm", [128, KC, 704], BF16)
                    Wv = P.sb(st, "Wukvv", [128, 2, 512], BF16)
                    gq = P.sb(st, "gq", [128, 640], F32)
                    junk = P.sb(st, "junkm", [128, 384], BF16)
                    ss = P.sb(st, "ssm", [128, 2 * NT], F32)
                    rs = P.sb(st, "rsm", [128, 2 * NT], F32)
                    cb = [P.sb(st, "cb%d" % j, [128, 640], BF16) for j in range(2)]
                    t1 = P.sb(st, "t1m", [128, 512], F32)
                    t2 = P.sb(st, "t2m", [128, 512], F32)
                    load_w(W, w_in[l], KC, O_CQ, O_CQ + 672)
                    load_w(W, w_x[l], KC, 1024, 1056, dst_c0=672)
                    load_w(Wv, ukvv[l], 2, 0, 512)
                    DMA("sync", gq[:, 0:384], cq_g[l:l + 1, :].to_broadcast([128, 384]), writes=[gq])
                    DMA("sync", gq[:, 384:640], ckv_g[l:l + 1, :].to_broadcast([128, 256]), writes=[gq])
                    MEMSET("vector", ss[:], 0.0, [ss])
                    MEMSET("gpsimd", Va[:].rearrange("p a b c -> p (a b c)"), 1.0, [Va])
                    for i in range(NT):
                        c = cb[i % 2]
                        for part, (wc0, n, dc0) in enumerate(((0, 384, 0), (384, 256, 384))):
                            p = proj_tok(i, W, wc0, n)
                            col = 2 * i + part
                            ACT(junk[:, 0:n], p[:, 0:n], AF.Square, [p], [junk, ss], accum_out=ss[:, col:col + 1])
                            RSTD(rs[:, col:col + 1], ss[:, col:col + 1], n, [ss], [rs])
                            STT("vector", c[:, dc0:dc0 + n], p[:, 0:n], rs[:, col:col + 1], gq[:, dc0:dc0 + n], ALU.mult, ALU.mult, [p, rs, gq], [c])
                        t = tb[i % 2]
                        for k in range(5):
                            TR(t[:, k * 128:(k + 1) * 128], c[:, k * 128:(k + 1) * 128], [c], [t], inc=(k == 4))
                        CP("scalar", cnT[:, :, i * 128:(i + 1) * 128], t[:, 0:640].rearrange("p (k c) -> p k c", k=5), [t], [cnT])
                    for (c0, n) in CHUNKS:
                        pa = fb[4]
                        pb = fb[5]
                        proj_feat(pa, c0, n, W, 608, 64)
                        proj_feat(pb, c0, n, W, 640, 64)
                        TT("vector", t1[32:64, 0:n], pa[32:64, 0:n], ropec[32:64, c0:c0 + n], ALU.mult, [pa, ropec], [t1])
                        TT("vector", t2[32:64, 0:n], pb[32:64, 0:n], ropes[32:64, c0:c0 + n], ALU.mult, [pb, ropes], [t2])
                        TT("vector", KT[32:64, c0:c0 + n], t1[32:64, 0:n], t2[32:64, 0:n], ALU.add, [t1, t2], [KT])
                    for i in range(NT):
                        p = proj_tok(i, Wv, 0, 512, kchunks=2, src=cnT, src_k0=3)
                        CP("scalar" if i % 2 else "vector", Va[:, i, :, 0:64], p[:, :].rearrange("p (h d) -> p h d", h=8), [p], [Va])
                    P.emit()
                with ExitStack() as st:
                    QT = P.sb(st, "QTm", [128, L], BF16)
                    Wa = P.sb(st, "Wuqa", [128, 3, 768], BF16)
                    Wb = P.sb(st, "Wuqb", [128, 3, 768], BF16)
                    Wk = P.sb(st, "Wukn", [128, 2, 768], BF16)
                    t1 = P.sb(st, "t1q", [128, 512], F32)
                    t2 = P.sb(st, "t2q", [128, 512], F32)
                    rcp = P.sb(st, "rcp", [128, 1], F32)
                    load_w(Wa, uqa[l], 3, 0, 768)
                    load_w(Wb, uqb[l], 3, 0, 768)
                    load_w(Wk, ukn[l], 2, 0, 768)
                    for h in range(8):
                        for cidx, (c0, n) in enumerate(CHUNKS):
                            pa, pb = fb[4 + 2 * (cidx % 2)], fb[5 + 2 * (cidx % 2)]
                            proj_feat(pa, c0, n, Wa, h * 96, 96, kchunks=3, src=cnT)
                            proj_feat(pb, c0, n, Wb, h * 96, 96, kchunks=3, src=cnT)
                            CP("scalar", QT[0:32, c0:c0 + n], pa[0:32, 0:n], [pa], [QT])
                            CP("scalar", QT[64:96, c0:c0 + n], pa[64:96, 0:n], [pa], [QT])
                            TT("vector", t1[32:64, 0:n], pa[32:64, 0:n], ropec[32:64, c0:c0 + n], ALU.mult, [pa, ropec], [t1])
                            TT("vector", t2[32:64, 0:n], pb[32:64, 0:n], ropes[32:64, c0:c0 + n], ALU.mult, [pb, ropes], [t2])
                            TT("vector", QT[32:64, c0:c0 + n], t1[32:64, 0:n], t2[32:64, 0:n], ALU.add, [t1, t2], [QT])
                            pk = pb
                            proj_feat(pk, c0, n, Wk, h * 96, 96, kchunks=2, src=cnT, src_k0=3)
                            CP("scalar", KT[0:32, c0:c0 + n], pk[0:32, 0:n], [pk], [KT])
                            CP("vector", KT[64:96, c0:c0 + n], pk[64:96, 0:n], [pk], [KT])

                        def fin(i, Os, h=h):
                            O = Os[0]
                            P.op("vector", lambda e: e.reciprocal(out=rcp[:], in_=O[:, 64:65]), [O], [rcp])
                            TS("vector", og[:, i, h * 64:(h + 1) * 64], O[:, 0:64], rcp[:, 0:1], None, ALU.mult, None, [O, rcp], [og])

                        with ExitStack() as st2:
                            softmax_attn(st2, [(QT, KT, 0, 96, (lambda kb, h=h: Va[:, kb, h, 0:65]))], 65, 1.0 / math.sqrt(96.0), fin)
                            P.emit()
            epilogue(l, 1, O_MZ)

        def branch_diff(l):
            lam_init = 0.8 - 0.6 * math.exp(-0.3 * l)
            with ExitStack() as bst:
                Vd = P.sb(bst, "Vd", [128, NT, 4, 132], BF16)
                lam = P.sb(bst, "lam", [128, 1], F32)
                gd = P.sb(bst, "gd", [128, 128], F32)
                with ExitStack() as st:
                    Wv = P.sb(st, "Wdv", [128, KC, 512], BF16)
                    dl = P.sb(st, "dl", [128, 256], F32)
                    pr_ = P.sb(st, "prd", [128, 128], F32)
                    sm = P.sb(st, "smd", [128, 2], F32)
                    load_w(Wv, w_in[l], KC, O_DV, O_DV + 512)
                    DMA("sync", dl[:], dlam[l:l + 1, :].to_broadcast([128, 256]), writes=[dl])
                    DMA("sync", gd[:], dng[l:l + 1, :].to_broadcast([128, 128]), writes=[gd])
                    dl3 = dl[:].rearrange("p (a b) -> p a b", a=2)
                    TT("vector", pr_[:].rearrange("p (a b) -> p a b", a=2), dl3[:, :, 0:64], dl3[:, :, 64:128], ALU.mult, [dl], [pr_])
                    P.op("vector", lambda e: e.reduce_sum(out=sm[:], in_=pr_[:].rearrange("p (a b) -> p a b", a=2), axis=mybir.AxisListType.X), [pr_], [sm])
                    ACT(sm[:], sm[:], AF.Exp, [sm], [sm])
                    TT("vector", lam[:], sm[:, 0:1], sm[:, 1:2], ALU.subtract, [sm], [lam])
                    TS("vector", lam[:], lam[:], lam_init, None, ALU.add, None, [lam], [lam])
                    MEMSET("gpsimd", Vd[:].rearrange("p a b c -> p (a b c)"), 1.0, [Vd])
                    for i in range(NT):
                        p = proj_tok(i, Wv, 0, 512)
                        CP("scalar" if i % 2 else "vector", Vd[:, i, :, 0:128], p[:, :].rearrange("p (h d) -> p h d", h=4), [p], [Vd])
                    P.emit()
                with ExitStack() as st:
                    Wq = P.sb(st, "Wdq", [128, KC, 512], BF16)
                    QT = P.sb(st, "QTd", [128, L], BF16)
                    KT = P.sb(st, "KTd", [128, L], BF16)
                    t1 = P.sb(st, "t1d", [128, 512], F32)
                    t2 = P.sb(st, "t2d", [128, 512], F32)
                    rc = P.sb(st, "rcd", [128, 2], F32)
                    tm = P.sb(st, "tmd", [128, 128], F32)
                    oc_ = P.sb(st, "ocd", [128, 128], F32)
                    jk = P.sb(st, "jkd", [128, 128], BF16)
                    ssd = P.sb(st, "ssd", [128, 1], F32)
                    rsd = P.sb(st, "rsd", [128, 1], F32)
                    for h in range(4):
                        load_w(Wq, w_in[l], KC, O_DQ + h * 128, O_DQ + (h + 1) * 128, dst_c0=0)
                        load_w(Wq, w_x[l], KC, h * 128, (h + 1) * 128, dst_c0=128)
                        load_w(Wq, w_in[l], KC, O_DK + h * 128, O_DK + (h + 1) * 128, dst_c0=256)
                        load_w(Wq, w_x[l], KC, 512 + h * 128, 512 + (h + 1) * 128, dst_c0=384)
                        cc = 0
                        for (dst, wc) in ((QT, 0), (KT, 256)):
                            for (c0, n) in CHUNKS:
                                pa, pb = fb[2 * (cc % 2)], fb[1 + 2 * (cc % 2)]
                                cc += 1
                                proj_feat(pa, c0, n, Wq, wc, 128)
                                proj_feat(pb, c0, n, Wq, wc + 128, 128)
                                TT("vector", t1[:, 0:n], pa[:, 0:n], ropec[:, c0:c0 + n], ALU.mult, [pa, ropec], [t1])
                                TT("vector", t2[:, 0:n], pb[:, 0:n], ropes[:, c0:c0 + n], ALU.mult, [pb, ropes], [t2])
                                TT("vector", dst[:, c0:c0 + n], t1[:, 0:n], t2[:, 0:n], ALU.add, [t1, t2], [dst])
                                CP("scalar", dst[32:64, c0:c0 + n], pa[32:64, 0:n], [pa], [dst])
                        P.emit()

                        def fin(i, Os, h=h):
                            O0, O1 = Os
                            P.op("vector", lambda e: e.reciprocal(out=rc[:, 0:1], in_=O0[:, 128:129]), [O0], [rc])
                            P.op("vector", lambda e: e.reciprocal(out=rc[:, 1:2], in_=O1[:, 128:129]), [O1], [rc])
                            TT("vector", rc[:, 1:2], rc[:, 1:2], lam[:, 0:1], ALU.mult, [rc, lam], [rc])
                            TS("vector", tm[:], O1[:, 0:128], rc[:, 1:2], None, ALU.mult, None, [O1, rc], [tm])
                            STT("vector", oc_[:], O0[:, 0:128], rc[:, 0:1], tm[:], ALU.mult, ALU.subtract, [O0, rc, tm], [oc_])
                            MEMSET("vector", ssd[:], 0.0, [ssd])
                            ACT(jk[:], oc_[:], AF.Square, [oc_], [jk, ssd], accum_out=ssd[:, 0:1])
                            RSTD(rsd[:], ssd[:], 128, [ssd], [rsd], mult=1.0 - lam_init)
                            STT("vector", og[:, i, h * 128:(h + 1) * 128], oc_[:], rsd[:, 0:1], gd[:], ALU.mult, ALU.mult, [oc_, rsd, gd], [og])

                        with ExitStack() as st2:
                            diff_attn(st2, QT, KT, (lambda kb, h=h: Vd[:, kb, h, 0:129]), fin)
                            P.emit()
            epilogue(l, 2, O_DZ)

        import os as _os2
        _epi_only = _os2.environ.get("EPI_ONLY")
        for l in range(nlayers):
            phase_norm(l)
            if _epi_only:
                for _ in range(int(_epi_only)):
                    epilogue(l, 0, O_SBZ)
                continue
            if 0 in branches:
                branch_sb(l)
            if 1 in branches:
                branch_mla(l)
            if 2 in branches:
                branch_diff(l)

        with ExitStack() as st:
            grep = P.sb(st, "grepf", [128, D], F32)
            junk = P.sb(st, "junkf", [128, D], BF16)
            ss = P.sb(st, "ssf", [128, NT], F32)
            rs = P.sb(st, "rsf", [128, NT], F32)
            yo = [P.sb(st, "yo%d" % j, [128, D], F32) for j in range(2)]
            bcast_load(grep, final_g[0:1, :], D)
            MEMSET("vector", ss[:], 0.0, [ss])
            for i in range(NT):
                o = yo[i % 2]
                if final_norm:
                    ACT(junk[:], X[:, i, :], AF.Square, [X], [junk, ss], accum_out=ss[:, i:i + 1])
                    RSTD(rs[:, i:i + 1], ss[:, i:i + 1], D, [ss], [rs])
                    STT("vector", o[:], X[:, i, :], rs[:, i:i + 1], grep[:], ALU.mult, ALU.mult, [X, rs, grep], [o])
                else:
                    CP("vector", o[:], X[:, i, :], [X], [o])
                p_lo = NMETA if i == 0 else 0
                p_hi = NMETA if i == NT - 1 else 128
                s0 = 128 * i - NMETA + p_lo
                DMA("sync", y[s0:s0 + (p_hi - p_lo), :], o[p_lo:p_hi, :], reads=[o])
            P.wait_all("sync", yo)
            P.emit()
    return nc


_CACHE = {}


def _consts():
    if "c" not in _CACHE:
        C, Sg = _rope_tables()
        tri, m01, ident = _masks()
        _CACHE["c"] = {"c_ropec": C, "c_ropes": Sg, "c_tri": tri, "c_m01": m01, "c_ident": ident}
    return _CACHE["c"]


def make_in_maps(inp):
    x = np.asarray(inp["x"], np.float32)
    B = x.shape[0]
    meta = np.asarray(inp["meta_tokens"], np.float32)
    lay = _host_layouts({k: np.asarray(v) for k, v in inp.items()})
    shared = dict(_consts())
    shared.update(lay)
    for k in ("norm_g", "w_in", "mla_cq_g", "mla_ckv_g", "diff_norm_g", "w_o_sb", "w_o_mla", "w_o_diff", "w_out"):
        shared[k] = np.ascontiguousarray(np.asarray(inp[k], np.float32))
    shared["diff_lambda"] = np.ascontiguousarray(np.asarray(inp["diff_lambda"], np.float32).reshape(2, 256))
    shared["final_g"] = np.ascontiguousarray(np.asarray(inp["final_g"], np.float32).reshape(1, D))
    maps = []
    for b in range(B):
        h0 = np.concatenate([meta, x[b], np.zeros((L - NMETA - S, D), np.float32)], axis=0)
        m = dict(shared)
        m["h0"] = np.ascontiguousarray(h0)
        maps.append(m)
    return maps


def kernel(**inputs):
    maps = make_in_maps(inputs)
    if "nc" not in _CACHE:
        _CACHE["nc"] = build_nc()
    res = run_bass_kernel_spmd(_CACHE["nc"], maps, core_ids=list(range(len(maps))))
    return np.stack([np.asarray(r["y"], np.float32) for r in res.results], axis=0)
```

```python
import math
import numpy as np
import ml_dtypes
from contextlib import ExitStack
import concourse.bass as bass
import concourse.mybir as mybir
from concourse.bass_utils import run_bass_kernel_spmd

F32 = mybir.dt.float32
BF16 = mybir.dt.bfloat16
AF = mybir.ActivationFunctionType
ALU = mybir.AluOpType

D = 1024
S = 2048
NMETA = 16
NT = 17
L = NT * 128
KC = 8
EPS = 1e-6
THETA = 500000.0
CHUNKS = [(0, 512), (512, 512), (1024, 512), (1536, 512), (2048, 128)]


class Buf:
    __slots__ = ("name", "t", "w", "r", "dsem", "dcnt")

    def __init__(self, name, t=None):
        self.name = name
        self.t = t
        self.w = []
        self.r = []
        self.dsem = None
        self.dcnt = 0

    def __getitem__(self, idx):
        return self.t[idx]


class Prog:
    ENGS = ("tensor", "vector", "scalar", "gpsimd", "sync")

    def __init__(self, nc, stack):
        self.nc = nc
        self.stack = stack
        self.sems = {}
        self.cnt = {e: 0 for e in self.ENGS}
        self.seen = {e: {} for e in self.ENGS}
        self.q = {e: [] for e in self.ENGS}
        self.snaps = {}
        for e in self.ENGS:
            self._sem("E_" + e)
        self.nbuf = 0

    def _sem(self, key):
        if key not in self.sems:
            self.sems[key] = self.stack.enter_context(self.nc.semaphore(key))
        return key

    def sb(self, st, name, shape, dt):
        self.uid = getattr(self, "uid", 0) + 1
        name = "%s_%d" % (name, self.uid)
        t = st.enter_context(self.nc.sbuf_tensor(name, list(shape), dt))
        return Buf(name, t)

    def ps(self, st, name, shape, dt=F32):
        t = st.enter_context(self.nc.psum_tensor(name, list(shape), dt))
        return Buf(name, t)

    def _waits(self, eng, reads, writes):
        own = "E_" + eng
        need = {}
        for b in reads:
            for (k, v) in b.w:
                if k == own and (eng == "tensor" or v > self.cnt[eng]):
                    continue
                if need.get(k, 0) < v:
                    need[k] = v
        for b in writes:
            for (k, v) in b.w:
                if k == own and (eng == "tensor" or v > self.cnt[eng]):
                    continue
                if need.get(k, 0) < v:
                    need[k] = v
            for (k, v) in b.r:
                if k == own and (eng == "tensor" or v > self.cnt[eng]):
                    continue
                if need.get(k, 0) < v:
                    need[k] = v
        out = []
        seen = self.seen[eng]
        snaps = self.snaps
        for k, v in sorted(need.items(), key=lambda kv: 0 if kv[0].startswith("E_") else 1):
            if seen.get(k, 0) < v:
                seen[k] = v
                out.append((k, v))
                sn = snaps.get((k, v))
                if sn:
                    for k2, v2 in sn.items():
                        if seen.get(k2, 0) < v2:
                            seen[k2] = v2
        return out

    def op(self, eng, fn, reads=(), writes=(), inc=True):
        waits = self._waits(eng, reads, writes)
        key = "E_" + eng
        val = self.cnt[eng] + 1
        if inc:
            self.cnt[eng] = val
        ev = (key, val)
        if inc:
            self.snaps[ev] = dict(self.seen[eng])
        for b in writes:
            b.w = [ev]
            b.r = []
        for b in reads:
            b.r = [e for e in b.r if e[0] != key] + [ev]
        self.q[eng].append((waits, fn, [(key, 1)] if inc else []))

    def dma(self, eng, fn, reads=(), writes=(), sem_buf=None):
        waits = self._waits(eng, reads, writes)
        sb = sem_buf or (writes[0] if writes else reads[0])
        if sb.dsem is None:
            sb.dsem = self._sem("D_%d" % self.nbuf)
            self.nbuf += 1
        sb.dcnt += 16
        ev = (sb.dsem, sb.dcnt)
        self.snaps[ev] = dict(self.seen[eng])
        for b in writes:
            b.w = [e for e in b.w if e[0] != sb.dsem and e[0].startswith("D_")] + [ev]
            b.r = []
        for b in reads:
            b.r = [e for e in b.r if e[0] != sb.dsem] + [ev]
        self.q[eng].append((waits, fn, [(sb.dsem, 16)]))

    def wait_all(self, eng, bufs):
        waits = self._waits(eng, (), bufs)
        self.q[eng].append((waits, None, []))

    def emit(self):
        nc = self.nc
        qs = self.q
        self.q = {e: [] for e in self.ENGS}
        sems = self.sems
        with nc.Block() as block:
            def mk(ename):
                items = qs[ename]

                def body(e):
                    for waits, fn, incs in items:
                        if fn is None:
                            for (k, v) in waits:
                                e.wait_ge(sems[k], v)
                            continue
                        for (k, v) in waits[1:]:
                            e.wait_ge(sems[k], v)
                        ins = fn(e)
                        if waits:
                            ins._wait_ge(sems[waits[0][0]], waits[0][1])
                        for (k, n) in incs:
                            ins.then_inc(sems[k], n)
                return body
            block.tensor(mk("tensor"))
            block.vector(mk("vector"))
            block.scalar(mk("scalar"))
            block.gpsimd(mk("gpsimd"))
            block.sync(mk("sync"))


def _rope_tables():
    pos = np.arange(L, dtype=np.float32)
    C = np.ones((128, L), np.float32)
    Sg = np.zeros((128, L), np.float32)
    inv_d = (np.float32(THETA) ** (-np.arange(0, 16, 2, dtype=np.float32) / np.float32(16))).astype(np.float32)
    ang_d = (pos[:, None] * inv_d[None, :]).astype(np.float32)
    cd, sd = np.cos(ang_d).astype(np.float32), np.sin(ang_d).astype(np.float32)
    for base in (0, 64):
        for r in range(16):
            C[base + r] = cd[:, r % 8]
            Sg[base + r] = -sd[:, r % 8] if r < 8 else sd[:, r % 8]
    inv_m = (np.float32(THETA) ** (-np.arange(0, 32, 2, dtype=np.float32) / np.float32(32))).astype(np.float32)
    ang_m = (pos[:, None] * inv_m[None, :]).astype(np.float32)
    cm, sm = np.cos(ang_m).astype(np.float32), np.sin(ang_m).astype(np.float32)
    for r in range(32):
        C[32 + r] = cm[:, r % 16]
        Sg[32 + r] = -sm[:, r % 16] if r < 16 else sm[:, r % 16]
    return C, Sg


def _masks():
    a = np.arange(128)
    tri = (a[None, :] < a[:, None]).astype(np.float32)
    cq = (a + 48) // 64
    m0 = (cq[:, None] <= cq[None, :])
    m1 = ((a[:, None] < 16) & (a[None, :] >= 80))
    m01 = np.concatenate([m0, m1], axis=1).astype(np.float32).astype(ml_dtypes.bfloat16)
    ident = np.eye(128, dtype=np.float32).astype(ml_dtypes.bfloat16)
    return tri, m01, ident


O_SBQ, O_SBK, O_SBV, O_SBZ = 0, 512, 1024, 1536
O_CQ, O_CKV, O_KR, O_MZ = 2048, 2432, 2688, 2720
O_DQ, O_DK, O_DV, O_DZ = 3232, 3744, 4256, 4768
O_G = 5280


def _host_layouts(inp):
    w_in = inp["w_in"]
    swap64 = np.concatenate([np.arange(8, 16), np.arange(0, 8), np.arange(16, 64)])
    idx_d = np.concatenate([m * 64 + swap64 for m in range(8)])
    kr_sw = np.concatenate([np.arange(16, 32), np.arange(0, 16)])
    w_x = np.concatenate([w_in[:, :, O_DQ + idx_d], w_in[:, :, O_DK + idx_d], w_in[:, :, O_KR + kr_sw]], axis=2)
    uq = inp["mla_w_uq"]
    ia, ib = [], []
    for h in range(8):
        b = 96 * h
        ia += list(range(b, b + 32)) + list(range(b + 64, b + 96)) + list(range(b + 32, b + 64))
        ib += list(range(b, b + 32)) + list(range(b + 80, b + 96)) + list(range(b + 64, b + 80)) + list(range(b + 32, b + 64))
    uqa = uq[:, :, np.array(ia)]
    uqb = uq[:, :, np.array(ib)]
    ukv = inp["mla_w_ukv"]
    ikn, iv = [], []
    for h in range(8):
        b = 128 * h
        ikn += list(range(b, b + 32)) + list(range(b, b + 32)) + list(range(b + 32, b + 64))
        iv += list(range(b + 64, b + 128))
    ukn = ukv[:, :, np.array(ikn)]
    ukvv = ukv[:, :, np.array(iv)]
    bg = inp["b_gate"].reshape(2, 3, 8, 128).transpose(0, 1, 3, 2)
    return {
        "w_x": np.ascontiguousarray(w_x),
        "uqa": np.ascontiguousarray(uqa), "uqb": np.ascontiguousarray(uqb),
        "ukn": np.ascontiguousarray(ukn), "ukvv": np.ascontiguousarray(ukvv),
        "bg": np.ascontiguousarray(bg),
    }


def build_nc(nlayers=2, final_norm=True, branches=(0, 1, 2)):
    nc = bass.Bass("TRN2", target_bir_lowering=False)

    def din(name, shape, dt=F32):
        return nc.dram_tensor(name, list(shape), dt, kind="ExternalInput").ap()

    h0 = din("h0", [L, D])
    norm_g = din("norm_g", [2, D])
    w_in = din("w_in", [2, D, 8352])
    w_x = din("w_x", [2, D, 1056])
    bg = din("bg", [2, 3, 128, 8])
    cq_g = din("mla_cq_g", [2, 384])
    ckv_g = din("mla_ckv_g", [2, 256])
    uqa = din("uqa", [2, 384, 768])
    uqb = din("uqb", [2, 384, 768])
    ukn = din("ukn", [2, 256, 768])
    ukvv = din("ukvv", [2, 256, 512])
    dlam = din("diff_lambda", [2, 256])
    dng = din("diff_norm_g", [2, 128])
    w_o = [din("w_o_sb", [2, 512, D]), din("w_o_mla", [2, 512, D]), din("w_o_diff", [2, 512, D])]
    w_out = din("w_out", [2, D, D])
    final_g = din("final_g", [1, D])
    c_ropec = din("c_ropec", [128, L])
    c_ropes = din("c_ropes", [128, L])
    c_tri = din("c_tri", [128, 128])
    c_m01 = din("c_m01", [128, 256], BF16)
    c_ident = din("c_ident", [128, 128], BF16)
    y = nc.dram_tensor("y", [S, D], F32, kind="ExternalOutput").ap()

    with ExitStack() as top:
        P = Prog(nc, top)
        X = P.sb(top, "X", [128, NT, D], F32)
        hT = P.sb(top, "hT", [128, KC, L], BF16)
        og = P.sb(top, "og", [128, NT, 512], BF16)
        ropec = P.sb(top, "ropec", [128, L], F32)
        ropes = P.sb(top, "ropes", [128, L], F32)
        tri = P.sb(top, "tri", [128, 128], F32)
        m01 = P.sb(top, "m01", [128, 256], BF16)
        ident = P.sb(top, "ident", [128, 128], BF16)
        z2 = [P.ps(top, "z2_%d" % i, [128, 1024], F32) for i in range(2)]
        fb = [Buf("fb0", z2[0][:, 0:512]), Buf("fb1", z2[0][:, 512:1024]), Buf("fb2", z2[1][:, 0:512]), Buf("fb3", z2[1][:, 512:1024])]
        fb += [P.ps(top, "fb%d" % i, [128, 512], F32) for i in range(4, 8)]
        tb = [Buf("tb%d" % i, fb[6 + i][:].bitcast(BF16)) for i in range(2)]
        for i in range(2):
            tb[i].w, tb[i].r = fb[6 + i].w, fb[6 + i].r

        def MM(out, lhsT, rhs, start, stop, reads, writes, inc=True):
            P.op("tensor", lambda e: e.matmul(out, lhsT=lhsT, rhs=rhs, start=start, stop=stop), reads, writes, inc)

        def TR(out, in_, reads, writes, inc=True):
            P.op("tensor", lambda e: e.transpose(out=out, in_=in_, identity=ident[:]), list(reads) + [ident], writes, inc)

        def ACT(out, in_, func, reads, writes, bias=None, scale=None, accum_out=None):
            kw = {}
            if bias is not None:
                kw["bias"] = bias
            if scale is not None:
                kw["scale"] = scale
            if accum_out is not None:
                kw["accum_out"] = accum_out
            P.op("scalar", lambda e: e.activation(out=out, in_=in_, func=func, **kw), reads, writes)

        def TT(eng, out, in0, in1, op, reads, writes):
            P.op(eng, lambda e: e.tensor_tensor(out=out, in0=in0, in1=in1, op=op), reads, writes)

        def TS(eng, out, in0, s1, s2, op0, op1, reads, writes):
            if op1 is None:
                P.op(eng, lambda e: e.tensor_scalar(out=out, in0=in0, scalar1=s1, scalar2=None, op0=op0), reads, writes)
            else:
                P.op(eng, lambda e: e.tensor_scalar(out=out, in0=in0, scalar1=s1, scalar2=s2, op0=op0, op1=op1), reads, writes)

        def STT(eng, out, in0, scalar, in1, op0, op1, reads, writes):
            P.op(eng, lambda e: e.scalar_tensor_tensor(out=out, in0=in0, scalar=scalar, in1=in1, op0=op0, op1=op1), reads, writes)

        def RSTD(out, ss_ap, n, reads, writes, mult=1.0):
            ACT(out, ss_ap, AF.Ln, reads, writes, bias=EPS, scale=1.0 / n)
            ACT(out, out, AF.Exp, writes, writes, bias=(math.log(mult) if mult != 1.0 else None), scale=-0.5)

        def CP(eng, out, in_, reads, writes):
            if eng == "scalar":
                P.op(eng, lambda e: e.copy(out=out, in_=in_), reads, writes)
            else:
                P.op(eng, lambda e: e.tensor_copy(out=out, in_=in_), reads, writes)

        def MEMSET(eng, ap, val, writes):
            P.op(eng, lambda e: e.memset(ap, val), (), writes)

        def DMA(eng, out, in_, reads=(), writes=()):
            P.dma(eng, lambda e: e.dma_start(out=out, in_=in_), reads, writes)

        def load_w(buf, dram2d, k_chunks, c0, c1, dst_c0=0):
            v = dram2d.rearrange("(k p) c -> p k c", p=128)
            DMA("gpsimd", buf[:, 0:k_chunks, dst_c0:dst_c0 + (c1 - c0)], v[:, :, c0:c1], writes=[buf])

        def bcast_load(buf, row_ap, n):
            DMA("sync", buf[:], row_ap.to_broadcast([128, n]), writes=[buf])

        fctr = [0]

        def next_f(lo=0, hi=4):
            b = fb[lo + fctr[0] % (hi - lo)]
            fctr[0] += 1
            return b

        DMA("sync", ropec[:], c_ropec[:, :], writes=[ropec])
        DMA("sync", ropes[:], c_ropes[:, :], writes=[ropes])
        DMA("sync", tri[:], c_tri[:, :], writes=[tri])
        DMA("sync", m01[:], c_m01[:, :], writes=[m01])
        DMA("sync", ident[:], c_ident[:, :], writes=[ident])
        h0v = h0.rearrange("(t p) d -> p t d", p=128)
        for i in range(NT):
            DMA("sync", X[:, i, :], h0v[:, i, :], writes=[X])
        P.emit()

        def phase_norm(l):
            with ExitStack() as st:
                grep = P.sb(st, "grep", [128, D], F32)
                junk = P.sb(st, "junk", [128, D], BF16)
                ss = P.sb(st, "ss", [128, NT], F32)
                rs = P.sb(st, "rs", [128, NT], F32)
                hn = [P.sb(st, "hn%d" % j, [128, D], BF16) for j in range(2)]
                bcast_load(grep, norm_g[l:l + 1, :], D)
                MEMSET("vector", ss[:], 0.0, [ss])
                for i in range(NT):
                    ACT(junk[:], X[:, i, :], AF.Square, [X], [junk, ss], accum_out=ss[:, i:i + 1])
                    RSTD(rs[:, i:i + 1], ss[:, i:i + 1], D, [ss], [rs])
                    h = hn[i % 2]
                    STT("vector", h[:], X[:, i, :], rs[:, i:i + 1], grep[:], ALU.mult, ALU.mult, [X, rs, grep], [h])
                    t = tb[i % 2]
                    for k in range(KC):
                        TR(t[:, k * 128:(k + 1) * 128], h[:, k * 128:(k + 1) * 128], [h], [t], inc=(k == KC - 1))
                    CP("scalar", hT[:, :, i * 128:(i + 1) * 128], t[:].rearrange("p (k c) -> p k c", k=KC), [t], [hT])
                P.emit()

        def proj_tok(i, W, c0, n, kchunks=KC, src=None, src_k0=0):
            src = src or hT
            p = next_f()
            for k in range(kchunks):
                MM(p[:, 0:n], src[:, src_k0 + k, i * 128:(i + 1) * 128], W[:, k, c0:c0 + n], k == 0, k == kchunks - 1,
                   [src, W], [p], inc=(k == kchunks - 1))
            return p

        def proj_feat(p, c0, n, W, wc0, M, kchunks=KC, src=None, src_k0=0):
            src = src or hT
            for k in range(kchunks):
                MM(p[0:M, 0:n], W[:, k, wc0:wc0 + M], src[:, src_k0 + k, c0:c0 + n], k == 0, k == kchunks - 1,
                   [src, W], [p], inc=(k == kchunks - 1))

        def epilogue(l, b, zoff):
            with ExitStack() as st:
                Wz = P.sb(st, "Wz", [128, KC, 512], BF16)
                Wo = P.sb(st, "Wo", [128, 4, D], BF16)
                Wg = P.sb(st, "Wg", [128, KC, D], BF16)
                Wout = P.sb(st, "Wout", [128, KC, D], BF16)
                bgt = P.sb(st, "bgt", [128, 8], F32)
                G = [P.sb(st, "G%d" % j, [128, 512], BF16) for j in range(2)]
                ogg = [P.sb(st, "ogg%d" % j, [128, 512], BF16) for j in range(4)]
                oggT = P.sb(st, "oggT", [128, 4, 512], BF16)
                sg = [P.sb(st, "sg%d" % j, [128, 512], F32) for j in range(2)]
                mT = P.sb(st, "mT", [128, KC, 512], BF16)
                load_w(Wz, w_in[l], KC, zoff, zoff + 512)
                load_w(Wo, w_o[b][l], 4, 0, D)
                load_w(Wg, w_in[l], KC, O_G + b * D, O_G + (b + 1) * D)
                load_w(Wout, w_out[l], KC, 0, D)
                DMA("sync", bgt[:], bg[l, b, :, :], writes=[bgt])
                cnt = 0
                for (c0, n) in CHUNKS:
                    tiles = list(range(c0 // 128, (c0 + n) // 128))
                    pzs = [proj_tok(i, Wz, 0, 512) for i in tiles]
                    for oc in range(2):
                        proj_feat(fb[4 + oc], c0, n, Wg, oc * 128, 128)
                    for j, i in enumerate(tiles):
                        pz = pzs[j]
                        g_, o_ = G[cnt % 2], ogg[cnt % 4]
                        ACT(g_[:], pz[:, :], AF.Silu, [pz], [g_])
                        TT("vector", o_[:], og[:, i, :], g_[:], ALU.mult, [og, g_], [o_])
                        cnt += 1
                        t = tb[j % 2]
                        for c in range(4):
                            TR(t[:, c * 128:(c + 1) * 128], o_[:, c * 128:(c + 1) * 128], [o_], [t], inc=(c == 3))
                        CP("vector", oggT[:, :, j * 128:(j + 1) * 128], t[:, 0:512].rearrange("p (c q) -> p c q", c=4), [t], [oggT])
                    for oc in range(8):
                        pg = fb[4 + oc % 2]
                        if oc >= 2:
                            proj_feat(pg, c0, n, Wg, oc * 128, 128)
                        py = next_f()
                        for c in range(4):
                            MM(py[:, 0:n], Wo[:, c, oc * 128:(oc + 1) * 128], oggT[:, c, 0:n], c == 0, c == 3, [Wo, oggT], [py], inc=(c == 3))
                        s_ = sg[oc % 2]
                        ACT(s_[:, 0:n], pg[:, 0:n], AF.Sigmoid, [pg, bgt], [s_], bias=bgt[:, oc:oc + 1])
                        TT("vector", mT[:, oc, 0:n], s_[:, 0:n], py[:, 0:n], ALU.mult, [s_, py], [mT])
                    for j, i in enumerate(tiles):
                        for half in range(2):
                            po = next_f()
                            for k in range(KC):
                                MM(po[:, :], mT[:, k, j * 128:(j + 1) * 128], Wout[:, k, half * 512:(half + 1) * 512], k == 0, k == KC - 1,
                                   [mT, Wout], [po], inc=(k == KC - 1))
                            TT("vector", X[:, i, half * 512:(half + 1) * 512], X[:, i, half * 512:(half + 1) * 512], po[:, :], ALU.add, [X, po], [X])
                P.emit()

        def branch_sb(l):
            with ExitStack() as bst:
                V = P.sb(bst, "Vsb", [128, NT, 512], BF16)
                with ExitStack() as st:
                    Wv = P.sb(st, "Wv", [128, KC, 512], BF16)
                    load_w(Wv, w_in[l], KC, O_SBV, O_SBV + 512)
                    for i in range(NT):
                        p = proj_tok(i, Wv, 0, 512)
                        CP("scalar" if i % 2 else "vector", V[:, i, :], p[:, :], [p], [V])
                    P.emit()
                with ExitStack() as st:
                    Wqk = P.sb(st, "Wqk", [128, KC, 256], BF16)
                    qT = P.sb(st, "qT", [128, L], BF16)
                    kT = P.sb(st, "kT", [128, L], BF16)
                    NE, NSP, NC = 4, 3, 3
                    ez = [P.sb(st, "ez%d" % j, [128, 512], F32) for j in range(NE)]
                    sp = [P.sb(st, "sp%d" % j, [128, 516], F32) for j in range(NSP)]
                    Cb = [P.sb(st, "Cb%d" % j, [128, 512], F32) for j in range(NC)]
                    ctot = [P.sb(st, "ctot%d" % j, [128, 1], F32) for j in range(3)]
                    for j in range(NSP):
                        MEMSET("vector", sp[j][:], 0.0, [sp[j]])
                    wb = [P.sb(st, "wb%d" % j, [128, 512], BF16) for j in range(2)]
                    wT = [P.sb(st, "wT%d" % j, [128, 512], BF16) for j in range(2)]
                    cn = [P.sb(st, "cn%d" % j, [128, 1], F32) for j in range(3)]
                    for pr in range(4):
                        load_w(Wqk, w_in[l], KC, O_SBQ + pr * 128, O_SBQ + (pr + 1) * 128, dst_c0=0)
                        load_w(Wqk, w_in[l], KC, O_SBK + pr * 128, O_SBK + (pr + 1) * 128, dst_c0=128)
                        for (c0, n) in CHUNKS:
                            p = next_f(4, 6)
                            proj_feat(p, c0, n, Wqk, 0, 128)
                            CP("scalar", qT[:, c0:c0 + n], p[:, 0:n], [p], [qT])
                            p = next_f(4, 6)
                            proj_feat(p, c0, n, Wqk, 128, 128)
                            CP("vector", kT[:, c0:c0 + n], p[:, 0:n], [p], [kT])
                        items = []
                        for hh in range(2):
                            for i in range(NT):
                                nk = (i + 1) * 128
                                chs = [(k0, min(512, nk - k0)) for k0 in range(0, nk, 512)][::-1]
                                for ci, (k0, n) in enumerate(chs):
                                    items.append((hh, i, ci, k0, n, ci == len(chs) - 1))
                        N = len(items)
                        obank = {}
                        ocnt = [0]

                        def st_mm(j):
                            hh, i, ci, k0, n, last = items[j]
                            r0 = 64 * hh
                            z = fb[j % 2]
                            MM(z[:, 0:n], qT[r0:r0 + 64, i * 128:(i + 1) * 128], kT[r0:r0 + 64, k0:k0 + n], True, True, [qT, kT], [z])

                        def st_expz(j):
                            hh, i, ci, k0, n, last = items[j]
                            e_ = ez[j % NE]
                            ACT(e_[:, 0:n], fb[j % 2][:, 0:n], AF.Exp, [fb[j % 2]], [e_], scale=0.125)
                            if ci == 0:
                                TT("gpsimd", e_[:, n - 128:n], e_[:, n - 128:n], tri[:], ALU.mult, [e_, tri], [e_])

                        def st_ln(j):
                            hh, i, ci, k0, n, last = items[j]
                            ACT(sp[j % NSP][:, 1:n + 1], ez[j % NE][:, 0:n], AF.Ln, [ez[j % NE]], [sp[j % NSP], ctot[j % 3]], bias=1.0,
                                accum_out=ctot[j % 3][:])

                        def st_scan(j):
                            hh, i, ci, k0, n, last = items[j]
                            s_, c_ = sp[j % NSP], Cb[j % NC]
                            P.op("vector", lambda e: e.tensor_tensor_scan(out=c_[:, 0:n], data0=s_[:, 0:n], data1=s_[:, 0:n],
                                                                           initial=0.0, op0=ALU.add, op1=ALU.max), [s_], [c_])

                        def st_cn(j):
                            hh, i, ci, k0, n, last = items[j]
                            if ci == 0:
                                TS("vector", cn[j % 3][:], ctot[j % 3][:], -1.0, None, ALU.mult, None, [ctot[j % 3]], [cn[j % 3]])
                            else:
                                TT("vector", cn[j % 3][:], cn[(j - 1) % 3][:], ctot[j % 3][:], ALU.subtract, [cn[(j - 1) % 3], ctot[j % 3]], [cn[j % 3]])

                        def st_expt(j):
                            hh, i, ci, k0, n, last = items[j]
                            ACT(Cb[j % NC][:, 0:n], Cb[j % NC][:, 0:n], AF.Exp, [Cb[j % NC], cn[j % 3]], [Cb[j % NC]], bias=cn[j % 3][:, 0:1])

                        def st_mult(j):
                            hh, i, ci, k0, n, last = items[j]
                            TT("gpsimd", wb[j % 2][:, 0:n], ez[j % NE][:, 0:n], Cb[j % NC][:, 0:n], ALU.mult, [ez[j % NE], Cb[j % NC]], [wb[j % 2]])

                        def st_tr(j):
                            hh, i, ci, k0, n, last = items[j]
                            t = tb[j % 2]
                            nb = n // 128
                            for jb in range(nb):
                                TR(t[:, jb * 128:(jb + 1) * 128], wb[j % 2][:, jb * 128:(jb + 1) * 128], [wb[j % 2]], [t], inc=(jb == nb - 1))

                        def st_evac(j):
                            hh, i, ci, k0, n, last = items[j]
                            CP("scalar" if j % 2 == 0 else "vector", wT[j % 2][:, 0:n], tb[j % 2][:, 0:n], [tb[j % 2]], [wT[j % 2]])

                        def st_pv(j):
                            hh, i, ci, k0, n, last = items[j]
                            h = 2 * pr + hh
                            nb = n // 128
                            if ci == 0:
                                obank[(hh, i)] = fb[2 + ocnt[0] % 2]
                                ocnt[0] += 1
                            O = obank[(hh, i)]
                            for jb in range(nb):
                                kb = k0 // 128 + jb
                                MM(O[:, 0:64], wT[j % 2][:, jb * 128:(jb + 1) * 128], V[:, kb, h * 64:(h + 1) * 64],
                                   ci == 0 and jb == 0, last and jb == nb - 1, [wT[j % 2], V], [O], inc=(jb == nb - 1))
                            if last:
                                CP("vector", og[:, i, h * 64:(h + 1) * 64], O[:, 0:64], [O], [og])

                        sched = [(st_mm, 0), (st_expz, 1), (st_expt, 3), (st_ln, 1), (st_evac, 6), (st_cn, 2), (st_scan, 2),
                                 (st_mult, 4), (st_tr, 5), (st_pv, 7)]
                        for step in range(N + 7):
                            for fn, off in sched:
                                if 0 <= step - off < N:
                                    fn(step - off)
                    P.emit()
            epilogue(l, 0, O_SBZ)

        def softmax_attn(st, units, dv1, scale, finalize):
            NS = 5
            zbanks = [fb[0], fb[1], fb[6], fb[7]]
            import os as _os3
            if _os3.environ.get("SKIP_ATTN"):
                return
            PT = [P.sb(st, "PT%d" % j, [128, 512], BF16) for j in range(NS)]
            items = []
            for i in range(NT):
                kbs = list(range(0, min(i + 2, NT)))
                groups = [kbs[a:a + 4] for a in range(0, len(kbs), 4)]
                for u in range(len(units)):
                    for gi, g in enumerate(groups):
                        items.append((i, u, g, gi == 0, gi == len(groups) - 1))
            N = len(items)
            nu = len(units)

            def s1(j):
                i, u, g, first, last = items[j]
                QTb, KTb, r0, nr, vfn = units[u]
                z = zbanks[j % 4]
                for a, kb in enumerate(g):
                    MM(z[:, a * 128:(a + 1) * 128], KTb[r0:r0 + nr, kb * 128:(kb + 1) * 128], QTb[r0:r0 + nr, i * 128:(i + 1) * 128],
                       True, True, [QTb, KTb], [z], inc=(a == len(g) - 1))

            def s2(j):
                i, u, g, first, last = items[j]
                z = zbanks[j % 4]
                s = j % NS
                n = len(g) * 128
                ACT(PT[s][:, 0:n], z[:, 0:n], AF.Exp, [z], [PT[s]], scale=scale)
                for a, kb in enumerate(g):
                    if kb == i:
                        TT("gpsimd", PT[s][:, a * 128:(a + 1) * 128], PT[s][:, a * 128:(a + 1) * 128], m01[:, 0:128], ALU.mult, [PT[s], m01], [PT[s]])
                    elif kb == i + 1:
                        TT("gpsimd", PT[s][:, a * 128:(a + 1) * 128], PT[s][:, a * 128:(a + 1) * 128], m01[:, 128:256], ALU.mult, [PT[s], m01], [PT[s]])

            def s3(j):
                i, u, g, first, last = items[j]
                QTb, KTb, r0, nr, vfn = units[u]
                s = j % NS
                O = fb[2 + u] if nu > 1 else fb[2 + i % 2]
                for a, kb in enumerate(g):
                    MM(O[:, 0:dv1], PT[s][:, a * 128:(a + 1) * 128], vfn(kb), first and a == 0, last and a == len(g) - 1,
                       [PT[s]], [O], inc=(a == len(g) - 1))
                if last and u == nu - 1:
                    finalize(i, [fb[2 + uu] for uu in range(nu)] if nu > 1 else [O])

            for step in range(N + 2):
                if step < N:
                    s1(step)
                if 0 <= step - 1 < N:
                    s2(step - 1)
                if 0 <= step - 2 < N:
                    s3(step - 2)

        def diff_attn(st, QTb, KTb, vfn, finalize):
            import os as _os4
            if _os4.environ.get("SKIP_ATTN"):
                return
            NS = 3
            PT = [P.sb(st, "PTd%d" % j, [128, 1024], BF16) for j in range(NS)]
            items = []
            for i in range(NT):
                kbs = list(range(0, min(i + 2, NT)))
                groups = [kbs[a:a + 4] for a in range(0, len(kbs), 4)]
                for gi, g in enumerate(groups):
                    items.append((i, g, gi == 0, gi == len(groups) - 1))
            N = len(items)

            def s1(j):
                i, g, first, last = items[j]
                zt = z2[j % 2]
                for a, kb in enumerate(g):
                    for u in range(2):
                        r0 = 64 * u
                        MM(zt[:, u * 512 + a * 128:u * 512 + (a + 1) * 128], KTb[r0:r0 + 64, kb * 128:(kb + 1) * 128],
                           QTb[r0:r0 + 64, i * 128:(i + 1) * 128], True, True, [QTb, KTb], [zt], inc=(a == len(g) - 1 and u == 1))

            def s2(j):
                i, g, first, last = items[j]
                zt = z2[j % 2]
                p_ = PT[j % NS]
                n = len(g) * 128
                ACT(p_[:].rearrange("p (u c) -> p u c", u=2)[:, :, 0:n], zt[:].rearrange("p (u c) -> p u c", u=2)[:, :, 0:n],
                    AF.Exp, [zt], [p_], scale=0.125)
                for a, kb in enumerate(g):
                    if kb == i or kb == i + 1:
                        m = m01[:, 0:128] if kb == i else m01[:, 128:256]
                        for u in range(2):
                            sl = p_[:, u * 512 + a * 128:u * 512 + (a + 1) * 128]
                            TT("gpsimd", sl, sl, m, ALU.mult, [p_, m01], [p_])

            def s3(j):
                i, g, first, last = items[j]
                p_ = PT[j % NS]
                Os = [fb[4 + 2 * (i % 2)], fb[5 + 2 * (i % 2)]]
                for a, kb in enumerate(g):
                    for u in range(2):
                        MM(Os[u][:, 0:129], p_[:, u * 512 + a * 128:u * 512 + (a + 1) * 128], vfn(kb),
                           first and a == 0, last and a == len(g) - 1, [p_], [Os[u]], inc=(a == len(g) - 1))
                if last:
                    finalize(i, Os)

            for step in range(N + 2):
                if step < N:
                    s1(step)
                if 0 <= step - 1 < N:
                    s2(step - 1)
                if 0 <= step - 2 < N:
                    s3(step - 2)

        def branch_mla(l):
            with ExitStack() as bst:
                cnT = P.sb(bst, "cnT", [128, 5, L], BF16)
                Va = P.sb(bst, "Va", [128, NT, 8, 68], BF16)
                KT = P.sb(bst, "KTm", [128, L], BF16)
                with ExitStack() as st:
                    W = P.sb(st, "Wm", [128, KC, 704], BF16)
                    Wv = P.sb(st, "Wukvv", [128, 2, 512], BF16)
                    gq = P.sb(st, "gq", [128, 640], F32)
                    junk = P.sb(st, "junkm", [128, 384], BF16)
                    ss = P.sb(st, "ssm", [128, 2 * NT], F32)
                    rs = P.sb(st, "rsm", [128, 2 * NT], F32)
                    cb = [P.sb(st, "cb%d" % j, [128, 640], BF16) for j in range(2)]
                    t1 = P.sb(st, "t1m", [128, 512], F32)
                    t2 = P.sb(st, "t2m", [128, 512], F32)
                    load_w(W, w_in[l], KC, O_CQ, O_CQ + 672)
                    load_w(W, w_x[l], KC, 1024, 1056, dst_c0=672)
                    load_w(Wv, ukvv[l], 2, 0, 512)
                    DMA("sync", gq[:, 0:384], cq_g[l:l + 1, :].to_broadcast([128, 384]), writes=[gq])
                    DMA("sync", gq[:, 384:640], ckv_g[l:l + 1, :].to_broadcast([128, 256]), writes=[gq])
                    MEMSET("vector", ss[:], 0.0, [ss])
                    MEMSET("gpsimd", Va[:].rearrange("p a b c -> p (a b c)"), 1.0, [Va])
                    for i in range(NT):
                        c = cb[i % 2]
                        for part, (wc0, n, dc0) in enumerate(((0, 384, 0), (384, 256, 384))):
                            p = proj_tok(i, W, wc0, n)
                            col = 2 * i + part
                            ACT(junk[:, 0:n], p[:, 0:n], AF.Square, [p], [junk, ss], accum_out=ss[:, col:col + 1])
                            RSTD(rs[:, col:col + 1], ss[:, col:col + 1], n, [ss], [rs])
                            STT("vector", c[:, dc0:dc0 + n], p[:, 0:n], rs[:, col:col + 1], gq[:, dc0:dc0 + n], ALU.mult, ALU.mult, [p, rs, gq], [c])
                        t = tb[i % 2]
                        for k in range(5):
                            TR(t[:, k * 128:(k + 1) * 128], c[:, k * 128:(k + 1) * 128], [c], [t], inc=(k == 4))
                        CP("scalar", cnT[:, :, i * 128:(i + 1) * 128], t[:, 0:640].rearrange("p (k c) -> p k c", k=5), [t], [cnT])
                    for (c0, n) in CHUNKS:
                        pa = fb[4]
                        pb = fb[5]
                        proj_feat(pa, c0, n, W, 608, 64)
                        proj_feat(pb, c0, n, W, 640, 64)
                        TT("vector", t1[32:64, 0:n], pa[32:64, 0:n], ropec[32:64, c0:c0 + n], ALU.mult, [pa, ropec], [t1])
                        TT("vector", t2[32:64, 0:n], pb[32:64, 0:n], ropes[32:64, c0:c0 + n], ALU.mult, [pb, ropes], [t2])
                        TT("vector", KT[32:64, c0:c0 + n], t1[32:64, 0:n], t2[32:64, 0:n], ALU.add, [t1, t2], [KT])
                    for i in range(NT):
                        p = proj_tok(i, Wv, 0, 512, kchunks=2, src=cnT, src_k0=3)
                        CP("scalar" if i % 2 else "vector", Va[:, i, :, 0:64], p[:, :].rearrange("p (h d) -> p h d", h=8), [p], [Va])
                    P.emit()
                with ExitStack() as st:
                    QT = P.sb(st, "QTm", [128, L], BF16)
                    Wa = P.sb(st, "Wuqa", [128, 3, 768], BF16)
                    Wb = P.sb(st, "Wuqb", [128, 3, 768], BF16)
                    Wk = P.sb(st, "Wukn", [128, 2, 768], BF16)
                    t1 = P.sb(st, "t1q", [128, 512], F32)
                    t2 = P.sb(st, "t2q", [128, 512], F32)
                    rcp = P.sb(st, "rcp", [128, 1], F32)
                    load_w(Wa, uqa[l], 3, 0, 768)
                    load_w(Wb, uqb[l], 3, 0, 768)
                    load_w(Wk, ukn[l], 2, 0, 768)
                    for h in range(8):
                        for cidx, (c0, n) in enumerate(CHUNKS):
                            pa, pb = fb[4 + 2 * (cidx % 2)], fb[5 + 2 * (cidx % 2)]
                            proj_feat(pa, c0, n, Wa, h * 96, 96, kchunks=3, src=cnT)
                            proj_feat(pb, c0, n, Wb, h * 96, 96, kchunks=3, src=cnT)
                            CP("scalar", QT[0:32, c0:c0 + n], pa[0:32, 0:n], [pa], [QT])
                            CP("scalar", QT[64:96, c0:c0 + n], pa[64:96, 0:n], [pa], [QT])
                            TT("vector", t1[32:64, 0:n], pa[32:64, 0:n], ropec[32:64, c0:c0 + n], ALU.mult, [pa, ropec], [t1])
                            TT("vector", t2[32:64, 0:n], pb[32:64, 0:n], ropes[32:64, c0:c0 + n], ALU.mult, [pb, ropes], [t2])
                            TT("vector", QT[32:64, c0:c0 + n], t1[32:64, 0:n], t2[32:64, 0:n], ALU.add, [t1, t2], [QT])
                            pk = pb
                            proj_feat(pk, c0, n, Wk, h * 96, 96, kchunks=2, src=cnT, src_k0=3)
                            CP("scalar", KT[0:32, c0:c0 + n], pk[0:32, 0:n], [pk], [KT])
                            CP("vector", KT[64:96, c0:c0 + n], pk[64:96, 0:n], [pk], [KT])

                        def fin(i, Os, h=h):
                            O = Os[0]
                            P.op("vector", lambda e: e.reciprocal(out=rcp[:], in_=O[:, 64:65]), [O], [rcp])
                            TS("vector", og[:, i, h * 64:(h + 1) * 64], O[:, 0:64], rcp[:, 0:1], None, ALU.mult, None, [O, rcp], [og])

                        with ExitStack() as st2:
                            softmax_attn(st2, [(QT, KT, 0, 96, (lambda kb, h=h: Va[:, kb, h, 0:65]))], 65, 1.0 / math.sqrt(96.0), fin)
                            P.emit()
            epilogue(l, 1, O_MZ)

        def branch_diff(l):
            lam_init = 0.8 - 0.6 * math.exp(-0.3 * l)
            with ExitStack() as bst:
                Vd = P.sb(bst, "Vd", [128, NT, 4, 132], BF16)
                lam = P.sb(bst, "lam", [128, 1], F32)
                gd = P.sb(bst, "gd", [128, 128], F32)
                with ExitStack() as st:
                    Wv = P.sb(st, "Wdv", [128, KC, 512], BF16)
                    dl = P.sb(st, "dl", [128, 256], F32)
                    pr_ = P.sb(st, "prd", [128, 128], F32)
                    sm = P.sb(st, "smd", [128, 2], F32)
                    load_w(Wv, w_in[l], KC, O_DV, O_DV + 512)
                    DMA("sync", dl[:], dlam[l:l + 1, :].to_broadcast([128, 256]), writes=[dl])
                    DMA("sync", gd[:], dng[l:l + 1, :].to_broadcast([128, 128]), writes=[gd])
                    dl3 = dl[:].rearrange("p (a b) -> p a b", a=2)
                    TT("vector", pr_[:].rearrange("p (a b) -> p a b", a=2), dl3[:, :, 0:64], dl3[:, :, 64:128], ALU.mult, [dl], [pr_])
                    P.op("vector", lambda e: e.reduce_sum(out=sm[:], in_=pr_[:].rearrange("p (a b) -> p a b", a=2), axis=mybir.AxisListType.X), [pr_], [sm])
                    ACT(sm[:], sm[:], AF.Exp, [sm], [sm])
                    TT("vector", lam[:], sm[:, 0:1], sm[:, 1:2], ALU.subtract, [sm], [lam])
                    TS("vector", lam[:], lam[:], lam_init, None, ALU.add, None, [lam], [lam])
                    MEMSET("gpsimd", Vd[:].rearrange("p a b c -> p (a b c)"), 1.0, [Vd])
                    for i in range(NT):
                        p = proj_tok(i, Wv, 0, 512)
                        CP("scalar" if i % 2 else "vector", Vd[:, i, :, 0:128], p[:, :].rearrange("p (h d) -> p h d", h=4), [p], [Vd])
                    P.emit()
                with ExitStack() as st:
                    Wq = P.sb(st, "Wdq", [128, KC, 512], BF16)
                    QT = P.sb(st, "QTd", [128, L], BF16)
                    KT = P.sb(st, "KTd", [128, L], BF16)
                    t1 = P.sb(st, "t1d", [128, 512], F32)
                    t2 = P.sb(st, "t2d", [128, 512], F32)
                    rc = P.sb(st, "rcd", [128, 2], F32)
                    tm = P.sb(st, "tmd", [128, 128], F32)
                    oc_ = P.sb(st, "ocd", [128, 128], F32)
                    jk = P.sb(st, "jkd", [128, 128], BF16)
                    ssd = P.sb(st, "ssd", [128, 1], F32)
                    rsd = P.sb(st, "rsd", [128, 1], F32)
                    for h in range(4):
                        load_w(Wq, w_in[l], KC, O_DQ + h * 128, O_DQ + (h + 1) * 128, dst_c0=0)
                        load_w(Wq, w_x[l], KC, h * 128, (h + 1) * 128, dst_c0=128)
                        load_w(Wq, w_in[l], KC, O_DK + h * 128, O_DK + (h + 1) * 128, dst_c0=256)
                        load_w(Wq, w_x[l], KC, 512 + h * 128, 512 + (h + 1) * 128, dst_c0=384)
                        cc = 0
                        for (dst, wc) in ((QT, 0), (KT, 256)):
                            for (c0, n) in CHUNKS:
                                pa, pb = fb[2 * (cc % 2)], fb[1 + 2 * (cc % 2)]
                                cc += 1
                                proj_feat(pa, c0, n, Wq, wc, 128)
                                proj_feat(pb, c0, n, Wq, wc + 128, 128)
                                TT("vector", t1[:, 0:n], pa[:, 0:n], ropec[:, c0:c0 + n], ALU.mult, [pa, ropec], [t1])
                                TT("vector", t2[:, 0:n], pb[:, 0:n], ropes[:, c0:c0 + n], ALU.mult, [pb, ropes], [t2])
                                TT("vector", dst[:, c0:c0 + n], t1[:, 0:n], t2[:, 0:n], ALU.add, [t1, t2], [dst])
                                CP("scalar", dst[32:64, c0:c0 + n], pa[32:64, 0:n], [pa], [dst])
                        P.emit()

                        def fin(i, Os, h=h):
                            O0, O1 = Os
                            P.op("vector", lambda e: e.reciprocal(out=rc[:, 0:1], in_=O0[:, 128:129]), [O0], [rc])
                            P.op("vector", lambda e: e.reciprocal(out=rc[:, 1:2], in_=O1[:, 128:129]), [O1], [rc])
                            TT("vector", rc[:, 1:2], rc[:, 1:2], lam[:, 0:1], ALU.mult, [rc, lam], [rc])
                            TS("vector", tm[:], O1[:, 0:128], rc[:, 1:2], None, ALU.mult, None, [O1, rc], [tm])
                            STT("vector", oc_[:], O0[:, 0:128], rc[:, 0:1], tm[:], ALU.mult, ALU.subtract, [O0, rc, tm], [oc_])
                            MEMSET("vector", ssd[:], 0.0, [ssd])
                            ACT(jk[:], oc_[:], AF.Square, [oc_], [jk, ssd], accum_out=ssd[:, 0:1])
                            RSTD(rsd[:], ssd[:], 128, [ssd], [rsd], mult=1.0 - lam_init)
                            STT("vector", og[:, i, h * 128:(h + 1) * 128], oc_[:], rsd[:, 0:1], gd[:], ALU.mult, ALU.mult, [oc_, rsd, gd], [og])

                        with ExitStack() as st2:
                            diff_attn(st2, QT, KT, (lambda kb, h=h: Vd[:, kb, h, 0:129]), fin)
                            P.emit()
            epilogue(l, 2, O_DZ)

        import os as _os2
        _epi_only = _os2.environ.get("EPI_ONLY")
        for l in range(nlayers):
            phase_norm(l)
            if _epi_only:
                for _ in range(int(_epi_only)):
                    epilogue(l, 0, O_SBZ)
                continue
            if 0 in branches:
                branch_sb(l)
            if 1 in branches:
                branch_mla(l)
            if 2 in branches:
                branch_diff(l)

        with ExitStack() as st:
            grep = P.sb(st, "grepf", [128, D], F32)
            junk = P.sb(st, "junkf", [128, D], BF16)
            ss = P.sb(st, "ssf", [128, NT], F32)
            rs = P.sb(st, "rsf", [128, NT], F32)
            yo = [P.sb(st, "yo%d" % j, [128, D], F32) for j in range(2)]
            bcast_load(grep, final_g[0:1, :], D)
            MEMSET("vector", ss[:], 0.0, [ss])
            for i in range(NT):
                o = yo[i % 2]
                if final_norm:
                    ACT(junk[:], X[:, i, :], AF.Square, [X], [junk, ss], accum_out=ss[:, i:i + 1])
                    RSTD(rs[:, i:i + 1], ss[:, i:i + 1], D, [ss], [rs])
                    STT("vector", o[:], X[:, i, :], rs[:, i:i + 1], grep[:], ALU.mult, ALU.mult, [X, rs, grep], [o])
                else:
                    CP("vector", o[:], X[:, i, :], [X], [o])
                p_lo = NMETA if i == 0 else 0
                p_hi = NMETA if i == NT - 1 else 128
                s0 = 128 * i - NMETA + p_lo
                DMA("sync", y[s0:s0 + (p_hi - p_lo), :], o[p_lo:p_hi, :], reads=[o])
            P.wait_all("sync", yo)
            P.emit()
    return nc


_CACHE = {}


def _consts():
    if "c" not in _CACHE:
        C, Sg = _rope_tables()
        tri, m01, ident = _masks()
        _CACHE["c"] = {"c_ropec": C, "c_ropes": Sg, "c_tri": tri, "c_m01": m01, "c_ident": ident}
    return _CACHE["c"]


def make_in_maps(inp):
    x = np.asarray(inp["x"], np.float32)
    B = x.shape[0]
    meta = np.asarray(inp["meta_tokens"], np.float32)
    lay = _host_layouts({k: np.asarray(v) for k, v in inp.items()})
    shared = dict(_consts())
    shared.update(lay)
    for k in ("norm_g", "w_in", "mla_cq_g", "mla_ckv_g", "diff_norm_g", "w_o_sb", "w_o_mla", "w_o_diff", "w_out"):
        shared[k] = np.ascontiguousarray(np.asarray(inp[k], np.float32))
    shared["diff_lambda"] = np.ascontiguousarray(np.asarray(inp["diff_lambda"], np.float32).reshape(2, 256))
    shared["final_g"] = np.ascontiguousarray(np.asarray(inp["final_g"], np.float32).reshape(1, D))
    maps = []
    for b in range(B):
        h0 = np.concatenate([meta, x[b], np.zeros((L - NMETA - S, D), np.float32)], axis=0)
        m = dict(shared)
        m["h0"] = np.ascontiguousarray(h0)
        maps.append(m)
    return maps


def kernel(**inputs):
    maps = make_in_maps(inputs)
    if "nc" not in _CACHE:
        _CACHE["nc"] = build_nc()
    res = run_bass_kernel_spmd(_CACHE["nc"], maps, core_ids=list(range(len(maps))))
    return np.stack([np.asarray(r["y"], np.float32) for r in res.results], axis=0)
```

```python
import math
import numpy as np
import ml_dtypes
from contextlib import ExitStack
import concourse.bass as bass
import concourse.mybir as mybir
from concourse.bass_utils import run_bass_kernel_spmd

F32 = mybir.dt.float32
BF16 = mybir.dt.bfloat16
AF = mybir.ActivationFunctionType
ALU = mybir.AluOpType

D = 1024
S = 2048
NMETA = 16
NT = 17
L = NT * 128
KC = 8
EPS = 1e-6
THETA = 500000.0
CHUNKS = [(0, 512), (512, 512), (1024, 512), (1536, 512), (2048, 128)]


class Buf:
    __slots__ = ("name", "t", "w", "r", "dsem", "dcnt")

    def __init__(self, name, t=None):
        self.name = name
        self.t = t
        self.w = []
        self.r = []
        self.dsem = None
        self.dcnt = 0

    def __getitem__(self, idx):
        return self.t[idx]


class Prog:
    ENGS = ("tensor", "vector", "scalar", "gpsimd", "sync")

    def __init__(self, nc, stack):
        self.nc = nc
        self.stack = stack
        self.sems = {}
        self.cnt = {e: 0 for e in self.ENGS}
        self.seen = {e: {} for e in self.ENGS}
        self.q = {e: [] for e in self.ENGS}
        self.snaps = {}
        for e in self.ENGS:
            self._sem("E_" + e)
        self.nbuf = 0

    def _sem(self, key):
        if key not in self.sems:
            self.sems[key] = self.stack.enter_context(self.nc.semaphore(key))
        return key

    def sb(self, st, name, shape, dt):
        self.uid = getattr(self, "uid", 0) + 1
        name = "%s_%d" % (name, self.uid)
        t = st.enter_context(self.nc.sbuf_tensor(name, list(shape), dt))
        return Buf(name, t)

    def ps(self, st, name, shape, dt=F32):
        t = st.enter_context(self.nc.psum_tensor(name, list(shape), dt))
        return Buf(name, t)

    def _waits(self, eng, reads, writes):
        own = "E_" + eng
        need = {}
        for b in reads:
            for (k, v) in b.w:
                if k == own and (eng == "tensor" or v > self.cnt[eng]):
                    continue
                if need.get(k, 0) < v:
                    need[k] = v
        for b in writes:
            for (k, v) in b.w:
                if k == own and (eng == "tensor" or v > self.cnt[eng]):
                    continue
                if need.get(k, 0) < v:
                    need[k] = v
            for (k, v) in b.r:
                if k == own and (eng == "tensor" or v > self.cnt[eng]):
                    continue
                if need.get(k, 0) < v:
                    need[k] = v
        out = []
        seen = self.seen[eng]
        snaps = self.snaps
        for k, v in sorted(need.items(), key=lambda kv: 0 if kv[0].startswith("E_") else 1):
            if seen.get(k, 0) < v:
                seen[k] = v
                out.append((k, v))
                sn = snaps.get((k, v))
                if sn:
                    for k2, v2 in sn.items():
                        if seen.get(k2, 0) < v2:
                            seen[k2] = v2
        return out

    def op(self, eng, fn, reads=(), writes=(), inc=True):
        waits = self._waits(eng, reads, writes)
        key = "E_" + eng
        val = self.cnt[eng] + 1
        if inc:
            self.cnt[eng] = val
        ev = (key, val)
        if inc:
            self.snaps[ev] = dict(self.seen[eng])
        for b in writes:
            b.w = [ev]
            b.r = []
        for b in reads:
            b.r = [e for e in b.r if e[0] != key] + [ev]
        self.q[eng].append((waits, fn, [(key, 1)] if inc else []))

    def dma(self, eng, fn, reads=(), writes=(), sem_buf=None):
        waits = self._waits(eng, reads, writes)
        sb = sem_buf or (writes[0] if writes else reads[0])
        if sb.dsem is None:
            sb.dsem = self._sem("D_%d" % self.nbuf)
            self.nbuf += 1
        sb.dcnt += 16
        ev = (sb.dsem, sb.dcnt)
        self.snaps[ev] = dict(self.seen[eng])
        for b in writes:
            b.w = [e for e in b.w if e[0] != sb.dsem and e[0].startswith("D_")] + [ev]
            b.r = []
        for b in reads:
            b.r = [e for e in b.r if e[0] != sb.dsem] + [ev]
        self.q[eng].append((waits, fn, [(sb.dsem, 16)]))

    def wait_all(self, eng, bufs):
        waits = self._waits(eng, (), bufs)
        self.q[eng].append((waits, None, []))

    def emit(self):
        nc = self.nc
        qs = self.q
        self.q = {e: [] for e in self.ENGS}
        sems = self.sems
        with nc.Block() as block:
            def mk(ename):
                items = qs[ename]

                def body(e):
                    for waits, fn, incs in items:
                        if fn is None:
                            for (k, v) in waits:
                                e.wait_ge(sems[k], v)
                            continue
                        for (k, v) in waits[1:]:
                            e.wait_ge(sems[k], v)
                        ins = fn(e)
                        if waits:
                            ins._wait_ge(sems[waits[0][0]], waits[0][1])
                        for (k, n) in incs:
                            ins.then_inc(sems[k], n)
                return body
            block.tensor(mk("tensor"))
            block.vector(mk("vector"))
            block.scalar(mk("scalar"))
            block.gpsimd(mk("gpsimd"))
            block.sync(mk("sync"))


def _rope_tables():
    pos = np.arange(L, dtype=np.float32)
    C = np.ones((128, L), np.float32)
    Sg = np.zeros((128, L), np.float32)
    inv_d = (np.float32(THETA) ** (-np.arange(0, 16, 2, dtype=np.float32) / np.float32(16))).astype(np.float32)
    ang_d = (pos[:, None] * inv_d[None, :]).astype(np.float32)
    cd, sd = np.cos(ang_d).astype(np.float32), np.sin(ang_d).astype(np.float32)
    for base in (0, 64):
        for r in range(16):
            C[base + r] = cd[:, r % 8]
            Sg[base + r] = -sd[:, r % 8] if r < 8 else sd[:, r % 8]
    inv_m = (np.float32(THETA) ** (-np.arange(0, 32, 2, dtype=np.float32) / np.float32(32))).astype(np.float32)
    ang_m = (pos[:, None] * inv_m[None, :]).astype(np.float32)
    cm, sm = np.cos(ang_m).astype(np.float32), np.sin(ang_m).astype(np.float32)
    for r in range(32):
        C[32 + r] = cm[:, r % 16]
        Sg[32 + r] = -sm[:, r % 16] if r < 16 else sm[:, r % 16]
    return C, Sg


def _masks():
    a = np.arange(128)
    tri = (a[None, :] < a[:, None]).astype(np.float32)
    cq = (a + 48) // 64
    m0 = (cq[:, None] <= cq[None, :])
    m1 = ((a[:, None] < 16) & (a[None, :] >= 80))
    m01 = np.concatenate([m0, m1], axis=1).astype(np.float32).astype(ml_dtypes.bfloat16)
    ident = np.eye(128, dtype=np.float32).astype(ml_dtypes.bfloat16)
    BIG = 30000.0
    mk = np.zeros((8, 128), np.float32)
    mk[0] = -BIG * (cq == 1); mk[1] = -BIG * (cq == 2)
    mk[2] = (cq < 1); mk[3] = (cq < 2)
    mk[4] = -BIG; mk[5] = BIG * (a < 16)
    mk[6] = 1.0; mk[7] = (a >= 80)
    mk = mk.astype(ml_dtypes.bfloat16)
    return tri, m01, ident, mk


O_SBQ, O_SBK, O_SBV, O_SBZ = 0, 512, 1024, 1536
O_CQ, O_CKV, O_KR, O_MZ = 2048, 2432, 2688, 2720
O_DQ, O_DK, O_DV, O_DZ = 3232, 3744, 4256, 4768
O_G = 5280


def _host_layouts(inp):
    w_in = inp["w_in"]
    swap64 = np.concatenate([np.arange(8, 16), np.arange(0, 8), np.arange(16, 64)])
    idx_d = np.concatenate([m * 64 + swap64 for m in range(8)])
    kr_sw = np.concatenate([np.arange(16, 32), np.arange(0, 16)])
    w_x = np.concatenate([w_in[:, :, O_DQ + idx_d], w_in[:, :, O_DK + idx_d], w_in[:, :, O_KR + kr_sw]], axis=2)
    uq = inp["mla_w_uq"]
    ia, ib = [], []
    for h in range(8):
        b = 96 * h
        ia += list(range(b, b + 32)) + list(range(b + 64, b + 96)) + list(range(b + 32, b + 64))
        ib += list(range(b, b + 32)) + list(range(b + 80, b + 96)) + list(range(b + 64, b + 80)) + list(range(b + 32, b + 64))
    uqa = uq[:, :, np.array(ia)]
    uqb = uq[:, :, np.array(ib)]
    ukv = inp["mla_w_ukv"]
    ikn, iv = [], []
    for h in range(8):
        b = 128 * h
        ikn += list(range(b, b + 32)) + list(range(b, b + 32)) + list(range(b + 32, b + 64))
        iv += list(range(b + 64, b + 128))
    ukn = ukv[:, :, np.array(ikn)]
    ukvv = ukv[:, :, np.array(iv)]
    bg = inp["b_gate"].reshape(2, 3, 8, 128).transpose(0, 1, 3, 2)
    return {
        "w_x": np.ascontiguousarray(w_x),
        "uqa": np.ascontiguousarray(uqa), "uqb": np.ascontiguousarray(uqb),
        "ukn": np.ascontiguousarray(ukn), "ukvv": np.ascontiguousarray(ukvv),
        "bg": np.ascontiguousarray(bg),
    }


def build_nc(nlayers=2, final_norm=True, branches=(0, 1, 2)):
    nc = bass.Bass("TRN2", target_bir_lowering=False)

    def din(name, shape, dt=F32):
        return nc.dram_tensor(name, list(shape), dt, kind="ExternalInput").ap()

    h0 = din("h0", [L, D])
    norm_g = din("norm_g", [2, D])
    w_in = din("w_in", [2, D, 8352])
    w_x = din("w_x", [2, D, 1056])
    bg = din("bg", [2, 3, 128, 8])
    cq_g = din("mla_cq_g", [2, 384])
    ckv_g = din("mla_ckv_g", [2, 256])
    uqa = din("uqa", [2, 384, 768])
    uqb = din("uqb", [2, 384, 768])
    ukn = din("ukn", [2, 256, 768])
    ukvv = din("ukvv", [2, 256, 512])
    dlam = din("diff_lambda", [2, 256])
    dng = din("diff_norm_g", [2, 128])
    w_o = [din("w_o_sb", [2, 512, D]), din("w_o_mla", [2, 512, D]), din("w_o_diff", [2, 512, D])]
    w_out = din("w_out", [2, D, D])
    final_g = din("final_g", [1, D])
    c_ropec = din("c_ropec", [128, L])
    c_ropes = din("c_ropes", [128, L])
    c_tri = din("c_tri", [128, 128])
    c_m01 = din("c_m01", [128, 256], BF16)
    c_ident = din("c_ident", [128, 128], BF16)
    c_mk = din("c_mk", [8, 128], BF16)
    y = nc.dram_tensor("y", [S, D], F32, kind="ExternalOutput").ap()

    with ExitStack() as top:
        P = Prog(nc, top)
        X = P.sb(top, "X", [128, NT, D], F32)
        hT = P.sb(top, "hT", [128, KC, L], BF16)
        og = P.sb(top, "og", [128, NT, 512], BF16)
        ropec = P.sb(top, "ropec", [128, L], F32)
        ropes = P.sb(top, "ropes", [128, L], F32)
        tri = P.sb(top, "tri", [128, 128], F32)
        m01 = P.sb(top, "m01", [128, 256], BF16)
        ident = P.sb(top, "ident", [128, 128], BF16)
        mkt = [P.sb(top, "mk%d" % j, [2, 128], BF16) for j in range(4)]
        z2 = [P.ps(top, "z2_%d" % i, [128, 1024], F32) for i in range(2)]
        fb = [Buf("fb0", z2[0][:, 0:512]), Buf("fb1", z2[0][:, 512:1024]), Buf("fb2", z2[1][:, 0:512]), Buf("fb3", z2[1][:, 512:1024])]
        fb += [P.ps(top, "fb%d" % i, [128, 512], F32) for i in range(4, 8)]
        tb = [Buf("tb%d" % i, fb[6 + i][:].bitcast(BF16)) for i in range(2)]
        for i in range(2):
            tb[i].w, tb[i].r = fb[6 + i].w, fb[6 + i].r

        def MM(out, lhsT, rhs, start, stop, reads, writes, inc=True):
            P.op("tensor", lambda e: e.matmul(out, lhsT=lhsT, rhs=rhs, start=start, stop=stop), reads, writes, inc)

        def TR(out, in_, reads, writes, inc=True):
            P.op("tensor", lambda e: e.transpose(out=out, in_=in_, identity=ident[:]), list(reads) + [ident], writes, inc)

        def ACT(out, in_, func, reads, writes, bias=None, scale=None, accum_out=None):
            kw = {}
            if bias is not None:
                kw["bias"] = bias
            if scale is not None:
                kw["scale"] = scale
            if accum_out is not None:
                kw["accum_out"] = accum_out
            P.op("scalar", lambda e: e.activation(out=out, in_=in_, func=func, **kw), reads, writes)

        def TT(eng, out, in0, in1, op, reads, writes):
            P.op(eng, lambda e: e.tensor_tensor(out=out, in0=in0, in1=in1, op=op), reads, writes)

        def TS(eng, out, in0, s1, s2, op0, op1, reads, writes):
            if op1 is None:
                P.op(eng, lambda e: e.tensor_scalar(out=out, in0=in0, scalar1=s1, scalar2=None, op0=op0), reads, writes)
            else:
                P.op(eng, lambda e: e.tensor_scalar(out=out, in0=in0, scalar1=s1, scalar2=s2, op0=op0, op1=op1), reads, writes)

        def STT(eng, out, in0, scalar, in1, op0, op1, reads, writes):
            P.op(eng, lambda e: e.scalar_tensor_tensor(out=out, in0=in0, scalar=scalar, in1=in1, op0=op0, op1=op1), reads, writes)

        def RSTD(out, ss_ap, n, reads, writes, mult=1.0):
            ACT(out, ss_ap, AF.Ln, reads, writes, bias=EPS, scale=1.0 / n)
            ACT(out, out, AF.Exp, writes, writes, bias=(math.log(mult) if mult != 1.0 else None), scale=-0.5)

        def CP(eng, out, in_, reads, writes):
            if eng == "scalar":
                P.op(eng, lambda e: e.copy(out=out, in_=in_), reads, writes)
            else:
                P.op(eng, lambda e: e.tensor_copy(out=out, in_=in_), reads, writes)

        def MEMSET(eng, ap, val, writes):
            P.op(eng, lambda e: e.memset(ap, val), (), writes)

        def DMA(eng, out, in_, reads=(), writes=()):
            P.dma(eng, lambda e: e.dma_start(out=out, in_=in_), reads, writes)

        def load_w(buf, dram2d, k_chunks, c0, c1, dst_c0=0):
            v = dram2d.rearrange("(k p) c -> p k c", p=128)
            DMA("gpsimd", buf[:, 0:k_chunks, dst_c0:dst_c0 + (c1 - c0)], v[:, :, c0:c1], writes=[buf])

        def bcast_load(buf, row_ap, n):
            DMA("sync", buf[:], row_ap.to_broadcast([128, n]), writes=[buf])

        fctr = [0]

        def next_f(lo=0, hi=4):
            b = fb[lo + fctr[0] % (hi - lo)]
            fctr[0] += 1
            return b

        DMA("scalar", ropec[:], c_ropec[:, :], writes=[ropec])
        DMA("scalar", ropes[:], c_ropes[:, :], writes=[ropes])
        DMA("sync", tri[:], c_tri[:, :], writes=[tri])
        DMA("sync", m01[:], c_m01[:, :], writes=[m01])
        DMA("sync", ident[:], c_ident[:, :], writes=[ident])
        for j in range(4):
            DMA("sync", mkt[j][:], c_mk[2 * j:2 * j + 2, :], writes=[mkt[j]])
        h0v = h0.rearrange("(t p) d -> p t d", p=128)
        DMA("sync", X[:, 0:6, :], h0v[:, 0:6, :], writes=[X])
        DMA("scalar", X[:, 6:12, :], h0v[:, 6:12, :], writes=[X])
        DMA("sync", X[:, 12:NT, :], h0v[:, 12:NT, :], writes=[X])
        P.emit()

        def phase_norm(l):
            with ExitStack() as st:
                grep = P.sb(st, "grep", [128, D], F32)
                junk = [P.sb(st, "junk%d" % j, [128, D], BF16) for j in range(2)]
                ss = P.sb(st, "ss", [128, NT], F32)
                rs = P.sb(st, "rs", [128, NT], F32)
                ssb = [Buf("ss_%d" % i, ss.t) for i in range(NT)]
                rsb = [Buf("rs_%d" % i, rs.t) for i in range(NT)]
                hn = [P.sb(st, "hn%d" % j, [128, D], BF16) for j in range(3)]
                bcast_load(grep, norm_g[l:l + 1, :], D)

                def sa(i):
                    ACT(junk[i % 2][:], X[:, i, :], AF.Square, [X], [junk[i % 2], ssb[i]], accum_out=ss[:, i:i + 1])
                    RSTD(rs[:, i:i + 1], ss[:, i:i + 1], D, [ssb[i]], [rsb[i]])
                    h = hn[i % 3]
                    STT("vector", h[:], X[:, i, :], rs[:, i:i + 1], grep[:], ALU.mult, ALU.mult, [X, rsb[i], grep], [h])

                def sb_(i):
                    h, t = hn[i % 3], tb[i % 2]
                    for k in range(KC):
                        TR(t[:, k * 128:(k + 1) * 128], h[:, k * 128:(k + 1) * 128], [h], [t], inc=(k == KC - 1))

                def sc(i):
                    t = tb[i % 2]
                    CP("vector", hT[:, :, i * 128:(i + 1) * 128], t[:].rearrange("p (k c) -> p k c", k=KC), [t], [hT])

                for step in range(NT + 2):
                    if step < NT:
                        sa(step)
                    if 0 <= step - 1 < NT:
                        sb_(step - 1)
                    if 0 <= step - 2 < NT:
                        sc(step - 2)
                P.emit()

        def proj_tok(i, W, c0, n, kchunks=KC, src=None, src_k0=0):
            src = src or hT
            p = next_f()
            for k in range(kchunks):
                MM(p[:, 0:n], src[:, src_k0 + k, i * 128:(i + 1) * 128], W[:, k, c0:c0 + n], k == 0, k == kchunks - 1,
                   [src, W], [p], inc=(k == kchunks - 1))
            return p

        def proj_feat(p, c0, n, W, wc0, M, kchunks=KC, src=None, src_k0=0):
            src = src or hT
            for k in range(kchunks):
                MM(p[0:M, 0:n], W[:, k, wc0:wc0 + M], src[:, src_k0 + k, c0:c0 + n], k == 0, k == kchunks - 1,
                   [src, W], [p], inc=(k == kchunks - 1))

        def epilogue(l, b, zoff):
            with ExitStack() as st:
                Wz = P.sb(st, "Wz", [128, KC, 512], BF16)
                Wo = P.sb(st, "Wo", [128, 4, D], BF16)
                Wg = P.sb(st, "Wg", [128, KC, D], BF16)
                Wout = P.sb(st, "Wout", [128, KC, D], BF16)
                bgt = P.sb(st, "bgt", [128, 8], F32)
                G = [P.sb(st, "G%d" % j, [128, 512], BF16) for j in range(2)]
                ogg = [P.sb(st, "ogg%d" % j, [128, 512], BF16) for j in range(3)]
                oggT = P.sb(st, "oggT", [128, 4, 512], BF16)
                sg = [P.sb(st, "sg%d" % j, [128, 512], F32) for j in range(2)]
                mT = P.sb(st, "mT", [128, KC, 512], BF16)
                load_w(Wz, w_in[l], KC, zoff, zoff + 512)
                load_w(Wo, w_o[b][l], 4, 0, D)
                load_w(Wg, w_in[l], KC, O_G + b * D, O_G + (b + 1) * D)
                load_w(Wout, w_out[l], KC, 0, D)
                DMA("sync", bgt[:], bg[l, b, :, :], writes=[bgt])
                cnt = 0
                for (c0, n) in CHUNKS:
                    tiles = list(range(c0 // 128, (c0 + n) // 128))
                    pzs = [proj_tok(i, Wz, 0, 512) for i in tiles]
                    for oc in range(2):
                        proj_feat(fb[4 + oc], c0, n, Wg, oc * 128, 128)
                    for j, i in enumerate(tiles):
                        pz = pzs[j]
                        g_, o_ = G[cnt % 2], ogg[cnt % 3]
                        ACT(g_[:], pz[:, :], AF.Silu, [pz], [g_])
                        TT("vector", o_[:], og[:, i, :], g_[:], ALU.mult, [og, g_], [o_])
                        cnt += 1
                        t = tb[j % 2]
                        for c in range(4):
                            TR(t[:, c * 128:(c + 1) * 128], o_[:, c * 128:(c + 1) * 128], [o_], [t], inc=(c == 3))
                        CP("vector", oggT[:, :, j * 128:(j + 1) * 128], t[:, 0:512].rearrange("p (c q) -> p c q", c=4), [t], [oggT])
                    for oc in range(8):
                        pg = fb[4 + oc % 2]
                        if oc >= 2:
                            proj_feat(pg, c0, n, Wg, oc * 128, 128)
                        py = next_f()
                        for c in range(4):
                            MM(py[:, 0:n], Wo[:, c, oc * 128:(oc + 1) * 128], oggT[:, c, 0:n], c == 0, c == 3, [Wo, oggT], [py], inc=(c == 3))
                        s_ = sg[oc % 2]
                        ACT(s_[:, 0:n], pg[:, 0:n], AF.Sigmoid, [pg, bgt], [s_], bias=bgt[:, oc:oc + 1])
                        TT("vector", mT[:, oc, 0:n], s_[:, 0:n], py[:, 0:n], ALU.mult, [s_, py], [mT])
                    for j, i in enumerate(tiles):
                        for half in range(2):
                            po = next_f()
                            for k in range(KC):
                                MM(po[:, :], mT[:, k, j * 128:(j + 1) * 128], Wout[:, k, half * 512:(half + 1) * 512], k == 0, k == KC - 1,
                                   [mT, Wout], [po], inc=(k == KC - 1))
                            TT("vector", X[:, i, half * 512:(half + 1) * 512], X[:, i, half * 512:(half + 1) * 512], po[:, :], ALU.add, [X, po], [X])
                P.emit()

        def branch_sb(l):
            with ExitStack() as bst:
                V = P.sb(bst, "Vsb", [128, NT, 512], BF16)
                with ExitStack() as st:
                    Wv = P.sb(st, "Wv", [128, KC, 512], BF16)
                    load_w(Wv, w_in[l], KC, O_SBV, O_SBV + 512)
                    for i in range(NT):
                        p = proj_tok(i, Wv, 0, 512)
                        CP("scalar" if i % 2 else "vector", V[:, i, :], p[:, :], [p], [V])
                    P.emit()
                with ExitStack() as st:
                    Wqk = P.sb(st, "Wqk", [128, KC, 256], BF16)
                    qT = P.sb(st, "qT", [128, L], BF16)
                    kT = P.sb(st, "kT", [128, L], BF16)
                    NE, NSP, NC = 4, 3, 3
                    ez = [P.sb(st, "ez%d" % j, [128, 512], F32) for j in range(NE)]
                    sp = [P.sb(st, "sp%d" % j, [128, 516], F32) for j in range(NSP)]
                    Cb = [P.sb(st, "Cb%d" % j, [128, 512], F32) for j in range(NC)]
                    ctot = [P.sb(st, "ctot%d" % j, [128, 1], F32) for j in range(3)]
                    for j in range(NSP):
                        MEMSET("vector", sp[j][:], 0.0, [sp[j]])
                    wb = [P.sb(st, "wb%d" % j, [128, 512], BF16) for j in range(2)]
                    wT = [P.sb(st, "wT%d" % j, [128, 512], BF16) for j in range(2)]
                    cn = [P.sb(st, "cn%d" % j, [128, 1], F32) for j in range(3)]
                    for pr in range(4):
                        load_w(Wqk, w_in[l], KC, O_SBQ + pr * 128, O_SBQ + (pr + 1) * 128, dst_c0=0)
                        load_w(Wqk, w_in[l], KC, O_SBK + pr * 128, O_SBK + (pr + 1) * 128, dst_c0=128)
                        for (c0, n) in CHUNKS:
                            p = next_f(4, 6)
                            proj_feat(p, c0, n, Wqk, 0, 128)
                            CP("scalar", qT[:, c0:c0 + n], p[:, 0:n], [p], [qT])
                            p = next_f(4, 6)
                            proj_feat(p, c0, n, Wqk, 128, 128)
                            CP("vector", kT[:, c0:c0 + n], p[:, 0:n], [p], [kT])
                        items = []
                        for hh in range(2):
                            for i in range(NT):
                                nk = (i + 1) * 128
                                chs = [(k0, min(512, nk - k0)) for k0 in range(0, nk, 512)][::-1]
                                for ci, (k0, n) in enumerate(chs):
                                    items.append((hh, i, ci, k0, n, ci == len(chs) - 1))
                        N = len(items)
                        obank = {}
                        ocnt = [0]

                        def st_mm(j):
                            hh, i, ci, k0, n, last = items[j]
                            r0 = 64 * hh
                            z = fb[j % 2]
                            MM(z[:, 0:n], qT[r0:r0 + 64, i * 128:(i + 1) * 128], kT[r0:r0 + 64, k0:k0 + n], True, True, [qT, kT], [z])

                        def st_expz(j):
                            hh, i, ci, k0, n, last = items[j]
                            e_ = ez[j % NE]
                            ACT(e_[:, 0:n], fb[j % 2][:, 0:n], AF.Exp, [fb[j % 2]], [e_], scale=0.125)
                            if ci == 0:
                                TT("gpsimd", e_[:, n - 128:n], e_[:, n - 128:n], tri[:], ALU.mult, [e_, tri], [e_])

                        def st_ln(j):
                            hh, i, ci, k0, n, last = items[j]
                            ACT(sp[j % NSP][:, 1:n + 1], ez[j % NE][:, 0:n], AF.Ln, [ez[j % NE]], [sp[j % NSP], ctot[j % 3]], bias=1.0,
                                accum_out=ctot[j % 3][:])

                        def st_scan(j):
                            hh, i, ci, k0, n, last = items[j]
                            s_, c_ = sp[j % NSP], Cb[j % NC]
                            P.op("vector", lambda e: e.tensor_tensor_scan(out=c_[:, 0:n], data0=s_[:, 0:n], data1=s_[:, 0:n],
                                                                           initial=0.0, op0=ALU.add, op1=ALU.max), [s_], [c_])

                        def st_cn(j):
                            hh, i, ci, k0, n, last = items[j]
                            if ci == 0:
                                TS("vector", cn[j % 3][:], ctot[j % 3][:], -1.0, None, ALU.mult, None, [ctot[j % 3]], [cn[j % 3]])
                            else:
                                TT("vector", cn[j % 3][:], cn[(j - 1) % 3][:], ctot[j % 3][:], ALU.subtract, [cn[(j - 1) % 3], ctot[j % 3]], [cn[j % 3]])

                        def st_expt(j):
                            hh, i, ci, k0, n, last = items[j]
                            ACT(Cb[j % NC][:, 0:n], Cb[j % NC][:, 0:n], AF.Exp, [Cb[j % NC], cn[j % 3]], [Cb[j % NC]], bias=cn[j % 3][:, 0:1])

                        def st_mult(j):
                            hh, i, ci, k0, n, last = items[j]
                            TT("gpsimd", wb[j % 2][:, 0:n], ez[j % NE][:, 0:n], Cb[j % NC][:, 0:n], ALU.mult, [ez[j % NE], Cb[j % NC]], [wb[j % 2]])

                        def st_tr(j):
                            hh, i, ci, k0, n, last = items[j]
                            t = tb[j % 2]
                            nb = n // 128
                            for jb in range(nb):
                                TR(t[:, jb * 128:(jb + 1) * 128], wb[j % 2][:, jb * 128:(jb + 1) * 128], [wb[j % 2]], [t], inc=(jb == nb - 1))

                        def st_evac(j):
                            hh, i, ci, k0, n, last = items[j]
                            CP("scalar" if j % 2 == 0 else "vector", wT[j % 2][:, 0:n], tb[j % 2][:, 0:n], [tb[j % 2]], [wT[j % 2]])

                        def st_pv(j):
                            hh, i, ci, k0, n, last = items[j]
                            h = 2 * pr + hh
                            nb = n // 128
                            if ci == 0:
                                obank[(hh, i)] = fb[2 + ocnt[0] % 2]
                                ocnt[0] += 1
                            O = obank[(hh, i)]
                            for jb in range(nb):
                                kb = k0 // 128 + jb
                                MM(O[:, 0:64], wT[j % 2][:, jb * 128:(jb + 1) * 128], V[:, kb, h * 64:(h + 1) * 64],
                                   ci == 0 and jb == 0, last and jb == nb - 1, [wT[j % 2], V], [O], inc=(jb == nb - 1))
                            if last:
                                CP("vector", og[:, i, h * 64:(h + 1) * 64], O[:, 0:64], [O], [og])

                        sched = [(st_mm, 0), (st_expz, 1), (st_expt, 3), (st_ln, 1), (st_evac, 6), (st_cn, 2), (st_scan, 2),
                                 (st_mult, 4), (st_tr, 5), (st_pv, 7)]
                        for step in range(N + 7):
                            for fn, off in sched:
                                if 0 <= step - off < N:
                                    fn(step - off)
                    P.emit()
            epilogue(l, 0, O_SBZ)

        def softmax_attn(st, units, dv1, scale, finalize):
            NS = 5
            zbanks = [fb[0], fb[1], fb[6], fb[7]]
            import os as _os3
            if _os3.environ.get("SKIP_ATTN"):
                return
            PT = [P.sb(st, "PT%d" % j, [128, 512], BF16) for j in range(NS)]
            items = []
            for i in range(NT):
                kbs = list(range(0, min(i + 2, NT)))
                groups = [kbs[a:a + 4] for a in range(0, len(kbs), 4)]
                for u in range(len(units)):
                    for gi, g in enumerate(groups):
                        items.append((i, u, g, gi == 0, gi == len(groups) - 1))
            N = len(items)
            nu = len(units)

            def s1(j):
                i, u, g, first, last = items[j]
                QTb, KTb, r0, nr, vfn = units[u]
                z = zbanks[j % 4]
                for a, kb in enumerate(g):
                    msk = kb >= i
                    MM(z[:, a * 128:(a + 1) * 128], KTb[r0:r0 + nr, kb * 128:(kb + 1) * 128], QTb[r0:r0 + nr, i * 128:(i + 1) * 128],
                       True, not msk, [QTb, KTb], [z], inc=(a == len(g) - 1 and not msk))
                    if msk:
                        mo = 0 if kb == i else 2
                        MM(z[:, a * 128:(a + 1) * 128], mkt[mo][:, :], mkt[mo + 1][:, :], False, True, [mkt[mo], mkt[mo + 1]], [z],
                           inc=(a == len(g) - 1))

            def s2(j):
                i, u, g, first, last = items[j]
                z = zbanks[j % 4]
                s = j % NS
                n = len(g) * 128
                ACT(PT[s][:, 0:n], z[:, 0:n], AF.Exp, [z], [PT[s]], scale=scale)

            def s3(j):
                i, u, g, first, last = items[j]
                QTb, KTb, r0, nr, vfn = units[u]
                s = j % NS
                O = fb[2 + u] if nu > 1 else fb[2 + i % 2]
                for a, kb in enumerate(g):
                    MM(O[:, 0:dv1], PT[s][:, a * 128:(a + 1) * 128], vfn(kb), first and a == 0, last and a == len(g) - 1,
                       [PT[s]], [O], inc=(a == len(g) - 1))
                if last and u == nu - 1:
                    finalize(i, [fb[2 + uu] for uu in range(nu)] if nu > 1 else [O])

            for step in range(N + 2):
                if step < N:
                    s1(step)
                if 0 <= step - 1 < N:
                    s2(step - 1)
                if 0 <= step - 2 < N:
                    s3(step - 2)

        def diff_attn(st, QTb, KTb, vfn, finalize):
            import os as _os4
            if _os4.environ.get("SKIP_ATTN"):
                return
            NS = 3
            PT = [P.sb(st, "PTd%d" % j, [128, 1024], BF16) for j in range(NS)]
            items = []
            for i in range(NT):
                kbs = list(range(0, min(i + 2, NT)))
                groups = [kbs[a:a + 4] for a in range(0, len(kbs), 4)]
                for gi, g in enumerate(groups):
                    items.append((i, g, gi == 0, gi == len(groups) - 1))
            N = len(items)

            def s1(j):
                i, g, first, last = items[j]
                zt = z2[j % 2]
                for a, kb in enumerate(g):
                    msk = kb >= i
                    for u in range(2):
                        r0 = 64 * u
                        MM(zt[:, u * 512 + a * 128:u * 512 + (a + 1) * 128], KTb[r0:r0 + 64, kb * 128:(kb + 1) * 128],
                           QTb[r0:r0 + 64, i * 128:(i + 1) * 128], True, not msk, [QTb, KTb], [zt],
                           inc=(a == len(g) - 1 and u == 1 and not msk))
                    if msk:
                        mo = 0 if kb == i else 2
                        for u in range(2):
                            MM(zt[:, u * 512 + a * 128:u * 512 + (a + 1) * 128], mkt[mo][:, :], mkt[mo + 1][:, :], False, True,
                               [mkt[mo], mkt[mo + 1]], [zt], inc=(a == len(g) - 1 and u == 1))

            def s2(j):
                i, g, first, last = items[j]
                zt = z2[j % 2]
                p_ = PT[j % NS]
                n = len(g) * 128
                ACT(p_[:].rearrange("p (u c) -> p u c", u=2)[:, :, 0:n], zt[:].rearrange("p (u c) -> p u c", u=2)[:, :, 0:n],
                    AF.Exp, [zt], [p_], scale=0.125)

            def s3(j):
                i, g, first, last = items[j]
                p_ = PT[j % NS]
                Os = [fb[4 + 2 * (i % 2)], fb[5 + 2 * (i % 2)]]
                for a, kb in enumerate(g):
                    for u in range(2):
                        MM(Os[u][:, 0:129], p_[:, u * 512 + a * 128:u * 512 + (a + 1) * 128], vfn(kb),
                           first and a == 0, last and a == len(g) - 1, [p_], [Os[u]], inc=(a == len(g) - 1))
                if last:
                    finalize(i, Os)

            for step in range(N + 2):
                if step < N:
                    s1(step)
                if 0 <= step - 1 < N:
                    s2(step - 1)
                if 0 <= step - 2 < N:
                    s3(step - 2)

        def branch_mla(l):
            with ExitStack() as bst:
                cnT = P.sb(bst, "cnT", [128, 5, L], BF16)
                Va = P.sb(bst, "Va", [128, NT, 8, 68], BF16)
                KT = P.sb(bst, "KTm", [128, L], BF16)
                with ExitStack() as st:
                    W = P.sb(st, "Wm", [128, KC, 704], BF16)
                    Wv = P.sb(st, "Wukvv", [128, 2, 512], BF16)
                    gq = P.sb(st, "gq", [128, 640], F32)
                    junk = P.sb(st, "junkm", [128, 384], BF16)
                    ss = P.sb(st, "ssm", [128, 2 * NT], F32)
                    rs = P.sb(st, "rsm", [128, 2 * NT], F32)
                    cb = [P.sb(st, "cb%d" % j, [128, 640], BF16) for j in range(2)]
                    t1 = P.sb(st, "t1m", [128, 512], F32)
                    t2 = P.sb(st, "t2m", [128, 512], F32)
                    load_w(W, w_in[l], KC, O_CQ, O_CQ + 672)
                    load_w(W, w_x[l], KC, 1024, 1056, dst_c0=672)
                    load_w(Wv, ukvv[l], 2, 0, 512)
                    DMA("sync", gq[:, 0:384], cq_g[l:l + 1, :].to_broadcast([128, 384]), writes=[gq])
                    DMA("sync", gq[:, 384:640], ckv_g[l:l + 1, :].to_broadcast([128, 256]), writes=[gq])
                    MEMSET("gpsimd", Va[:].rearrange("p a b c -> p (a b c)"), 1.0, [Va])
                    ssb = [Buf("ssm_%d" % i, ss.t) for i in range(2 * NT)]
                    rsb = [Buf("rsm_%d" % i, rs.t) for i in range(2 * NT)]
                    cb3 = cb + [P.sb(st, "cb2", [128, 640], BF16)]
                    junk2 = [junk, P.sb(st, "junkm2", [128, 384], BF16)]

                    def sa(i):
                        c = cb3[i % 3]
                        for part, (wc0, n, dc0) in enumerate(((0, 384, 0), (384, 256, 384))):
                            p = proj_tok(i, W, wc0, n)
                            col = 2 * i + part
                            ACT(junk2[part][:, 0:n], p[:, 0:n], AF.Square, [p], [junk2[part], ssb[col]], accum_out=ss[:, col:col + 1])
                            RSTD(rs[:, col:col + 1], ss[:, col:col + 1], n, [ssb[col]], [rsb[col]])
                            STT("vector", c[:, dc0:dc0 + n], p[:, 0:n], rs[:, col:col + 1], gq[:, dc0:dc0 + n], ALU.mult, ALU.mult, [p, rsb[col], gq], [c])

                    def sb_(i):
                        c, t = cb3[i % 3], tb[i % 2]
                        for k in range(5):
                            TR(t[:, k * 128:(k + 1) * 128], c[:, k * 128:(k + 1) * 128], [c], [t], inc=(k == 4))

                    def sc(i):
                        t = tb[i % 2]
                        CP("vector" if i % 2 else "scalar", cnT[:, :, i * 128:(i + 1) * 128], t[:, 0:640].rearrange("p (k c) -> p k c", k=5), [t], [cnT])

                    for step in range(NT + 2):
                        if step < NT:
                            sa(step)
                        if 0 <= step - 1 < NT:
                            sb_(step - 1)
                        if 0 <= step - 2 < NT:
                            sc(step - 2)
                    for (c0, n) in CHUNKS:
                        pa = fb[4]
                        pb = fb[5]
                        proj_feat(pa, c0, n, W, 608, 64)
                        proj_feat(pb, c0, n, W, 640, 64)
                        TT("vector", t1[32:64, 0:n], pa[32:64, 0:n], ropec[32:64, c0:c0 + n], ALU.mult, [pa, ropec], [t1])
                        TT("vector", t2[32:64, 0:n], pb[32:64, 0:n], ropes[32:64, c0:c0 + n], ALU.mult, [pb, ropes], [t2])
                        TT("vector", KT[32:64, c0:c0 + n], t1[32:64, 0:n], t2[32:64, 0:n], ALU.add, [t1, t2], [KT])
                    for i in range(NT):
                        p = proj_tok(i, Wv, 0, 512, kchunks=2, src=cnT, src_k0=3)
                        CP("scalar" if i % 2 else "vector", Va[:, i, :, 0:64], p[:, :].rearrange("p (h d) -> p h d", h=8), [p], [Va])
                    P.emit()
                with ExitStack() as st:
                    QT = P.sb(st, "QTm", [128, L], BF16)
                    Wa = P.sb(st, "Wuqa", [128, 3, 768], BF16)
                    Wb = P.sb(st, "Wuqb", [128, 3, 768], BF16)
                    Wk = P.sb(st, "Wukn", [128, 2, 768], BF16)
                    t1 = P.sb(st, "t1q", [128, 512], F32)
                    t2 = P.sb(st, "t2q", [128, 512], F32)
                    rcp = P.sb(st, "rcp", [128, 1], F32)
                    load_w(Wa, uqa[l], 3, 0, 768)
                    load_w(Wb, uqb[l], 3, 0, 768)
                    load_w(Wk, ukn[l], 2, 0, 768)
                    for h in range(8):
                        for cidx, (c0, n) in enumerate(CHUNKS):
                            pa, pb = fb[4 + 2 * (cidx % 2)], fb[5 + 2 * (cidx % 2)]
                            proj_feat(pa, c0, n, Wa, h * 96, 96, kchunks=3, src=cnT)
                            proj_feat(pb, c0, n, Wb, h * 96, 96, kchunks=3, src=cnT)
                            CP("scalar", QT[0:32, c0:c0 + n], pa[0:32, 0:n], [pa], [QT])
                            CP("scalar", QT[64:96, c0:c0 + n], pa[64:96, 0:n], [pa], [QT])
                            TT("vector", t1[32:64, 0:n], pa[32:64, 0:n], ropec[32:64, c0:c0 + n], ALU.mult, [pa, ropec], [t1])
                            TT("vector", t2[32:64, 0:n], pb[32:64, 0:n], ropes[32:64, c0:c0 + n], ALU.mult, [pb, ropes], [t2])
                            TT("vector", QT[32:64, c0:c0 + n], t1[32:64, 0:n], t2[32:64, 0:n], ALU.add, [t1, t2], [QT])
                            pk = pb
                            proj_feat(pk, c0, n, Wk, h * 96, 96, kchunks=2, src=cnT, src_k0=3)
                            CP("scalar", KT[0:32, c0:c0 + n], pk[0:32, 0:n], [pk], [KT])
                            CP("vector", KT[64:96, c0:c0 + n], pk[64:96, 0:n], [pk], [KT])

                        def fin(i, Os, h=h):
                            O = Os[0]
                            P.op("vector", lambda e: e.reciprocal(out=rcp[:], in_=O[:, 64:65]), [O], [rcp])
                            TS("vector", og[:, i, h * 64:(h + 1) * 64], O[:, 0:64], rcp[:, 0:1], None, ALU.mult, None, [O, rcp], [og])

                        with ExitStack() as st2:
                            softmax_attn(st2, [(QT, KT, 0, 96, (lambda kb, h=h: Va[:, kb, h, 0:65]))], 65, 1.0 / math.sqrt(96.0), fin)
                            P.emit()
            epilogue(l, 1, O_MZ)

        def branch_diff(l):
            lam_init = 0.8 - 0.6 * math.exp(-0.3 * l)
            with ExitStack() as bst:
                Vd = P.sb(bst, "Vd", [128, NT, 4, 132], BF16)
                lam = P.sb(bst, "lam", [128, 1], F32)
                gd = P.sb(bst, "gd", [128, 128], F32)
                with ExitStack() as st:
                    Wv = P.sb(st, "Wdv", [128, KC, 512], BF16)
                    dl = P.sb(st, "dl", [128, 256], F32)
                    pr_ = P.sb(st, "prd", [128, 128], F32)
                    sm = P.sb(st, "smd", [128, 2], F32)
                    load_w(Wv, w_in[l], KC, O_DV, O_DV + 512)
                    DMA("sync", dl[:], dlam[l:l + 1, :].to_broadcast([128, 256]), writes=[dl])
                    DMA("sync", gd[:], dng[l:l + 1, :].to_broadcast([128, 128]), writes=[gd])
                    dl3 = dl[:].rearrange("p (a b) -> p a b", a=2)
                    TT("vector", pr_[:].rearrange("p (a b) -> p a b", a=2), dl3[:, :, 0:64], dl3[:, :, 64:128], ALU.mult, [dl], [pr_])
                    P.op("vector", lambda e: e.reduce_sum(out=sm[:], in_=pr_[:].rearrange("p (a b) -> p a b", a=2), axis=mybir.AxisListType.X), [pr_], [sm])
                    ACT(sm[:], sm[:], AF.Exp, [sm], [sm])
                    TT("vector", lam[:], sm[:, 0:1], sm[:, 1:2], ALU.subtract, [sm], [lam])
                    TS("vector", lam[:], lam[:], lam_init, None, ALU.add, None, [lam], [lam])
                    MEMSET("gpsimd", Vd[:].rearrange("p a b c -> p (a b c)"), 1.0, [Vd])
                    for i in range(NT):
                        p = proj_tok(i, Wv, 0, 512)
                        CP("scalar" if i % 2 else "vector", Vd[:, i, :, 0:128], p[:, :].rearrange("p (h d) -> p h d", h=4), [p], [Vd])
                    P.emit()
                with ExitStack() as st:
                    Wq2 = [P.sb(st, "Wdq%d" % j, [128, KC, 512], BF16) for j in range(2)]

                    def load_head_w(h):
                        Wq_ = Wq2[h % 2]
                        load_w(Wq_, w_in[l], KC, O_DQ + h * 128, O_DQ + (h + 1) * 128, dst_c0=0)
                        load_w(Wq_, w_x[l], KC, h * 128, (h + 1) * 128, dst_c0=128)
                        load_w(Wq_, w_in[l], KC, O_DK + h * 128, O_DK + (h + 1) * 128, dst_c0=256)
                        load_w(Wq_, w_x[l], KC, 512 + h * 128, 512 + (h + 1) * 128, dst_c0=384)
                    load_head_w(0)
                    QT = P.sb(st, "QTd", [128, L], BF16)
                    KT = P.sb(st, "KTd", [128, L], BF16)
                    t1 = P.sb(st, "t1d", [128, 512], F32)
                    t2 = P.sb(st, "t2d", [128, 512], F32)
                    rc = P.sb(st, "rcd", [128, 2], F32)
                    tm = P.sb(st, "tmd", [128, 128], F32)
                    oc_ = P.sb(st, "ocd", [128, 128], F32)
                    jk = P.sb(st, "jkd", [128, 128], BF16)
                    ssd = P.sb(st, "ssd", [128, 1], F32)
                    rsd = P.sb(st, "rsd", [128, 1], F32)
                    for h in range(4):
                        cc = 0
                        Wq = Wq2[h % 2]
                        if h + 1 < 4:
                            load_head_w(h + 1)
                        for (dst, wc) in ((QT, 0), (KT, 256)):
                            for (c0, n) in CHUNKS:
                                pa, pb = fb[2 * (cc % 2)], fb[1 + 2 * (cc % 2)]
                                cc += 1
                                proj_feat(pa, c0, n, Wq, wc, 128)
                                proj_feat(pb, c0, n, Wq, wc + 128, 128)
                                TT("vector", t1[:, 0:n], pa[:, 0:n], ropec[:, c0:c0 + n], ALU.mult, [pa, ropec], [t1])
                                TT("vector", t2[:, 0:n], pb[:, 0:n], ropes[:, c0:c0 + n], ALU.mult, [pb, ropes], [t2])
                                TT("vector", dst[:, c0:c0 + n], t1[:, 0:n], t2[:, 0:n], ALU.add, [t1, t2], [dst])
                                CP("scalar", dst[32:64, c0:c0 + n], pa[32:64, 0:n], [pa], [dst])
                        P.emit()

                        def fin(i, Os, h=h):
                            O0, O1 = Os
                            P.op("vector", lambda e: e.reciprocal(out=rc[:, 0:1], in_=O0[:, 128:129]), [O0], [rc])
                            P.op("vector", lambda e: e.reciprocal(out=rc[:, 1:2], in_=O1[:, 128:129]), [O1], [rc])
                            TT("vector", rc[:, 1:2], rc[:, 1:2], lam[:, 0:1], ALU.mult, [rc, lam], [rc])
                            TS("vector", tm[:], O1[:, 0:128], rc[:, 1:2], None, ALU.mult, None, [O1, rc], [tm])
                            STT("vector", oc_[:], O0[:, 0:128], rc[:, 0:1], tm[:], ALU.mult, ALU.subtract, [O0, rc, tm], [oc_])
                            MEMSET("vector", ssd[:], 0.0, [ssd])
                            ACT(jk[:], oc_[:], AF.Square, [oc_], [jk, ssd], accum_out=ssd[:, 0:1])
                            RSTD(rsd[:], ssd[:], 128, [ssd], [rsd], mult=1.0 - lam_init)
                            STT("vector", og[:, i, h * 128:(h + 1) * 128], oc_[:], rsd[:, 0:1], gd[:], ALU.mult, ALU.mult, [oc_, rsd, gd], [og])

                        with ExitStack() as st2:
                            diff_attn(st2, QT, KT, (lambda kb, h=h: Vd[:, kb, h, 0:129]), fin)
                            P.emit()
            epilogue(l, 2, O_DZ)

        import os as _os2
        _epi_only = _os2.environ.get("EPI_ONLY")
        for l in range(nlayers):
            phase_norm(l)
            if _epi_only:
                for _ in range(int(_epi_only)):
                    epilogue(l, 0, O_SBZ)
                continue
            if 0 in branches:
                branch_sb(l)
            if 1 in branches:
                branch_mla(l)
            if 2 in branches:
                branch_diff(l)

        with ExitStack() as st:
            grep = P.sb(st, "grepf", [128, D], F32)
            junk = P.sb(st, "junkf", [128, D], BF16)
            ss = P.sb(st, "ssf", [128, NT], F32)
            rs = P.sb(st, "rsf", [128, NT], F32)
            yo = [P.sb(st, "yo%d" % j, [128, D], F32) for j in range(2)]
            bcast_load(grep, final_g[0:1, :], D)
            MEMSET("vector", ss[:], 0.0, [ss])
            for i in range(NT):
                o = yo[i % 2]
                if final_norm:
                    ACT(junk[:], X[:, i, :], AF.Square, [X], [junk, ss], accum_out=ss[:, i:i + 1])
                    RSTD(rs[:, i:i + 1], ss[:, i:i + 1], D, [ss], [rs])
                    STT("vector", o[:], X[:, i, :], rs[:, i:i + 1], grep[:], ALU.mult, ALU.mult, [X, rs, grep], [o])
                else:
                    CP("vector", o[:], X[:, i, :], [X], [o])
                p_lo = NMETA if i == 0 else 0
                p_hi = NMETA if i == NT - 1 else 128
                s0 = 128 * i - NMETA + p_lo
                DMA("sync", y[s0:s0 + (p_hi - p_lo), :], o[p_lo:p_hi, :], reads=[o])
            P.wait_all("sync", yo)
            P.emit()
    return nc


_CACHE = {}


def _consts():
    if "c" not in _CACHE:
        C, Sg = _rope_tables()
        tri, m01, ident, mk = _masks()
        _CACHE["c"] = {"c_ropec": C, "c_ropes": Sg, "c_tri": tri, "c_m01": m01, "c_ident": ident, "c_mk": mk}
    return _CACHE["c"]


def make_in_maps(inp):
    x = np.asarray(inp["x"], np.float32)
    B = x.shape[0]
    meta = np.asarray(inp["meta_tokens"], np.float32)
    lay = _host_layouts({k: np.asarray(v) for k, v in inp.items()})
    shared = dict(_consts())
    shared.update(lay)
    for k in ("norm_g", "w_in", "mla_cq_g", "mla_ckv_g", "diff_norm_g", "w_o_sb", "w_o_mla", "w_o_diff", "w_out"):
        shared[k] = np.ascontiguousarray(np.asarray(inp[k], np.float32))
    shared["diff_lambda"] = np.ascontiguousarray(np.asarray(inp["diff_lambda"], np.float32).reshape(2, 256))
    shared["final_g"] = np.ascontiguousarray(np.asarray(inp["final_g"], np.float32).reshape(1, D))
    maps = []
    for b in range(B):
        h0 = np.concatenate([meta, x[b], np.zeros((L - NMETA - S, D), np.float32)], axis=0)
        m = dict(shared)
        m["h0"] = np.ascontiguousarray(h0)
        maps.append(m)
    return maps


def kernel(**inputs):
    maps = make_in_maps(inputs)
    if "nc" not in _CACHE:
        _CACHE["nc"] = build_nc()
    res = run_bass_kernel_spmd(_CACHE["nc"], maps, core_ids=list(range(len(maps))))
    return np.stack([np.asarray(r["y"], np.float32) for r in res.results], axis=0)
```

```python
import math
import numpy as np
import ml_dtypes
from contextlib import ExitStack
import concourse.bass as bass
import concourse.mybir as mybir
from concourse.bass_utils import run_bass_kernel_spmd

F32 = mybir.dt.float32
BF16 = mybir.dt.bfloat16
AF = mybir.ActivationFunctionType
ALU = mybir.AluOpType

D = 1024
S = 2048
NMETA = 16
NT = 17
L = NT * 128
KC = 8
EPS = 1e-6
THETA = 500000.0
CHUNKS = [(0, 512), (512, 512), (1024, 512), (1536, 512), (2048, 128)]


class Buf:
    __slots__ = ("name", "t", "w", "r", "dsem", "dcnt")

    def __init__(self, name, t=None):
        self.name = name
        self.t = t
        self.w = []
        self.r = []
        self.dsem = None
        self.dcnt = 0

    def __getitem__(self, idx):
        return self.t[idx]


class Prog:
    ENGS = ("tensor", "vector", "scalar", "gpsimd", "sync")

    def __init__(self, nc, stack):
        self.nc = nc
        self.stack = stack
        self.sems = {}
        self.cnt = {e: 0 for e in self.ENGS}
        self.seen = {e: {} for e in self.ENGS}
        self.q = {e: [] for e in self.ENGS}
        self.snaps = {}
        for e in self.ENGS:
            self._sem("E_" + e)
        self.nbuf = 0

    def _sem(self, key):
        if key not in self.sems:
            self.sems[key] = self.stack.enter_context(self.nc.semaphore(key))
        return key

    def sb(self, st, name, shape, dt):
        self.uid = getattr(self, "uid", 0) + 1
        name = "%s_%d" % (name, self.uid)
        t = st.enter_context(self.nc.sbuf_tensor(name, list(shape), dt))
        return Buf(name, t)

    def ps(self, st, name, shape, dt=F32):
        t = st.enter_context(self.nc.psum_tensor(name, list(shape), dt))
        return Buf(name, t)

    def _waits(self, eng, reads, writes):
        own = "E_" + eng
        need = {}
        for b in reads:
            for (k, v) in b.w:
                if k == own and (eng == "tensor" or v > self.cnt[eng]):
                    continue
                if need.get(k, 0) < v:
                    need[k] = v
        for b in writes:
            for (k, v) in b.w:
                if k == own and (eng == "tensor" or v > self.cnt[eng]):
                    continue
                if need.get(k, 0) < v:
                    need[k] = v
            for (k, v) in b.r:
                if k == own and (eng == "tensor" or v > self.cnt[eng]):
                    continue
                if need.get(k, 0) < v:
                    need[k] = v
        out = []
        seen = self.seen[eng]
        snaps = self.snaps
        for k, v in sorted(need.items(), key=lambda kv: 0 if kv[0].startswith("E_") else 1):
            if seen.get(k, 0) < v:
                seen[k] = v
                out.append((k, v))
                sn = snaps.get((k, v))
                if sn:
                    for k2, v2 in sn.items():
                        if seen.get(k2, 0) < v2:
                            seen[k2] = v2
        return out

    def op(self, eng, fn, reads=(), writes=(), inc=True):
        waits = self._waits(eng, reads, writes)
        key = "E_" + eng
        val = self.cnt[eng] + 1
        if inc:
            self.cnt[eng] = val
        ev = (key, val)
        if inc:
            self.snaps[ev] = dict(self.seen[eng])
        for b in writes:
            b.w = [ev]
            b.r = []
        for b in reads:
            b.r = [e for e in b.r if e[0] != key] + [ev]
        self.q[eng].append((waits, fn, [(key, 1)] if inc else []))

    def dma(self, eng, fn, reads=(), writes=(), sem_buf=None):
        waits = self._waits(eng, reads, writes)
        sb = sem_buf or (writes[0] if writes else reads[0])
        if sb.dsem is None:
            sb.dsem = self._sem("D_%d" % self.nbuf)
            self.nbuf += 1
        sb.dcnt += 16
        ev = (sb.dsem, sb.dcnt)
        self.snaps[ev] = dict(self.seen[eng])
        for b in writes:
            b.w = [e for e in b.w if e[0] != sb.dsem and e[0].startswith("D_")] + [ev]
            b.r = []
        for b in reads:
            b.r = [e for e in b.r if e[0] != sb.dsem] + [ev]
        self.q[eng].append((waits, fn, [(sb.dsem, 16)]))

    def wait_all(self, eng, bufs):
        waits = self._waits(eng, (), bufs)
        self.q[eng].append((waits, None, []))

    def emit(self):
        nc = self.nc
        qs = self.q
        self.q = {e: [] for e in self.ENGS}
        sems = self.sems
        with nc.Block() as block:
            def mk(ename):
                items = qs[ename]

                def body(e):
                    for waits, fn, incs in items:
                        if fn is None:
                            for (k, v) in waits:
                                e.wait_ge(sems[k], v)
                            continue
                        for (k, v) in waits[1:]:
                            e.wait_ge(sems[k], v)
                        ins = fn(e)
                        if waits:
                            ins._wait_ge(sems[waits[0][0]], waits[0][1])
                        for (k, n) in incs:
                            ins.then_inc(sems[k], n)
                return body
            block.tensor(mk("tensor"))
            block.vector(mk("vector"))
            block.scalar(mk("scalar"))
            block.gpsimd(mk("gpsimd"))
            block.sync(mk("sync"))


def _rope_tables():
    pos = np.arange(L, dtype=np.float32)
    C = np.ones((128, L), np.float32)
    Sg = np.zeros((128, L), np.float32)
    inv_d = (np.float32(THETA) ** (-np.arange(0, 16, 2, dtype=np.float32) / np.float32(16))).astype(np.float32)
    ang_d = (pos[:, None] * inv_d[None, :]).astype(np.float32)
    cd, sd = np.cos(ang_d).astype(np.float32), np.sin(ang_d).astype(np.float32)
    for base in (0, 64):
        for r in range(16):
            C[base + r] = cd[:, r % 8]
            Sg[base + r] = -sd[:, r % 8] if r < 8 else sd[:, r % 8]
    inv_m = (np.float32(THETA) ** (-np.arange(0, 32, 2, dtype=np.float32) / np.float32(32))).astype(np.float32)
    ang_m = (pos[:, None] * inv_m[None, :]).astype(np.float32)
    cm, sm = np.cos(ang_m).astype(np.float32), np.sin(ang_m).astype(np.float32)
    for r in range(32):
        C[32 + r] = cm[:, r % 16]
        Sg[32 + r] = -sm[:, r % 16] if r < 16 else sm[:, r % 16]
    return C, Sg


def _masks():
    a = np.arange(128)
    tri = (a[None, :] < a[:, None]).astype(np.float32)
    cq = (a + 48) // 64
    m0 = (cq[:, None] <= cq[None, :])
    m1 = ((a[:, None] < 16) & (a[None, :] >= 80))
    m01 = np.concatenate([m0, m1], axis=1).astype(np.float32).astype(ml_dtypes.bfloat16)
    ident = np.eye(128, dtype=np.float32).astype(ml_dtypes.bfloat16)
    BIG = 30000.0
    mk = np.zeros((8, 128), np.float32)
    mk[0] = -BIG * (cq == 1); mk[1] = -BIG * (cq == 2)
    mk[2] = (cq < 1); mk[3] = (cq < 2)
    mk[4] = -BIG; mk[5] = BIG * (a < 16)
    mk[6] = 1.0; mk[7] = (a >= 80)
    mk = mk.astype(ml_dtypes.bfloat16)
    return tri, m01, ident, mk


O_SBQ, O_SBK, O_SBV, O_SBZ = 0, 512, 1024, 1536
O_CQ, O_CKV, O_KR, O_MZ = 2048, 2432, 2688, 2720
O_DQ, O_DK, O_DV, O_DZ = 3232, 3744, 4256, 4768
O_G = 5280


def _host_layouts(inp):
    w_in = inp["w_in"]
    swap64 = np.concatenate([np.arange(8, 16), np.arange(0, 8), np.arange(16, 64)])
    idx_d = np.concatenate([m * 64 + swap64 for m in range(8)])
    kr_sw = np.concatenate([np.arange(16, 32), np.arange(0, 16)])
    w_x = np.concatenate([w_in[:, :, O_DQ + idx_d], w_in[:, :, O_DK + idx_d], w_in[:, :, O_KR + kr_sw]], axis=2)
    uq = inp["mla_w_uq"]
    ia, ib = [], []
    for h in range(8):
        b = 96 * h
        ia += list(range(b, b + 32)) + list(range(b + 64, b + 96)) + list(range(b + 32, b + 64))
        ib += list(range(b, b + 32)) + list(range(b + 80, b + 96)) + list(range(b + 64, b + 80)) + list(range(b + 32, b + 64))
    uqa = uq[:, :, np.array(ia)]
    uqb = uq[:, :, np.array(ib)]
    ukv = inp["mla_w_ukv"]
    ikn, iv = [], []
    for h in range(8):
        b = 128 * h
        ikn += list(range(b, b + 32)) + list(range(b, b + 32)) + list(range(b + 32, b + 64))
        iv += list(range(b + 64, b + 128))
    ukn = ukv[:, :, np.array(ikn)]
    ukvv = ukv[:, :, np.array(iv)]
    bg = inp["b_gate"].reshape(2, 3, 8, 128).transpose(0, 1, 3, 2)
    return {
        "w_x": np.ascontiguousarray(w_x),
        "uqa": np.ascontiguousarray(uqa), "uqb": np.ascontiguousarray(uqb),
        "ukn": np.ascontiguousarray(ukn), "ukvv": np.ascontiguousarray(ukvv),
        "bg": np.ascontiguousarray(bg),
    }


def build_nc(nlayers=2, final_norm=True, branches=(0, 1, 2)):
    nc = bass.Bass("TRN2", target_bir_lowering=False)

    def din(name, shape, dt=F32):
        return nc.dram_tensor(name, list(shape), dt, kind="ExternalInput").ap()

    h0 = din("h0", [L, D])
    norm_g = din("norm_g", [2, D])
    w_in = din("w_in", [2, D, 8352])
    w_x = din("w_x", [2, D, 1056])
    bg = din("bg", [2, 3, 128, 8])
    cq_g = din("mla_cq_g", [2, 384])
    ckv_g = din("mla_ckv_g", [2, 256])
    uqa = din("uqa", [2, 384, 768])
    uqb = din("uqb", [2, 384, 768])
    ukn = din("ukn", [2, 256, 768])
    ukvv = din("ukvv", [2, 256, 512])
    dlam = din("diff_lambda", [2, 256])
    dng = din("diff_norm_g", [2, 128])
    w_o = [din("w_o_sb", [2, 512, D]), din("w_o_mla", [2, 512, D]), din("w_o_diff", [2, 512, D])]
    w_out = din("w_out", [2, D, D])
    final_g = din("final_g", [1, D])
    c_ropec = din("c_ropec", [128, L])
    c_ropes = din("c_ropes", [128, L])
    c_tri = din("c_tri", [128, 128])
    c_m01 = din("c_m01", [128, 256], BF16)
    c_ident = din("c_ident", [128, 128], BF16)
    c_mk = din("c_mk", [8, 128], BF16)
    y = nc.dram_tensor("y", [S, D], F32, kind="ExternalOutput").ap()

    with ExitStack() as top:
        P = Prog(nc, top)
        X = P.sb(top, "X", [128, NT, D], F32)
        hT = P.sb(top, "hT", [128, KC, L], BF16)
        og = P.sb(top, "og", [128, NT, 512], BF16)
        ropec = P.sb(top, "ropec", [128, L], F32)
        ropes = P.sb(top, "ropes", [128, L], F32)
        tri = P.sb(top, "tri", [128, 128], F32)
        m01 = P.sb(top, "m01", [128, 256], BF16)
        ident = P.sb(top, "ident", [128, 128], BF16)
        mkt = [P.sb(top, "mk%d" % j, [2, 128], BF16) for j in range(4)]
        z2 = [P.ps(top, "z2_%d" % i, [128, 1024], F32) for i in range(2)]
        fb = [Buf("fb0", z2[0][:, 0:512]), Buf("fb1", z2[0][:, 512:1024]), Buf("fb2", z2[1][:, 0:512]), Buf("fb3", z2[1][:, 512:1024])]
        fb += [P.ps(top, "fb%d" % i, [128, 512], F32) for i in range(4, 8)]
        tb = [Buf("tb%d" % i, fb[6 + i][:].bitcast(BF16)) for i in range(2)]
        for i in range(2):
            tb[i].w, tb[i].r = fb[6 + i].w, fb[6 + i].r

        def MM(out, lhsT, rhs, start, stop, reads, writes, inc=True):
            P.op("tensor", lambda e: e.matmul(out, lhsT=lhsT, rhs=rhs, start=start, stop=stop), reads, writes, inc)

        def TR(out, in_, reads, writes, inc=True):
            P.op("tensor", lambda e: e.transpose(out=out, in_=in_, identity=ident[:]), list(reads) + [ident], writes, inc)

        def ACT(out, in_, func, reads, writes, bias=None, scale=None, accum_out=None):
            kw = {}
            if bias is not None:
                kw["bias"] = bias
            if scale is not None:
                kw["scale"] = scale
            if accum_out is not None:
                kw["accum_out"] = accum_out
            P.op("scalar", lambda e: e.activation(out=out, in_=in_, func=func, **kw), reads, writes)

        def TT(eng, out, in0, in1, op, reads, writes):
            P.op(eng, lambda e: e.tensor_tensor(out=out, in0=in0, in1=in1, op=op), reads, writes)

        def TS(eng, out, in0, s1, s2, op0, op1, reads, writes):
            if op1 is None:
                P.op(eng, lambda e: e.tensor_scalar(out=out, in0=in0, scalar1=s1, scalar2=None, op0=op0), reads, writes)
            else:
                P.op(eng, lambda e: e.tensor_scalar(out=out, in0=in0, scalar1=s1, scalar2=s2, op0=op0, op1=op1), reads, writes)

        def STT(eng, out, in0, scalar, in1, op0, op1, reads, writes):
            P.op(eng, lambda e: e.scalar_tensor_tensor(out=out, in0=in0, scalar=scalar, in1=in1, op0=op0, op1=op1), reads, writes)

        def RSTD(out, ss_ap, n, reads, writes, mult=1.0):
            ACT(out, ss_ap, AF.Ln, reads, writes, bias=EPS, scale=1.0 / n)
            ACT(out, out, AF.Exp, writes, writes, bias=(math.log(mult) if mult != 1.0 else None), scale=-0.5)

        def CP(eng, out, in_, reads, writes):
            if eng == "scalar":
                P.op(eng, lambda e: e.copy(out=out, in_=in_), reads, writes)
            else:
                P.op(eng, lambda e: e.tensor_copy(out=out, in_=in_), reads, writes)

        def MEMSET(eng, ap, val, writes):
            P.op(eng, lambda e: e.memset(ap, val), (), writes)

        def DMA(eng, out, in_, reads=(), writes=()):
            P.dma(eng, lambda e: e.dma_start(out=out, in_=in_), reads, writes)

        def load_w(buf, dram2d, k_chunks, c0, c1, dst_c0=0):
            v = dram2d.rearrange("(k p) c -> p k c", p=128)
            DMA("gpsimd", buf[:, 0:k_chunks, dst_c0:dst_c0 + (c1 - c0)], v[:, :, c0:c1], writes=[buf])

        def bcast_load(buf, row_ap, n):
            DMA("sync", buf[:], row_ap.to_broadcast([128, n]), writes=[buf])

        fctr = [0]

        def next_f(lo=0, hi=4):
            b = fb[lo + fctr[0] % (hi - lo)]
            fctr[0] += 1
            return b

        DMA("scalar", ropec[:], c_ropec[:, :], writes=[ropec])
        DMA("scalar", ropes[:], c_ropes[:, :], writes=[ropes])
        DMA("sync", tri[:], c_tri[:, :], writes=[tri])
        DMA("sync", m01[:], c_m01[:, :], writes=[m01])
        DMA("sync", ident[:], c_ident[:, :], writes=[ident])
        for j in range(4):
            DMA("sync", mkt[j][:], c_mk[2 * j:2 * j + 2, :], writes=[mkt[j]])
        h0v = h0.rearrange("(t p) d -> p t d", p=128)
        DMA("sync", X[:, 0:6, :], h0v[:, 0:6, :], writes=[X])
        DMA("scalar", X[:, 6:12, :], h0v[:, 6:12, :], writes=[X])
        DMA("sync", X[:, 12:NT, :], h0v[:, 12:NT, :], writes=[X])
        P.emit()

        def phase_norm(l):
            with ExitStack() as st:
                grep = P.sb(st, "grep", [128, D], F32)
                junk = [P.sb(st, "junk%d" % j, [128, D], BF16) for j in range(2)]
                ss = P.sb(st, "ss", [128, NT], F32)
                rs = P.sb(st, "rs", [128, NT], F32)
                ssb = [Buf("ss_%d" % i, ss.t) for i in range(NT)]
                rsb = [Buf("rs_%d" % i, rs.t) for i in range(NT)]
                hn = [P.sb(st, "hn%d" % j, [128, D], BF16) for j in range(3)]
                bcast_load(grep, norm_g[l:l + 1, :], D)

                def sa(i):
                    ACT(junk[i % 2][:], X[:, i, :], AF.Square, [X], [junk[i % 2], ssb[i]], accum_out=ss[:, i:i + 1])
                    RSTD(rs[:, i:i + 1], ss[:, i:i + 1], D, [ssb[i]], [rsb[i]])
                    h = hn[i % 3]
                    STT("vector", h[:], X[:, i, :], rs[:, i:i + 1], grep[:], ALU.mult, ALU.mult, [X, rsb[i], grep], [h])

                def sb_(i):
                    h, t = hn[i % 3], tb[i % 2]
                    for k in range(KC):
                        TR(t[:, k * 128:(k + 1) * 128], h[:, k * 128:(k + 1) * 128], [h], [t], inc=(k == KC - 1))

                def sc(i):
                    t = tb[i % 2]
                    CP("vector", hT[:, :, i * 128:(i + 1) * 128], t[:].rearrange("p (k c) -> p k c", k=KC), [t], [hT])

                for step in range(NT + 2):
                    if step < NT:
                        sa(step)
                    if 0 <= step - 1 < NT:
                        sb_(step - 1)
                    if 0 <= step - 2 < NT:
                        sc(step - 2)
                P.emit()

        def proj_tok(i, W, c0, n, kchunks=KC, src=None, src_k0=0):
            src = src or hT
            p = next_f()
            for k in range(kchunks):
                MM(p[:, 0:n], src[:, src_k0 + k, i * 128:(i + 1) * 128], W[:, k, c0:c0 + n], k == 0, k == kchunks - 1,
                   [src, W], [p], inc=(k == kchunks - 1))
            return p

        def proj_feat(p, c0, n, W, wc0, M, kchunks=KC, src=None, src_k0=0):
            src = src or hT
            for k in range(kchunks):
                MM(p[0:M, 0:n], W[:, k, wc0:wc0 + M], src[:, src_k0 + k, c0:c0 + n], k == 0, k == kchunks - 1,
                   [src, W], [p], inc=(k == kchunks - 1))

        def epilogue(l, b, zoff):
            with ExitStack() as st:
                Wz = P.sb(st, "Wz", [128, KC, 512], BF16)
                Wo = P.sb(st, "Wo", [128, 4, D], BF16)
                Wg = P.sb(st, "Wg", [128, KC, D], BF16)
                Wout = P.sb(st, "Wout", [128, KC, D], BF16)
                bgt = P.sb(st, "bgt", [128, 8], F32)
                G = [P.sb(st, "G%d" % j, [128, 512], BF16) for j in range(2)]
                ogg = [P.sb(st, "ogg%d" % j, [128, 512], BF16) for j in range(3)]
                oggT = P.sb(st, "oggT", [128, 4, 512], BF16)
                sg = [P.sb(st, "sg%d" % j, [128, 512], F32) for j in range(2)]
                mT = P.sb(st, "mT", [128, KC, 512], BF16)
                load_w(Wz, w_in[l], KC, zoff, zoff + 512)
                load_w(Wo, w_o[b][l], 4, 0, D)
                load_w(Wg, w_in[l], KC, O_G + b * D, O_G + (b + 1) * D)
                load_w(Wout, w_out[l], KC, 0, D)
                DMA("sync", bgt[:], bg[l, b, :, :], writes=[bgt])
                cnt = 0
                for (c0, n) in CHUNKS:
                    tiles = list(range(c0 // 128, (c0 + n) // 128))
                    pzs = [proj_tok(i, Wz, 0, 512) for i in tiles]
                    for oc in range(2):
                        proj_feat(fb[4 + oc], c0, n, Wg, oc * 128, 128)
                    for j, i in enumerate(tiles):
                        pz = pzs[j]
                        g_, o_ = G[cnt % 2], ogg[cnt % 3]
                        ACT(g_[:], pz[:, :], AF.Silu, [pz], [g_])
                        TT("vector", o_[:], og[:, i, :], g_[:], ALU.mult, [og, g_], [o_])
                        cnt += 1
                        t = tb[j % 2]
                        for c in range(4):
                            TR(t[:, c * 128:(c + 1) * 128], o_[:, c * 128:(c + 1) * 128], [o_], [t], inc=(c == 3))
                        CP("vector", oggT[:, :, j * 128:(j + 1) * 128], t[:, 0:512].rearrange("p (c q) -> p c q", c=4), [t], [oggT])
                    for oc in range(8):
                        pg = fb[4 + oc % 2]
                        if oc >= 2:
                            proj_feat(pg, c0, n, Wg, oc * 128, 128)
                        py = next_f()
                        for c in range(4):
                            MM(py[:, 0:n], Wo[:, c, oc * 128:(oc + 1) * 128], oggT[:, c, 0:n], c == 0, c == 3, [Wo, oggT], [py], inc=(c == 3))
                        s_ = sg[oc % 2]
                        ACT(s_[:, 0:n], pg[:, 0:n], AF.Sigmoid, [pg, bgt], [s_], bias=bgt[:, oc:oc + 1])
                        TT("vector", mT[:, oc, 0:n], s_[:, 0:n], py[:, 0:n], ALU.mult, [s_, py], [mT])
                    for j, i in enumerate(tiles):
                        for half in range(2):
                            po = next_f()
                            for k in range(KC):
                                MM(po[:, :], mT[:, k, j * 128:(j + 1) * 128], Wout[:, k, half * 512:(half + 1) * 512], k == 0, k == KC - 1,
                                   [mT, Wout], [po], inc=(k == KC - 1))
                            TT("vector", X[:, i, half * 512:(half + 1) * 512], X[:, i, half * 512:(half + 1) * 512], po[:, :], ALU.add, [X, po], [X])
                P.emit()

        def branch_sb(l):
            with ExitStack() as bst:
                V = P.sb(bst, "Vsb", [128, NT, 512], BF16)
                with ExitStack() as st:
                    Wv = P.sb(st, "Wv", [128, KC, 512], BF16)
                    load_w(Wv, w_in[l], KC, O_SBV, O_SBV + 512)
                    for i in range(NT):
                        p = proj_tok(i, Wv, 0, 512)
                        CP("scalar" if i % 2 else "vector", V[:, i, :], p[:, :], [p], [V])
                    P.emit()
                with ExitStack() as st:
                    Wqk2 = [P.sb(st, "Wqk%d" % j, [128, KC, 256], BF16) for j in range(2)]
                    qT = P.sb(st, "qT", [128, L], BF16)
                    kT = P.sb(st, "kT", [128, L], BF16)
                    NE, NSP, NC = 4, 3, 3
                    ez = [P.sb(st, "ez%d" % j, [128, 512], F32) for j in range(NE)]
                    sp = [P.sb(st, "sp%d" % j, [128, 516], F32) for j in range(NSP)]
                    Cb = [P.sb(st, "Cb%d" % j, [128, 512], F32) for j in range(NC)]
                    ctot = [P.sb(st, "ctot%d" % j, [128, 1], F32) for j in range(3)]
                    for j in range(NSP):
                        MEMSET("vector", sp[j][:], 0.0, [sp[j]])
                    wb = [P.sb(st, "wb%d" % j, [128, 512], BF16) for j in range(2)]
                    wT = [P.sb(st, "wT%d" % j, [128, 512], BF16) for j in range(2)]
                    cn = [P.sb(st, "cn%d" % j, [128, 1], F32) for j in range(3)]
                    def load_pair_w(pr_):
                        load_w(Wqk2[pr_ % 2], w_in[l], KC, O_SBQ + pr_ * 128, O_SBQ + (pr_ + 1) * 128, dst_c0=0)
                        load_w(Wqk2[pr_ % 2], w_in[l], KC, O_SBK + pr_ * 128, O_SBK + (pr_ + 1) * 128, dst_c0=128)
                    load_pair_w(0)
                    for pr in range(4):
                        Wqk = Wqk2[pr % 2]
                        if pr + 1 < 4:
                            load_pair_w(pr + 1)
                        for (c0, n) in CHUNKS:
                            p = next_f(4, 6)
                            proj_feat(p, c0, n, Wqk, 0, 128)
                            CP("scalar", qT[:, c0:c0 + n], p[:, 0:n], [p], [qT])
                            p = next_f(4, 6)
                            proj_feat(p, c0, n, Wqk, 128, 128)
                            CP("vector", kT[:, c0:c0 + n], p[:, 0:n], [p], [kT])
                        items = []
                        for hh in range(2):
                            for i in range(NT):
                                nk = (i + 1) * 128
                                chs = [(k0, min(512, nk - k0)) for k0 in range(0, nk, 512)][::-1]
                                for ci, (k0, n) in enumerate(chs):
                                    items.append((hh, i, ci, k0, n, ci == len(chs) - 1))
                        N = len(items)
                        obank = {}
                        ocnt = [0]

                        def st_mm(j):
                            hh, i, ci, k0, n, last = items[j]
                            r0 = 64 * hh
                            z = fb[j % 2]
                            MM(z[:, 0:n], qT[r0:r0 + 64, i * 128:(i + 1) * 128], kT[r0:r0 + 64, k0:k0 + n], True, True, [qT, kT], [z])

                        def st_expz(j):
                            hh, i, ci, k0, n, last = items[j]
                            e_ = ez[j % NE]
                            ACT(e_[:, 0:n], fb[j % 2][:, 0:n], AF.Exp, [fb[j % 2]], [e_], scale=0.125)
                            if ci == 0:
                                TT("gpsimd", e_[:, n - 128:n], e_[:, n - 128:n], tri[:], ALU.mult, [e_, tri], [e_])

                        def st_ln(j):
                            hh, i, ci, k0, n, last = items[j]
                            ACT(sp[j % NSP][:, 1:n + 1], ez[j % NE][:, 0:n], AF.Ln, [ez[j % NE]], [sp[j % NSP], ctot[j % 3]], bias=1.0,
                                accum_out=ctot[j % 3][:])

                        def st_scan(j):
                            hh, i, ci, k0, n, last = items[j]
                            s_, c_ = sp[j % NSP], Cb[j % NC]
                            P.op("vector", lambda e: e.tensor_tensor_scan(out=c_[:, 0:n], data0=s_[:, 0:n], data1=s_[:, 0:n],
                                                                           initial=0.0, op0=ALU.add, op1=ALU.max), [s_], [c_])

                        def st_cn(j):
                            hh, i, ci, k0, n, last = items[j]
                            if ci == 0:
                                TS("vector", cn[j % 3][:], ctot[j % 3][:], -1.0, None, ALU.mult, None, [ctot[j % 3]], [cn[j % 3]])
                            else:
                                TT("vector", cn[j % 3][:], cn[(j - 1) % 3][:], ctot[j % 3][:], ALU.subtract, [cn[(j - 1) % 3], ctot[j % 3]], [cn[j % 3]])

                        def st_expt(j):
                            hh, i, ci, k0, n, last = items[j]
                            ACT(Cb[j % NC][:, 0:n], Cb[j % NC][:, 0:n], AF.Exp, [Cb[j % NC], cn[j % 3]], [Cb[j % NC]], bias=cn[j % 3][:, 0:1])

                        def st_mult(j):
                            hh, i, ci, k0, n, last = items[j]
                            TT("gpsimd", wb[j % 2][:, 0:n], ez[j % NE][:, 0:n], Cb[j % NC][:, 0:n], ALU.mult, [ez[j % NE], Cb[j % NC]], [wb[j % 2]])

                        def st_tr(j):
                            hh, i, ci, k0, n, last = items[j]
                            t = tb[j % 2]
                            nb = n // 128
                            for jb in range(nb):
                                TR(t[:, jb * 128:(jb + 1) * 128], wb[j % 2][:, jb * 128:(jb + 1) * 128], [wb[j % 2]], [t], inc=(jb == nb - 1))

                        def st_evac(j):
                            hh, i, ci, k0, n, last = items[j]
                            CP("scalar" if j % 2 == 0 else "vector", wT[j % 2][:, 0:n], tb[j % 2][:, 0:n], [tb[j % 2]], [wT[j % 2]])

                        def st_pv(j):
                            hh, i, ci, k0, n, last = items[j]
                            h = 2 * pr + hh
                            nb = n // 128
                            if ci == 0:
                                obank[(hh, i)] = fb[2 + ocnt[0] % 2]
                                ocnt[0] += 1
                            O = obank[(hh, i)]
                            for jb in range(nb):
                                kb = k0 // 128 + jb
                                MM(O[:, 0:64], wT[j % 2][:, jb * 128:(jb + 1) * 128], V[:, kb, h * 64:(h + 1) * 64],
                                   ci == 0 and jb == 0, last and jb == nb - 1, [wT[j % 2], V], [O], inc=(jb == nb - 1))
                            if last:
                                CP("vector", og[:, i, h * 64:(h + 1) * 64], O[:, 0:64], [O], [og])

                        sched = [(st_mm, 0), (st_expz, 1), (st_expt, 3), (st_ln, 1), (st_evac, 6), (st_cn, 2), (st_scan, 2),
                                 (st_mult, 4), (st_tr, 5), (st_pv, 7)]
                        for step in range(N + 7):
                            for fn, off in sched:
                                if 0 <= step - off < N:
                                    fn(step - off)
                    P.emit()
            epilogue(l, 0, O_SBZ)

        def softmax_attn(st, units, dv1, scale, finalize, PT=None):
            NS = 5
            zbanks = [fb[0], fb[1], fb[6], fb[7]]
            import os as _os3
            if _os3.environ.get("SKIP_ATTN"):
                return
            if PT is None:
                PT = [P.sb(st, "PT%d" % j, [128, 512], BF16) for j in range(NS)]
            items = []
            for i in range(NT):
                kbs = list(range(0, min(i + 2, NT)))
                groups = [kbs[a:a + 4] for a in range(0, len(kbs), 4)]
                for u in range(len(units)):
                    for gi, g in enumerate(groups):
                        items.append((i, u, g, gi == 0, gi == len(groups) - 1))
            N = len(items)
            nu = len(units)

            def s1(j):
                i, u, g, first, last = items[j]
                QTb, KTb, r0, nr, vfn = units[u]
                z = zbanks[j % 4]
                for a, kb in enumerate(g):
                    msk = kb >= i
                    MM(z[:, a * 128:(a + 1) * 128], KTb[r0:r0 + nr, kb * 128:(kb + 1) * 128], QTb[r0:r0 + nr, i * 128:(i + 1) * 128],
                       True, not msk, [QTb, KTb], [z], inc=(a == len(g) - 1 and not msk))
                    if msk:
                        mo = 0 if kb == i else 2
                        MM(z[:, a * 128:(a + 1) * 128], mkt[mo][:, :], mkt[mo + 1][:, :], False, True, [mkt[mo], mkt[mo + 1]], [z],
                           inc=(a == len(g) - 1))

            def s2(j):
                i, u, g, first, last = items[j]
                z = zbanks[j % 4]
                s = j % NS
                n = len(g) * 128
                ACT(PT[s][:, 0:n], z[:, 0:n], AF.Exp, [z], [PT[s]], scale=scale)

            def s3(j):
                i, u, g, first, last = items[j]
                QTb, KTb, r0, nr, vfn = units[u]
                s = j % NS
                O = fb[2 + u] if nu > 1 else fb[2 + i % 2]
                for a, kb in enumerate(g):
                    MM(O[:, 0:dv1], PT[s][:, a * 128:(a + 1) * 128], vfn(kb), first and a == 0, last and a == len(g) - 1,
                       [PT[s]], [O], inc=(a == len(g) - 1))
                if last and u == nu - 1:
                    finalize(i, [fb[2 + uu] for uu in range(nu)] if nu > 1 else [O])

            for step in range(N + 2):
                if step < N:
                    s1(step)
                if 0 <= step - 1 < N:
                    s2(step - 1)
                if 0 <= step - 2 < N:
                    s3(step - 2)

        def diff_attn(st, QTb, KTb, vfn, finalize, PT=None):
            import os as _os4
            if _os4.environ.get("SKIP_ATTN"):
                return
            NS = 3
            if PT is None:
                PT = [P.sb(st, "PTd%d" % j, [128, 1024], BF16) for j in range(NS)]
            items = []
            for i in range(NT):
                kbs = list(range(0, min(i + 2, NT)))
                groups = [kbs[a:a + 4] for a in range(0, len(kbs), 4)]
                for gi, g in enumerate(groups):
                    items.append((i, g, gi == 0, gi == len(groups) - 1))
            N = len(items)

            def s1(j):
                i, g, first, last = items[j]
                zt = z2[j % 2]
                for a, kb in enumerate(g):
                    msk = kb >= i
                    for u in range(2):
                        r0 = 64 * u
                        MM(zt[:, u * 512 + a * 128:u * 512 + (a + 1) * 128], KTb[r0:r0 + 64, kb * 128:(kb + 1) * 128],
                           QTb[r0:r0 + 64, i * 128:(i + 1) * 128], True, not msk, [QTb, KTb], [zt],
                           inc=(a == len(g) - 1 and u == 1 and not msk))
                    if msk:
                        mo = 0 if kb == i else 2
                        for u in range(2):
                            MM(zt[:, u * 512 + a * 128:u * 512 + (a + 1) * 128], mkt[mo][:, :], mkt[mo + 1][:, :], False, True,
                               [mkt[mo], mkt[mo + 1]], [zt], inc=(a == len(g) - 1 and u == 1))

            def s2(j):
                i, g, first, last = items[j]
                zt = z2[j % 2]
                p_ = PT[j % NS]
                n = len(g) * 128
                ACT(p_[:].rearrange("p (u c) -> p u c", u=2)[:, :, 0:n], zt[:].rearrange("p (u c) -> p u c", u=2)[:, :, 0:n],
                    AF.Exp, [zt], [p_], scale=0.125)

            def s3(j):
                i, g, first, last = items[j]
                p_ = PT[j % NS]
                Os = [fb[4 + 2 * (i % 2)], fb[5 + 2 * (i % 2)]]
                for a, kb in enumerate(g):
                    for u in range(2):
                        MM(Os[u][:, 0:129], p_[:, u * 512 + a * 128:u * 512 + (a + 1) * 128], vfn(kb),
                           first and a == 0, last and a == len(g) - 1, [p_], [Os[u]], inc=(a == len(g) - 1))
                if last:
                    finalize(i, Os)

            for step in range(N + 2):
                if step < N:
                    s1(step)
                if 0 <= step - 1 < N:
                    s2(step - 1)
                if 0 <= step - 2 < N:
                    s3(step - 2)

        def branch_mla(l):
            with ExitStack() as bst:
                cnT = P.sb(bst, "cnT", [128, 5, L], BF16)
                Va = P.sb(bst, "Va", [128, NT, 8, 68], BF16)
                KT = P.sb(bst, "KTm", [128, L], BF16)
                with ExitStack() as st:
                    W = P.sb(st, "Wm", [128, KC, 704], BF16)
                    Wv = P.sb(st, "Wukvv", [128, 2, 512], BF16)
                    gq = P.sb(st, "gq", [128, 640], F32)
                    junk = P.sb(st, "junkm", [128, 384], BF16)
                    ss = P.sb(st, "ssm", [128, 2 * NT], F32)
                    rs = P.sb(st, "rsm", [128, 2 * NT], F32)
                    cb = [P.sb(st, "cb%d" % j, [128, 640], BF16) for j in range(2)]
                    t1 = P.sb(st, "t1m", [128, 512], F32)
                    t2 = P.sb(st, "t2m", [128, 512], F32)
                    load_w(W, w_in[l], KC, O_CQ, O_CQ + 672)
                    load_w(W, w_x[l], KC, 1024, 1056, dst_c0=672)
                    load_w(Wv, ukvv[l], 2, 0, 512)
                    DMA("sync", gq[:, 0:384], cq_g[l:l + 1, :].to_broadcast([128, 384]), writes=[gq])
                    DMA("sync", gq[:, 384:640], ckv_g[l:l + 1, :].to_broadcast([128, 256]), writes=[gq])
                    MEMSET("gpsimd", Va[:].rearrange("p a b c -> p (a b c)"), 1.0, [Va])
                    ssb = [Buf("ssm_%d" % i, ss.t) for i in range(2 * NT)]
                    rsb = [Buf("rsm_%d" % i, rs.t) for i in range(2 * NT)]
                    cb3 = cb + [P.sb(st, "cb2", [128, 640], BF16)]
                    junk2 = [junk, P.sb(st, "junkm2", [128, 384], BF16)]

                    def sa(i):
                        c = cb3[i % 3]
                        for part, (wc0, n, dc0) in enumerate(((0, 384, 0), (384, 256, 384))):
                            p = proj_tok(i, W, wc0, n)
                            col = 2 * i + part
                            ACT(junk2[part][:, 0:n], p[:, 0:n], AF.Square, [p], [junk2[part], ssb[col]], accum_out=ss[:, col:col + 1])
                            RSTD(rs[:, col:col + 1], ss[:, col:col + 1], n, [ssb[col]], [rsb[col]])
                            STT("vector", c[:, dc0:dc0 + n], p[:, 0:n], rs[:, col:col + 1], gq[:, dc0:dc0 + n], ALU.mult, ALU.mult, [p, rsb[col], gq], [c])

                    def sb_(i):
                        c, t = cb3[i % 3], tb[i % 2]
                        for k in range(5):
                            TR(t[:, k * 128:(k + 1) * 128], c[:, k * 128:(k + 1) * 128], [c], [t], inc=(k == 4))

                    def sc(i):
                        t = tb[i % 2]
                        CP("vector" if i % 2 else "scalar", cnT[:, :, i * 128:(i + 1) * 128], t[:, 0:640].rearrange("p (k c) -> p k c", k=5), [t], [cnT])

                    for step in range(NT + 2):
                        if step < NT:
                            sa(step)
                        if 0 <= step - 1 < NT:
                            sb_(step - 1)
                        if 0 <= step - 2 < NT:
                            sc(step - 2)
                    for (c0, n) in CHUNKS:
                        pa = fb[4]
                        pb = fb[5]
                        proj_feat(pa, c0, n, W, 608, 64)
                        proj_feat(pb, c0, n, W, 640, 64)
                        TT("vector", t1[32:64, 0:n], pa[32:64, 0:n], ropec[32:64, c0:c0 + n], ALU.mult, [pa, ropec], [t1])
                        TT("vector", t2[32:64, 0:n], pb[32:64, 0:n], ropes[32:64, c0:c0 + n], ALU.mult, [pb, ropes], [t2])
                        TT("vector", KT[32:64, c0:c0 + n], t1[32:64, 0:n], t2[32:64, 0:n], ALU.add, [t1, t2], [KT])
                    for i in range(NT):
                        p = proj_tok(i, Wv, 0, 512, kchunks=2, src=cnT, src_k0=3)
                        CP("scalar" if i % 2 else "vector", Va[:, i, :, 0:64], p[:, :].rearrange("p (h d) -> p h d", h=8), [p], [Va])
                    P.emit()
                with ExitStack() as st:
                    QT = P.sb(st, "QTm", [128, L], BF16)
                    Wa = P.sb(st, "Wuqa", [128, 3, 768], BF16)
                    Wb = P.sb(st, "Wuqb", [128, 3, 768], BF16)
                    Wk = P.sb(st, "Wukn", [128, 2, 768], BF16)
                    t1 = P.sb(st, "t1q", [128, 512], F32)
                    t2 = P.sb(st, "t2q", [128, 512], F32)
                    rcp = P.sb(st, "rcp", [128, 1], F32)
                    PTm = [P.sb(st, "PTm%d" % j, [128, 512], BF16) for j in range(5)]
                    load_w(Wa, uqa[l], 3, 0, 768)
                    load_w(Wb, uqb[l], 3, 0, 768)
                    load_w(Wk, ukn[l], 2, 0, 768)
                    for h in range(8):
                        for cidx, (c0, n) in enumerate(CHUNKS):
                            pa, pb = fb[4 + 2 * (cidx % 2)], fb[5 + 2 * (cidx % 2)]
                            proj_feat(pa, c0, n, Wa, h * 96, 96, kchunks=3, src=cnT)
                            proj_feat(pb, c0, n, Wb, h * 96, 96, kchunks=3, src=cnT)
                            CP("scalar", QT[0:32, c0:c0 + n], pa[0:32, 0:n], [pa], [QT])
                            CP("scalar", QT[64:96, c0:c0 + n], pa[64:96, 0:n], [pa], [QT])
                            TT("vector", t1[32:64, 0:n], pa[32:64, 0:n], ropec[32:64, c0:c0 + n], ALU.mult, [pa, ropec], [t1])
                            TT("vector", t2[32:64, 0:n], pb[32:64, 0:n], ropes[32:64, c0:c0 + n], ALU.mult, [pb, ropes], [t2])
                            TT("vector", QT[32:64, c0:c0 + n], t1[32:64, 0:n], t2[32:64, 0:n], ALU.add, [t1, t2], [QT])
                            pk = pb
                            proj_feat(pk, c0, n, Wk, h * 96, 96, kchunks=2, src=cnT, src_k0=3)
                            CP("scalar", KT[0:32, c0:c0 + n], pk[0:32, 0:n], [pk], [KT])
                            CP("vector", KT[64:96, c0:c0 + n], pk[64:96, 0:n], [pk], [KT])

                        def fin(i, Os, h=h):
                            O = Os[0]
                            P.op("vector", lambda e: e.reciprocal(out=rcp[:], in_=O[:, 64:65]), [O], [rcp])
                            TS("vector", og[:, i, h * 64:(h + 1) * 64], O[:, 0:64], rcp[:, 0:1], None, ALU.mult, None, [O, rcp], [og])

                        softmax_attn(st, [(QT, KT, 0, 96, (lambda kb, h=h: Va[:, kb, h, 0:65]))], 65, 1.0 / math.sqrt(96.0), fin, PT=PTm)
                    P.emit()
            epilogue(l, 1, O_MZ)

        def branch_diff(l):
            lam_init = 0.8 - 0.6 * math.exp(-0.3 * l)
            with ExitStack() as bst:
                Vd = P.sb(bst, "Vd", [128, NT, 4, 132], BF16)
                lam = P.sb(bst, "lam", [128, 1], F32)
                gd = P.sb(bst, "gd", [128, 128], F32)
                with ExitStack() as st:
                    Wv = P.sb(st, "Wdv", [128, KC, 512], BF16)
                    dl = P.sb(st, "dl", [128, 256], F32)
                    pr_ = P.sb(st, "prd", [128, 128], F32)
                    sm = P.sb(st, "smd", [128, 2], F32)
                    load_w(Wv, w_in[l], KC, O_DV, O_DV + 512)
                    DMA("sync", dl[:], dlam[l:l + 1, :].to_broadcast([128, 256]), writes=[dl])
                    DMA("sync", gd[:], dng[l:l + 1, :].to_broadcast([128, 128]), writes=[gd])
                    dl3 = dl[:].rearrange("p (a b) -> p a b", a=2)
                    TT("vector", pr_[:].rearrange("p (a b) -> p a b", a=2), dl3[:, :, 0:64], dl3[:, :, 64:128], ALU.mult, [dl], [pr_])
                    P.op("vector", lambda e: e.reduce_sum(out=sm[:], in_=pr_[:].rearrange("p (a b) -> p a b", a=2), axis=mybir.AxisListType.X), [pr_], [sm])
                    ACT(sm[:], sm[:], AF.Exp, [sm], [sm])
                    TT("vector", lam[:], sm[:, 0:1], sm[:, 1:2], ALU.subtract, [sm], [lam])
                    TS("vector", lam[:], lam[:], lam_init, None, ALU.add, None, [lam], [lam])
                    MEMSET("gpsimd", Vd[:].rearrange("p a b c -> p (a b c)"), 1.0, [Vd])
                    for i in range(NT):
                        p = proj_tok(i, Wv, 0, 512)
                        CP("scalar" if i % 2 else "vector", Vd[:, i, :, 0:128], p[:, :].rearrange("p (h d) -> p h d", h=4), [p], [Vd])
                    P.emit()
                with ExitStack() as st:
                    PTd = [P.sb(st, "PTd%d" % j, [128, 1024], BF16) for j in range(3)]
                    Wq2 = [P.sb(st, "Wdq%d" % j, [128, KC, 512], BF16) for j in range(2)]

                    def load_head_w(h):
                        Wq_ = Wq2[h % 2]
                        load_w(Wq_, w_in[l], KC, O_DQ + h * 128, O_DQ + (h + 1) * 128, dst_c0=0)
                        load_w(Wq_, w_x[l], KC, h * 128, (h + 1) * 128, dst_c0=128)
                        load_w(Wq_, w_in[l], KC, O_DK + h * 128, O_DK + (h + 1) * 128, dst_c0=256)
                        load_w(Wq_, w_x[l], KC, 512 + h * 128, 512 + (h + 1) * 128, dst_c0=384)
                    load_head_w(0)
                    QT = P.sb(st, "QTd", [128, L], BF16)
                    KT = P.sb(st, "KTd", [128, L], BF16)
                    t1 = P.sb(st, "t1d", [128, 512], F32)
                    t2 = P.sb(st, "t2d", [128, 512], F32)
                    rc = P.sb(st, "rcd", [128, 2], F32)
                    tm = P.sb(st, "tmd", [128, 128], F32)
                    oc_ = P.sb(st, "ocd", [128, 128], F32)
                    jk = P.sb(st, "jkd", [128, 128], BF16)
                    ssd = P.sb(st, "ssd", [128, 1], F32)
                    rsd = P.sb(st, "rsd", [128, 1], F32)
                    for h in range(4):
                        cc = 0
                        Wq = Wq2[h % 2]
                        if h + 1 < 4:
                            load_head_w(h + 1)
                        for (dst, wc) in ((QT, 0), (KT, 256)):
                            for (c0, n) in CHUNKS:
                                pa, pb = fb[4 + 2 * (cc % 2)], fb[5 + 2 * (cc % 2)]
                                cc += 1
                                proj_feat(pa, c0, n, Wq, wc, 128)
                                proj_feat(pb, c0, n, Wq, wc + 128, 128)
                                TT("vector", t1[:, 0:n], pa[:, 0:n], ropec[:, c0:c0 + n], ALU.mult, [pa, ropec], [t1])
                                TT("vector", t2[:, 0:n], pb[:, 0:n], ropes[:, c0:c0 + n], ALU.mult, [pb, ropes], [t2])
                                TT("vector", dst[:, c0:c0 + n], t1[:, 0:n], t2[:, 0:n], ALU.add, [t1, t2], [dst])
                                CP("scalar", dst[32:64, c0:c0 + n], pa[32:64, 0:n], [pa], [dst])

                        def fin(i, Os, h=h):
                            O0, O1 = Os
                            P.op("vector", lambda e: e.reciprocal(out=rc[:, 0:1], in_=O0[:, 128:129]), [O0], [rc])
                            P.op("vector", lambda e: e.reciprocal(out=rc[:, 1:2], in_=O1[:, 128:129]), [O1], [rc])
                            TT("vector", rc[:, 1:2], rc[:, 1:2], lam[:, 0:1], ALU.mult, [rc, lam], [rc])
                            TS("vector", tm[:], O1[:, 0:128], rc[:, 1:2], None, ALU.mult, None, [O1, rc], [tm])
                            STT("vector", oc_[:], O0[:, 0:128], rc[:, 0:1], tm[:], ALU.mult, ALU.subtract, [O0, rc, tm], [oc_])
                            MEMSET("vector", ssd[:], 0.0, [ssd])
                            ACT(jk[:], oc_[:], AF.Square, [oc_], [jk, ssd], accum_out=ssd[:, 0:1])
                            RSTD(rsd[:], ssd[:], 128, [ssd], [rsd], mult=1.0 - lam_init)
                            STT("vector", og[:, i, h * 128:(h + 1) * 128], oc_[:], rsd[:, 0:1], gd[:], ALU.mult, ALU.mult, [oc_, rsd, gd], [og])

                        diff_attn(st, QT, KT, (lambda kb, h=h: Vd[:, kb, h, 0:129]), fin, PT=PTd)
                    P.emit()
            epilogue(l, 2, O_DZ)

        import os as _os2
        _epi_only = _os2.environ.get("EPI_ONLY")
        for l in range(nlayers):
            phase_norm(l)
            if _epi_only:
                for _ in range(int(_epi_only)):
                    epilogue(l, 0, O_SBZ)
                continue
            if 0 in branches:
                branch_sb(l)
            if 1 in branches:
                branch_mla(l)
            if 2 in branches:
                branch_diff(l)

        with ExitStack() as st:
            grep = P.sb(st, "grepf", [128, D], F32)
            junk = P.sb(st, "junkf", [128, D], BF16)
            ss = P.sb(st, "ssf", [128, NT], F32)
            rs = P.sb(st, "rsf", [128, NT], F32)
            yo = [P.sb(st, "yo%d" % j, [128, D], F32) for j in range(2)]
            bcast_load(grep, final_g[0:1, :], D)
            MEMSET("vector", ss[:], 0.0, [ss])
            for i in range(NT):
                o = yo[i % 2]
                if final_norm:
                    ACT(junk[:], X[:, i, :], AF.Square, [X], [junk, ss], accum_out=ss[:, i:i + 1])
                    RSTD(rs[:, i:i + 1], ss[:, i:i + 1], D, [ss], [rs])
                    STT("vector", o[:], X[:, i, :], rs[:, i:i + 1], grep[:], ALU.mult, ALU.mult, [X, rs, grep], [o])
                else:
                    CP("vector", o[:], X[:, i, :], [X], [o])
                p_lo = NMETA if i == 0 else 0
                p_hi = NMETA if i == NT - 1 else 128
                s0 = 128 * i - NMETA + p_lo
                DMA("sync", y[s0:s0 + (p_hi - p_lo), :], o[p_lo:p_hi, :], reads=[o])
            P.wait_all("sync", yo)
            P.emit()
    return nc


_CACHE = {}


def _consts():
    if "c" not in _CACHE:
        C, Sg = _rope_tables()
        tri, m01, ident, mk = _masks()
        _CACHE["c"] = {"c_ropec": C, "c_ropes": Sg, "c_tri": tri, "c_m01": m01, "c_ident": ident, "c_mk": mk}
    return _CACHE["c"]


def make_in_maps(inp):
    x = np.asarray(inp["x"], np.float32)
    B = x.shape[0]
    meta = np.asarray(inp["meta_tokens"], np.float32)
    lay = _host_layouts({k: np.asarray(v) for k, v in inp.items()})
    shared = dict(_consts())
    shared.update(lay)
    for k in ("norm_g", "w_in", "mla_cq_g", "mla_ckv_g", "diff_norm_g", "w_o_sb", "w_o_mla", "w_o_diff", "w_out"):
        shared[k] = np.ascontiguousarray(np.asarray(inp[k], np.float32))
    shared["diff_lambda"] = np.ascontiguousarray(np.asarray(inp["diff_lambda"], np.float32).reshape(2, 256))
    shared["final_g"] = np.ascontiguousarray(np.asarray(inp["final_g"], np.float32).reshape(1, D))
    maps = []
    for b in range(B):
        h0 = np.concatenate([meta, x[b], np.zeros((L - NMETA - S, D), np.float32)], axis=0)
        m = dict(shared)
        m["h0"] = np.ascontiguousarray(h0)
        maps.append(m)
    return maps


def kernel(**inputs):
    maps = make_in_maps(inputs)
    if "nc" not in _CACHE:
        _CACHE["nc"] = build_nc()
    res = run_bass_kernel_spmd(_CACHE["nc"], maps, core_ids=list(range(len(maps))))
    return np.stack([np.asarray(r["y"], np.float32) for r in res.results], axis=0)
```

```python
import math
import numpy as np
import ml_dtypes
from contextlib import ExitStack
import concourse.bass as bass
import concourse.mybir as mybir
from concourse.bass_utils import run_bass_kernel_spmd

F32 = mybir.dt.float32
BF16 = mybir.dt.bfloat16
AF = mybir.ActivationFunctionType
ALU = mybir.AluOpType

D = 1024
S = 2048
NMETA = 16
NT = 17
L = NT * 128
KC = 8
EPS = 1e-6
THETA = 500000.0
CHUNKS = [(0, 512), (512, 512), (1024, 512), (1536, 512), (2048, 128)]


class Buf:
    __slots__ = ("name", "t", "w", "r", "dsem", "dcnt")

    def __init__(self, name, t=None):
        self.name = name
        self.t = t
        self.w = []
        self.r = []
        self.dsem = None
        self.dcnt = 0

    def __getitem__(self, idx):
        return self.t[idx]


class Prog:
    ENGS = ("tensor", "vector", "scalar", "gpsimd", "sync")

    def __init__(self, nc, stack):
        self.nc = nc
        self.stack = stack
        self.sems = {}
        self.cnt = {e: 0 for e in self.ENGS}
        self.seen = {e: {} for e in self.ENGS}
        self.q = {e: [] for e in self.ENGS}
        self.snaps = {}
        for e in self.ENGS:
            self._sem("E_" + e)
        self.nbuf = 0

    def _sem(self, key):
        if key not in self.sems:
            self.sems[key] = self.stack.enter_context(self.nc.semaphore(key))
        return key

    def sb(self, st, name, shape, dt):
        self.uid = getattr(self, "uid", 0) + 1
        name = "%s_%d" % (name, self.uid)
        t = st.enter_context(self.nc.sbuf_tensor(name, list(shape), dt))
        return Buf(name, t)

    def ps(self, st, name, shape, dt=F32):
        t = st.enter_context(self.nc.psum_tensor(name, list(shape), dt))
        return Buf(name, t)

    def _waits(self, eng, reads, writes):
        own = "E_" + eng
        need = {}
        for b in reads:
            for (k, v) in b.w:
                if k == own and (eng == "tensor" or v > self.cnt[eng]):
                    continue
                if need.get(k, 0) < v:
                    need[k] = v
        for b in writes:
            for (k, v) in b.w:
                if k == own and (eng == "tensor" or v > self.cnt[eng]):
                    continue
                if need.get(k, 0) < v:
                    need[k] = v
            for (k, v) in b.r:
                if k == own and (eng == "tensor" or v > self.cnt[eng]):
                    continue
                if need.get(k, 0) < v:
                    need[k] = v
        out = []
        seen = self.seen[eng]
        snaps = self.snaps
        for k, v in sorted(need.items(), key=lambda kv: 0 if kv[0].startswith("E_") else 1):
            if seen.get(k, 0) < v:
                seen[k] = v
                out.append((k, v))
                sn = snaps.get((k, v))
                if sn:
                    for k2, v2 in sn.items():
                        if seen.get(k2, 0) < v2:
                            seen[k2] = v2
        return out

    def op(self, eng, fn, reads=(), writes=(), inc=True):
        waits = self._waits(eng, reads, writes)
        key = "E_" + eng
        val = self.cnt[eng] + 1
        if inc:
            self.cnt[eng] = val
        ev = (key, val)
        if inc:
            self.snaps[ev] = dict(self.seen[eng])
        for b in writes:
            b.w = [ev]
            b.r = []
        for b in reads:
            b.r = [e for e in b.r if e[0] != key] + [ev]
        self.q[eng].append((waits, fn, [(key, 1)] if inc else []))

    def dma(self, eng, fn, reads=(), writes=(), sem_buf=None):
        waits = self._waits(eng, reads, writes)
        sb = sem_buf or (writes[0] if writes else reads[0])
        if sb.dsem is None:
            sb.dsem = self._sem("D_%d" % self.nbuf)
            self.nbuf += 1
        sb.dcnt += 16
        ev = (sb.dsem, sb.dcnt)
        self.snaps[ev] = dict(self.seen[eng])
        for b in writes:
            b.w = [e for e in b.w if e[0] != sb.dsem and e[0].startswith("D_")] + [ev]
            b.r = []
        for b in reads:
            b.r = [e for e in b.r if e[0] != sb.dsem] + [ev]
        self.q[eng].append((waits, fn, [(sb.dsem, 16)]))

    def wait_all(self, eng, bufs):
        waits = self._waits(eng, (), bufs)
        self.q[eng].append((waits, None, []))

    def emit(self):
        nc = self.nc
        qs = self.q
        self.q = {e: [] for e in self.ENGS}
        sems = self.sems
        with nc.Block() as block:
            def mk(ename):
                items = qs[ename]

                def body(e):
                    for waits, fn, incs in items:
                        if fn is None:
                            for (k, v) in waits:
                                e.wait_ge(sems[k], v)
                            continue
                        for (k, v) in waits[1:]:
                            e.wait_ge(sems[k], v)
                        ins = fn(e)
                        if waits:
                            ins._wait_ge(sems[waits[0][0]], waits[0][1])
                        for (k, n) in incs:
                            ins.then_inc(sems[k], n)
                return body
            block.tensor(mk("tensor"))
            block.vector(mk("vector"))
            block.scalar(mk("scalar"))
            block.gpsimd(mk("gpsimd"))
            block.sync(mk("sync"))


def _rope_tables():
    pos = np.arange(L, dtype=np.float32)
    C = np.ones((128, L), np.float32)
    Sg = np.zeros((128, L), np.float32)
    inv_d = (np.float32(THETA) ** (-np.arange(0, 16, 2, dtype=np.float32) / np.float32(16))).astype(np.float32)
    ang_d = (pos[:, None] * inv_d[None, :]).astype(np.float32)
    cd, sd = np.cos(ang_d).astype(np.float32), np.sin(ang_d).astype(np.float32)
    for base in (0, 64):
        for r in range(16):
            C[base + r] = cd[:, r % 8]
            Sg[base + r] = -sd[:, r % 8] if r < 8 else sd[:, r % 8]
    inv_m = (np.float32(THETA) ** (-np.arange(0, 32, 2, dtype=np.float32) / np.float32(32))).astype(np.float32)
    ang_m = (pos[:, None] * inv_m[None, :]).astype(np.float32)
    cm, sm = np.cos(ang_m).astype(np.float32), np.sin(ang_m).astype(np.float32)
    for r in range(32):
        C[32 + r] = cm[:, r % 16]
        Sg[32 + r] = -sm[:, r % 16] if r < 16 else sm[:, r % 16]
    return C, Sg


def _masks():
    a = np.arange(128)
    tri = (a[None, :] < a[:, None]).astype(np.float32)
    cq = (a + 48) // 64
    m0 = (cq[:, None] <= cq[None, :])
    m1 = ((a[:, None] < 16) & (a[None, :] >= 80))
    m01 = np.concatenate([m0, m1], axis=1).astype(np.float32).astype(ml_dtypes.bfloat16)
    ident = np.eye(128, dtype=np.float32).astype(ml_dtypes.bfloat16)
    BIG = 30000.0
    mk = np.zeros((8, 128), np.float32)
    mk[0] = -BIG * (cq == 1); mk[1] = -BIG * (cq == 2)
    mk[2] = (cq < 1); mk[3] = (cq < 2)
    mk[4] = -BIG; mk[5] = BIG * (a < 16)
    mk[6] = 1.0; mk[7] = (a >= 80)
    mk = mk.astype(ml_dtypes.bfloat16)
    return tri, m01, ident, mk


O_SBQ, O_SBK, O_SBV, O_SBZ = 0, 512, 1024, 1536
O_CQ, O_CKV, O_KR, O_MZ = 2048, 2432, 2688, 2720
O_DQ, O_DK, O_DV, O_DZ = 3232, 3744, 4256, 4768
O_G = 5280


def _host_layouts(inp):
    w_in = inp["w_in"]
    swap64 = np.concatenate([np.arange(8, 16), np.arange(0, 8), np.arange(16, 64)])
    idx_d = np.concatenate([m * 64 + swap64 for m in range(8)])
    kr_sw = np.concatenate([np.arange(16, 32), np.arange(0, 16)])
    w_x = np.concatenate([w_in[:, :, O_DQ + idx_d], w_in[:, :, O_DK + idx_d], w_in[:, :, O_KR + kr_sw]], axis=2)
    uq = inp["mla_w_uq"]
    ia, ib = [], []
    for h in range(8):
        b = 96 * h
        ia += list(range(b, b + 32)) + list(range(b + 64, b + 96)) + list(range(b + 32, b + 64))
        ib += list(range(b, b + 32)) + list(range(b + 80, b + 96)) + list(range(b + 64, b + 80)) + list(range(b + 32, b + 64))
    uqa = uq[:, :, np.array(ia)]
    uqb = uq[:, :, np.array(ib)]
    ukv = inp["mla_w_ukv"]
    ikn, iv = [], []
    for h in range(8):
        b = 128 * h
        ikn += list(range(b, b + 32)) + list(range(b, b + 32)) + list(range(b + 32, b + 64))
        iv += list(range(b + 64, b + 128))
    ukn = ukv[:, :, np.array(ikn)]
    ukvv = ukv[:, :, np.array(iv)]
    bg = inp["b_gate"].reshape(2, 3, 8, 128).transpose(0, 1, 3, 2)
    return {
        "w_x": np.ascontiguousarray(w_x),
        "uqa": np.ascontiguousarray(uqa), "uqb": np.ascontiguousarray(uqb),
        "ukn": np.ascontiguousarray(ukn), "ukvv": np.ascontiguousarray(ukvv),
        "bg": np.ascontiguousarray(bg),
    }


def build_nc(nlayers=2, final_norm=True, branches=(0, 1, 2)):
    nc = bass.Bass("TRN2", target_bir_lowering=False)

    def din(name, shape, dt=F32):
        return nc.dram_tensor(name, list(shape), dt, kind="ExternalInput").ap()

    h0 = din("h0", [L, D])
    norm_g = din("norm_g", [2, D])
    w_in = din("w_in", [2, D, 8352])
    w_x = din("w_x", [2, D, 1056])
    bg = din("bg", [2, 3, 128, 8])
    cq_g = din("mla_cq_g", [2, 384])
    ckv_g = din("mla_ckv_g", [2, 256])
    uqa = din("uqa", [2, 384, 768])
    uqb = din("uqb", [2, 384, 768])
    ukn = din("ukn", [2, 256, 768])
    ukvv = din("ukvv", [2, 256, 512])
    dlam = din("diff_lambda", [2, 256])
    dng = din("diff_norm_g", [2, 128])
    w_o = [din("w_o_sb", [2, 512, D]), din("w_o_mla", [2, 512, D]), din("w_o_diff", [2, 512, D])]
    w_out = din("w_out", [2, D, D])
    final_g = din("final_g", [1, D])
    c_ropec = din("c_ropec", [128, L])
    c_ropes = din("c_ropes", [128, L])
    c_tri = din("c_tri", [128, 128])
    c_m01 = din("c_m01", [128, 256], BF16)
    c_ident = din("c_ident", [128, 128], BF16)
    c_mk = din("c_mk", [8, 128], BF16)
    y = nc.dram_tensor("y", [S, D], F32, kind="ExternalOutput").ap()

    with ExitStack() as top:
        P = Prog(nc, top)
        X = P.sb(top, "X", [128, NT, D], F32)
        hT = P.sb(top, "hT", [128, KC, L], BF16)
        og = P.sb(top, "og", [128, NT, 512], BF16)
        ropec = P.sb(top, "ropec", [128, L], F32)
        ropes = P.sb(top, "ropes", [128, L], F32)
        tri = P.sb(top, "tri", [128, 128], F32)
        m01 = P.sb(top, "m01", [128, 256], BF16)
        ident = P.sb(top, "ident", [128, 128], BF16)
        mkt = [P.sb(top, "mk%d" % j, [2, 128], BF16) for j in range(4)]
        z2 = [P.ps(top, "z2_%d" % i, [128, 1024], F32) for i in range(2)]
        fb = [Buf("fb0", z2[0][:, 0:512]), Buf("fb1", z2[0][:, 512:1024]), Buf("fb2", z2[1][:, 0:512]), Buf("fb3", z2[1][:, 512:1024])]
        fb += [P.ps(top, "fb%d" % i, [128, 512], F32) for i in range(4, 8)]
        tb = [Buf("tb%d" % i, fb[6 + i][:].bitcast(BF16)) for i in range(2)]
        for i in range(2):
            tb[i].w, tb[i].r = fb[6 + i].w, fb[6 + i].r

        def MM(out, lhsT, rhs, start, stop, reads, writes, inc=True):
            P.op("tensor", lambda e: e.matmul(out, lhsT=lhsT, rhs=rhs, start=start, stop=stop), reads, writes, inc)

        def TR(out, in_, reads, writes, inc=True):
            P.op("tensor", lambda e: e.transpose(out=out, in_=in_, identity=ident[:]), list(reads) + [ident], writes, inc)

        def ACT(out, in_, func, reads, writes, bias=None, scale=None, accum_out=None):
            kw = {}
            if bias is not None:
                kw["bias"] = bias
            if scale is not None:
                kw["scale"] = scale
            if accum_out is not None:
                kw["accum_out"] = accum_out
            P.op("scalar", lambda e: e.activation(out=out, in_=in_, func=func, **kw), reads, writes)

        def TT(eng, out, in0, in1, op, reads, writes):
            P.op(eng, lambda e: e.tensor_tensor(out=out, in0=in0, in1=in1, op=op), reads, writes)

        def TS(eng, out, in0, s1, s2, op0, op1, reads, writes):
            if op1 is None:
                P.op(eng, lambda e: e.tensor_scalar(out=out, in0=in0, scalar1=s1, scalar2=None, op0=op0), reads, writes)
            else:
                P.op(eng, lambda e: e.tensor_scalar(out=out, in0=in0, scalar1=s1, scalar2=s2, op0=op0, op1=op1), reads, writes)

        def STT(eng, out, in0, scalar, in1, op0, op1, reads, writes):
            P.op(eng, lambda e: e.scalar_tensor_tensor(out=out, in0=in0, scalar=scalar, in1=in1, op0=op0, op1=op1), reads, writes)

        def RSTD(out, ss_ap, n, reads, writes, mult=1.0):
            ACT(out, ss_ap, AF.Ln, reads, writes, bias=EPS, scale=1.0 / n)
            ACT(out, out, AF.Exp, writes, writes, bias=(math.log(mult) if mult != 1.0 else None), scale=-0.5)

        def CP(eng, out, in_, reads, writes):
            if eng == "scalar":
                P.op(eng, lambda e: e.copy(out=out, in_=in_), reads, writes)
            else:
                P.op(eng, lambda e: e.tensor_copy(out=out, in_=in_), reads, writes)

        def MEMSET(eng, ap, val, writes):
            P.op(eng, lambda e: e.memset(ap, val), (), writes)

        def DMA(eng, out, in_, reads=(), writes=()):
            P.dma(eng, lambda e: e.dma_start(out=out, in_=in_), reads, writes)

        def load_w(buf, dram2d, k_chunks, c0, c1, dst_c0=0):
            v = dram2d.rearrange("(k p) c -> p k c", p=128)
            DMA("gpsimd", buf[:, 0:k_chunks, dst_c0:dst_c0 + (c1 - c0)], v[:, :, c0:c1], writes=[buf])

        def bcast_load(buf, row_ap, n):
            DMA("sync", buf[:], row_ap.to_broadcast([128, n]), writes=[buf])

        fctr = [0]

        def next_f(lo=0, hi=4):
            b = fb[lo + fctr[0] % (hi - lo)]
            fctr[0] += 1
            return b

        DMA("scalar", ropec[:], c_ropec[:, :], writes=[ropec])
        DMA("scalar", ropes[:], c_ropes[:, :], writes=[ropes])
        DMA("sync", tri[:], c_tri[:, :], writes=[tri])
        DMA("sync", m01[:], c_m01[:, :], writes=[m01])
        DMA("sync", ident[:], c_ident[:, :], writes=[ident])
        for j in range(4):
            DMA("sync", mkt[j][:], c_mk[2 * j:2 * j + 2, :], writes=[mkt[j]])
        h0v = h0.rearrange("(t p) d -> p t d", p=128)
        DMA("sync", X[:, 0:6, :], h0v[:, 0:6, :], writes=[X])
        DMA("scalar", X[:, 6:12, :], h0v[:, 6:12, :], writes=[X])
        DMA("sync", X[:, 12:NT, :], h0v[:, 12:NT, :], writes=[X])
        P.emit()

        def phase_norm(l):
            with ExitStack() as st:
                grep = P.sb(st, "grep", [128, D], F32)
                junk = [P.sb(st, "junk%d" % j, [128, D], BF16) for j in range(2)]
                ss = P.sb(st, "ss", [128, NT], F32)
                rs = P.sb(st, "rs", [128, NT], F32)
                ssb = [Buf("ss_%d" % i, ss.t) for i in range(NT)]
                rsb = [Buf("rs_%d" % i, rs.t) for i in range(NT)]
                hn = [P.sb(st, "hn%d" % j, [128, D], BF16) for j in range(3)]
                bcast_load(grep, norm_g[l:l + 1, :], D)

                def sa(i):
                    ACT(junk[i % 2][:], X[:, i, :], AF.Square, [X], [junk[i % 2], ssb[i]], accum_out=ss[:, i:i + 1])
                    RSTD(rs[:, i:i + 1], ss[:, i:i + 1], D, [ssb[i]], [rsb[i]])
                    h = hn[i % 3]
                    STT("vector", h[:], X[:, i, :], rs[:, i:i + 1], grep[:], ALU.mult, ALU.mult, [X, rsb[i], grep], [h])

                def sb_(i):
                    h, t = hn[i % 3], tb[i % 2]
                    for k in range(KC):
                        TR(t[:, k * 128:(k + 1) * 128], h[:, k * 128:(k + 1) * 128], [h], [t], inc=(k == KC - 1))

                def sc(i):
                    t = tb[i % 2]
                    CP("vector", hT[:, :, i * 128:(i + 1) * 128], t[:].rearrange("p (k c) -> p k c", k=KC), [t], [hT])

                for step in range(NT + 2):
                    if step < NT:
                        sa(step)
                    if 0 <= step - 1 < NT:
                        sb_(step - 1)
                    if 0 <= step - 2 < NT:
                        sc(step - 2)
                P.emit()

        def proj_tok(i, W, c0, n, kchunks=KC, src=None, src_k0=0):
            src = src or hT
            p = next_f()
            for k in range(kchunks):
                MM(p[:, 0:n], src[:, src_k0 + k, i * 128:(i + 1) * 128], W[:, k, c0:c0 + n], k == 0, k == kchunks - 1,
                   [src, W], [p], inc=(k == kchunks - 1))
            return p

        def proj_feat(p, c0, n, W, wc0, M, kchunks=KC, src=None, src_k0=0):
            src = src or hT
            for k in range(kchunks):
                MM(p[0:M, 0:n], W[:, k, wc0:wc0 + M], src[:, src_k0 + k, c0:c0 + n], k == 0, k == kchunks - 1,
                   [src, W], [p], inc=(k == kchunks - 1))

        def epilogue(l, b, zoff):
            with ExitStack() as st:
                Wz = P.sb(st, "Wz", [128, KC, 512], BF16)
                Wo = P.sb(st, "Wo", [128, 4, D], BF16)
                Wg = P.sb(st, "Wg", [128, KC, D], BF16)
                Wout = P.sb(st, "Wout", [128, KC, D], BF16)
                bgt = P.sb(st, "bgt", [128, 8], F32)
                G = [P.sb(st, "G%d" % j, [128, 512], BF16) for j in range(2)]
                ogg = [P.sb(st, "ogg%d" % j, [128, 512], BF16) for j in range(3)]
                oggT = P.sb(st, "oggT", [128, 4, 512], BF16)
                sg = [P.sb(st, "sg%d" % j, [128, 512], F32) for j in range(2)]
                mT = P.sb(st, "mT", [128, KC, 512], BF16)
                load_w(Wz, w_in[l], KC, zoff, zoff + 512)
                load_w(Wo, w_o[b][l], 4, 0, D)
                load_w(Wg, w_in[l], KC, O_G + b * D, O_G + (b + 1) * D)
                load_w(Wout, w_out[l], KC, 0, D)
                DMA("sync", bgt[:], bg[l, b, :, :], writes=[bgt])
                cnt = 0
                for (c0, n) in CHUNKS:
                    tiles = list(range(c0 // 128, (c0 + n) // 128))
                    pzs = [proj_tok(i, Wz, 0, 512) for i in tiles]
                    for oc in range(2):
                        proj_feat(fb[4 + oc], c0, n, Wg, oc * 128, 128)
                    for j, i in enumerate(tiles):
                        pz = pzs[j]
                        g_, o_ = G[cnt % 2], ogg[cnt % 3]
                        ACT(g_[:], pz[:, :], AF.Silu, [pz], [g_])
                        TT("vector", o_[:], og[:, i, :], g_[:], ALU.mult, [og, g_], [o_])
                        cnt += 1
                        t = tb[j % 2]
                        for c in range(4):
                            TR(t[:, c * 128:(c + 1) * 128], o_[:, c * 128:(c + 1) * 128], [o_], [t], inc=(c == 3))
                        CP("vector", oggT[:, :, j * 128:(j + 1) * 128], t[:, 0:512].rearrange("p (c q) -> p c q", c=4), [t], [oggT])
                    for oc in range(8):
                        pg = fb[4 + oc % 2]
                        if oc >= 2:
                            proj_feat(pg, c0, n, Wg, oc * 128, 128)
                        py = next_f()
                        for c in range(4):
                            MM(py[:, 0:n], Wo[:, c, oc * 128:(oc + 1) * 128], oggT[:, c, 0:n], c == 0, c == 3, [Wo, oggT], [py], inc=(c == 3))
                        s_ = sg[oc % 2]
                        ACT(s_[:, 0:n], pg[:, 0:n], AF.Sigmoid, [pg, bgt], [s_], bias=bgt[:, oc:oc + 1])
                        TT("vector", mT[:, oc, 0:n], s_[:, 0:n], py[:, 0:n], ALU.mult, [s_, py], [mT])
                    for j, i in enumerate(tiles):
                        for half in range(2):
                            po = next_f()
                            for k in range(KC):
                                MM(po[:, :], mT[:, k, j * 128:(j + 1) * 128], Wout[:, k, half * 512:(half + 1) * 512], k == 0, k == KC - 1,
                                   [mT, Wout], [po], inc=(k == KC - 1))
                            TT("vector", X[:, i, half * 512:(half + 1) * 512], X[:, i, half * 512:(half + 1) * 512], po[:, :], ALU.add, [X, po], [X])
                P.emit()

        def branch_sb(l):
            with ExitStack() as bst:
                V = P.sb(bst, "Vsb", [128, NT, 512], BF16)
                with ExitStack() as st:
                    Wv = P.sb(st, "Wv", [128, KC, 512], BF16)
                    load_w(Wv, w_in[l], KC, O_SBV, O_SBV + 512)
                    for i in range(NT):
                        p = proj_tok(i, Wv, 0, 512)
                        CP("scalar" if i % 2 else "vector", V[:, i, :], p[:, :], [p], [V])
                    P.emit()
                with ExitStack() as st:
                    Wqk2 = [P.sb(st, "Wqk%d" % j, [128, KC, 256], BF16) for j in range(2)]
                    qT = P.sb(st, "qT", [128, L], BF16)
                    kT = P.sb(st, "kT", [128, L], BF16)
                    NE, NSP, NC = 4, 3, 3
                    ez = [P.sb(st, "ez%d" % j, [128, 512], F32) for j in range(NE)]
                    sp = [P.sb(st, "sp%d" % j, [128, 516], F32) for j in range(NSP)]
                    Cb = [P.sb(st, "Cb%d" % j, [128, 512], F32) for j in range(NC)]
                    ctot = [P.sb(st, "ctot%d" % j, [128, 1], F32) for j in range(3)]
                    for j in range(NSP):
                        MEMSET("vector", sp[j][:], 0.0, [sp[j]])
                    wb = [P.sb(st, "wb%d" % j, [128, 512], BF16) for j in range(2)]
                    wT = [P.sb(st, "wT%d" % j, [128, 512], BF16) for j in range(2)]
                    cn = [P.sb(st, "cn%d" % j, [128, 1], F32) for j in range(3)]
                    def load_pair_w(pr_):
                        load_w(Wqk2[pr_ % 2], w_in[l], KC, O_SBQ + pr_ * 128, O_SBQ + (pr_ + 1) * 128, dst_c0=0)
                        load_w(Wqk2[pr_ % 2], w_in[l], KC, O_SBK + pr_ * 128, O_SBK + (pr_ + 1) * 128, dst_c0=128)
                    load_pair_w(0)
                    for pr in range(4):
                        Wqk = Wqk2[pr % 2]
                        if pr + 1 < 4:
                            load_pair_w(pr + 1)
                        for (c0, n) in CHUNKS:
                            p = next_f(4, 6)
                            proj_feat(p, c0, n, Wqk, 0, 128)
                            CP("scalar", qT[:, c0:c0 + n], p[:, 0:n], [p], [qT])
                            p = next_f(4, 6)
                            proj_feat(p, c0, n, Wqk, 128, 128)
                            CP("vector", kT[:, c0:c0 + n], p[:, 0:n], [p], [kT])
                        items = []
                        for hh in range(2):
                            for i in range(NT):
                                nk = (i + 1) * 128
                                chs = [(k0, min(512, nk - k0)) for k0 in range(0, nk, 512)][::-1]
                                for ci, (k0, n) in enumerate(chs):
                                    items.append((hh, i, ci, k0, n, ci == len(chs) - 1))
                        N = len(items)
                        obank = {}
                        ocnt = [0]

                        def st_mm(j):
                            hh, i, ci, k0, n, last = items[j]
                            r0 = 64 * hh
                            z = fb[j % 2]
                            MM(z[:, 0:n], qT[r0:r0 + 64, i * 128:(i + 1) * 128], kT[r0:r0 + 64, k0:k0 + n], True, True, [qT, kT], [z])

                        def st_expz(j):
                            hh, i, ci, k0, n, last = items[j]
                            e_ = ez[j % NE]
                            ACT(e_[:, 0:n], fb[j % 2][:, 0:n], AF.Exp, [fb[j % 2]], [e_], scale=0.125)
                            if ci == 0:
                                TT("gpsimd", e_[:, n - 128:n], e_[:, n - 128:n], tri[:], ALU.mult, [e_, tri], [e_])

                        def st_ln(j):
                            hh, i, ci, k0, n, last = items[j]
                            ACT(sp[j % NSP][:, 1:n + 1], ez[j % NE][:, 0:n], AF.Ln, [ez[j % NE]], [sp[j % NSP], ctot[j % 3]], bias=1.0,
                                accum_out=ctot[j % 3][:])

                        def st_scan(j):
                            hh, i, ci, k0, n, last = items[j]
                            s_, c_ = sp[j % NSP], Cb[j % NC]
                            P.op("vector", lambda e: e.tensor_tensor_scan(out=c_[:, 0:n], data0=s_[:, 0:n], data1=s_[:, 0:n],
                                                                           initial=0.0, op0=ALU.add, op1=ALU.max), [s_], [c_])

                        def st_cn(j):
                            hh, i, ci, k0, n, last = items[j]
                            if ci == 0:
                                TS("vector", cn[j % 3][:], ctot[j % 3][:], -1.0, None, ALU.mult, None, [ctot[j % 3]], [cn[j % 3]])
                            else:
                                TT("vector", cn[j % 3][:], cn[(j - 1) % 3][:], ctot[j % 3][:], ALU.subtract, [cn[(j - 1) % 3], ctot[j % 3]], [cn[j % 3]])

                        def st_expt(j):
                            hh, i, ci, k0, n, last = items[j]
                            ACT(Cb[j % NC][:, 0:n], Cb[j % NC][:, 0:n], AF.Exp, [Cb[j % NC], cn[j % 3]], [Cb[j % NC]], bias=cn[j % 3][:, 0:1])

                        def st_mult(j):
                            hh, i, ci, k0, n, last = items[j]
                            TT("gpsimd", wb[j % 2][:, 0:n], ez[j % NE][:, 0:n], Cb[j % NC][:, 0:n], ALU.mult, [ez[j % NE], Cb[j % NC]], [wb[j % 2]])

                        def st_tr(j):
                            hh, i, ci, k0, n, last = items[j]
                            t = tb[j % 2]
                            nb = n // 128
                            for jb in range(nb):
                                TR(t[:, jb * 128:(jb + 1) * 128], wb[j % 2][:, jb * 128:(jb + 1) * 128], [wb[j % 2]], [t], inc=(jb == nb - 1))

                        def st_evac(j):
                            hh, i, ci, k0, n, last = items[j]
                            CP("scalar" if j % 2 == 0 else "vector", wT[j % 2][:, 0:n], tb[j % 2][:, 0:n], [tb[j % 2]], [wT[j % 2]])

                        def st_pv(j):
                            hh, i, ci, k0, n, last = items[j]
                            h = 2 * pr + hh
                            nb = n // 128
                            if ci == 0:
                                obank[(hh, i)] = fb[2 + ocnt[0] % 2]
                                ocnt[0] += 1
                            O = obank[(hh, i)]
                            for jb in range(nb):
                                kb = k0 // 128 + jb
                                MM(O[:, 0:64], wT[j % 2][:, jb * 128:(jb + 1) * 128], V[:, kb, h * 64:(h + 1) * 64],
                                   ci == 0 and jb == 0, last and jb == nb - 1, [wT[j % 2], V], [O], inc=(jb == nb - 1))
                            if last:
                                CP("vector", og[:, i, h * 64:(h + 1) * 64], O[:, 0:64], [O], [og])

                        sched = [(st_mm, 0), (st_expz, 1), (st_expt, 3), (st_ln, 1), (st_evac, 6), (st_cn, 2), (st_scan, 2),
                                 (st_mult, 4), (st_tr, 5), (st_pv, 7)]
                        for step in range(N + 7):
                            for fn, off in sched:
                                if 0 <= step - off < N:
                                    fn(step - off)
                    P.emit()
            epilogue(l, 0, O_SBZ)

        def softmax_attn(st, units, dv1, scale, finalize, PT=None):
            NS = 5
            zbanks = [fb[0], fb[1], fb[6], fb[7]]
            if PT is None:
                PT = [P.sb(st, "PT%d" % j, [128, 512], BF16) for j in range(NS)]
            items = []
            for i in range(NT):
                kbs = list(range(0, min(i + 2, NT)))
                groups = [kbs[a:a + 4] for a in range(0, len(kbs), 4)]
                for u in range(len(units)):
                    for gi, g in enumerate(groups):
                        items.append((i, u, g, gi == 0, gi == len(groups) - 1))
            N = len(items)
            nu = len(units)

            def s1(j):
                i, u, g, first, last = items[j]
                QTb, KTb, r0, nr, vfn = units[u]
                z = zbanks[j % 4]
                for a, kb in enumerate(g):
                    msk = kb >= i
                    MM(z[:, a * 128:(a + 1) * 128], KTb[r0:r0 + nr, kb * 128:(kb + 1) * 128], QTb[r0:r0 + nr, i * 128:(i + 1) * 128],
                       True, not msk, [QTb, KTb], [z], inc=(a == len(g) - 1 and not msk))
                    if msk:
                        mo = 0 if kb == i else 2
                        MM(z[:, a * 128:(a + 1) * 128], mkt[mo][:, :], mkt[mo + 1][:, :], False, True, [mkt[mo], mkt[mo + 1]], [z],
                           inc=(a == len(g) - 1))

            def s2(j):
                i, u, g, first, last = items[j]
                z = zbanks[j % 4]
                s = j % NS
                n = len(g) * 128
                ACT(PT[s][:, 0:n], z[:, 0:n], AF.Exp, [z], [PT[s]], scale=scale)

            def s3(j):
                i, u, g, first, last = items[j]
                QTb, KTb, r0, nr, vfn = units[u]
                s = j % NS
                O = fb[2 + u] if nu > 1 else fb[2 + i % 2]
                for a, kb in enumerate(g):
                    MM(O[:, 0:dv1], PT[s][:, a * 128:(a + 1) * 128], vfn(kb), first and a == 0, last and a == len(g) - 1,
                       [PT[s]], [O], inc=(a == len(g) - 1))
                if last and u == nu - 1:
                    finalize(i, [fb[2 + uu] for uu in range(nu)] if nu > 1 else [O])

            for step in range(N + 2):
                if step < N:
                    s1(step)
                if 0 <= step - 1 < N:
                    s2(step - 1)
                if 0 <= step - 2 < N:
                    s3(step - 2)

        def diff_attn(st, QTb, KTb, vfn, finalize, PT=None):
            NS = 3
            if PT is None:
                PT = [P.sb(st, "PTd%d" % j, [128, 1024], BF16) for j in range(NS)]
            items = []
            for i in range(NT):
                kbs = list(range(0, min(i + 2, NT)))
                groups = [kbs[a:a + 4] for a in range(0, len(kbs), 4)]
                for gi, g in enumerate(groups):
                    items.append((i, g, gi == 0, gi == len(groups) - 1))
            N = len(items)

            def s1(j):
                i, g, first, last = items[j]
                zt = z2[j % 2]
                for a, kb in enumerate(g):
                    msk = kb >= i
                    for u in range(2):
                        r0 = 64 * u
                        MM(zt[:, u * 512 + a * 128:u * 512 + (a + 1) * 128], KTb[r0:r0 + 64, kb * 128:(kb + 1) * 128],
                           QTb[r0:r0 + 64, i * 128:(i + 1) * 128], True, not msk, [QTb, KTb], [zt],
                           inc=(a == len(g) - 1 and u == 1 and not msk))
                    if msk:
                        mo = 0 if kb == i else 2
                        for u in range(2):
                            MM(zt[:, u * 512 + a * 128:u * 512 + (a + 1) * 128], mkt[mo][:, :], mkt[mo + 1][:, :], False, True,
                               [mkt[mo], mkt[mo + 1]], [zt], inc=(a == len(g) - 1 and u == 1))

            def s2(j):
                i, g, first, last = items[j]
                zt = z2[j % 2]
                p_ = PT[j % NS]
                n = len(g) * 128
                ACT(p_[:].rearrange("p (u c) -> p u c", u=2)[:, :, 0:n], zt[:].rearrange("p (u c) -> p u c", u=2)[:, :, 0:n],
                    AF.Exp, [zt], [p_], scale=0.125)

            def s3(j):
                i, g, first, last = items[j]
                p_ = PT[j % NS]
                Os = [fb[4 + 2 * (i % 2)], fb[5 + 2 * (i % 2)]]
                for a, kb in enumerate(g):
                    for u in range(2):
                        MM(Os[u][:, 0:129], p_[:, u * 512 + a * 128:u * 512 + (a + 1) * 128], vfn(kb),
                           first and a == 0, last and a == len(g) - 1, [p_], [Os[u]], inc=(a == len(g) - 1))
                if last:
                    finalize(i, Os)

            for step in range(N + 2):
                if step < N:
                    s1(step)
                if 0 <= step - 1 < N:
                    s2(step - 1)
                if 0 <= step - 2 < N:
                    s3(step - 2)

        def branch_mla(l):
            with ExitStack() as bst:
                cnT = P.sb(bst, "cnT", [128, 5, L], BF16)
                Va = P.sb(bst, "Va", [128, NT, 8, 68], BF16)
                KT = P.sb(bst, "KTm", [128, L], BF16)
                with ExitStack() as st:
                    W = P.sb(st, "Wm", [128, KC, 704], BF16)
                    Wv = P.sb(st, "Wukvv", [128, 2, 512], BF16)
                    gq = P.sb(st, "gq", [128, 640], F32)
                    junk = P.sb(st, "junkm", [128, 384], BF16)
                    ss = P.sb(st, "ssm", [128, 2 * NT], F32)
                    rs = P.sb(st, "rsm", [128, 2 * NT], F32)
                    cb = [P.sb(st, "cb%d" % j, [128, 640], BF16) for j in range(2)]
                    t1 = P.sb(st, "t1m", [128, 512], F32)
                    t2 = P.sb(st, "t2m", [128, 512], F32)
                    load_w(W, w_in[l], KC, O_CQ, O_CQ + 672)
                    load_w(W, w_x[l], KC, 1024, 1056, dst_c0=672)
                    load_w(Wv, ukvv[l], 2, 0, 512)
                    DMA("sync", gq[:, 0:384], cq_g[l:l + 1, :].to_broadcast([128, 384]), writes=[gq])
                    DMA("sync", gq[:, 384:640], ckv_g[l:l + 1, :].to_broadcast([128, 256]), writes=[gq])
                    MEMSET("gpsimd", Va[:].rearrange("p a b c -> p (a b c)"), 1.0, [Va])
                    ssb = [Buf("ssm_%d" % i, ss.t) for i in range(2 * NT)]
                    rsb = [Buf("rsm_%d" % i, rs.t) for i in range(2 * NT)]
                    cb3 = cb + [P.sb(st, "cb2", [128, 640], BF16)]
                    junk2 = [junk, P.sb(st, "junkm2", [128, 384], BF16)]

                    def sa(i):
                        c = cb3[i % 3]
                        for part, (wc0, n, dc0) in enumerate(((0, 384, 0), (384, 256, 384))):
                            p = proj_tok(i, W, wc0, n)
                            col = 2 * i + part
                            ACT(junk2[part][:, 0:n], p[:, 0:n], AF.Square, [p], [junk2[part], ssb[col]], accum_out=ss[:, col:col + 1])
                            RSTD(rs[:, col:col + 1], ss[:, col:col + 1], n, [ssb[col]], [rsb[col]])
                            STT("vector", c[:, dc0:dc0 + n], p[:, 0:n], rs[:, col:col + 1], gq[:, dc0:dc0 + n], ALU.mult, ALU.mult, [p, rsb[col], gq], [c])

                    def sb_(i):
                        c, t = cb3[i % 3], tb[i % 2]
                        for k in range(5):
                            TR(t[:, k * 128:(k + 1) * 128], c[:, k * 128:(k + 1) * 128], [c], [t], inc=(k == 4))

                    def sc(i):
                        t = tb[i % 2]
                        CP("vector" if i % 2 else "scalar", cnT[:, :, i * 128:(i + 1) * 128], t[:, 0:640].rearrange("p (k c) -> p k c", k=5), [t], [cnT])

                    for step in range(NT + 2):
                        if step < NT:
                            sa(step)
                        if 0 <= step - 1 < NT:
                            sb_(step - 1)
                        if 0 <= step - 2 < NT:
                            sc(step - 2)
                    for (c0, n) in CHUNKS:
                        pa = fb[4]
                        pb = fb[5]
                        proj_feat(pa, c0, n, W, 608, 64)
                        proj_feat(pb, c0, n, W, 640, 64)
                        TT("vector", t1[32:64, 0:n], pa[32:64, 0:n], ropec[32:64, c0:c0 + n], ALU.mult, [pa, ropec], [t1])
                        TT("vector", t2[32:64, 0:n], pb[32:64, 0:n], ropes[32:64, c0:c0 + n], ALU.mult, [pb, ropes], [t2])
                        TT("vector", KT[32:64, c0:c0 + n], t1[32:64, 0:n], t2[32:64, 0:n], ALU.add, [t1, t2], [KT])
                    for i in range(NT):
                        p = proj_tok(i, Wv, 0, 512, kchunks=2, src=cnT, src_k0=3)
                        CP("scalar" if i % 2 else "vector", Va[:, i, :, 0:64], p[:, :].rearrange("p (h d) -> p h d", h=8), [p], [Va])
                    P.emit()
                with ExitStack() as st:
                    QT = P.sb(st, "QTm", [128, L], BF16)
                    Wa = P.sb(st, "Wuqa", [128, 3, 768], BF16)
                    Wb = P.sb(st, "Wuqb", [128, 3, 768], BF16)
                    Wk = P.sb(st, "Wukn", [128, 2, 768], BF16)
                    t1 = P.sb(st, "t1q", [128, 512], F32)
                    t2 = P.sb(st, "t2q", [128, 512], F32)
                    rcp = P.sb(st, "rcp", [128, 1], F32)
                    PTm = [P.sb(st, "PTm%d" % j, [128, 512], BF16) for j in range(5)]
                    load_w(Wa, uqa[l], 3, 0, 768)
                    load_w(Wb, uqb[l], 3, 0, 768)
                    load_w(Wk, ukn[l], 2, 0, 768)
                    for h in range(8):
                        for cidx, (c0, n) in enumerate(CHUNKS):
                            pa, pb = fb[4 + 2 * (cidx % 2)], fb[5 + 2 * (cidx % 2)]
                            proj_feat(pa, c0, n, Wa, h * 96, 96, kchunks=3, src=cnT)
                            proj_feat(pb, c0, n, Wb, h * 96, 96, kchunks=3, src=cnT)
                            CP("scalar", QT[0:32, c0:c0 + n], pa[0:32, 0:n], [pa], [QT])
                            CP("scalar", QT[64:96, c0:c0 + n], pa[64:96, 0:n], [pa], [QT])
                            TT("vector", t1[32:64, 0:n], pa[32:64, 0:n], ropec[32:64, c0:c0 + n], ALU.mult, [pa, ropec], [t1])
                            TT("vector", t2[32:64, 0:n], pb[32:64, 0:n], ropes[32:64, c0:c0 + n], ALU.mult, [pb, ropes], [t2])
                            TT("vector", QT[32:64, c0:c0 + n], t1[32:64, 0:n], t2[32:64, 0:n], ALU.add, [t1, t2], [QT])
                            pk = pb
                            proj_feat(pk, c0, n, Wk, h * 96, 96, kchunks=2, src=cnT, src_k0=3)
                            CP("scalar", KT[0:32, c0:c0 + n], pk[0:32, 0:n], [pk], [KT])
                            CP("vector", KT[64:96, c0:c0 + n], pk[64:96, 0:n], [pk], [KT])

                        def fin(i, Os, h=h):
                            O = Os[0]
                            P.op("vector", lambda e: e.reciprocal(out=rcp[:], in_=O[:, 64:65]), [O], [rcp])
                            TS("vector", og[:, i, h * 64:(h + 1) * 64], O[:, 0:64], rcp[:, 0:1], None, ALU.mult, None, [O, rcp], [og])

                        softmax_attn(st, [(QT, KT, 0, 96, (lambda kb, h=h: Va[:, kb, h, 0:65]))], 65, 1.0 / math.sqrt(96.0), fin, PT=PTm)
                    P.emit()
            epilogue(l, 1, O_MZ)

        def branch_diff(l):
            lam_init = 0.8 - 0.6 * math.exp(-0.3 * l)
            with ExitStack() as bst:
                Vd = P.sb(bst, "Vd", [128, NT, 4, 132], BF16)
                lam = P.sb(bst, "lam", [128, 1], F32)
                gd = P.sb(bst, "gd", [128, 128], F32)
                with ExitStack() as st:
                    Wv = P.sb(st, "Wdv", [128, KC, 512], BF16)
                    dl = P.sb(st, "dl", [128, 256], F32)
                    pr_ = P.sb(st, "prd", [128, 128], F32)
                    sm = P.sb(st, "smd", [128, 2], F32)
                    load_w(Wv, w_in[l], KC, O_DV, O_DV + 512)
                    DMA("sync", dl[:], dlam[l:l + 1, :].to_broadcast([128, 256]), writes=[dl])
                    DMA("sync", gd[:], dng[l:l + 1, :].to_broadcast([128, 128]), writes=[gd])
                    dl3 = dl[:].rearrange("p (a b) -> p a b", a=2)
                    TT("vector", pr_[:].rearrange("p (a b) -> p a b", a=2), dl3[:, :, 0:64], dl3[:, :, 64:128], ALU.mult, [dl], [pr_])
                    P.op("vector", lambda e: e.reduce_sum(out=sm[:], in_=pr_[:].rearrange("p (a b) -> p a b", a=2), axis=mybir.AxisListType.X), [pr_], [sm])
                    ACT(sm[:], sm[:], AF.Exp, [sm], [sm])
                    TT("vector", lam[:], sm[:, 0:1], sm[:, 1:2], ALU.subtract, [sm], [lam])
                    TS("vector", lam[:], lam[:], lam_init, None, ALU.add, None, [lam], [lam])
                    MEMSET("gpsimd", Vd[:].rearrange("p a b c -> p (a b c)"), 1.0, [Vd])
                    for i in range(NT):
                        p = proj_tok(i, Wv, 0, 512)
                        CP("scalar" if i % 2 else "vector", Vd[:, i, :, 0:128], p[:, :].rearrange("p (h d) -> p h d", h=4), [p], [Vd])
                    P.emit()
                with ExitStack() as st:
                    PTd = [P.sb(st, "PTd%d" % j, [128, 1024], BF16) for j in range(3)]
                    Wq2 = [P.sb(st, "Wdq%d" % j, [128, KC, 512], BF16) for j in range(2)]

                    def load_head_w(h):
                        Wq_ = Wq2[h % 2]
                        load_w(Wq_, w_in[l], KC, O_DQ + h * 128, O_DQ + (h + 1) * 128, dst_c0=0)
                        load_w(Wq_, w_x[l], KC, h * 128, (h + 1) * 128, dst_c0=128)
                        load_w(Wq_, w_in[l], KC, O_DK + h * 128, O_DK + (h + 1) * 128, dst_c0=256)
                        load_w(Wq_, w_x[l], KC, 512 + h * 128, 512 + (h + 1) * 128, dst_c0=384)
                    load_head_w(0)
                    QT = P.sb(st, "QTd", [128, L], BF16)
                    KT = P.sb(st, "KTd", [128, L], BF16)
                    t1 = P.sb(st, "t1d", [128, 512], F32)
                    t2 = P.sb(st, "t2d", [128, 512], F32)
                    rc = P.sb(st, "rcd", [128, 2], F32)
                    tm = P.sb(st, "tmd", [128, 128], F32)
                    oc_ = P.sb(st, "ocd", [128, 128], F32)
                    jk = P.sb(st, "jkd", [128, 128], BF16)
                    ssd = P.sb(st, "ssd", [128, 1], F32)
                    rsd = P.sb(st, "rsd", [128, 1], F32)
                    for h in range(4):
                        cc = 0
                        Wq = Wq2[h % 2]
                        if h + 1 < 4:
                            load_head_w(h + 1)
                        for (dst, wc) in ((QT, 0), (KT, 256)):
                            for (c0, n) in CHUNKS:
                                pa, pb = fb[4 + 2 * (cc % 2)], fb[5 + 2 * (cc % 2)]
                                cc += 1
                                proj_feat(pa, c0, n, Wq, wc, 128)
                                proj_feat(pb, c0, n, Wq, wc + 128, 128)
                                TT("vector", t1[:, 0:n], pa[:, 0:n], ropec[:, c0:c0 + n], ALU.mult, [pa, ropec], [t1])
                                TT("vector", t2[:, 0:n], pb[:, 0:n], ropes[:, c0:c0 + n], ALU.mult, [pb, ropes], [t2])
                                TT("vector", dst[:, c0:c0 + n], t1[:, 0:n], t2[:, 0:n], ALU.add, [t1, t2], [dst])
                                CP("scalar", dst[32:64, c0:c0 + n], pa[32:64, 0:n], [pa], [dst])

                        def fin(i, Os, h=h):
                            O0, O1 = Os
                            P.op("vector", lambda e: e.reciprocal(out=rc[:, 0:1], in_=O0[:, 128:129]), [O0], [rc])
                            P.op("vector", lambda e: e.reciprocal(out=rc[:, 1:2], in_=O1[:, 128:129]), [O1], [rc])
                            TT("vector", rc[:, 1:2], rc[:, 1:2], lam[:, 0:1], ALU.mult, [rc, lam], [rc])
                            TS("vector", tm[:], O1[:, 0:128], rc[:, 1:2], None, ALU.mult, None, [O1, rc], [tm])
                            STT("vector", oc_[:], O0[:, 0:128], rc[:, 0:1], tm[:], ALU.mult, ALU.subtract, [O0, rc, tm], [oc_])
                            MEMSET("vector", ssd[:], 0.0, [ssd])
                            ACT(jk[:], oc_[:], AF.Square, [oc_], [jk, ssd], accum_out=ssd[:, 0:1])
                            RSTD(rsd[:], ssd[:], 128, [ssd], [rsd], mult=1.0 - lam_init)
                            STT("vector", og[:, i, h * 128:(h + 1) * 128], oc_[:], rsd[:, 0:1], gd[:], ALU.mult, ALU.mult, [oc_, rsd, gd], [og])

                        diff_attn(st, QT, KT, (lambda kb, h=h: Vd[:, kb, h, 0:129]), fin, PT=PTd)
                    P.emit()
            epilogue(l, 2, O_DZ)

        for l in range(nlayers):
            phase_norm(l)
            if 0 in branches:
                branch_sb(l)
            if 1 in branches:
                branch_mla(l)
            if 2 in branches:
                branch_diff(l)

        with ExitStack() as st:
            grep = P.sb(st, "grepf", [128, D], F32)
            junk = P.sb(st, "junkf", [128, D], BF16)
            ss = P.sb(st, "ssf", [128, NT], F32)
            rs = P.sb(st, "rsf", [128, NT], F32)
            yo = [P.sb(st, "yo%d" % j, [128, D], F32) for j in range(2)]
            bcast_load(grep, final_g[0:1, :], D)
            MEMSET("vector", ss[:], 0.0, [ss])
            for i in range(NT):
                o = yo[i % 2]
                if final_norm:
                    ACT(junk[:], X[:, i, :], AF.Square, [X], [junk, ss], accum_out=ss[:, i:i + 1])
                    RSTD(rs[:, i:i + 1], ss[:, i:i + 1], D, [ss], [rs])
                    STT("vector", o[:], X[:, i, :], rs[:, i:i + 1], grep[:], ALU.mult, ALU.mult, [X, rs, grep], [o])
                else:
                    CP("vector", o[:], X[:, i, :], [X], [o])
                p_lo = NMETA if i == 0 else 0
                p_hi = NMETA if i == NT - 1 else 128
                s0 = 128 * i - NMETA + p_lo
                DMA("sync", y[s0:s0 + (p_hi - p_lo), :], o[p_lo:p_hi, :], reads=[o])
            P.wait_all("sync", yo)
            P.emit()
    return nc


_CACHE = {}


def _consts():
    if "c" not in _CACHE:
        C, Sg = _rope_tables()
        tri, m01, ident, mk = _masks()
        _CACHE["c"] = {"c_ropec": C, "c_ropes": Sg, "c_tri": tri, "c_m01": m01, "c_ident": ident, "c_mk": mk}
    return _CACHE["c"]


def make_in_maps(inp):
    x = np.asarray(inp["x"], np.float32)
    B = x.shape[0]
    meta = np.asarray(inp["meta_tokens"], np.float32)
    lay = _host_layouts({k: np.asarray(v) for k, v in inp.items()})
    shared = dict(_consts())
    shared.update(lay)
    for k in ("norm_g", "w_in", "mla_cq_g", "mla_ckv_g", "diff_norm_g", "w_o_sb", "w_o_mla", "w_o_diff", "w_out"):
        shared[k] = np.ascontiguousarray(np.asarray(inp[k], np.float32))
    shared["diff_lambda"] = np.ascontiguousarray(np.asarray(inp["diff_lambda"], np.float32).reshape(2, 256))
    shared["final_g"] = np.ascontiguousarray(np.asarray(inp["final_g"], np.float32).reshape(1, D))
    maps = []
    for b in range(B):
        h0 = np.concatenate([meta, x[b], np.zeros((L - NMETA - S, D), np.float32)], axis=0)
        m = dict(shared)
        m["h0"] = np.ascontiguousarray(h0)
        maps.append(m)
    return maps


def kernel(**inputs):
    maps = make_in_maps(inputs)
    if "nc" not in _CACHE:
        _CACHE["nc"] = build_nc()
    res = run_bass_kernel_spmd(_CACHE["nc"], maps, core_ids=list(range(len(maps))))
    return np.stack([np.asarray(r["y"], np.float32) for r in res.results], axis=0)
```

```python
import math
import numpy as np
import ml_dtypes
from contextlib import ExitStack
import concourse.bass as bass
import concourse.mybir as mybir
from concourse.bass_utils import run_bass_kernel_spmd

F32 = mybir.dt.float32
BF16 = mybir.dt.bfloat16
AF = mybir.ActivationFunctionType
ALU = mybir.AluOpType

D = 1024
S = 2048
NMETA = 16
NT = 17
L = NT * 128
KC = 8
EPS = 1e-6
THETA = 500000.0
CHUNKS = [(0, 512), (512, 512), (1024, 512), (1536, 512), (2048, 128)]


class Buf:
    __slots__ = ("name", "t", "w", "r", "dsem", "dcnt")

    def __init__(self, name, t=None):
        self.name = name
        self.t = t
        self.w = []
        self.r = []
        self.dsem = None
        self.dcnt = 0

    def __getitem__(self, idx):
        return self.t[idx]


class Prog:
    ENGS = ("tensor", "vector", "scalar", "gpsimd", "sync")

    def __init__(self, nc, stack):
        self.nc = nc
        self.stack = stack
        self.sems = {}
        self.cnt = {e: 0 for e in self.ENGS}
        self.seen = {e: {} for e in self.ENGS}
        self.q = {e: [] for e in self.ENGS}
        self.snaps = {}
        for e in self.ENGS:
            self._sem("E_" + e)
        self.nbuf = 0

    def _sem(self, key):
        if key not in self.sems:
            self.sems[key] = self.stack.enter_context(self.nc.semaphore(key))
        return key

    def sb(self, st, name, shape, dt):
        self.uid = getattr(self, "uid", 0) + 1
        name = "%s_%d" % (name, self.uid)
        t = st.enter_context(self.nc.sbuf_tensor(name, list(shape), dt))
        return Buf(name, t)

    def ps(self, st, name, shape, dt=F32):
        t = st.enter_context(self.nc.psum_tensor(name, list(shape), dt))
        return Buf(name, t)

    def _waits(self, eng, reads, writes):
        own = "E_" + eng
        need = {}
        for b in reads:
            for (k, v) in b.w:
                if k == own and (eng == "tensor" or v > self.cnt[eng]):
                    continue
                if need.get(k, 0) < v:
                    need[k] = v
        for b in writes:
            for (k, v) in b.w:
                if k == own and (eng == "tensor" or v > self.cnt[eng]):
                    continue
                if need.get(k, 0) < v:
                    need[k] = v
            for (k, v) in b.r:
                if k == own and (eng == "tensor" or v > self.cnt[eng]):
                    continue
                if need.get(k, 0) < v:
                    need[k] = v
        out = []
        seen = self.seen[eng]
        snaps = self.snaps
        for k, v in sorted(need.items(), key=lambda kv: 0 if kv[0].startswith("E_") else 1):
            if seen.get(k, 0) < v:
                seen[k] = v
                out.append((k, v))
                sn = snaps.get((k, v))
                if sn:
                    for k2, v2 in sn.items():
                        if seen.get(k2, 0) < v2:
                            seen[k2] = v2
        return out

    def op(self, eng, fn, reads=(), writes=(), inc=True):
        waits = self._waits(eng, reads, writes)
        key = "E_" + eng
        val = self.cnt[eng] + 1
        if inc:
            self.cnt[eng] = val
        ev = (key, val)
        if inc:
            self.snaps[ev] = dict(self.seen[eng])
        for b in writes:
            b.w = [ev]
            b.r = []
        for b in reads:
            b.r = [e for e in b.r if e[0] != key] + [ev]
        self.q[eng].append((waits, fn, [(key, 1)] if inc else []))

    def dma(self, eng, fn, reads=(), writes=(), sem_buf=None):
        waits = self._waits(eng, reads, writes)
        sb = sem_buf or (writes[0] if writes else reads[0])
        if sb.dsem is None:
            sb.dsem = self._sem("D_%d" % self.nbuf)
            self.nbuf += 1
        sb.dcnt += 16
        ev = (sb.dsem, sb.dcnt)
        self.snaps[ev] = dict(self.seen[eng])
        for b in writes:
            b.w = [e for e in b.w if e[0] != sb.dsem and e[0].startswith("D_")] + [ev]
            b.r = []
        for b in reads:
            b.r = [e for e in b.r if e[0] != sb.dsem] + [ev]
        self.q[eng].append((waits, fn, [(sb.dsem, 16)]))

    def wait_all(self, eng, bufs):
        waits = self._waits(eng, (), bufs)
        self.q[eng].append((waits, None, []))

    def emit(self):
        nc = self.nc
        qs = self.q
        self.q = {e: [] for e in self.ENGS}
        sems = self.sems
        with nc.Block() as block:
            def mk(ename):
                items = qs[ename]

                def body(e):
                    for waits, fn, incs in items:
                        if fn is None:
                            for (k, v) in waits:
                                e.wait_ge(sems[k], v)
                            continue
                        for (k, v) in waits[1:]:
                            e.wait_ge(sems[k], v)
                        ins = fn(e)
                        if waits:
                            ins._wait_ge(sems[waits[0][0]], waits[0][1])
                        for (k, n) in incs:
                            ins.then_inc(sems[k], n)
                return body
            block.tensor(mk("tensor"))
            block.vector(mk("vector"))
            block.scalar(mk("scalar"))
            block.gpsimd(mk("gpsimd"))
            block.sync(mk("sync"))


def _rope_tables():
    pos = np.arange(L, dtype=np.float32)
    C = np.ones((128, L), np.float32)
    Sg = np.zeros((128, L), np.float32)
    inv_d = (np.float32(THETA) ** (-np.arange(0, 16, 2, dtype=np.float32) / np.float32(16))).astype(np.float32)
    ang_d = (pos[:, None] * inv_d[None, :]).astype(np.float32)
    cd, sd = np.cos(ang_d).astype(np.float32), np.sin(ang_d).astype(np.float32)
    for base in (0, 64):
        for r in range(16):
            C[base + r] = cd[:, r % 8]
            Sg[base + r] = -sd[:, r % 8] if r < 8 else sd[:, r % 8]
    inv_m = (np.float32(THETA) ** (-np.arange(0, 32, 2, dtype=np.float32) / np.float32(32))).astype(np.float32)
    ang_m = (pos[:, None] * inv_m[None, :]).astype(np.float32)
    cm, sm = np.cos(ang_m).astype(np.float32), np.sin(ang_m).astype(np.float32)
    for r in range(32):
        C[32 + r] = cm[:, r % 16]
        Sg[32 + r] = -sm[:, r % 16] if r < 16 else sm[:, r % 16]
    return C, Sg


def _masks():
    a = np.arange(128)
    tri = (a[None, :] < a[:, None]).astype(np.float32)
    cq = (a + 48) // 64
    m0 = (cq[:, None] <= cq[None, :])
    m1 = ((a[:, None] < 16) & (a[None, :] >= 80))
    m01 = np.concatenate([m0, m1], axis=1).astype(np.float32).astype(ml_dtypes.bfloat16)
    ident = np.eye(128, dtype=np.float32).astype(ml_dtypes.bfloat16)
    BIG = 30000.0
    mk = np.zeros((8, 128), np.float32)
    mk[0] = -BIG * (cq == 1); mk[1] = -BIG * (cq == 2)
    mk[2] = (cq < 1); mk[3] = (cq < 2)
    mk[4] = -BIG; mk[5] = BIG * (a < 16)
    mk[6] = 1.0; mk[7] = (a >= 80)
    mk = mk.astype(ml_dtypes.bfloat16)
    return tri, m01, ident, mk


O_SBQ, O_SBK, O_SBV, O_SBZ = 0, 512, 1024, 1536
O_CQ, O_CKV, O_KR, O_MZ = 2048, 2432, 2688, 2720
O_DQ, O_DK, O_DV, O_DZ = 3232, 3744, 4256, 4768
O_G = 5280


def _host_layouts(inp):
    w_in = inp["w_in"]
    swap64 = np.concatenate([np.arange(8, 16), np.arange(0, 8), np.arange(16, 64)])
    idx_d = np.concatenate([m * 64 + swap64 for m in range(8)])
    kr_sw = np.concatenate([np.arange(16, 32), np.arange(0, 16)])
    w_x = np.concatenate([w_in[:, :, O_DQ + idx_d], w_in[:, :, O_DK + idx_d], w_in[:, :, O_KR + kr_sw]], axis=2)
    uq = inp["mla_w_uq"]
    ia, ib = [], []
    for h in range(8):
        b = 96 * h
        ia += list(range(b, b + 32)) + list(range(b + 64, b + 96)) + list(range(b + 32, b + 64))
        ib += list(range(b, b + 32)) + list(range(b + 80, b + 96)) + list(range(b + 64, b + 80)) + list(range(b + 32, b + 64))
    uqa = uq[:, :, np.array(ia)]
    uqb = uq[:, :, np.array(ib)]
    ukv = inp["mla_w_ukv"]
    ikn, iv = [], []
    for h in range(8):
        b = 128 * h
        ikn += list(range(b, b + 32)) + list(range(b, b + 32)) + list(range(b + 32, b + 64))
        iv += list(range(b + 64, b + 128))
    ukn = ukv[:, :, np.array(ikn)]
    ukvv = ukv[:, :, np.array(iv)]
    bg = inp["b_gate"].reshape(2, 3, 8, 128).transpose(0, 1, 3, 2)
    return {
        "w_x": np.ascontiguousarray(w_x),
        "uqa": np.ascontiguousarray(uqa), "uqb": np.ascontiguousarray(uqb),
        "ukn": np.ascontiguousarray(ukn), "ukvv": np.ascontiguousarray(ukvv),
        "bg": np.ascontiguousarray(bg),
    }


def build_nc(nlayers=2, final_norm=True, branches=(0, 1, 2)):
    nc = bass.Bass("TRN2", target_bir_lowering=False)

    def din(name, shape, dt=F32):
        return nc.dram_tensor(name, list(shape), dt, kind="ExternalInput").ap()

    h0 = din("h0", [L, D])
    norm_g = din("norm_g", [2, D])
    w_in = din("w_in", [2, D, 8352])
    w_x = din("w_x", [2, D, 1056])
    bg = din("bg", [2, 3, 128, 8])
    cq_g = din("mla_cq_g", [2, 384])
    ckv_g = din("mla_ckv_g", [2, 256])
    uqa = din("uqa", [2, 384, 768])
    uqb = din("uqb", [2, 384, 768])
    ukn = din("ukn", [2, 256, 768])
    ukvv = din("ukvv", [2, 256, 512])
    dlam = din("diff_lambda", [2, 256])
    dng = din("diff_norm_g", [2, 128])
    w_o = [din("w_o_sb", [2, 512, D]), din("w_o_mla", [2, 512, D]), din("w_o_diff", [2, 512, D])]
    w_out = din("w_out", [2, D, D])
    final_g = din("final_g", [1, D])
    c_ropec = din("c_ropec", [128, L])
    c_ropes = din("c_ropes", [128, L])
    c_tri = din("c_tri", [128, 128])
    c_m01 = din("c_m01", [128, 256], BF16)
    c_ident = din("c_ident", [128, 128], BF16)
    c_mk = din("c_mk", [8, 128], BF16)
    y = nc.dram_tensor("y", [S, D], F32, kind="ExternalOutput").ap()

    with ExitStack() as top:
        P = Prog(nc, top)
        X = P.sb(top, "X", [128, NT, D], F32)
        hT = P.sb(top, "hT", [128, KC, L], BF16)
        og = P.sb(top, "og", [128, NT, 512], BF16)
        ropec = P.sb(top, "ropec", [128, L], F32)
        ropes = P.sb(top, "ropes", [128, L], F32)
        tri = P.sb(top, "tri", [128, 128], F32)
        m01 = P.sb(top, "m01", [128, 256], BF16)
        ident = P.sb(top, "ident", [128, 128], BF16)
        mkt = [P.sb(top, "mk%d" % j, [2, 128], BF16) for j in range(4)]
        z2 = [P.ps(top, "z2_%d" % i, [128, 1024], F32) for i in range(2)]
        fb = [Buf("fb0", z2[0][:, 0:512]), Buf("fb1", z2[0][:, 512:1024]), Buf("fb2", z2[1][:, 0:512]), Buf("fb3", z2[1][:, 512:1024])]
        fb += [P.ps(top, "fb%d" % i, [128, 512], F32) for i in range(4, 8)]
        tb = [Buf("tb%d" % i, fb[6 + i][:].bitcast(BF16)) for i in range(2)]
        for i in range(2):
            tb[i].w, tb[i].r = fb[6 + i].w, fb[6 + i].r

        def MM(out, lhsT, rhs, start, stop, reads, writes, inc=True):
            P.op("tensor", lambda e: e.matmul(out, lhsT=lhsT, rhs=rhs, start=start, stop=stop), reads, writes, inc)

        def TR(out, in_, reads, writes, inc=True):
            P.op("tensor", lambda e: e.transpose(out=out, in_=in_, identity=ident[:]), list(reads) + [ident], writes, inc)

        def ACT(out, in_, func, reads, writes, bias=None, scale=None, accum_out=None):
            kw = {}
            if bias is not None:
                kw["bias"] = bias
            if scale is not None:
                kw["scale"] = scale
            if accum_out is not None:
                kw["accum_out"] = accum_out
            P.op("scalar", lambda e: e.activation(out=out, in_=in_, func=func, **kw), reads, writes)

        def TT(eng, out, in0, in1, op, reads, writes):
            P.op(eng, lambda e: e.tensor_tensor(out=out, in0=in0, in1=in1, op=op), reads, writes)

        def TS(eng, out, in0, s1, s2, op0, op1, reads, writes):
            if op1 is None:
                P.op(eng, lambda e: e.tensor_scalar(out=out, in0=in0, scalar1=s1, scalar2=None, op0=op0), reads, writes)
            else:
                P.op(eng, lambda e: e.tensor_scalar(out=out, in0=in0, scalar1=s1, scalar2=s2, op0=op0, op1=op1), reads, writes)

        def STT(eng, out, in0, scalar, in1, op0, op1, reads, writes):
            P.op(eng, lambda e: e.scalar_tensor_tensor(out=out, in0=in0, scalar=scalar, in1=in1, op0=op0, op1=op1), reads, writes)

        def RSTD(out, ss_ap, n, reads, writes, mult=1.0):
            ACT(out, ss_ap, AF.Ln, reads, writes, bias=EPS, scale=1.0 / n)
            ACT(out, out, AF.Exp, writes, writes, bias=(math.log(mult) if mult != 1.0 else None), scale=-0.5)

        def CP(eng, out, in_, reads, writes):
            if eng == "scalar":
                P.op(eng, lambda e: e.copy(out=out, in_=in_), reads, writes)
            else:
                P.op(eng, lambda e: e.tensor_copy(out=out, in_=in_), reads, writes)

        def MEMSET(eng, ap, val, writes):
            P.op(eng, lambda e: e.memset(ap, val), (), writes)

        def DMA(eng, out, in_, reads=(), writes=()):
            P.dma(eng, lambda e: e.dma_start(out=out, in_=in_), reads, writes)

        def load_w(buf, dram2d, k_chunks, c0, c1, dst_c0=0):
            v = dram2d.rearrange("(k p) c -> p k c", p=128)
            DMA("gpsimd", buf[:, 0:k_chunks, dst_c0:dst_c0 + (c1 - c0)], v[:, :, c0:c1], writes=[buf])

        def bcast_load(buf, row_ap, n):
            DMA("sync", buf[:], row_ap.to_broadcast([128, n]), writes=[buf])

        fctr = [0]

        def next_f(lo=0, hi=4):
            b = fb[lo + fctr[0] % (hi - lo)]
            fctr[0] += 1
            return b

        DMA("scalar", ropec[:], c_ropec[:, :], writes=[ropec])
        DMA("scalar", ropes[:], c_ropes[:, :], writes=[ropes])
        DMA("sync", tri[:], c_tri[:, :], writes=[tri])
        DMA("sync", m01[:], c_m01[:, :], writes=[m01])
        DMA("sync", ident[:], c_ident[:, :], writes=[ident])
        for j in range(4):
            DMA("sync", mkt[j][:], c_mk[2 * j:2 * j + 2, :], writes=[mkt[j]])
        h0v = h0.rearrange("(t p) d -> p t d", p=128)
        Xr = [Buf("Xr%d" % j, X.t) for j in range(3)]
        DMA("sync", X[:, 0:6, :], h0v[:, 0:6, :], writes=[Xr[0]])
        DMA("scalar", X[:, 6:12, :], h0v[:, 6:12, :], writes=[Xr[1]])
        DMA("sync", X[:, 12:NT, :], h0v[:, 12:NT, :], writes=[Xr[2]])

        def phase_norm(l):
            with ExitStack() as st:
                grep = P.sb(st, "grep", [128, D], F32)
                junk = [P.sb(st, "junk%d" % j, [128, D], BF16) for j in range(2)]
                ss = P.sb(st, "ss", [128, NT], F32)
                rs = P.sb(st, "rs", [128, NT], F32)
                ssb = [Buf("ss_%d" % i, ss.t) for i in range(NT)]
                rsb = [Buf("rs_%d" % i, rs.t) for i in range(NT)]
                hn = [P.sb(st, "hn%d" % j, [128, D], BF16) for j in range(3)]
                bcast_load(grep, norm_g[l:l + 1, :], D)

                def sa(i):
                    Xd = Xr[i // 6] if l == 0 else X
                    ACT(junk[i % 2][:], X[:, i, :], AF.Square, [Xd], [junk[i % 2], ssb[i]], accum_out=ss[:, i:i + 1])
                    RSTD(rs[:, i:i + 1], ss[:, i:i + 1], D, [ssb[i]], [rsb[i]])
                    h = hn[i % 3]
                    STT("vector", h[:], X[:, i, :], rs[:, i:i + 1], grep[:], ALU.mult, ALU.mult, [Xd, rsb[i], grep], [h])

                def sb_(i):
                    h, t = hn[i % 3], tb[i % 2]
                    for k in range(KC):
                        TR(t[:, k * 128:(k + 1) * 128], h[:, k * 128:(k + 1) * 128], [h], [t], inc=(k == KC - 1))

                def sc(i):
                    t = tb[i % 2]
                    CP("vector", hT[:, :, i * 128:(i + 1) * 128], t[:].rearrange("p (k c) -> p k c", k=KC), [t], [hT])

                for step in range(NT + 2):
                    if step < NT:
                        sa(step)
                    if 0 <= step - 1 < NT:
                        sb_(step - 1)
                    if 0 <= step - 2 < NT:
                        sc(step - 2)
                P.emit()

        def proj_tok(i, W, c0, n, kchunks=KC, src=None, src_k0=0):
            src = src or hT
            p = next_f()
            for k in range(kchunks):
                MM(p[:, 0:n], src[:, src_k0 + k, i * 128:(i + 1) * 128], W[:, k, c0:c0 + n], k == 0, k == kchunks - 1,
                   [src, W], [p], inc=(k == kchunks - 1))
            return p

        def proj_feat(p, c0, n, W, wc0, M, kchunks=KC, src=None, src_k0=0):
            src = src or hT
            for k in range(kchunks):
                MM(p[0:M, 0:n], W[:, k, wc0:wc0 + M], src[:, src_k0 + k, c0:c0 + n], k == 0, k == kchunks - 1,
                   [src, W], [p], inc=(k == kchunks - 1))

        def epilogue(l, b, zoff):
            with ExitStack() as st:
                Wz = P.sb(st, "Wz", [128, KC, 512], BF16)
                Wo = P.sb(st, "Wo", [128, 4, D], BF16)
                Wg = P.sb(st, "Wg", [128, KC, D], BF16)
                Wout = P.sb(st, "Wout", [128, KC, D], BF16)
                bgt = P.sb(st, "bgt", [128, 8], F32)
                G = [P.sb(st, "G%d" % j, [128, 512], BF16) for j in range(2)]
                ogg = [P.sb(st, "ogg%d" % j, [128, 512], BF16) for j in range(3)]
                oggT = P.sb(st, "oggT", [128, 4, 512], BF16)
                sg = [P.sb(st, "sg%d" % j, [128, 512], F32) for j in range(2)]
                mT = P.sb(st, "mT", [128, KC, 512], BF16)
                load_w(Wz, w_in[l], KC, zoff, zoff + 512)
                load_w(Wo, w_o[b][l], 4, 0, D)
                load_w(Wg, w_in[l], KC, O_G + b * D, O_G + (b + 1) * D)
                load_w(Wout, w_out[l], KC, 0, D)
                DMA("sync", bgt[:], bg[l, b, :, :], writes=[bgt])
                cnt = 0
                for (c0, n) in CHUNKS:
                    tiles = list(range(c0 // 128, (c0 + n) // 128))
                    pzs = [proj_tok(i, Wz, 0, 512) for i in tiles]
                    for oc in range(2):
                        proj_feat(fb[4 + oc], c0, n, Wg, oc * 128, 128)
                    for j, i in enumerate(tiles):
                        pz = pzs[j]
                        g_, o_ = G[cnt % 2], ogg[cnt % 3]
                        ACT(g_[:], pz[:, :], AF.Silu, [pz], [g_])
                        TT("vector", o_[:], og[:, i, :], g_[:], ALU.mult, [og, g_], [o_])
                        cnt += 1
                        t = tb[j % 2]
                        for c in range(4):
                            TR(t[:, c * 128:(c + 1) * 128], o_[:, c * 128:(c + 1) * 128], [o_], [t], inc=(c == 3))
                        CP("vector", oggT[:, :, j * 128:(j + 1) * 128], t[:, 0:512].rearrange("p (c q) -> p c q", c=4), [t], [oggT])
                    for oc in range(8):
                        pg = fb[4 + oc % 2]
                        if oc >= 2:
                            proj_feat(pg, c0, n, Wg, oc * 128, 128)
                        py = next_f()
                        for c in range(4):
                            MM(py[:, 0:n], Wo[:, c, oc * 128:(oc + 1) * 128], oggT[:, c, 0:n], c == 0, c == 3, [Wo, oggT], [py], inc=(c == 3))
                        s_ = sg[oc % 2]
                        ACT(s_[:, 0:n], pg[:, 0:n], AF.Sigmoid, [pg, bgt], [s_], bias=bgt[:, oc:oc + 1])
                        TT("vector", mT[:, oc, 0:n], s_[:, 0:n], py[:, 0:n], ALU.mult, [s_, py], [mT])
                    for j, i in enumerate(tiles):
                        for half in range(2):
                            po = next_f()
                            for k in range(KC):
                                MM(po[:, :], mT[:, k, j * 128:(j + 1) * 128], Wout[:, k, half * 512:(half + 1) * 512], k == 0, k == KC - 1,
                                   [mT, Wout], [po], inc=(k == KC - 1))
                            TT("vector", X[:, i, half * 512:(half + 1) * 512], X[:, i, half * 512:(half + 1) * 512], po[:, :], ALU.add, [X, po], [X])
                P.emit()

        def branch_sb(l):
            with ExitStack() as bst:
                V = P.sb(bst, "Vsb", [128, NT, 512], BF16)
                with ExitStack() as st:
                    Wv = P.sb(st, "Wv", [128, KC, 512], BF16)
                    load_w(Wv, w_in[l], KC, O_SBV, O_SBV + 512)
                    for i in range(NT):
                        p = proj_tok(i, Wv, 0, 512)
                        CP("scalar" if i % 2 else "vector", V[:, i, :], p[:, :], [p], [V])
                    P.emit()
                with ExitStack() as st:
                    Wqk2 = [P.sb(st, "Wqk%d" % j, [128, KC, 256], BF16) for j in range(2)]
                    qT = P.sb(st, "qT", [128, L], BF16)
                    kT = P.sb(st, "kT", [128, L], BF16)
                    NE, NSP, NC = 4, 3, 3
                    ez = [P.sb(st, "ez%d" % j, [128, 512], F32) for j in range(NE)]
                    sp = [P.sb(st, "sp%d" % j, [128, 516], F32) for j in range(NSP)]
                    Cb = [P.sb(st, "Cb%d" % j, [128, 512], F32) for j in range(NC)]
                    ctot = [P.sb(st, "ctot%d" % j, [128, 1], F32) for j in range(3)]
                    for j in range(NSP):
                        MEMSET("vector", sp[j][:], 0.0, [sp[j]])
                    wb = [P.sb(st, "wb%d" % j, [128, 512], BF16) for j in range(2)]
                    wT = [P.sb(st, "wT%d" % j, [128, 512], BF16) for j in range(2)]
                    cn = [P.sb(st, "cn%d" % j, [128, 1], F32) for j in range(3)]
                    def load_pair_w(pr_):
                        load_w(Wqk2[pr_ % 2], w_in[l], KC, O_SBQ + pr_ * 128, O_SBQ + (pr_ + 1) * 128, dst_c0=0)
                        load_w(Wqk2[pr_ % 2], w_in[l], KC, O_SBK + pr_ * 128, O_SBK + (pr_ + 1) * 128, dst_c0=128)
                    load_pair_w(0)
                    for pr in range(4):
                        Wqk = Wqk2[pr % 2]
                        if pr + 1 < 4:
                            load_pair_w(pr + 1)
                        for (c0, n) in CHUNKS:
                            p = next_f(4, 6)
                            proj_feat(p, c0, n, Wqk, 0, 128)
                            CP("scalar", qT[:, c0:c0 + n], p[:, 0:n], [p], [qT])
                            p = next_f(4, 6)
                            proj_feat(p, c0, n, Wqk, 128, 128)
                            CP("vector", kT[:, c0:c0 + n], p[:, 0:n], [p], [kT])
                        items = []
                        for hh in range(2):
                            for i in range(NT):
                                nk = (i + 1) * 128
                                chs = [(k0, min(512, nk - k0)) for k0 in range(0, nk, 512)][::-1]
                                for ci, (k0, n) in enumerate(chs):
                                    items.append((hh, i, ci, k0, n, ci == len(chs) - 1))
                        N = len(items)
                        obank = {}
                        ocnt = [0]

                        def st_mm(j):
                            hh, i, ci, k0, n, last = items[j]
                            r0 = 64 * hh
                            z = fb[j % 2]
                            MM(z[:, 0:n], qT[r0:r0 + 64, i * 128:(i + 1) * 128], kT[r0:r0 + 64, k0:k0 + n], True, True, [qT, kT], [z])

                        def st_expz(j):
                            hh, i, ci, k0, n, last = items[j]
                            e_ = ez[j % NE]
                            ACT(e_[:, 0:n], fb[j % 2][:, 0:n], AF.Exp, [fb[j % 2]], [e_], scale=0.125)
                            if ci == 0:
                                TT("gpsimd", e_[:, n - 128:n], e_[:, n - 128:n], tri[:], ALU.mult, [e_, tri], [e_])

                        def st_ln(j):
                            hh, i, ci, k0, n, last = items[j]
                            ACT(sp[j % NSP][:, 1:n + 1], ez[j % NE][:, 0:n], AF.Ln, [ez[j % NE]], [sp[j % NSP], ctot[j % 3]], bias=1.0,
                                accum_out=ctot[j % 3][:])

                        def st_scan(j):
                            hh, i, ci, k0, n, last = items[j]
                            s_, c_ = sp[j % NSP], Cb[j % NC]
                            P.op("vector", lambda e: e.tensor_tensor_scan(out=c_[:, 0:n], data0=s_[:, 0:n], data1=s_[:, 0:n],
                                                                           initial=0.0, op0=ALU.add, op1=ALU.max), [s_], [c_])

                        def st_cn(j):
                            hh, i, ci, k0, n, last = items[j]
                            if ci == 0:
                                TS("vector", cn[j % 3][:], ctot[j % 3][:], -1.0, None, ALU.mult, None, [ctot[j % 3]], [cn[j % 3]])
                            else:
                                TT("vector", cn[j % 3][:], cn[(j - 1) % 3][:], ctot[j % 3][:], ALU.subtract, [cn[(j - 1) % 3], ctot[j % 3]], [cn[j % 3]])

                        def st_expt(j):
                            hh, i, ci, k0, n, last = items[j]
                            ACT(Cb[j % NC][:, 0:n], Cb[j % NC][:, 0:n], AF.Exp, [Cb[j % NC], cn[j % 3]], [Cb[j % NC]], bias=cn[j % 3][:, 0:1])

                        def st_mult(j):
                            hh, i, ci, k0, n, last = items[j]
                            TT("gpsimd", wb[j % 2][:, 0:n], ez[j % NE][:, 0:n], Cb[j % NC][:, 0:n], ALU.mult, [ez[j % NE], Cb[j % NC]], [wb[j % 2]])

                        def st_tr(j):
                            hh, i, ci, k0, n, last = items[j]
                            t = tb[j % 2]
                            nb = n // 128
                            for jb in range(nb):
                                TR(t[:, jb * 128:(jb + 1) * 128], wb[j % 2][:, jb * 128:(jb + 1) * 128], [wb[j % 2]], [t], inc=(jb == nb - 1))

                        def st_evac(j):
                            hh, i, ci, k0, n, last = items[j]
                            CP("scalar" if j % 2 == 0 else "vector", wT[j % 2][:, 0:n], tb[j % 2][:, 0:n], [tb[j % 2]], [wT[j % 2]])

                        def st_pv(j):
                            hh, i, ci, k0, n, last = items[j]
                            h = 2 * pr + hh
                            nb = n // 128
                            if ci == 0:
                                obank[(hh, i)] = fb[2 + ocnt[0] % 2]
                                ocnt[0] += 1
                            O = obank[(hh, i)]
                            for jb in range(nb):
                                kb = k0 // 128 + jb
                                MM(O[:, 0:64], wT[j % 2][:, jb * 128:(jb + 1) * 128], V[:, kb, h * 64:(h + 1) * 64],
                                   ci == 0 and jb == 0, last and jb == nb - 1, [wT[j % 2], V], [O], inc=(jb == nb - 1))
                            if last:
                                CP("vector", og[:, i, h * 64:(h + 1) * 64], O[:, 0:64], [O], [og])

                        sched = [(st_mm, 0), (st_expz, 1), (st_expt, 3), (st_ln, 1), (st_evac, 6), (st_cn, 2), (st_scan, 2),
                                 (st_mult, 4), (st_tr, 5), (st_pv, 7)]
                        for step in range(N + 7):
                            for fn, off in sched:
                                if 0 <= step - off < N:
                                    fn(step - off)
                    P.emit()
            epilogue(l, 0, O_SBZ)

        def softmax_attn(st, units, dv1, scale, finalize, PT=None):
            NS = 5
            zbanks = [fb[0], fb[1], fb[6], fb[7]]
            if PT is None:
                PT = [P.sb(st, "PT%d" % j, [128, 512], BF16) for j in range(NS)]
            items = []
            for i in range(NT):
                kbs = list(range(0, min(i + 2, NT)))
                groups = [kbs[a:a + 4] for a in range(0, len(kbs), 4)]
                for u in range(len(units)):
                    for gi, g in enumerate(groups):
                        items.append((i, u, g, gi == 0, gi == len(groups) - 1))
            N = len(items)
            nu = len(units)

            def s1(j):
                i, u, g, first, last = items[j]
                QTb, KTb, r0, nr, vfn = units[u]
                z = zbanks[j % 4]
                for a, kb in enumerate(g):
                    msk = kb >= i
                    MM(z[:, a * 128:(a + 1) * 128], KTb[r0:r0 + nr, kb * 128:(kb + 1) * 128], QTb[r0:r0 + nr, i * 128:(i + 1) * 128],
                       True, not msk, [QTb, KTb], [z], inc=(a == len(g) - 1 and not msk))
                    if msk:
                        mo = 0 if kb == i else 2
                        MM(z[:, a * 128:(a + 1) * 128], mkt[mo][:, :], mkt[mo + 1][:, :], False, True, [mkt[mo], mkt[mo + 1]], [z],
                           inc=(a == len(g) - 1))

            def s2(j):
                i, u, g, first, last = items[j]
                z = zbanks[j % 4]
                s = j % NS
                n = len(g) * 128
                ACT(PT[s][:, 0:n], z[:, 0:n], AF.Exp, [z], [PT[s]], scale=scale)

            def s3(j):
                i, u, g, first, last = items[j]
                QTb, KTb, r0, nr, vfn = units[u]
                s = j % NS
                O = fb[2 + u] if nu > 1 else fb[2 + i % 2]
                for a, kb in enumerate(g):
                    MM(O[:, 0:dv1], PT[s][:, a * 128:(a + 1) * 128], vfn(kb), first and a == 0, last and a == len(g) - 1,
                       [PT[s]], [O], inc=(a == len(g) - 1))
                if last and u == nu - 1:
                    finalize(i, [fb[2 + uu] for uu in range(nu)] if nu > 1 else [O])

            for step in range(N + 2):
                if step < N:
                    s1(step)
                if 0 <= step - 1 < N:
                    s2(step - 1)
                if 0 <= step - 2 < N:
                    s3(step - 2)

        def diff_attn(st, QTb, KTb, vfn, finalize, PT=None):
            NS = 3
            if PT is None:
                PT = [P.sb(st, "PTd%d" % j, [128, 1024], BF16) for j in range(NS)]
            items = []
            for i in range(NT):
                kbs = list(range(0, min(i + 2, NT)))
                groups = [kbs[a:a + 4] for a in range(0, len(kbs), 4)]
                for gi, g in enumerate(groups):
                    items.append((i, g, gi == 0, gi == len(groups) - 1))
            N = len(items)

            def s1(j):
                i, g, first, last = items[j]
                zt = z2[j % 2]
                for a, kb in enumerate(g):
                    msk = kb >= i
                    for u in range(2):
                        r0 = 64 * u
                        MM(zt[:, u * 512 + a * 128:u * 512 + (a + 1) * 128], KTb[r0:r0 + 64, kb * 128:(kb + 1) * 128],
                           QTb[r0:r0 + 64, i * 128:(i + 1) * 128], True, not msk, [QTb, KTb], [zt],
                           inc=(a == len(g) - 1 and u == 1 and not msk))
                    if msk:
                        mo = 0 if kb == i else 2
                        for u in range(2):
                            MM(zt[:, u * 512 + a * 128:u * 512 + (a + 1) * 128], mkt[mo][:, :], mkt[mo + 1][:, :], False, True,
                               [mkt[mo], mkt[mo + 1]], [zt], inc=(a == len(g) - 1 and u == 1))

            def s2(j):
                i, g, first, last = items[j]
                zt = z2[j % 2]
                p_ = PT[j % NS]
                n = len(g) * 128
                ACT(p_[:].rearrange("p (u c) -> p u c", u=2)[:, :, 0:n], zt[:].rearrange("p (u c) -> p u c", u=2)[:, :, 0:n],
                    AF.Exp, [zt], [p_], scale=0.125)

            def s3(j):
                i, g, first, last = items[j]
                p_ = PT[j % NS]
                Os = [fb[4 + 2 * (i % 2)], fb[5 + 2 * (i % 2)]]
                for a, kb in enumerate(g):
                    for u in range(2):
                        MM(Os[u][:, 0:129], p_[:, u * 512 + a * 128:u * 512 + (a + 1) * 128], vfn(kb),
                           first and a == 0, last and a == len(g) - 1, [p_], [Os[u]], inc=(a == len(g) - 1))
                if last:
                    finalize(i, Os)

            for step in range(N + 2):
                if step < N:
                    s1(step)
                if 0 <= step - 1 < N:
                    s2(step - 1)
                if 0 <= step - 2 < N:
                    s3(step - 2)

        def branch_mla(l):
            with ExitStack() as bst:
                cnT = P.sb(bst, "cnT", [128, 5, L], BF16)
                Va = P.sb(bst, "Va", [128, NT, 8, 68], BF16)
                KT = P.sb(bst, "KTm", [128, L], BF16)
                with ExitStack() as st:
                    W = P.sb(st, "Wm", [128, KC, 704], BF16)
                    Wv = P.sb(st, "Wukvv", [128, 2, 512], BF16)
                    gq = P.sb(st, "gq", [128, 640], F32)
                    junk = P.sb(st, "junkm", [128, 384], BF16)
                    ss = P.sb(st, "ssm", [128, 2 * NT], F32)
                    rs = P.sb(st, "rsm", [128, 2 * NT], F32)
                    cb = [P.sb(st, "cb%d" % j, [128, 640], BF16) for j in range(2)]
                    t1 = P.sb(st, "t1m", [128, 512], F32)
                    t2 = P.sb(st, "t2m", [128, 512], F32)
                    load_w(W, w_in[l], KC, O_CQ, O_CQ + 672)
                    load_w(W, w_x[l], KC, 1024, 1056, dst_c0=672)
                    load_w(Wv, ukvv[l], 2, 0, 512)
                    DMA("sync", gq[:, 0:384], cq_g[l:l + 1, :].to_broadcast([128, 384]), writes=[gq])
                    DMA("sync", gq[:, 384:640], ckv_g[l:l + 1, :].to_broadcast([128, 256]), writes=[gq])
                    MEMSET("gpsimd", Va[:].rearrange("p a b c -> p (a b c)"), 1.0, [Va])
                    ssb = [Buf("ssm_%d" % i, ss.t) for i in range(2 * NT)]
                    rsb = [Buf("rsm_%d" % i, rs.t) for i in range(2 * NT)]
                    cb3 = cb + [P.sb(st, "cb2", [128, 640], BF16)]
                    junk2 = [junk, P.sb(st, "junkm2", [128, 384], BF16)]

                    def sa(i):
                        c = cb3[i % 3]
                        for part, (wc0, n, dc0) in enumerate(((0, 384, 0), (384, 256, 384))):
                            p = proj_tok(i, W, wc0, n)
                            col = 2 * i + part
                            ACT(junk2[part][:, 0:n], p[:, 0:n], AF.Square, [p], [junk2[part], ssb[col]], accum_out=ss[:, col:col + 1])
                            RSTD(rs[:, col:col + 1], ss[:, col:col + 1], n, [ssb[col]], [rsb[col]])
                            STT("vector", c[:, dc0:dc0 + n], p[:, 0:n], rs[:, col:col + 1], gq[:, dc0:dc0 + n], ALU.mult, ALU.mult, [p, rsb[col], gq], [c])

                    def sb_(i):
                        c, t = cb3[i % 3], tb[i % 2]
                        for k in range(5):
                            TR(t[:, k * 128:(k + 1) * 128], c[:, k * 128:(k + 1) * 128], [c], [t], inc=(k == 4))

                    def sc(i):
                        t = tb[i % 2]
                        CP("vector" if i % 2 else "scalar", cnT[:, :, i * 128:(i + 1) * 128], t[:, 0:640].rearrange("p (k c) -> p k c", k=5), [t], [cnT])

                    for step in range(NT + 2):
                        if step < NT:
                            sa(step)
                        if 0 <= step - 1 < NT:
                            sb_(step - 1)
                        if 0 <= step - 2 < NT:
                            sc(step - 2)
                    for (c0, n) in CHUNKS:
                        pa = fb[4]
                        pb = fb[5]
                        proj_feat(pa, c0, n, W, 608, 64)
                        proj_feat(pb, c0, n, W, 640, 64)
                        TT("vector", t1[32:64, 0:n], pa[32:64, 0:n], ropec[32:64, c0:c0 + n], ALU.mult, [pa, ropec], [t1])
                        TT("vector", t2[32:64, 0:n], pb[32:64, 0:n], ropes[32:64, c0:c0 + n], ALU.mult, [pb, ropes], [t2])
                        TT("vector", KT[32:64, c0:c0 + n], t1[32:64, 0:n], t2[32:64, 0:n], ALU.add, [t1, t2], [KT])
                    for i in range(NT):
                        p = proj_tok(i, Wv, 0, 512, kchunks=2, src=cnT, src_k0=3)
                        CP("scalar" if i % 2 else "vector", Va[:, i, :, 0:64], p[:, :].rearrange("p (h d) -> p h d", h=8), [p], [Va])
                    P.emit()
                with ExitStack() as st:
                    QT = P.sb(st, "QTm", [128, L], BF16)
                    Wa = P.sb(st, "Wuqa", [128, 3, 768], BF16)
                    Wb = P.sb(st, "Wuqb", [128, 3, 768], BF16)
                    Wk = P.sb(st, "Wukn", [128, 2, 768], BF16)
                    t1 = P.sb(st, "t1q", [128, 512], F32)
                    t2 = P.sb(st, "t2q", [128, 512], F32)
                    rcp = P.sb(st, "rcp", [128, 1], F32)
                    PTm = [P.sb(st, "PTm%d" % j, [128, 512], BF16) for j in range(5)]
                    load_w(Wa, uqa[l], 3, 0, 768)
                    load_w(Wb, uqb[l], 3, 0, 768)
                    load_w(Wk, ukn[l], 2, 0, 768)
                    for h in range(8):
                        for cidx, (c0, n) in enumerate(CHUNKS):
                            pa, pb = fb[4 + 2 * (cidx % 2)], fb[5 + 2 * (cidx % 2)]
                            proj_feat(pa, c0, n, Wa, h * 96, 96, kchunks=3, src=cnT)
                            proj_feat(pb, c0, n, Wb, h * 96, 96, kchunks=3, src=cnT)
                            CP("scalar", QT[0:32, c0:c0 + n], pa[0:32, 0:n], [pa], [QT])
                            CP("scalar", QT[64:96, c0:c0 + n], pa[64:96, 0:n], [pa], [QT])
                            TT("vector", t1[32:64, 0:n], pa[32:64, 0:n], ropec[32:64, c0:c0 + n], ALU.mult, [pa, ropec], [t1])
                            TT("vector", t2[32:64, 0:n], pb[32:64, 0:n], ropes[32:64, c0:c0 + n], ALU.mult, [pb, ropes], [t2])
                            TT("vector", QT[32:64, c0:c0 + n], t1[32:64, 0:n], t2[32:64, 0:n], ALU.add, [t1, t2], [QT])
                            pk = pb
                            proj_feat(pk, c0, n, Wk, h * 96, 96, kchunks=2, src=cnT, src_k0=3)
                            CP("scalar", KT[0:32, c0:c0 + n], pk[0:32, 0:n], [pk], [KT])
                            CP("vector", KT[64:96, c0:c0 + n], pk[64:96, 0:n], [pk], [KT])

                        def fin(i, Os, h=h):
                            O = Os[0]
                            P.op("vector", lambda e: e.reciprocal(out=rcp[:], in_=O[:, 64:65]), [O], [rcp])
                            TS("vector", og[:, i, h * 64:(h + 1) * 64], O[:, 0:64], rcp[:, 0:1], None, ALU.mult, None, [O, rcp], [og])

                        softmax_attn(st, [(QT, KT, 0, 96, (lambda kb, h=h: Va[:, kb, h, 0:65]))], 65, 1.0 / math.sqrt(96.0), fin, PT=PTm)
                    P.emit()
            epilogue(l, 1, O_MZ)

        def branch_diff(l):
            lam_init = 0.8 - 0.6 * math.exp(-0.3 * l)
            with ExitStack() as bst:
                Vd = P.sb(bst, "Vd", [128, NT, 4, 132], BF16)
                lam = P.sb(bst, "lam", [128, 1], F32)
                gd = P.sb(bst, "gd", [128, 128], F32)
                with ExitStack() as st:
                    Wv = P.sb(st, "Wdv", [128, KC, 512], BF16)
                    dl = P.sb(st, "dl", [128, 256], F32)
                    pr_ = P.sb(st, "prd", [128, 128], F32)
                    sm = P.sb(st, "smd", [128, 2], F32)
                    load_w(Wv, w_in[l], KC, O_DV, O_DV + 512)
                    DMA("sync", dl[:], dlam[l:l + 1, :].to_broadcast([128, 256]), writes=[dl])
                    DMA("sync", gd[:], dng[l:l + 1, :].to_broadcast([128, 128]), writes=[gd])
                    dl3 = dl[:].rearrange("p (a b) -> p a b", a=2)
                    TT("vector", pr_[:].rearrange("p (a b) -> p a b", a=2), dl3[:, :, 0:64], dl3[:, :, 64:128], ALU.mult, [dl], [pr_])
                    P.op("vector", lambda e: e.reduce_sum(out=sm[:], in_=pr_[:].rearrange("p (a b) -> p a b", a=2), axis=mybir.AxisListType.X), [pr_], [sm])
                    ACT(sm[:], sm[:], AF.Exp, [sm], [sm])
                    TT("vector", lam[:], sm[:, 0:1], sm[:, 1:2], ALU.subtract, [sm], [lam])
                    TS("vector", lam[:], lam[:], lam_init, None, ALU.add, None, [lam], [lam])
                    MEMSET("gpsimd", Vd[:].rearrange("p a b c -> p (a b c)"), 1.0, [Vd])
                    for i in range(NT):
                        p = proj_tok(i, Wv, 0, 512)
                        CP("scalar" if i % 2 else "vector", Vd[:, i, :, 0:128], p[:, :].rearrange("p (h d) -> p h d", h=4), [p], [Vd])
                    P.emit()
                with ExitStack() as st:
                    PTd = [P.sb(st, "PTd%d" % j, [128, 1024], BF16) for j in range(3)]
                    Wq2 = [P.sb(st, "Wdq%d" % j, [128, KC, 512], BF16) for j in range(2)]

                    def load_head_w(h):
                        Wq_ = Wq2[h % 2]
                        load_w(Wq_, w_in[l], KC, O_DQ + h * 128, O_DQ + (h + 1) * 128, dst_c0=0)
                        load_w(Wq_, w_x[l], KC, h * 128, (h + 1) * 128, dst_c0=128)
                        load_w(Wq_, w_in[l], KC, O_DK + h * 128, O_DK + (h + 1) * 128, dst_c0=256)
                        load_w(Wq_, w_x[l], KC, 512 + h * 128, 512 + (h + 1) * 128, dst_c0=384)
                    load_head_w(0)
                    QT = P.sb(st, "QTd", [128, L], BF16)
                    KT = P.sb(st, "KTd", [128, L], BF16)
                    t1 = P.sb(st, "t1d", [128, 512], F32)
                    t2 = P.sb(st, "t2d", [128, 512], F32)
                    rc = P.sb(st, "rcd", [128, 2], F32)
                    tm = P.sb(st, "tmd", [128, 128], F32)
                    oc_ = P.sb(st, "ocd", [128, 128], F32)
                    jk = P.sb(st, "jkd", [128, 128], BF16)
                    ssd = P.sb(st, "ssd", [128, 1], F32)
                    rsd = P.sb(st, "rsd", [128, 1], F32)
                    for h in range(4):
                        cc = 0
                        Wq = Wq2[h % 2]
                        if h + 1 < 4:
                            load_head_w(h + 1)
                        for (dst, wc) in ((QT, 0), (KT, 256)):
                            for (c0, n) in CHUNKS:
                                pa, pb = fb[4 + 2 * (cc % 2)], fb[5 + 2 * (cc % 2)]
                                cc += 1
                                proj_feat(pa, c0, n, Wq, wc, 128)
                                proj_feat(pb, c0, n, Wq, wc + 128, 128)
                                TT("vector", t1[:, 0:n], pa[:, 0:n], ropec[:, c0:c0 + n], ALU.mult, [pa, ropec], [t1])
                                TT("vector", t2[:, 0:n], pb[:, 0:n], ropes[:, c0:c0 + n], ALU.mult, [pb, ropes], [t2])
                                TT("vector", dst[:, c0:c0 + n], t1[:, 0:n], t2[:, 0:n], ALU.add, [t1, t2], [dst])
                                CP("scalar", dst[32:64, c0:c0 + n], pa[32:64, 0:n], [pa], [dst])

                        def fin(i, Os, h=h):
                            O0, O1 = Os
                            P.op("vector", lambda e: e.reciprocal(out=rc[:, 0:1], in_=O0[:, 128:129]), [O0], [rc])
                            P.op("vector", lambda e: e.reciprocal(out=rc[:, 1:2], in_=O1[:, 128:129]), [O1], [rc])
                            TT("vector", rc[:, 1:2], rc[:, 1:2], lam[:, 0:1], ALU.mult, [rc, lam], [rc])
                            TS("vector", tm[:], O1[:, 0:128], rc[:, 1:2], None, ALU.mult, None, [O1, rc], [tm])
                            STT("vector", oc_[:], O0[:, 0:128], rc[:, 0:1], tm[:], ALU.mult, ALU.subtract, [O0, rc, tm], [oc_])
                            MEMSET("vector", ssd[:], 0.0, [ssd])
                            ACT(jk[:], oc_[:], AF.Square, [oc_], [jk, ssd], accum_out=ssd[:, 0:1])
                            RSTD(rsd[:], ssd[:], 128, [ssd], [rsd], mult=1.0 - lam_init)
                            STT("vector", og[:, i, h * 128:(h + 1) * 128], oc_[:], rsd[:, 0:1], gd[:], ALU.mult, ALU.mult, [oc_, rsd, gd], [og])

                        diff_attn(st, QT, KT, (lambda kb, h=h: Vd[:, kb, h, 0:129]), fin, PT=PTd)
                    P.emit()
            epilogue(l, 2, O_DZ)

        for l in range(nlayers):
            phase_norm(l)
            if 0 in branches:
                branch_sb(l)
            if 1 in branches:
                branch_mla(l)
            if 2 in branches:
                branch_diff(l)

        with ExitStack() as st:
            grep = P.sb(st, "grepf", [128, D], F32)
            junk = P.sb(st, "junkf", [128, D], BF16)
            ss = P.sb(st, "ssf", [128, NT], F32)
            rs = P.sb(st, "rsf", [128, NT], F32)
            yo = [P.sb(st, "yo%d" % j, [128, D], F32) for j in range(2)]
            bcast_load(grep, final_g[0:1, :], D)
            MEMSET("vector", ss[:], 0.0, [ss])
            for i in range(NT):
                o = yo[i % 2]
                if final_norm:
                    ACT(junk[:], X[:, i, :], AF.Square, [X], [junk, ss], accum_out=ss[:, i:i + 1])
                    RSTD(rs[:, i:i + 1], ss[:, i:i + 1], D, [ss], [rs])
                    STT("vector", o[:], X[:, i, :], rs[:, i:i + 1], grep[:], ALU.mult, ALU.mult, [X, rs, grep], [o])
                else:
                    CP("vector", o[:], X[:, i, :], [X], [o])
                p_lo = NMETA if i == 0 else 0
                p_hi = NMETA if i == NT - 1 else 128
                s0 = 128 * i - NMETA + p_lo
                DMA("sync", y[s0:s0 + (p_hi - p_lo), :], o[p_lo:p_hi, :], reads=[o])
            P.wait_all("sync", yo)
            P.emit()
    return nc


_CACHE = {}


def _consts():
    if "c" not in _CACHE:
        C, Sg = _rope_tables()
        tri, m01, ident, mk = _masks()
        _CACHE["c"] = {"c_ropec": C, "c_ropes": Sg, "c_tri": tri, "c_m01": m01, "c_ident": ident, "c_mk": mk}
    return _CACHE["c"]


def make_in_maps(inp):
    x = np.asarray(inp["x"], np.float32)
    B = x.shape[0]
    meta = np.asarray(inp["meta_tokens"], np.float32)
    lay = _host_layouts({k: np.asarray(v) for k, v in inp.items()})
    shared = dict(_consts())
    shared.update(lay)
    for k in ("norm_g", "w_in", "mla_cq_g", "mla_ckv_g", "diff_norm_g", "w_o_sb", "w_o_mla", "w_o_diff", "w_out"):
        shared[k] = np.ascontiguousarray(np.asarray(inp[k], np.float32))
    shared["diff_lambda"] = np.ascontiguousarray(np.asarray(inp["diff_lambda"], np.float32).reshape(2, 256))
    shared["final_g"] = np.ascontiguousarray(np.asarray(inp["final_g"], np.float32).reshape(1, D))
    maps = []
    for b in range(B):
        h0 = np.concatenate([meta, x[b], np.zeros((L - NMETA - S, D), np.float32)], axis=0)
        m = dict(shared)
        m["h0"] = np.ascontiguousarray(h0)
        maps.append(m)
    return maps


def kernel(**inputs):
    maps = make_in_maps(inputs)
    if "nc" not in _CACHE:
        _CACHE["nc"] = build_nc()
    res = run_bass_kernel_spmd(_CACHE["nc"], maps, core_ids=list(range(len(maps))))
    return np.stack([np.asarray(r["y"], np.float32) for r in res.results], axis=0)
```

```python
import math
import numpy as np
import ml_dtypes
from contextlib import ExitStack
import concourse.bass as bass
import concourse.mybir as mybir
from concourse.bass_utils import run_bass_kernel_spmd

F32 = mybir.dt.float32
BF16 = mybir.dt.bfloat16
AF = mybir.ActivationFunctionType
ALU = mybir.AluOpType

D = 1024
S = 2048
NMETA = 16
NT = 17
L = NT * 128
KC = 8
EPS = 1e-6
THETA = 500000.0
CHUNKS = [(0, 512), (512, 512), (1024, 512), (1536, 512), (2048, 128)]


class Buf:
    __slots__ = ("name", "t", "w", "r", "dsem", "dcnt")

    def __init__(self, name, t=None):
        self.name = name
        self.t = t
        self.w = []
        self.r = []
        self.dsem = None
        self.dcnt = 0

    def __getitem__(self, idx):
        return self.t[idx]


class Prog:
    ENGS = ("tensor", "vector", "scalar", "gpsimd", "sync")

    def __init__(self, nc, stack):
        self.nc = nc
        self.stack = stack
        self.sems = {}
        self.cnt = {e: 0 for e in self.ENGS}
        self.seen = {e: {} for e in self.ENGS}
        self.q = {e: [] for e in self.ENGS}
        self.snaps = {}
        for e in self.ENGS:
            self._sem("E_" + e)
        self.nbuf = 0

    def _sem(self, key):
        if key not in self.sems:
            self.sems[key] = self.stack.enter_context(self.nc.semaphore(key))
        return key

    def sb(self, st, name, shape, dt):
        self.uid = getattr(self, "uid", 0) + 1
        name = "%s_%d" % (name, self.uid)
        t = st.enter_context(self.nc.sbuf_tensor(name, list(shape), dt))
        return Buf(name, t)

    def ps(self, st, name, shape, dt=F32):
        t = st.enter_context(self.nc.psum_tensor(name, list(shape), dt))
        return Buf(name, t)

    def _waits(self, eng, reads, writes):
        own = "E_" + eng
        need = {}
        for b in reads:
            for (k, v) in b.w:
                if k == own and (eng == "tensor" or v > self.cnt[eng]):
                    continue
                if need.get(k, 0) < v:
                    need[k] = v
        for b in writes:
            for (k, v) in b.w:
                if k == own and (eng == "tensor" or v > self.cnt[eng]):
                    continue
                if need.get(k, 0) < v:
                    need[k] = v
            for (k, v) in b.r:
                if k == own and (eng == "tensor" or v > self.cnt[eng]):
                    continue
                if need.get(k, 0) < v:
                    need[k] = v
        out = []
        seen = self.seen[eng]
        snaps = self.snaps
        for k, v in sorted(need.items(), key=lambda kv: 0 if kv[0].startswith("E_") else 1):
            if seen.get(k, 0) < v:
                seen[k] = v
                out.append((k, v))
                sn = snaps.get((k, v))
                if sn:
                    for k2, v2 in sn.items():
                        if seen.get(k2, 0) < v2:
                            seen[k2] = v2
        return out

    def op(self, eng, fn, reads=(), writes=(), inc=True):
        waits = self._waits(eng, reads, writes)
        key = "E_" + eng
        val = self.cnt[eng] + 1
        if inc:
            self.cnt[eng] = val
        ev = (key, val)
        if inc:
            self.snaps[ev] = dict(self.seen[eng])
        for b in writes:
            b.w = [ev]
            b.r = []
        for b in reads:
            b.r = [e for e in b.r if e[0] != key] + [ev]
        self.q[eng].append((waits, fn, [(key, 1)] if inc else []))

    def dma(self, eng, fn, reads=(), writes=(), sem_buf=None):
        waits = self._waits(eng, reads, writes)
        sb = sem_buf or (writes[0] if writes else reads[0])
        if sb.dsem is None:
            sb.dsem = self._sem("D_%d" % self.nbuf)
            self.nbuf += 1
        sb.dcnt += 16
        ev = (sb.dsem, sb.dcnt)
        self.snaps[ev] = dict(self.seen[eng])
        for b in writes:
            b.w = [e for e in b.w if e[0] != sb.dsem and e[0].startswith("D_")] + [ev]
            b.r = []
        for b in reads:
            b.r = [e for e in b.r if e[0] != sb.dsem] + [ev]
        self.q[eng].append((waits, fn, [(sb.dsem, 16)]))

    def wait_all(self, eng, bufs):
        waits = self._waits(eng, (), bufs)
        self.q[eng].append((waits, None, []))

    def emit(self):
        nc = self.nc
        qs = self.q
        self.q = {e: [] for e in self.ENGS}
        sems = self.sems
        with nc.Block() as block:
            def mk(ename):
                items = qs[ename]

                def body(e):
                    for waits, fn, incs in items:
                        if fn is None:
                            for (k, v) in waits:
                                e.wait_ge(sems[k], v)
                            continue
                        for (k, v) in waits[1:]:
                            e.wait_ge(sems[k], v)
                        ins = fn(e)
                        if waits:
                            ins._wait_ge(sems[waits[0][0]], waits[0][1])
                        for (k, n) in incs:
                            ins.then_inc(sems[k], n)
                return body
            block.tensor(mk("tensor"))
            block.vector(mk("vector"))
            block.scalar(mk("scalar"))
            block.gpsimd(mk("gpsimd"))
            block.sync(mk("sync"))


def _rope_tables():
    pos = np.arange(L, dtype=np.float32)
    C = np.ones((128, L), np.float32)
    Sg = np.zeros((128, L), np.float32)
    inv_d = (np.float32(THETA) ** (-np.arange(0, 16, 2, dtype=np.float32) / np.float32(16))).astype(np.float32)
    ang_d = (pos[:, None] * inv_d[None, :]).astype(np.float32)
    cd, sd = np.cos(ang_d).astype(np.float32), np.sin(ang_d).astype(np.float32)
    for base in (0, 64):
        for r in range(16):
            C[base + r] = cd[:, r % 8]
            Sg[base + r] = -sd[:, r % 8] if r < 8 else sd[:, r % 8]
    inv_m = (np.float32(THETA) ** (-np.arange(0, 32, 2, dtype=np.float32) / np.float32(32))).astype(np.float32)
    ang_m = (pos[:, None] * inv_m[None, :]).astype(np.float32)
    cm, sm = np.cos(ang_m).astype(np.float32), np.sin(ang_m).astype(np.float32)
    for r in range(32):
        C[32 + r] = cm[:, r % 16]
        Sg[32 + r] = -sm[:, r % 16] if r < 16 else sm[:, r % 16]
    return C, Sg


def _masks():
    a = np.arange(128)
    tri = (a[None, :] < a[:, None]).astype(np.float32)
    cq = (a + 48) // 64
    m0 = (cq[:, None] <= cq[None, :])
    m1 = ((a[:, None] < 16) & (a[None, :] >= 80))
    m01 = np.concatenate([m0, m1], axis=1).astype(np.float32).astype(ml_dtypes.bfloat16)
    ident = np.eye(128, dtype=np.float32).astype(ml_dtypes.bfloat16)
    BIG = 30000.0
    mk = np.zeros((8, 128), np.float32)
    mk[0] = -BIG * (cq == 1); mk[1] = -BIG * (cq == 2)
    mk[2] = (cq < 1); mk[3] = (cq < 2)
    mk[4] = -BIG; mk[5] = BIG * (a < 16)
    mk[6] = 1.0; mk[7] = (a >= 80)
    mk = mk.astype(ml_dtypes.bfloat16)
    return tri, m01, ident, mk


O_SBQ, O_SBK, O_SBV, O_SBZ = 0, 512, 1024, 1536
O_CQ, O_CKV, O_KR, O_MZ = 2048, 2432, 2688, 2720
O_DQ, O_DK, O_DV, O_DZ = 3232, 3744, 4256, 4768
O_G = 5280


def _host_layouts(inp):
    w_in = inp["w_in"]
    swap64 = np.concatenate([np.arange(8, 16), np.arange(0, 8), np.arange(16, 64)])
    idx_d = np.concatenate([m * 64 + swap64 for m in range(8)])
    kr_sw = np.concatenate([np.arange(16, 32), np.arange(0, 16)])
    w_x = np.concatenate([w_in[:, :, O_DQ + idx_d], w_in[:, :, O_DK + idx_d], w_in[:, :, O_KR + kr_sw]], axis=2)
    uq = inp["mla_w_uq"]
    ia, ib = [], []
    for h in range(8):
        b = 96 * h
        ia += list(range(b, b + 32)) + list(range(b + 64, b + 96)) + list(range(b + 32, b + 64))
        ib += list(range(b, b + 32)) + list(range(b + 80, b + 96)) + list(range(b + 64, b + 80)) + list(range(b + 32, b + 64))
    uqa = uq[:, :, np.array(ia)]
    uqb = uq[:, :, np.array(ib)]
    ukv = inp["mla_w_ukv"]
    ikn, iv = [], []
    for h in range(8):
        b = 128 * h
        ikn += list(range(b, b + 32)) + list(range(b, b + 32)) + list(range(b + 32, b + 64))
        iv += list(range(b + 64, b + 128))
    ukn = ukv[:, :, np.array(ikn)]
    ukvv = ukv[:, :, np.array(iv)]
    bg = inp["b_gate"].reshape(2, 3, 8, 128).transpose(0, 1, 3, 2)
    return {
        "w_x": np.ascontiguousarray(w_x),
        "uqa": np.ascontiguousarray(uqa), "uqb": np.ascontiguousarray(uqb),
        "ukn": np.ascontiguousarray(ukn), "ukvv": np.ascontiguousarray(ukvv),
        "bg": np.ascontiguousarray(bg),
    }


def build_nc(nlayers=2, final_norm=True, branches=(0, 1, 2)):
    nc = bass.Bass("TRN2", target_bir_lowering=False)

    def din(name, shape, dt=F32):
        return nc.dram_tensor(name, list(shape), dt, kind="ExternalInput").ap()

    h0 = din("h0", [L, D])
    norm_g = din("norm_g", [2, D])
    w_in = din("w_in", [2, D, 8352])
    w_x = din("w_x", [2, D, 1056])
    bg = din("bg", [2, 3, 128, 8])
    cq_g = din("mla_cq_g", [2, 384])
    ckv_g = din("mla_ckv_g", [2, 256])
    uqa = din("uqa", [2, 384, 768])
    uqb = din("uqb", [2, 384, 768])
    ukn = din("ukn", [2, 256, 768])
    ukvv = din("ukvv", [2, 256, 512])
    dlam = din("diff_lambda", [2, 256])
    dng = din("diff_norm_g", [2, 128])
    w_o = [din("w_o_sb", [2, 512, D]), din("w_o_mla", [2, 512, D]), din("w_o_diff", [2, 512, D])]
    w_out = din("w_out", [2, D, D])
    final_g = din("final_g", [1, D])
    c_ropec = din("c_ropec", [128, L])
    c_ropes = din("c_ropes", [128, L])
    c_tri = din("c_tri", [128, 128])
    c_m01 = din("c_m01", [128, 256], BF16)
    c_ident = din("c_ident", [128, 128], BF16)
    c_mk = din("c_mk", [8, 128], BF16)
    y = nc.dram_tensor("y", [S, D], F32, kind="ExternalOutput").ap()

    with ExitStack() as top:
        P = Prog(nc, top)
        X = P.sb(top, "X", [128, NT, D], F32)
        hT = P.sb(top, "hT", [128, KC, L], BF16)
        og = P.sb(top, "og", [128, NT, 512], BF16)
        ropec = P.sb(top, "ropec", [128, L], F32)
        ropes = P.sb(top, "ropes", [128, L], F32)
        tri = P.sb(top, "tri", [128, 128], F32)
        m01 = P.sb(top, "m01", [128, 256], BF16)
        ident = P.sb(top, "ident", [128, 128], BF16)
        mkt = [P.sb(top, "mk%d" % j, [2, 128], BF16) for j in range(4)]
        z2 = [P.ps(top, "z2_%d" % i, [128, 1024], F32) for i in range(2)]
        fb = [Buf("fb0", z2[0][:, 0:512]), Buf("fb1", z2[0][:, 512:1024]), Buf("fb2", z2[1][:, 0:512]), Buf("fb3", z2[1][:, 512:1024])]
        fb += [P.ps(top, "fb%d" % i, [128, 512], F32) for i in range(4, 8)]
        tb = [Buf("tb%d" % i, fb[6 + i][:].bitcast(BF16)) for i in range(2)]
        for i in range(2):
            tb[i].w, tb[i].r = fb[6 + i].w, fb[6 + i].r

        def MM(out, lhsT, rhs, start, stop, reads, writes, inc=True):
            P.op("tensor", lambda e: e.matmul(out, lhsT=lhsT, rhs=rhs, start=start, stop=stop), reads, writes, inc)

        def TR(out, in_, reads, writes, inc=True):
            P.op("tensor", lambda e: e.transpose(out=out, in_=in_, identity=ident[:]), list(reads) + [ident], writes, inc)

        def ACT(out, in_, func, reads, writes, bias=None, scale=None, accum_out=None):
            kw = {}
            if bias is not None:
                kw["bias"] = bias
            if scale is not None:
                kw["scale"] = scale
            if accum_out is not None:
                kw["accum_out"] = accum_out
            P.op("scalar", lambda e: e.activation(out=out, in_=in_, func=func, **kw), reads, writes)

        def TT(eng, out, in0, in1, op, reads, writes):
            P.op(eng, lambda e: e.tensor_tensor(out=out, in0=in0, in1=in1, op=op), reads, writes)

        def TS(eng, out, in0, s1, s2, op0, op1, reads, writes):
            if op1 is None:
                P.op(eng, lambda e: e.tensor_scalar(out=out, in0=in0, scalar1=s1, scalar2=None, op0=op0), reads, writes)
            else:
                P.op(eng, lambda e: e.tensor_scalar(out=out, in0=in0, scalar1=s1, scalar2=s2, op0=op0, op1=op1), reads, writes)

        def STT(eng, out, in0, scalar, in1, op0, op1, reads, writes):
            P.op(eng, lambda e: e.scalar_tensor_tensor(out=out, in0=in0, scalar=scalar, in1=in1, op0=op0, op1=op1), reads, writes)

        def RSTD(out, ss_ap, n, reads, writes, mult=1.0):
            ACT(out, ss_ap, AF.Ln, reads, writes, bias=EPS, scale=1.0 / n)
            ACT(out, out, AF.Exp, writes, writes, bias=(math.log(mult) if mult != 1.0 else None), scale=-0.5)

        def CP(eng, out, in_, reads, writes):
            if eng == "scalar":
                P.op(eng, lambda e: e.copy(out=out, in_=in_), reads, writes)
            else:
                P.op(eng, lambda e: e.tensor_copy(out=out, in_=in_), reads, writes)

        def MEMSET(eng, ap, val, writes):
            P.op(eng, lambda e: e.memset(ap, val), (), writes)

        def DMA(eng, out, in_, reads=(), writes=()):
            P.dma(eng, lambda e: e.dma_start(out=out, in_=in_), reads, writes)

        def load_w(buf, dram2d, k_chunks, c0, c1, dst_c0=0):
            v = dram2d.rearrange("(k p) c -> p k c", p=128)
            DMA("gpsimd", buf[:, 0:k_chunks, dst_c0:dst_c0 + (c1 - c0)], v[:, :, c0:c1], writes=[buf])

        def bcast_load(buf, row_ap, n):
            DMA("sync", buf[:], row_ap.to_broadcast([128, n]), writes=[buf])

        fctr = [0]

        def next_f(lo=0, hi=4):
            b = fb[lo + fctr[0] % (hi - lo)]
            fctr[0] += 1
            return b

        DMA("scalar", ropec[:], c_ropec[:, :], writes=[ropec])
        DMA("scalar", ropes[:], c_ropes[:, :], writes=[ropes])
        DMA("sync", tri[:], c_tri[:, :], writes=[tri])
        DMA("sync", m01[:], c_m01[:, :], writes=[m01])
        DMA("sync", ident[:], c_ident[:, :], writes=[ident])
        for j in range(4):
            DMA("sync", mkt[j][:], c_mk[2 * j:2 * j + 2, :], writes=[mkt[j]])
        h0v = h0.rearrange("(t p) d -> p t d", p=128)
        Xr = [Buf("Xr%d" % j, X.t) for j in range(3)]
        DMA("sync", X[:, 0:6, :], h0v[:, 0:6, :], writes=[Xr[0]])
        DMA("scalar", X[:, 6:12, :], h0v[:, 6:12, :], writes=[Xr[1]])
        DMA("sync", X[:, 12:NT, :], h0v[:, 12:NT, :], writes=[Xr[2]])

        def phase_norm(l):
            with ExitStack() as st:
                grep = P.sb(st, "grep", [128, D], F32)
                junk = [P.sb(st, "junk%d" % j, [128, D], BF16) for j in range(2)]
                ss = P.sb(st, "ss", [128, NT], F32)
                rs = P.sb(st, "rs", [128, NT], F32)
                ssb = [Buf("ss_%d" % i, ss.t) for i in range(NT)]
                rsb = [Buf("rs_%d" % i, rs.t) for i in range(NT)]
                hn = [P.sb(st, "hn%d" % j, [128, D], BF16) for j in range(3)]
                bcast_load(grep, norm_g[l:l + 1, :], D)

                def sa(i):
                    Xd = Xr[i // 6] if l == 0 else X
                    ACT(junk[i % 2][:], X[:, i, :], AF.Square, [Xd], [junk[i % 2], ssb[i]], accum_out=ss[:, i:i + 1])
                    RSTD(rs[:, i:i + 1], ss[:, i:i + 1], D, [ssb[i]], [rsb[i]])
                    h = hn[i % 3]
                    STT("vector", h[:], X[:, i, :], rs[:, i:i + 1], grep[:], ALU.mult, ALU.mult, [Xd, rsb[i], grep], [h])

                def sb_(i):
                    h, t = hn[i % 3], tb[i % 2]
                    for k in range(KC):
                        TR(t[:, k * 128:(k + 1) * 128], h[:, k * 128:(k + 1) * 128], [h], [t], inc=(k == KC - 1))

                def sc(i):
                    t = tb[i % 2]
                    CP("vector", hT[:, :, i * 128:(i + 1) * 128], t[:].rearrange("p (k c) -> p k c", k=KC), [t], [hT])

                for step in range(NT + 2):
                    if step < NT:
                        sa(step)
                    if 0 <= step - 1 < NT:
                        sb_(step - 1)
                    if 0 <= step - 2 < NT:
                        sc(step - 2)
                P.emit()

        def proj_tok(i, W, c0, n, kchunks=KC, src=None, src_k0=0):
            src = src or hT
            p = next_f()
            for k in range(kchunks):
                MM(p[:, 0:n], src[:, src_k0 + k, i * 128:(i + 1) * 128], W[:, k, c0:c0 + n], k == 0, k == kchunks - 1,
                   [src, W], [p], inc=(k == kchunks - 1))
            return p

        def proj_feat(p, c0, n, W, wc0, M, kchunks=KC, src=None, src_k0=0):
            src = src or hT
            for k in range(kchunks):
                MM(p[0:M, 0:n], W[:, k, wc0:wc0 + M], src[:, src_k0 + k, c0:c0 + n], k == 0, k == kchunks - 1,
                   [src, W], [p], inc=(k == kchunks - 1))

        def epilogue(l, b, zoff):
            with ExitStack() as st:
                Wz = P.sb(st, "Wz", [128, KC, 512], BF16)
                Wo = P.sb(st, "Wo", [128, 4, D], BF16)
                Wg = P.sb(st, "Wg", [128, KC, D], BF16)
                Wout = P.sb(st, "Wout", [128, KC, D], BF16)
                bgt = P.sb(st, "bgt", [128, 8], F32)
                G = [P.sb(st, "G%d" % j, [128, 512], BF16) for j in range(2)]
                ogg = [P.sb(st, "ogg%d" % j, [128, 512], BF16) for j in range(3)]
                oggT = P.sb(st, "oggT", [128, 4, 512], BF16)
                sg = [P.sb(st, "sg%d" % j, [128, 512], F32) for j in range(2)]
                mT = P.sb(st, "mT", [128, KC, 512], BF16)
                load_w(Wz, w_in[l], KC, zoff, zoff + 512)
                load_w(Wo, w_o[b][l], 4, 0, D)
                load_w(Wg, w_in[l], KC, O_G + b * D, O_G + (b + 1) * D)
                load_w(Wout, w_out[l], KC, 0, D)
                DMA("sync", bgt[:], bg[l, b, :, :], writes=[bgt])
                cnt = 0
                for (c0, n) in CHUNKS:
                    tiles = list(range(c0 // 128, (c0 + n) // 128))
                    pzs = [proj_tok(i, Wz, 0, 512) for i in tiles]
                    for oc in range(2):
                        proj_feat(fb[4 + oc], c0, n, Wg, oc * 128, 128)
                    for j, i in enumerate(tiles):
                        pz = pzs[j]
                        g_, o_ = G[cnt % 2], ogg[cnt % 3]
                        ACT(g_[:], pz[:, :], AF.Silu, [pz], [g_])
                        TT("vector", o_[:], og[:, i, :], g_[:], ALU.mult, [og, g_], [o_])
                        cnt += 1
                        t = tb[j % 2]
                        for c in range(4):
                            TR(t[:, c * 128:(c + 1) * 128], o_[:, c * 128:(c + 1) * 128], [o_], [t], inc=(c == 3))
                        CP("vector", oggT[:, :, j * 128:(j + 1) * 128], t[:, 0:512].rearrange("p (c q) -> p c q", c=4), [t], [oggT])
                    for oc in range(8):
                        pg = fb[4 + oc % 2]
                        if oc >= 2:
                            proj_feat(pg, c0, n, Wg, oc * 128, 128)
                        py = next_f()
                        for c in range(4):
                            MM(py[:, 0:n], Wo[:, c, oc * 128:(oc + 1) * 128], oggT[:, c, 0:n], c == 0, c == 3, [Wo, oggT], [py], inc=(c == 3))
                        s_ = sg[oc % 2]
                        ACT(s_[:, 0:n], pg[:, 0:n], AF.Sigmoid, [pg, bgt], [s_], bias=bgt[:, oc:oc + 1])
                        TT("vector", mT[:, oc, 0:n], s_[:, 0:n], py[:, 0:n], ALU.mult, [s_, py], [mT])
                    for j, i in enumerate(tiles):
                        for half in range(2):
                            po = next_f()
                            for k in range(KC):
                                MM(po[:, :], mT[:, k, j * 128:(j + 1) * 128], Wout[:, k, half * 512:(half + 1) * 512], k == 0, k == KC - 1,
                                   [mT, Wout], [po], inc=(k == KC - 1))
                            TT("vector", X[:, i, half * 512:(half + 1) * 512], X[:, i, half * 512:(half + 1) * 512], po[:, :], ALU.add, [X, po], [X])
                P.emit()

        def branch_sb(l):
            with ExitStack() as bst:
                V = P.sb(bst, "Vsb", [128, NT, 512], BF16)
                Wqk2 = [P.sb(bst, "Wqk%d" % j, [128, KC, 256], BF16) for j in range(2)]

                def load_pair_w(pr_):
                    load_w(Wqk2[pr_ % 2], w_in[l], KC, O_SBQ + pr_ * 128, O_SBQ + (pr_ + 1) * 128, dst_c0=0)
                    load_w(Wqk2[pr_ % 2], w_in[l], KC, O_SBK + pr_ * 128, O_SBK + (pr_ + 1) * 128, dst_c0=128)
                with ExitStack() as st:
                    Wv = P.sb(st, "Wv", [128, KC, 512], BF16)
                    load_w(Wv, w_in[l], KC, O_SBV, O_SBV + 512)
                    load_pair_w(0)
                    load_pair_w(1)
                    for i in range(NT):
                        p = proj_tok(i, Wv, 0, 512)
                        CP("scalar" if i % 2 else "vector", V[:, i, :], p[:, :], [p], [V])
                    P.emit()
                with ExitStack() as st:
                    qT2 = [P.sb(st, "qT%d" % j, [128, L], BF16) for j in range(2)]
                    kT2 = [P.sb(st, "kT%d" % j, [128, L], BF16) for j in range(2)]
                    NE, NSP, NC = 4, 3, 3
                    ez = [P.sb(st, "ez%d" % j, [128, 512], F32) for j in range(NE)]
                    sp = [P.sb(st, "sp%d" % j, [128, 516], F32) for j in range(NSP)]
                    Cb = [P.sb(st, "Cb%d" % j, [128, 512], F32) for j in range(NC)]
                    ctot = [P.sb(st, "ctot%d" % j, [128, 1], F32) for j in range(3)]
                    for j in range(NSP):
                        MEMSET("vector", sp[j][:], 0.0, [sp[j]])
                    wb = [P.sb(st, "wb%d" % j, [128, 512], BF16) for j in range(2)]
                    wT = [P.sb(st, "wT%d" % j, [128, 512], BF16) for j in range(2)]
                    cn = [P.sb(st, "cn%d" % j, [128, 1], F32) for j in range(3)]
                    def proj_piece(pr_, idx):
                        c0, n = CHUNKS[idx // 2]
                        p = next_f(4, 6)
                        if idx % 2 == 0:
                            proj_feat(p, c0, n, Wqk2[pr_ % 2], 0, 128)
                            CP("scalar", qT2[pr_ % 2][:, c0:c0 + n], p[:, 0:n], [p], [qT2[pr_ % 2]])
                        else:
                            proj_feat(p, c0, n, Wqk2[pr_ % 2], 128, 128)
                            CP("vector", kT2[pr_ % 2][:, c0:c0 + n], p[:, 0:n], [p], [kT2[pr_ % 2]])

                    for idx in range(10):
                        proj_piece(0, idx)
                    for _once in (0,):
                        items = []
                        for pr in range(4):
                            for hh in range(2):
                                for i in range(NT):
                                    nk = (i + 1) * 128
                                    chs = [(k0, min(512, nk - k0)) for k0 in range(0, nk, 512)][::-1]
                                    for ci, (k0, n) in enumerate(chs):
                                        items.append((pr, hh, i, ci, k0, n, ci == len(chs) - 1))
                        N = len(items)
                        NPP = N // 4
                        obank = {}
                        ocnt = [0]

                        def st_mm(j):
                            pr, hh, i, ci, k0, n, last = items[j]
                            r0 = 64 * hh
                            z = fb[j % 2]
                            q_, k_ = qT2[pr % 2], kT2[pr % 2]
                            MM(z[:, 0:n], q_[r0:r0 + 64, i * 128:(i + 1) * 128], k_[r0:r0 + 64, k0:k0 + n], True, True, [q_, k_], [z])

                        def st_expz(j):
                            pr, hh, i, ci, k0, n, last = items[j]
                            e_ = ez[j % NE]
                            ACT(e_[:, 0:n], fb[j % 2][:, 0:n], AF.Exp, [fb[j % 2]], [e_], scale=0.125)
                            if ci == 0:
                                TT("gpsimd", e_[:, n - 128:n], e_[:, n - 128:n], tri[:], ALU.mult, [e_, tri], [e_])

                        def st_ln(j):
                            pr, hh, i, ci, k0, n, last = items[j]
                            ACT(sp[j % NSP][:, 1:n + 1], ez[j % NE][:, 0:n], AF.Ln, [ez[j % NE]], [sp[j % NSP], ctot[j % 3]], bias=1.0,
                                accum_out=ctot[j % 3][:])

                        def st_scan(j):
                            pr, hh, i, ci, k0, n, last = items[j]
                            s_, c_ = sp[j % NSP], Cb[j % NC]
                            P.op("vector", lambda e: e.tensor_tensor_scan(out=c_[:, 0:n], data0=s_[:, 0:n], data1=s_[:, 0:n],
                                                                           initial=0.0, op0=ALU.add, op1=ALU.max), [s_], [c_])

                        def st_cn(j):
                            pr, hh, i, ci, k0, n, last = items[j]
                            if ci == 0:
                                TS("vector", cn[j % 3][:], ctot[j % 3][:], -1.0, None, ALU.mult, None, [ctot[j % 3]], [cn[j % 3]])
                            else:
                                TT("vector", cn[j % 3][:], cn[(j - 1) % 3][:], ctot[j % 3][:], ALU.subtract, [cn[(j - 1) % 3], ctot[j % 3]], [cn[j % 3]])

                        def st_expt(j):
                            pr, hh, i, ci, k0, n, last = items[j]
                            ACT(Cb[j % NC][:, 0:n], Cb[j % NC][:, 0:n], AF.Exp, [Cb[j % NC], cn[j % 3]], [Cb[j % NC]], bias=cn[j % 3][:, 0:1])

                        def st_mult(j):
                            pr, hh, i, ci, k0, n, last = items[j]
                            TT("gpsimd", wb[j % 2][:, 0:n], ez[j % NE][:, 0:n], Cb[j % NC][:, 0:n], ALU.mult, [ez[j % NE], Cb[j % NC]], [wb[j % 2]])

                        def st_tr(j):
                            pr, hh, i, ci, k0, n, last = items[j]
                            t = tb[j % 2]
                            nb = n // 128
                            for jb in range(nb):
                                TR(t[:, jb * 128:(jb + 1) * 128], wb[j % 2][:, jb * 128:(jb + 1) * 128], [wb[j % 2]], [t], inc=(jb == nb - 1))

                        def st_evac(j):
                            pr, hh, i, ci, k0, n, last = items[j]
                            CP("scalar" if j % 2 == 0 else "vector", wT[j % 2][:, 0:n], tb[j % 2][:, 0:n], [tb[j % 2]], [wT[j % 2]])

                        def st_pv(j):
                            pr, hh, i, ci, k0, n, last = items[j]
                            h = 2 * pr + hh
                            nb = n // 128
                            if ci == 0:
                                obank[(pr, hh, i)] = fb[2 + ocnt[0] % 2]
                                ocnt[0] += 1
                            O = obank[(pr, hh, i)]
                            for jb in range(nb):
                                kb = k0 // 128 + jb
                                MM(O[:, 0:64], wT[j % 2][:, jb * 128:(jb + 1) * 128], V[:, kb, h * 64:(h + 1) * 64],
                                   ci == 0 and jb == 0, last and jb == nb - 1, [wT[j % 2], V], [O], inc=(jb == nb - 1))
                            if last:
                                CP("vector", og[:, i, h * 64:(h + 1) * 64], O[:, 0:64], [O], [og])

                        sched = [(st_mm, 0), (st_expz, 1), (st_expt, 3), (st_ln, 1), (st_evac, 6), (st_cn, 2), (st_scan, 2),
                                 (st_mult, 4), (st_tr, 5), (st_pv, 7)]
                        for step in range(N + 7):
                            for fn, off in sched:
                                if 0 <= step - off < N:
                                    fn(step - off)
                            pcur, rel = step // NPP, step % NPP
                            if pcur + 1 < 4 and rel >= 8 and (rel - 8) % 8 == 0 and (rel - 8) // 8 < 10:
                                proj_piece(pcur + 1, (rel - 8) // 8)
                            if pcur + 2 < 4 and rel == 8 + 8 * 10:
                                load_pair_w(pcur + 2)
                    P.emit()
            epilogue(l, 0, O_SBZ)

        def softmax_attn(st, units, dv1, scale, finalize, PT=None):
            NS = 5
            zbanks = [fb[0], fb[1], fb[6], fb[7]]
            if PT is None:
                PT = [P.sb(st, "PT%d" % j, [128, 512], BF16) for j in range(NS)]
            items = []
            for i in range(NT):
                kbs = list(range(0, min(i + 2, NT)))
                groups = [kbs[a:a + 4] for a in range(0, len(kbs), 4)]
                for u in range(len(units)):
                    for gi, g in enumerate(groups):
                        items.append((i, u, g, gi == 0, gi == len(groups) - 1))
            N = len(items)
            nu = len(units)

            def s1(j):
                i, u, g, first, last = items[j]
                QTb, KTb, r0, nr, vfn = units[u]
                z = zbanks[j % 4]
                for a, kb in enumerate(g):
                    msk = kb >= i
                    MM(z[:, a * 128:(a + 1) * 128], KTb[r0:r0 + nr, kb * 128:(kb + 1) * 128], QTb[r0:r0 + nr, i * 128:(i + 1) * 128],
                       True, not msk, [QTb, KTb], [z], inc=(a == len(g) - 1 and not msk))
                    if msk:
                        mo = 0 if kb == i else 2
                        MM(z[:, a * 128:(a + 1) * 128], mkt[mo][:, :], mkt[mo + 1][:, :], False, True, [mkt[mo], mkt[mo + 1]], [z],
                           inc=(a == len(g) - 1))

            def s2(j):
                i, u, g, first, last = items[j]
                z = zbanks[j % 4]
                s = j % NS
                n = len(g) * 128
                ACT(PT[s][:, 0:n], z[:, 0:n], AF.Exp, [z], [PT[s]], scale=scale)

            def s3(j):
                i, u, g, first, last = items[j]
                QTb, KTb, r0, nr, vfn = units[u]
                s = j % NS
                O = fb[2 + u] if nu > 1 else fb[2 + i % 2]
                for a, kb in enumerate(g):
                    MM(O[:, 0:dv1], PT[s][:, a * 128:(a + 1) * 128], vfn(kb), first and a == 0, last and a == len(g) - 1,
                       [PT[s]], [O], inc=(a == len(g) - 1))
                if last and u == nu - 1:
                    finalize(i, [fb[2 + uu] for uu in range(nu)] if nu > 1 else [O])

            for step in range(N + 2):
                if step < N:
                    s1(step)
                if 0 <= step - 1 < N:
                    s2(step - 1)
                if 0 <= step - 2 < N:
                    s3(step - 2)

        def diff_attn(st, QTb, KTb, vfn, finalize, PT=None):
            NS = 3
            if PT is None:
                PT = [P.sb(st, "PTd%d" % j, [128, 1024], BF16) for j in range(NS)]
            items = []
            for i in range(NT):
                kbs = list(range(0, min(i + 2, NT)))
                groups = [kbs[a:a + 4] for a in range(0, len(kbs), 4)]
                for gi, g in enumerate(groups):
                    items.append((i, g, gi == 0, gi == len(groups) - 1))
            N = len(items)

            def s1(j):
                i, g, first, last = items[j]
                zt = z2[j % 2]
                for a, kb in enumerate(g):
                    msk = kb >= i
                    for u in range(2):
                        r0 = 64 * u
                        MM(zt[:, u * 512 + a * 128:u * 512 + (a + 1) * 128], KTb[r0:r0 + 64, kb * 128:(kb + 1) * 128],
                           QTb[r0:r0 + 64, i * 128:(i + 1) * 128], True, not msk, [QTb, KTb], [zt],
                           inc=(a == len(g) - 1 and u == 1 and not msk))
                    if msk:
                        mo = 0 if kb == i else 2
                        for u in range(2):
                            MM(zt[:, u * 512 + a * 128:u * 512 + (a + 1) * 128], mkt[mo][:, :], mkt[mo + 1][:, :], False, True,
                               [mkt[mo], mkt[mo + 1]], [zt], inc=(a == len(g) - 1 and u == 1))

            def s2(j):
                i, g, first, last = items[j]
                zt = z2[j % 2]
                p_ = PT[j % NS]
                n = len(g) * 128
                ACT(p_[:].rearrange("p (u c) -> p u c", u=2)[:, :, 0:n], zt[:].rearrange("p (u c) -> p u c", u=2)[:, :, 0:n],
                    AF.Exp, [zt], [p_], scale=0.125)

            def s3(j):
                i, g, first, last = items[j]
                p_ = PT[j % NS]
                Os = [fb[4 + 2 * (i % 2)], fb[5 + 2 * (i % 2)]]
                for a, kb in enumerate(g):
                    for u in range(2):
                        MM(Os[u][:, 0:129], p_[:, u * 512 + a * 128:u * 512 + (a + 1) * 128], vfn(kb),
                           first and a == 0, last and a == len(g) - 1, [p_], [Os[u]], inc=(a == len(g) - 1))
                if last:
                    finalize(i, Os)

            for step in range(N + 2):
                if step < N:
                    s1(step)
                if 0 <= step - 1 < N:
                    s2(step - 1)
                if 0 <= step - 2 < N:
                    s3(step - 2)

        def branch_mla(l):
            with ExitStack() as bst:
                cnT = P.sb(bst, "cnT", [128, 5, L], BF16)
                Va = P.sb(bst, "Va", [128, NT, 8, 68], BF16)
                KT = P.sb(bst, "KTm", [128, L], BF16)
                with ExitStack() as st:
                    W = P.sb(st, "Wm", [128, KC, 704], BF16)
                    Wv = P.sb(st, "Wukvv", [128, 2, 512], BF16)
                    gq = P.sb(st, "gq", [128, 640], F32)
                    junk = P.sb(st, "junkm", [128, 384], BF16)
                    ss = P.sb(st, "ssm", [128, 2 * NT], F32)
                    rs = P.sb(st, "rsm", [128, 2 * NT], F32)
                    cb = [P.sb(st, "cb%d" % j, [128, 640], BF16) for j in range(2)]
                    t1 = P.sb(st, "t1m", [128, 512], F32)
                    t2 = P.sb(st, "t2m", [128, 512], F32)
                    load_w(W, w_in[l], KC, O_CQ, O_CQ + 672)
                    load_w(W, w_x[l], KC, 1024, 1056, dst_c0=672)
                    load_w(Wv, ukvv[l], 2, 0, 512)
                    DMA("sync", gq[:, 0:384], cq_g[l:l + 1, :].to_broadcast([128, 384]), writes=[gq])
                    DMA("sync", gq[:, 384:640], ckv_g[l:l + 1, :].to_broadcast([128, 256]), writes=[gq])
                    MEMSET("gpsimd", Va[:].rearrange("p a b c -> p (a b c)"), 1.0, [Va])
                    ssb = [Buf("ssm_%d" % i, ss.t) for i in range(2 * NT)]
                    rsb = [Buf("rsm_%d" % i, rs.t) for i in range(2 * NT)]
                    cb3 = cb + [P.sb(st, "cb2", [128, 640], BF16)]
                    junk2 = [junk, P.sb(st, "junkm2", [128, 384], BF16)]

                    def sa(i):
                        c = cb3[i % 3]
                        for part, (wc0, n, dc0) in enumerate(((0, 384, 0), (384, 256, 384))):
                            p = proj_tok(i, W, wc0, n)
                            col = 2 * i + part
                            ACT(junk2[part][:, 0:n], p[:, 0:n], AF.Square, [p], [junk2[part], ssb[col]], accum_out=ss[:, col:col + 1])
                            RSTD(rs[:, col:col + 1], ss[:, col:col + 1], n, [ssb[col]], [rsb[col]])
                            STT("vector", c[:, dc0:dc0 + n], p[:, 0:n], rs[:, col:col + 1], gq[:, dc0:dc0 + n], ALU.mult, ALU.mult, [p, rsb[col], gq], [c])

                    def sb_(i):
                        c, t = cb3[i % 3], tb[i % 2]
                        for k in range(5):
                            TR(t[:, k * 128:(k + 1) * 128], c[:, k * 128:(k + 1) * 128], [c], [t], inc=(k == 4))

                    def sc(i):
                        t = tb[i % 2]
                        CP("vector" if i % 2 else "scalar", cnT[:, :, i * 128:(i + 1) * 128], t[:, 0:640].rearrange("p (k c) -> p k c", k=5), [t], [cnT])

                    for step in range(NT + 2):
                        if step < NT:
                            sa(step)
                        if 0 <= step - 1 < NT:
                            sb_(step - 1)
                        if 0 <= step - 2 < NT:
                            sc(step - 2)
                    for (c0, n) in CHUNKS:
                        pa = fb[4]
                        pb = fb[5]
                        proj_feat(pa, c0, n, W, 608, 64)
                        proj_feat(pb, c0, n, W, 640, 64)
                        TT("vector", t1[32:64, 0:n], pa[32:64, 0:n], ropec[32:64, c0:c0 + n], ALU.mult, [pa, ropec], [t1])
                        TT("vector", t2[32:64, 0:n], pb[32:64, 0:n], ropes[32:64, c0:c0 + n], ALU.mult, [pb, ropes], [t2])
                        TT("vector", KT[32:64, c0:c0 + n], t1[32:64, 0:n], t2[32:64, 0:n], ALU.add, [t1, t2], [KT])
                    for i in range(NT):
                        p = proj_tok(i, Wv, 0, 512, kchunks=2, src=cnT, src_k0=3)
                        CP("scalar" if i % 2 else "vector", Va[:, i, :, 0:64], p[:, :].rearrange("p (h d) -> p h d", h=8), [p], [Va])
                    P.emit()
                with ExitStack() as st:
                    QT = P.sb(st, "QTm", [128, L], BF16)
                    Wa = P.sb(st, "Wuqa", [128, 3, 768], BF16)
                    Wb = P.sb(st, "Wuqb", [128, 3, 768], BF16)
                    Wk = P.sb(st, "Wukn", [128, 2, 768], BF16)
                    t1 = P.sb(st, "t1q", [128, 512], F32)
                    t2 = P.sb(st, "t2q", [128, 512], F32)
                    rcp = P.sb(st, "rcp", [128, 1], F32)
                    PTm = [P.sb(st, "PTm%d" % j, [128, 512], BF16) for j in range(5)]
                    load_w(Wa, uqa[l], 3, 0, 768)
                    load_w(Wb, uqb[l], 3, 0, 768)
                    load_w(Wk, ukn[l], 2, 0, 768)
                    for h in range(8):
                        for cidx, (c0, n) in enumerate(CHUNKS):
                            pa, pb = fb[4 + 2 * (cidx % 2)], fb[5 + 2 * (cidx % 2)]
                            proj_feat(pa, c0, n, Wa, h * 96, 96, kchunks=3, src=cnT)
                            proj_feat(pb, c0, n, Wb, h * 96, 96, kchunks=3, src=cnT)
                            CP("scalar", QT[0:32, c0:c0 + n], pa[0:32, 0:n], [pa], [QT])
                            CP("scalar", QT[64:96, c0:c0 + n], pa[64:96, 0:n], [pa], [QT])
                            TT("vector", t1[32:64, 0:n], pa[32:64, 0:n], ropec[32:64, c0:c0 + n], ALU.mult, [pa, ropec], [t1])
                            TT("vector", t2[32:64, 0:n], pb[32:64, 0:n], ropes[32:64, c0:c0 + n], ALU.mult, [pb, ropes], [t2])
                            TT("vector", QT[32:64, c0:c0 + n], t1[32:64, 0:n], t2[32:64, 0:n], ALU.add, [t1, t2], [QT])
                            pk = pb
                            proj_feat(pk, c0, n, Wk, h * 96, 96, kchunks=2, src=cnT, src_k0=3)
                            CP("scalar", KT[0:32, c0:c0 + n], pk[0:32, 0:n], [pk], [KT])
                            CP("vector", KT[64:96, c0:c0 + n], pk[64:96, 0:n], [pk], [KT])

                        def fin(i, Os, h=h):
                            O = Os[0]
                            P.op("vector", lambda e: e.reciprocal(out=rcp[:], in_=O[:, 64:65]), [O], [rcp])
                            TS("vector", og[:, i, h * 64:(h + 1) * 64], O[:, 0:64], rcp[:, 0:1], None, ALU.mult, None, [O, rcp], [og])

                        softmax_attn(st, [(QT, KT, 0, 96, (lambda kb, h=h: Va[:, kb, h, 0:65]))], 65, 1.0 / math.sqrt(96.0), fin, PT=PTm)
                    P.emit()
            epilogue(l, 1, O_MZ)

        def branch_diff(l):
            lam_init = 0.8 - 0.6 * math.exp(-0.3 * l)
            with ExitStack() as bst:
                Vd = P.sb(bst, "Vd", [128, NT, 4, 132], BF16)
                lam = P.sb(bst, "lam", [128, 1], F32)
                gd = P.sb(bst, "gd", [128, 128], F32)
                with ExitStack() as st:
                    Wv = P.sb(st, "Wdv", [128, KC, 512], BF16)
                    dl = P.sb(st, "dl", [128, 256], F32)
                    pr_ = P.sb(st, "prd", [128, 128], F32)
                    sm = P.sb(st, "smd", [128, 2], F32)
                    load_w(Wv, w_in[l], KC, O_DV, O_DV + 512)
                    DMA("sync", dl[:], dlam[l:l + 1, :].to_broadcast([128, 256]), writes=[dl])
                    DMA("sync", gd[:], dng[l:l + 1, :].to_broadcast([128, 128]), writes=[gd])
                    dl3 = dl[:].rearrange("p (a b) -> p a b", a=2)
                    TT("vector", pr_[:].rearrange("p (a b) -> p a b", a=2), dl3[:, :, 0:64], dl3[:, :, 64:128], ALU.mult, [dl], [pr_])
                    P.op("vector", lambda e: e.reduce_sum(out=sm[:], in_=pr_[:].rearrange("p (a b) -> p a b", a=2), axis=mybir.AxisListType.X), [pr_], [sm])
                    ACT(sm[:], sm[:], AF.Exp, [sm], [sm])
                    TT("vector", lam[:], sm[:, 0:1], sm[:, 1:2], ALU.subtract, [sm], [lam])
                    TS("vector", lam[:], lam[:], lam_init, None, ALU.add, None, [lam], [lam])
                    MEMSET("gpsimd", Vd[:].rearrange("p a b c -> p (a b c)"), 1.0, [Vd])
                    for i in range(NT):
                        p = proj_tok(i, Wv, 0, 512)
                        CP("scalar" if i % 2 else "vector", Vd[:, i, :, 0:128], p[:, :].rearrange("p (h d) -> p h d", h=4), [p], [Vd])
                    P.emit()
                with ExitStack() as st:
                    PTd = [P.sb(st, "PTd%d" % j, [128, 1024], BF16) for j in range(3)]
                    Wq2 = [P.sb(st, "Wdq%d" % j, [128, KC, 512], BF16) for j in range(2)]

                    def load_head_w(h):
                        Wq_ = Wq2[h % 2]
                        load_w(Wq_, w_in[l], KC, O_DQ + h * 128, O_DQ + (h + 1) * 128, dst_c0=0)
                        load_w(Wq_, w_x[l], KC, h * 128, (h + 1) * 128, dst_c0=128)
                        load_w(Wq_, w_in[l], KC, O_DK + h * 128, O_DK + (h + 1) * 128, dst_c0=256)
                        load_w(Wq_, w_x[l], KC, 512 + h * 128, 512 + (h + 1) * 128, dst_c0=384)
                    load_head_w(0)
                    QT = P.sb(st, "QTd", [128, L], BF16)
                    KT = P.sb(st, "KTd", [128, L], BF16)
                    t1 = P.sb(st, "t1d", [128, 512], F32)
                    t2 = P.sb(st, "t2d", [128, 512], F32)
                    rc = P.sb(st, "rcd", [128, 2], F32)
                    tm = P.sb(st, "tmd", [128, 128], F32)
                    oc_ = P.sb(st, "ocd", [128, 128], F32)
                    jk = P.sb(st, "jkd", [128, 128], BF16)
                    ssd = P.sb(st, "ssd", [128, 1], F32)
                    rsd = P.sb(st, "rsd", [128, 1], F32)
                    for h in range(4):
                        cc = 0
                        Wq = Wq2[h % 2]
                        if h + 1 < 4:
                            load_head_w(h + 1)
                        for (dst, wc) in ((QT, 0), (KT, 256)):
                            for (c0, n) in CHUNKS:
                                pa, pb = fb[4 + 2 * (cc % 2)], fb[5 + 2 * (cc % 2)]
                                cc += 1
                                proj_feat(pa, c0, n, Wq, wc, 128)
                                proj_feat(pb, c0, n, Wq, wc + 128, 128)
                                TT("vector", t1[:, 0:n], pa[:, 0:n], ropec[:, c0:c0 + n], ALU.mult, [pa, ropec], [t1])
                                TT("vector", t2[:, 0:n], pb[:, 0:n], ropes[:, c0:c0 + n], ALU.mult, [pb, ropes], [t2])
                                TT("vector", dst[:, c0:c0 + n], t1[:, 0:n], t2[:, 0:n], ALU.add, [t1, t2], [dst])
                                CP("scalar", dst[32:64, c0:c0 + n], pa[32:64, 0:n], [pa], [dst])

                        def fin(i, Os, h=h):
                            O0, O1 = Os
                            P.op("vector", lambda e: e.reciprocal(out=rc[:, 0:1], in_=O0[:, 128:129]), [O0], [rc])
                            P.op("vector", lambda e: e.reciprocal(out=rc[:, 1:2], in_=O1[:, 128:129]), [O1], [rc])
                            TT("vector", rc[:, 1:2], rc[:, 1:2], lam[:, 0:1], ALU.mult, [rc, lam], [rc])
                            TS("vector", tm[:], O1[:, 0:128], rc[:, 1:2], None, ALU.mult, None, [O1, rc], [tm])
                            STT("vector", oc_[:], O0[:, 0:128], rc[:, 0:1], tm[:], ALU.mult, ALU.subtract, [O0, rc, tm], [oc_])
                            MEMSET("vector", ssd[:], 0.0, [ssd])
                            ACT(jk[:], oc_[:], AF.Square, [oc_], [jk, ssd], accum_out=ssd[:, 0:1])
                            RSTD(rsd[:], ssd[:], 128, [ssd], [rsd], mult=1.0 - lam_init)
                            STT("vector", og[:, i, h * 128:(h + 1) * 128], oc_[:], rsd[:, 0:1], gd[:], ALU.mult, ALU.mult, [oc_, rsd, gd], [og])

                        diff_attn(st, QT, KT, (lambda kb, h=h: Vd[:, kb, h, 0:129]), fin, PT=PTd)
                    P.emit()
            epilogue(l, 2, O_DZ)

        for l in range(nlayers):
            phase_norm(l)
            if 0 in branches:
                branch_sb(l)
            if 1 in branches:
                branch_mla(l)
            if 2 in branches:
                branch_diff(l)

        with ExitStack() as st:
            grep = P.sb(st, "grepf", [128, D], F32)
            junk = P.sb(st, "junkf", [128, D], BF16)
            ss = P.sb(st, "ssf", [128, NT], F32)
            rs = P.sb(st, "rsf", [128, NT], F32)
            yo = [P.sb(st, "yo%d" % j, [128, D], F32) for j in range(2)]
            bcast_load(grep, final_g[0:1, :], D)
            MEMSET("vector", ss[:], 0.0, [ss])
            for i in range(NT):
                o = yo[i % 2]
                if final_norm:
                    ACT(junk[:], X[:, i, :], AF.Square, [X], [junk, ss], accum_out=ss[:, i:i + 1])
                    RSTD(rs[:, i:i + 1], ss[:, i:i + 1], D, [ss], [rs])
                    STT("vector", o[:], X[:, i, :], rs[:, i:i + 1], grep[:], ALU.mult, ALU.mult, [X, rs, grep], [o])
                else:
                    CP("vector", o[:], X[:, i, :], [X], [o])
                p_lo = NMETA if i == 0 else 0
                p_hi = NMETA if i == NT - 1 else 128
                s0 = 128 * i - NMETA + p_lo
                DMA("sync", y[s0:s0 + (p_hi - p_lo), :], o[p_lo:p_hi, :], reads=[o])
            P.wait_all("sync", yo)
            P.emit()
    return nc


_CACHE = {}


def _consts():
    if "c" not in _CACHE:
        C, Sg = _rope_tables()
        tri, m01, ident, mk = _masks()
        _CACHE["c"] = {"c_ropec": C, "c_ropes": Sg, "c_tri": tri, "c_m01": m01, "c_ident": ident, "c_mk": mk}
    return _CACHE["c"]


def make_in_maps(inp):
    x = np.asarray(inp["x"], np.float32)
    B = x.shape[0]
    meta = np.asarray(inp["meta_tokens"], np.float32)
    lay = _host_layouts({k: np.asarray(v) for k, v in inp.items()})
    shared = dict(_consts())
    shared.update(lay)
    for k in ("norm_g", "w_in", "mla_cq_g", "mla_ckv_g", "diff_norm_g", "w_o_sb", "w_o_mla", "w_o_diff", "w_out"):
        shared[k] = np.ascontiguousarray(np.asarray(inp[k], np.float32))
    shared["diff_lambda"] = np.ascontiguousarray(np.asarray(inp["diff_lambda"], np.float32).reshape(2, 256))
    shared["final_g"] = np.ascontiguousarray(np.asarray(inp["final_g"], np.float32).reshape(1, D))
    maps = []
    for b in range(B):
        h0 = np.concatenate([meta, x[b], np.zeros((L - NMETA - S, D), np.float32)], axis=0)
        m = dict(shared)
        m["h0"] = np.ascontiguousarray(h0)
        maps.append(m)
    return maps


def kernel(**inputs):
    maps = make_in_maps(inputs)
    if "nc" not in _CACHE:
        _CACHE["nc"] = build_nc()
    res = run_bass_kernel_spmd(_CACHE["nc"], maps, core_ids=list(range(len(maps))))
    return np.stack([np.asarray(r["y"], np.float32) for r in res.results], axis=0)
```

```python
import math
import numpy as np
import ml_dtypes
from contextlib import ExitStack
import concourse.bass as bass
import concourse.mybir as mybir
from concourse.bass_utils import run_bass_kernel_spmd

F32 = mybir.dt.float32
BF16 = mybir.dt.bfloat16
AF = mybir.ActivationFunctionType
ALU = mybir.AluOpType

D = 1024
S = 2048
NMETA = 16
NT = 17
L = NT * 128
KC = 8
EPS = 1e-6
THETA = 500000.0
CHUNKS = [(0, 512), (512, 512), (1024, 512), (1536, 512), (2048, 128)]


class Buf:
    __slots__ = ("name", "t", "w", "r", "dsem", "dcnt")

    def __init__(self, name, t=None):
        self.name = name
        self.t = t
        self.w = []
        self.r = []
        self.dsem = None
        self.dcnt = 0

    def __getitem__(self, idx):
        return self.t[idx]


class Prog:
    ENGS = ("tensor", "vector", "scalar", "gpsimd", "sync")

    def __init__(self, nc, stack):
        self.nc = nc
        self.stack = stack
        self.sems = {}
        self.cnt = {e: 0 for e in self.ENGS}
        self.seen = {e: {} for e in self.ENGS}
        self.q = {e: [] for e in self.ENGS}
        self.snaps = {}
        for e in self.ENGS:
            self._sem("E_" + e)
        self.nbuf = 0

    def _sem(self, key):
        if key not in self.sems:
            self.sems[key] = self.stack.enter_context(self.nc.semaphore(key))
        return key

    def sb(self, st, name, shape, dt):
        self.uid = getattr(self, "uid", 0) + 1
        name = "%s_%d" % (name, self.uid)
        t = st.enter_context(self.nc.sbuf_tensor(name, list(shape), dt))
        return Buf(name, t)

    def ps(self, st, name, shape, dt=F32):
        t = st.enter_context(self.nc.psum_tensor(name, list(shape), dt))
        return Buf(name, t)

    def _waits(self, eng, reads, writes):
        own = "E_" + eng
        need = {}
        for b in reads:
            for (k, v) in b.w:
                if k == own and (eng == "tensor" or v > self.cnt[eng]):
                    continue
                if need.get(k, 0) < v:
                    need[k] = v
        for b in writes:
            for (k, v) in b.w:
                if k == own and (eng == "tensor" or v > self.cnt[eng]):
                    continue
                if need.get(k, 0) < v:
                    need[k] = v
            for (k, v) in b.r:
                if k == own and (eng == "tensor" or v > self.cnt[eng]):
                    continue
                if need.get(k, 0) < v:
                    need[k] = v
        out = []
        seen = self.seen[eng]
        snaps = self.snaps
        for k, v in sorted(need.items(), key=lambda kv: 0 if kv[0].startswith("E_") else 1):
            if seen.get(k, 0) < v:
                seen[k] = v
                out.append((k, v))
                sn = snaps.get((k, v))
                if sn:
                    for k2, v2 in sn.items():
                        if seen.get(k2, 0) < v2:
                            seen[k2] = v2
        return out

    def op(self, eng, fn, reads=(), writes=(), inc=True):
        waits = self._waits(eng, reads, writes)
        key = "E_" + eng
        val = self.cnt[eng] + 1
        if inc:
            self.cnt[eng] = val
        ev = (key, val)
        if inc:
            self.snaps[ev] = dict(self.seen[eng])
        for b in writes:
            b.w = [ev]
            b.r = []
        for b in reads:
            b.r = [e for e in b.r if e[0] != key] + [ev]
        self.q[eng].append((waits, fn, [(key, 1)] if inc else []))

    def dma(self, eng, fn, reads=(), writes=(), sem_buf=None):
        waits = self._waits(eng, reads, writes)
        sb = sem_buf or (writes[0] if writes else reads[0])
        if sb.dsem is None:
            sb.dsem = self._sem("D_%d" % self.nbuf)
            self.nbuf += 1
        sb.dcnt += 16
        ev = (sb.dsem, sb.dcnt)
        self.snaps[ev] = dict(self.seen[eng])
        for b in writes:
            b.w = [e for e in b.w if e[0] != sb.dsem and e[0].startswith("D_")] + [ev]
            b.r = []
        for b in reads:
            b.r = [e for e in b.r if e[0] != sb.dsem] + [ev]
        self.q[eng].append((waits, fn, [(sb.dsem, 16)]))

    def wait_all(self, eng, bufs):
        waits = self._waits(eng, (), bufs)
        self.q[eng].append((waits, None, []))

    def emit(self):
        nc = self.nc
        qs = self.q
        self.q = {e: [] for e in self.ENGS}
        sems = self.sems
        with nc.Block() as block:
            def mk(ename):
                items = qs[ename]

                def body(e):
                    for waits, fn, incs in items:
                        if fn is None:
                            for (k, v) in waits:
                                e.wait_ge(sems[k], v)
                            continue
                        for (k, v) in waits[1:]:
                            e.wait_ge(sems[k], v)
                        ins = fn(e)
                        if waits:
                            ins._wait_ge(sems[waits[0][0]], waits[0][1])
                        for (k, n) in incs:
                            ins.then_inc(sems[k], n)
                return body
            block.tensor(mk("tensor"))
            block.vector(mk("vector"))
            block.scalar(mk("scalar"))
            block.gpsimd(mk("gpsimd"))
            block.sync(mk("sync"))


def _rope_tables():
    pos = np.arange(L, dtype=np.float32)
    C = np.ones((128, L), np.float32)
    Sg = np.zeros((128, L), np.float32)
    inv_d = (np.float32(THETA) ** (-np.arange(0, 16, 2, dtype=np.float32) / np.float32(16))).astype(np.float32)
    ang_d = (pos[:, None] * inv_d[None, :]).astype(np.float32)
    cd, sd = np.cos(ang_d).astype(np.float32), np.sin(ang_d).astype(np.float32)
    for base in (0, 64):
        for r in range(16):
            C[base + r] = cd[:, r % 8]
            Sg[base + r] = -sd[:, r % 8] if r < 8 else sd[:, r % 8]
    inv_m = (np.float32(THETA) ** (-np.arange(0, 32, 2, dtype=np.float32) / np.float32(32))).astype(np.float32)
    ang_m = (pos[:, None] * inv_m[None, :]).astype(np.float32)
    cm, sm = np.cos(ang_m).astype(np.float32), np.sin(ang_m).astype(np.float32)
    for r in range(32):
        C[32 + r] = cm[:, r % 16]
        Sg[32 + r] = -sm[:, r % 16] if r < 16 else sm[:, r % 16]
    return C, Sg


def _masks():
    a = np.arange(128)
    tri = (a[None, :] < a[:, None]).astype(np.float32)
    cq = (a + 48) // 64
    m0 = (cq[:, None] <= cq[None, :])
    m1 = ((a[:, None] < 16) & (a[None, :] >= 80))
    m01 = np.concatenate([m0, m1], axis=1).astype(np.float32).astype(ml_dtypes.bfloat16)
    ident = np.eye(128, dtype=np.float32).astype(ml_dtypes.bfloat16)
    BIG = 30000.0
    mk = np.zeros((8, 128), np.float32)
    mk[0] = -BIG * (cq == 1); mk[1] = -BIG * (cq == 2)
    mk[2] = (cq < 1); mk[3] = (cq < 2)
    mk[4] = -BIG; mk[5] = BIG * (a < 16)
    mk[6] = 1.0; mk[7] = (a >= 80)
    mk = mk.astype(ml_dtypes.bfloat16)
    return tri, m01, ident, mk


O_SBQ, O_SBK, O_SBV, O_SBZ = 0, 512, 1024, 1536
O_CQ, O_CKV, O_KR, O_MZ = 2048, 2432, 2688, 2720
O_DQ, O_DK, O_DV, O_DZ = 3232, 3744, 4256, 4768
O_G = 5280


def _host_layouts(inp):
    w_in = inp["w_in"]
    swap64 = np.concatenate([np.arange(8, 16), np.arange(0, 8), np.arange(16, 64)])
    idx_d = np.concatenate([m * 64 + swap64 for m in range(8)])
    kr_sw = np.concatenate([np.arange(16, 32), np.arange(0, 16)])
    w_x = np.concatenate([w_in[:, :, O_DQ + idx_d], w_in[:, :, O_DK + idx_d], w_in[:, :, O_KR + kr_sw]], axis=2)
    uq = inp["mla_w_uq"]
    ia, ib = [], []
    for h in range(8):
        b = 96 * h
        ia += list(range(b, b + 32)) + list(range(b + 64, b + 96)) + list(range(b + 32, b + 64))
        ib += list(range(b, b + 32)) + list(range(b + 80, b + 96)) + list(range(b + 64, b + 80)) + list(range(b + 32, b + 64))
    uqa = uq[:, :, np.array(ia)]
    uqb = uq[:, :, np.array(ib)]
    ukv = inp["mla_w_ukv"]
    ikn, iv = [], []
    for h in range(8):
        b = 128 * h
        ikn += list(range(b, b + 32)) + list(range(b, b + 32)) + list(range(b + 32, b + 64))
        iv += list(range(b + 64, b + 128))
    ukn = ukv[:, :, np.array(ikn)]
    ukvv = ukv[:, :, np.array(iv)]
    bg = inp["b_gate"].reshape(2, 3, 8, 128).transpose(0, 1, 3, 2)
    return {
        "w_x": np.ascontiguousarray(w_x),
        "uqa": np.ascontiguousarray(uqa), "uqb": np.ascontiguousarray(uqb),
        "ukn": np.ascontiguousarray(ukn), "ukvv": np.ascontiguousarray(ukvv),
        "bg": np.ascontiguousarray(bg),
    }


def build_nc(nlayers=2, final_norm=True, branches=(0, 1, 2)):
    nc = bass.Bass("TRN2", target_bir_lowering=False)

    def din(name, shape, dt=F32):
        return nc.dram_tensor(name, list(shape), dt, kind="ExternalInput").ap()

    h0 = din("h0", [L, D])
    norm_g = din("norm_g", [2, D])
    w_in = din("w_in", [2, D, 8352])
    w_x = din("w_x", [2, D, 1056])
    bg = din("bg", [2, 3, 128, 8])
    cq_g = din("mla_cq_g", [2, 384])
    ckv_g = din("mla_ckv_g", [2, 256])
    uqa = din("uqa", [2, 384, 768])
    uqb = din("uqb", [2, 384, 768])
    ukn = din("ukn", [2, 256, 768])
    ukvv = din("ukvv", [2, 256, 512])
    dlam = din("diff_lambda", [2, 256])
    dng = din("diff_norm_g", [2, 128])
    w_o = [din("w_o_sb", [2, 512, D]), din("w_o_mla", [2, 512, D]), din("w_o_diff", [2, 512, D])]
    w_out = din("w_out", [2, D, D])
    final_g = din("final_g", [1, D])
    c_ropec = din("c_ropec", [128, L])
    c_ropes = din("c_ropes", [128, L])
    c_tri = din("c_tri", [128, 128])
    c_m01 = din("c_m01", [128, 256], BF16)
    c_ident = din("c_ident", [128, 128], BF16)
    c_mk = din("c_mk", [8, 128], BF16)
    y = nc.dram_tensor("y", [S, D], F32, kind="ExternalOutput").ap()

    with ExitStack() as top:
        P = Prog(nc, top)
        X = P.sb(top, "X", [128, NT, D], F32)
        hT = P.sb(top, "hT", [128, KC, L], BF16)
        og = P.sb(top, "og", [128, NT, 512], BF16)
        ropec = P.sb(top, "ropec", [128, L], F32)
        ropes = P.sb(top, "ropes", [128, L], F32)
        tri = P.sb(top, "tri", [128, 128], F32)
        m01 = P.sb(top, "m01", [128, 256], BF16)
        ident = P.sb(top, "ident", [128, 128], BF16)
        mkt = [P.sb(top, "mk%d" % j, [2, 128], BF16) for j in range(4)]
        z2 = [P.ps(top, "z2_%d" % i, [128, 1024], F32) for i in range(2)]
        fb = [Buf("fb0", z2[0][:, 0:512]), Buf("fb1", z2[0][:, 512:1024]), Buf("fb2", z2[1][:, 0:512]), Buf("fb3", z2[1][:, 512:1024])]
        fb += [P.ps(top, "fb%d" % i, [128, 512], F32) for i in range(4, 8)]
        tb = [Buf("tb%d" % i, fb[6 + i][:].bitcast(BF16)) for i in range(2)]
        for i in range(2):
            tb[i].w, tb[i].r = fb[6 + i].w, fb[6 + i].r

        def MM(out, lhsT, rhs, start, stop, reads, writes, inc=True):
            P.op("tensor", lambda e: e.matmul(out, lhsT=lhsT, rhs=rhs, start=start, stop=stop), reads, writes, inc)

        def TR(out, in_, reads, writes, inc=True):
            P.op("tensor", lambda e: e.transpose(out=out, in_=in_, identity=ident[:]), list(reads) + [ident], writes, inc)

        def ACT(out, in_, func, reads, writes, bias=None, scale=None, accum_out=None):
            kw = {}
            if bias is not None:
                kw["bias"] = bias
            if scale is not None:
                kw["scale"] = scale
            if accum_out is not None:
                kw["accum_out"] = accum_out
            P.op("scalar", lambda e: e.activation(out=out, in_=in_, func=func, **kw), reads, writes)

        def TT(eng, out, in0, in1, op, reads, writes):
            P.op(eng, lambda e: e.tensor_tensor(out=out, in0=in0, in1=in1, op=op), reads, writes)

        def TS(eng, out, in0, s1, s2, op0, op1, reads, writes):
            if op1 is None:
                P.op(eng, lambda e: e.tensor_scalar(out=out, in0=in0, scalar1=s1, scalar2=None, op0=op0), reads, writes)
            else:
                P.op(eng, lambda e: e.tensor_scalar(out=out, in0=in0, scalar1=s1, scalar2=s2, op0=op0, op1=op1), reads, writes)

        def STT(eng, out, in0, scalar, in1, op0, op1, reads, writes):
            P.op(eng, lambda e: e.scalar_tensor_tensor(out=out, in0=in0, scalar=scalar, in1=in1, op0=op0, op1=op1), reads, writes)

        def RSTD(out, ss_ap, n, reads, writes, mult=1.0):
            ACT(out, ss_ap, AF.Ln, reads, writes, bias=EPS, scale=1.0 / n)
            ACT(out, out, AF.Exp, writes, writes, bias=(math.log(mult) if mult != 1.0 else None), scale=-0.5)

        def CP(eng, out, in_, reads, writes):
            if eng == "scalar":
                P.op(eng, lambda e: e.copy(out=out, in_=in_), reads, writes)
            else:
                P.op(eng, lambda e: e.tensor_copy(out=out, in_=in_), reads, writes)

        def MEMSET(eng, ap, val, writes):
            P.op(eng, lambda e: e.memset(ap, val), (), writes)

        def DMA(eng, out, in_, reads=(), writes=()):
            P.dma(eng, lambda e: e.dma_start(out=out, in_=in_), reads, writes)

        def load_w(buf, dram2d, k_chunks, c0, c1, dst_c0=0):
            v = dram2d.rearrange("(k p) c -> p k c", p=128)
            DMA("gpsimd", buf[:, 0:k_chunks, dst_c0:dst_c0 + (c1 - c0)], v[:, :, c0:c1], writes=[buf])

        def bcast_load(buf, row_ap, n):
            DMA("sync", buf[:], row_ap.to_broadcast([128, n]), writes=[buf])

        fctr = [0]

        def next_f(lo=0, hi=4):
            b = fb[lo + fctr[0] % (hi - lo)]
            fctr[0] += 1
            return b

        DMA("scalar", ropec[:], c_ropec[:, :], writes=[ropec])
        DMA("scalar", ropes[:], c_ropes[:, :], writes=[ropes])
        DMA("sync", tri[:], c_tri[:, :], writes=[tri])
        DMA("sync", m01[:], c_m01[:, :], writes=[m01])
        DMA("sync", ident[:], c_ident[:, :], writes=[ident])
        for j in range(4):
            DMA("sync", mkt[j][:], c_mk[2 * j:2 * j + 2, :], writes=[mkt[j]])
        h0v = h0.rearrange("(t p) d -> p t d", p=128)
        Xr = [Buf("Xr%d" % j, X.t) for j in range(3)]
        DMA("sync", X[:, 0:6, :], h0v[:, 0:6, :], writes=[Xr[0]])
        DMA("scalar", X[:, 6:12, :], h0v[:, 6:12, :], writes=[Xr[1]])
        DMA("sync", X[:, 12:NT, :], h0v[:, 12:NT, :], writes=[Xr[2]])

        def phase_norm(l):
            with ExitStack() as st:
                grep = P.sb(st, "grep", [128, D], F32)
                junk = [P.sb(st, "junk%d" % j, [128, D], BF16) for j in range(2)]
                ss = P.sb(st, "ss", [128, NT], F32)
                rs = P.sb(st, "rs", [128, NT], F32)
                ssb = [Buf("ss_%d" % i, ss.t) for i in range(NT)]
                rsb = [Buf("rs_%d" % i, rs.t) for i in range(NT)]
                hn = [P.sb(st, "hn%d" % j, [128, D], BF16) for j in range(3)]
                bcast_load(grep, norm_g[l:l + 1, :], D)

                def sa(i):
                    Xd = Xr[i // 6] if l == 0 else X
                    ACT(junk[i % 2][:], X[:, i, :], AF.Square, [Xd], [junk[i % 2], ssb[i]], accum_out=ss[:, i:i + 1])
                    RSTD(rs[:, i:i + 1], ss[:, i:i + 1], D, [ssb[i]], [rsb[i]])
                    h = hn[i % 3]
                    STT("vector", h[:], X[:, i, :], rs[:, i:i + 1], grep[:], ALU.mult, ALU.mult, [Xd, rsb[i], grep], [h])

                def sb_(i):
                    h, t = hn[i % 3], tb[i % 2]
                    for k in range(KC):
                        TR(t[:, k * 128:(k + 1) * 128], h[:, k * 128:(k + 1) * 128], [h], [t], inc=(k == KC - 1))

                def sc(i):
                    t = tb[i % 2]
                    CP("vector", hT[:, :, i * 128:(i + 1) * 128], t[:].rearrange("p (k c) -> p k c", k=KC), [t], [hT])

                for step in range(NT + 2):
                    if step < NT:
                        sa(step)
                    if 0 <= step - 1 < NT:
                        sb_(step - 1)
                    if 0 <= step - 2 < NT:
                        sc(step - 2)
                P.emit()

        def proj_tok(i, W, c0, n, kchunks=KC, src=None, src_k0=0):
            src = src or hT
            p = next_f()
            for k in range(kchunks):
                MM(p[:, 0:n], src[:, src_k0 + k, i * 128:(i + 1) * 128], W[:, k, c0:c0 + n], k == 0, k == kchunks - 1,
                   [src, W], [p], inc=(k == kchunks - 1))
            return p

        def proj_feat(p, c0, n, W, wc0, M, kchunks=KC, src=None, src_k0=0):
            src = src or hT
            for k in range(kchunks):
                MM(p[0:M, 0:n], W[:, k, wc0:wc0 + M], src[:, src_k0 + k, c0:c0 + n], k == 0, k == kchunks - 1,
                   [src, W], [p], inc=(k == kchunks - 1))

        def epilogue(l, b, zoff):
            with ExitStack() as st:
                Wz = P.sb(st, "Wz", [128, KC, 512], BF16)
                Wo = P.sb(st, "Wo", [128, 4, D], BF16)
                Wg = P.sb(st, "Wg", [128, KC, D], BF16)
                Wout = P.sb(st, "Wout", [128, KC, D], BF16)
                bgt = P.sb(st, "bgt", [128, 8], F32)
                G = [P.sb(st, "G%d" % j, [128, 512], BF16) for j in range(2)]
                ogg = [P.sb(st, "ogg%d" % j, [128, 512], BF16) for j in range(3)]
                oggT = P.sb(st, "oggT", [128, 4, 512], BF16)
                sg = [P.sb(st, "sg%d" % j, [128, 512], F32) for j in range(2)]
                mT = P.sb(st, "mT", [128, KC, 512], BF16)
                load_w(Wz, w_in[l], KC, zoff, zoff + 512)
                load_w(Wo, w_o[b][l], 4, 0, D)
                load_w(Wg, w_in[l], KC, O_G + b * D, O_G + (b + 1) * D)
                load_w(Wout, w_out[l], KC, 0, D)
                DMA("sync", bgt[:], bg[l, b, :, :], writes=[bgt])
                cnt = 0
                for (c0, n) in CHUNKS:
                    tiles = list(range(c0 // 128, (c0 + n) // 128))
                    pzs = [proj_tok(i, Wz, 0, 512) for i in tiles]
                    for oc in range(2):
                        proj_feat(fb[4 + oc], c0, n, Wg, oc * 128, 128)
                    for j, i in enumerate(tiles):
                        pz = pzs[j]
                        g_, o_ = G[cnt % 2], ogg[cnt % 3]
                        ACT(g_[:], pz[:, :], AF.Silu, [pz], [g_])
                        TT("vector", o_[:], og[:, i, :], g_[:], ALU.mult, [og, g_], [o_])
                        cnt += 1
                        t = tb[j % 2]
                        for c in range(4):
                            TR(t[:, c * 128:(c + 1) * 128], o_[:, c * 128:(c + 1) * 128], [o_], [t], inc=(c == 3))
                        CP("vector", oggT[:, :, j * 128:(j + 1) * 128], t[:, 0:512].rearrange("p (c q) -> p c q", c=4), [t], [oggT])
                    for oc in range(8):
                        pg = fb[4 + oc % 2]
                        if oc >= 2:
                            proj_feat(pg, c0, n, Wg, oc * 128, 128)
                        py = next_f()
                        for c in range(4):
                            MM(py[:, 0:n], Wo[:, c, oc * 128:(oc + 1) * 128], oggT[:, c, 0:n], c == 0, c == 3, [Wo, oggT], [py], inc=(c == 3))
                        s_ = sg[oc % 2]
                        ACT(s_[:, 0:n], pg[:, 0:n], AF.Sigmoid, [pg, bgt], [s_], bias=bgt[:, oc:oc + 1])
                        TT("vector", mT[:, oc, 0:n], s_[:, 0:n], py[:, 0:n], ALU.mult, [s_, py], [mT])
                    for j, i in enumerate(tiles):
                        for half in range(2):
                            po = next_f()
                            for k in range(KC):
                                MM(po[:, :], mT[:, k, j * 128:(j + 1) * 128], Wout[:, k, half * 512:(half + 1) * 512], k == 0, k == KC - 1,
                                   [mT, Wout], [po], inc=(k == KC - 1))
                            TT("vector", X[:, i, half * 512:(half + 1) * 512], X[:, i, half * 512:(half + 1) * 512], po[:, :], ALU.add, [X, po], [X])
                P.emit()

        def branch_sb(l):
            with ExitStack() as bst:
                V = P.sb(bst, "Vsb", [128, NT, 512], BF16)
                Wqk2 = [P.sb(bst, "Wqk%d" % j, [128, KC, 256], BF16) for j in range(2)]

                def load_pair_w(pr_):
                    load_w(Wqk2[pr_ % 2], w_in[l], KC, O_SBQ + pr_ * 128, O_SBQ + (pr_ + 1) * 128, dst_c0=0)
                    load_w(Wqk2[pr_ % 2], w_in[l], KC, O_SBK + pr_ * 128, O_SBK + (pr_ + 1) * 128, dst_c0=128)
                with ExitStack() as st:
                    Wv = P.sb(st, "Wv", [128, KC, 512], BF16)
                    load_w(Wv, w_in[l], KC, O_SBV, O_SBV + 512)
                    load_pair_w(0)
                    load_pair_w(1)
                    for i in range(NT):
                        p = proj_tok(i, Wv, 0, 512)
                        CP("scalar" if i % 2 else "vector", V[:, i, :], p[:, :], [p], [V])
                    P.emit()
                with ExitStack() as st:
                    qT2 = [P.sb(st, "qT%d" % j, [128, L], BF16) for j in range(2)]
                    kT2 = [P.sb(st, "kT%d" % j, [128, L], BF16) for j in range(2)]
                    NE, NSP, NC = 4, 3, 3
                    ez = [P.sb(st, "ez%d" % j, [128, 512], F32) for j in range(NE)]
                    sp = [P.sb(st, "sp%d" % j, [128, 516], F32) for j in range(NSP)]
                    Cb = [P.sb(st, "Cb%d" % j, [128, 512], F32) for j in range(NC)]
                    ctot = [P.sb(st, "ctot%d" % j, [128, 1], F32) for j in range(3)]
                    for j in range(NSP):
                        MEMSET("vector", sp[j][:], 0.0, [sp[j]])
                    wb = [P.sb(st, "wb%d" % j, [128, 512], BF16) for j in range(2)]
                    wT = [P.sb(st, "wT%d" % j, [128, 512], BF16) for j in range(2)]
                    cn = [P.sb(st, "cn%d" % j, [128, 1], F32) for j in range(3)]
                    def proj_piece(pr_, idx):
                        c0, n = CHUNKS[idx // 2]
                        p = next_f(4, 6)
                        if idx % 2 == 0:
                            proj_feat(p, c0, n, Wqk2[pr_ % 2], 0, 128)
                            CP("scalar", qT2[pr_ % 2][:, c0:c0 + n], p[:, 0:n], [p], [qT2[pr_ % 2]])
                        else:
                            proj_feat(p, c0, n, Wqk2[pr_ % 2], 128, 128)
                            CP("vector", kT2[pr_ % 2][:, c0:c0 + n], p[:, 0:n], [p], [kT2[pr_ % 2]])

                    for idx in range(10):
                        proj_piece(0, idx)
                    for _once in (0,):
                        items = []
                        for pr in range(4):
                            for hh in range(2):
                                for i in range(NT):
                                    nk = (i + 1) * 128
                                    chs = [(k0, min(512, nk - k0)) for k0 in range(0, nk, 512)][::-1]
                                    for ci, (k0, n) in enumerate(chs):
                                        items.append((pr, hh, i, ci, k0, n, ci == len(chs) - 1))
                        N = len(items)
                        NPP = N // 4
                        obank = {}
                        ocnt = [0]

                        def st_mm(j):
                            pr, hh, i, ci, k0, n, last = items[j]
                            r0 = 64 * hh
                            z = fb[j % 2]
                            q_, k_ = qT2[pr % 2], kT2[pr % 2]
                            MM(z[:, 0:n], q_[r0:r0 + 64, i * 128:(i + 1) * 128], k_[r0:r0 + 64, k0:k0 + n], True, True, [q_, k_], [z])

                        def st_expz(j):
                            pr, hh, i, ci, k0, n, last = items[j]
                            e_ = ez[j % NE]
                            ACT(e_[:, 0:n], fb[j % 2][:, 0:n], AF.Exp, [fb[j % 2]], [e_], scale=0.125)
                            if ci == 0:
                                TT("gpsimd", e_[:, n - 128:n], e_[:, n - 128:n], tri[:], ALU.mult, [e_, tri], [e_])

                        def st_ln(j):
                            pr, hh, i, ci, k0, n, last = items[j]
                            ACT(sp[j % NSP][:, 1:n + 1], ez[j % NE][:, 0:n], AF.Ln, [ez[j % NE]], [sp[j % NSP], ctot[j % 3]], bias=1.0,
                                accum_out=ctot[j % 3][:])

                        def st_scan(j):
                            pr, hh, i, ci, k0, n, last = items[j]
                            s_, c_ = sp[j % NSP], Cb[j % NC]
                            P.op("vector", lambda e: e.tensor_tensor_scan(out=c_[:, 0:n], data0=s_[:, 0:n], data1=s_[:, 0:n],
                                                                           initial=0.0, op0=ALU.add, op1=ALU.max), [s_], [c_])

                        def st_cn(j):
                            pr, hh, i, ci, k0, n, last = items[j]
                            if ci == 0:
                                TS("vector", cn[j % 3][:], ctot[j % 3][:], -1.0, None, ALU.mult, None, [ctot[j % 3]], [cn[j % 3]])
                            else:
                                TT("vector", cn[j % 3][:], cn[(j - 1) % 3][:], ctot[j % 3][:], ALU.subtract, [cn[(j - 1) % 3], ctot[j % 3]], [cn[j % 3]])

                        def st_expt(j):
                            pr, hh, i, ci, k0, n, last = items[j]
                            ACT(Cb[j % NC][:, 0:n], Cb[j % NC][:, 0:n], AF.Exp, [Cb[j % NC], cn[j % 3]], [Cb[j % NC]], bias=cn[j % 3][:, 0:1])

                        def st_mult(j):
                            pr, hh, i, ci, k0, n, last = items[j]
                            TT("gpsimd", wb[j % 2][:, 0:n], ez[j % NE][:, 0:n], Cb[j % NC][:, 0:n], ALU.mult, [ez[j % NE], Cb[j % NC]], [wb[j % 2]])

                        def st_tr(j):
                            pr, hh, i, ci, k0, n, last = items[j]
                            t = tb[j % 2]
                            nb = n // 128
                            for jb in range(nb):
                                TR(t[:, jb * 128:(jb + 1) * 128], wb[j % 2][:, jb * 128:(jb + 1) * 128], [wb[j % 2]], [t], inc=(jb == nb - 1))

                        def st_evac(j):
                            pr, hh, i, ci, k0, n, last = items[j]
                            CP("scalar" if j % 2 == 0 else "vector", wT[j % 2][:, 0:n], tb[j % 2][:, 0:n], [tb[j % 2]], [wT[j % 2]])

                        def st_pv(j):
                            pr, hh, i, ci, k0, n, last = items[j]
                            h = 2 * pr + hh
                            nb = n // 128
                            if ci == 0:
                                obank[(pr, hh, i)] = fb[2 + ocnt[0] % 2]
                                ocnt[0] += 1
                            O = obank[(pr, hh, i)]
                            for jb in range(nb):
                                kb = k0 // 128 + jb
                                MM(O[:, 0:64], wT[j % 2][:, jb * 128:(jb + 1) * 128], V[:, kb, h * 64:(h + 1) * 64],
                                   ci == 0 and jb == 0, last and jb == nb - 1, [wT[j % 2], V], [O], inc=(jb == nb - 1))
                            if last:
                                CP("vector", og[:, i, h * 64:(h + 1) * 64], O[:, 0:64], [O], [og])

                        sched = [(st_mm, 0), (st_expz, 1), (st_expt, 3), (st_ln, 1), (st_evac, 6), (st_cn, 2), (st_scan, 2),
                                 (st_mult, 4), (st_tr, 5), (st_pv, 7)]
                        for step in range(N + 7):
                            for fn, off in sched:
                                if 0 <= step - off < N:
                                    fn(step - off)
                            pcur, rel = step // NPP, step % NPP
                            if pcur + 1 < 4 and rel >= 8 and (rel - 8) % 8 == 0 and (rel - 8) // 8 < 10:
                                proj_piece(pcur + 1, (rel - 8) // 8)
                            if pcur + 2 < 4 and rel == 8 + 8 * 10:
                                load_pair_w(pcur + 2)
                    P.emit()
            epilogue(l, 0, O_SBZ)

        def softmax_attn(st, units, dv1, scale, finalize, PT=None):
            NS = 5
            zbanks = [fb[0], fb[1], fb[6], fb[7]]
            if PT is None:
                PT = [P.sb(st, "PT%d" % j, [128, 512], BF16) for j in range(NS)]
            items = []
            for i in range(NT):
                kbs = list(range(0, min(i + 2, NT)))
                groups = [kbs[a:a + 4] for a in range(0, len(kbs), 4)]
                for u in range(len(units)):
                    for gi, g in enumerate(groups):
                        items.append((i, u, g, gi == 0, gi == len(groups) - 1))
            N = len(items)
            nu = len(units)

            def s1(j):
                i, u, g, first, last = items[j]
                QTb, KTb, r0, nr, vfn = units[u]
                z = zbanks[j % 4]
                for a, kb in enumerate(g):
                    msk = kb >= i
                    MM(z[:, a * 128:(a + 1) * 128], KTb[r0:r0 + nr, kb * 128:(kb + 1) * 128], QTb[r0:r0 + nr, i * 128:(i + 1) * 128],
                       True, not msk, [QTb, KTb], [z], inc=(a == len(g) - 1 and not msk))
                    if msk:
                        mo = 0 if kb == i else 2
                        MM(z[:, a * 128:(a + 1) * 128], mkt[mo][:, :], mkt[mo + 1][:, :], False, True, [mkt[mo], mkt[mo + 1]], [z],
                           inc=(a == len(g) - 1))

            def s2(j):
                i, u, g, first, last = items[j]
                z = zbanks[j % 4]
                s = j % NS
                n = len(g) * 128
                ACT(PT[s][:, 0:n], z[:, 0:n], AF.Exp, [z], [PT[s]], scale=scale)

            def s3(j):
                i, u, g, first, last = items[j]
                QTb, KTb, r0, nr, vfn = units[u]
                s = j % NS
                O = fb[2 + u] if nu > 1 else fb[2 + i % 2]
                for a, kb in enumerate(g):
                    MM(O[:, 0:dv1], PT[s][:, a * 128:(a + 1) * 128], vfn(kb), first and a == 0, last and a == len(g) - 1,
                       [PT[s]], [O], inc=(a == len(g) - 1))
                if last and u == nu - 1:
                    finalize(i, [fb[2 + uu] for uu in range(nu)] if nu > 1 else [O])

            for step in range(N + 2):
                if step < N:
                    s1(step)
                if 0 <= step - 1 < N:
                    s2(step - 1)
                if 0 <= step - 2 < N:
                    s3(step - 2)

        def diff_attn(st, QTb, KTb, vfn, finalize, PT=None):
            NS = 3
            if PT is None:
                PT = [P.sb(st, "PTd%d" % j, [128, 1024], BF16) for j in range(NS)]
            items = []
            for i in range(NT):
                kbs = list(range(0, min(i + 2, NT)))
                groups = [kbs[a:a + 4] for a in range(0, len(kbs), 4)]
                for gi, g in enumerate(groups):
                    items.append((i, g, gi == 0, gi == len(groups) - 1))
            N = len(items)

            def s1(j):
                i, g, first, last = items[j]
                zt = z2[j % 2]
                for a, kb in enumerate(g):
                    msk = kb >= i
                    for u in range(2):
                        r0 = 64 * u
                        MM(zt[:, u * 512 + a * 128:u * 512 + (a + 1) * 128], KTb[r0:r0 + 64, kb * 128:(kb + 1) * 128],
                           QTb[r0:r0 + 64, i * 128:(i + 1) * 128], True, not msk, [QTb, KTb], [zt],
                           inc=(a == len(g) - 1 and u == 1 and not msk))
                    if msk:
                        mo = 0 if kb == i else 2
                        for u in range(2):
                            MM(zt[:, u * 512 + a * 128:u * 512 + (a + 1) * 128], mkt[mo][:, :], mkt[mo + 1][:, :], False, True,
                               [mkt[mo], mkt[mo + 1]], [zt], inc=(a == len(g) - 1 and u == 1))

            def s2(j):
                i, g, first, last = items[j]
                zt = z2[j % 2]
                p_ = PT[j % NS]
                n = len(g) * 128
                ACT(p_[:].rearrange("p (u c) -> p u c", u=2)[:, :, 0:n], zt[:].rearrange("p (u c) -> p u c", u=2)[:, :, 0:n],
                    AF.Exp, [zt], [p_], scale=0.125)

            def s3(j):
                i, g, first, last = items[j]
                p_ = PT[j % NS]
                Os = [fb[4 + 2 * (i % 2)], fb[5 + 2 * (i % 2)]]
                for a, kb in enumerate(g):
                    for u in range(2):
                        MM(Os[u][:, 0:129], p_[:, u * 512 + a * 128:u * 512 + (a + 1) * 128], vfn(kb),
                           first and a == 0, last and a == len(g) - 1, [p_], [Os[u]], inc=(a == len(g) - 1))
                if last:
                    finalize(i, Os)

            for step in range(N + 2):
                if step < N:
                    s1(step)
                if 0 <= step - 1 < N:
                    s2(step - 1)
                if 0 <= step - 2 < N:
                    s3(step - 2)

        def branch_mla(l):
            with ExitStack() as bst:
                cnT = P.sb(bst, "cnT", [128, 5, L], BF16)
                Va = P.sb(bst, "Va", [128, NT, 8, 68], BF16)
                KT = P.sb(bst, "KTm", [128, L], BF16)
                with ExitStack() as st:
                    W = P.sb(st, "Wm", [128, KC, 704], BF16)
                    Wv = P.sb(st, "Wukvv", [128, 2, 512], BF16)
                    gq = P.sb(st, "gq", [128, 640], F32)
                    junk = P.sb(st, "junkm", [128, 384], BF16)
                    ss = P.sb(st, "ssm", [128, 2 * NT], F32)
                    rs = P.sb(st, "rsm", [128, 2 * NT], F32)
                    cb = [P.sb(st, "cb%d" % j, [128, 640], BF16) for j in range(2)]
                    t1 = P.sb(st, "t1m", [128, 512], F32)
                    t2 = P.sb(st, "t2m", [128, 512], F32)
                    load_w(W, w_in[l], KC, O_CQ, O_CQ + 672)
                    load_w(W, w_x[l], KC, 1024, 1056, dst_c0=672)
                    load_w(Wv, ukvv[l], 2, 0, 512)
                    DMA("sync", gq[:, 0:384], cq_g[l:l + 1, :].to_broadcast([128, 384]), writes=[gq])
                    DMA("sync", gq[:, 384:640], ckv_g[l:l + 1, :].to_broadcast([128, 256]), writes=[gq])
                    MEMSET("gpsimd", Va[:].rearrange("p a b c -> p (a b c)"), 1.0, [Va])
                    ssb = [Buf("ssm_%d" % i, ss.t) for i in range(2 * NT)]
                    rsb = [Buf("rsm_%d" % i, rs.t) for i in range(2 * NT)]
                    cb3 = cb + [P.sb(st, "cb2", [128, 640], BF16)]
                    junk2 = [junk, P.sb(st, "junkm2", [128, 384], BF16)]

                    def sa(i):
                        c = cb3[i % 3]
                        for part, (wc0, n, dc0) in enumerate(((0, 384, 0), (384, 256, 384))):
                            p = proj_tok(i, W, wc0, n)
                            col = 2 * i + part
                            ACT(junk2[part][:, 0:n], p[:, 0:n], AF.Square, [p], [junk2[part], ssb[col]], accum_out=ss[:, col:col + 1])
                            RSTD(rs[:, col:col + 1], ss[:, col:col + 1], n, [ssb[col]], [rsb[col]])
                            STT("vector", c[:, dc0:dc0 + n], p[:, 0:n], rs[:, col:col + 1], gq[:, dc0:dc0 + n], ALU.mult, ALU.mult, [p, rsb[col], gq], [c])

                    def sb_(i):
                        c, t = cb3[i % 3], tb[i % 2]
                        for k in range(5):
                            TR(t[:, k * 128:(k + 1) * 128], c[:, k * 128:(k + 1) * 128], [c], [t], inc=(k == 4))

                    def sc(i):
                        t = tb[i % 2]
                        CP("vector" if i % 2 else "scalar", cnT[:, :, i * 128:(i + 1) * 128], t[:, 0:640].rearrange("p (k c) -> p k c", k=5), [t], [cnT])

                    for step in range(NT + 2):
                        if step < NT:
                            sa(step)
                        if 0 <= step - 1 < NT:
                            sb_(step - 1)
                        if 0 <= step - 2 < NT:
                            sc(step - 2)
                    for (c0, n) in CHUNKS:
                        pa = fb[4]
                        pb = fb[5]
                        proj_feat(pa, c0, n, W, 608, 64)
                        proj_feat(pb, c0, n, W, 640, 64)
                        TT("vector", t1[32:64, 0:n], pa[32:64, 0:n], ropec[32:64, c0:c0 + n], ALU.mult, [pa, ropec], [t1])
                        TT("vector", t2[32:64, 0:n], pb[32:64, 0:n], ropes[32:64, c0:c0 + n], ALU.mult, [pb, ropes], [t2])
                        TT("vector", KT[32:64, c0:c0 + n], t1[32:64, 0:n], t2[32:64, 0:n], ALU.add, [t1, t2], [KT])
                    for i in range(NT):
                        p = proj_tok(i, Wv, 0, 512, kchunks=2, src=cnT, src_k0=3)
                        CP("scalar" if i % 2 else "vector", Va[:, i, :, 0:64], p[:, :].rearrange("p (h d) -> p h d", h=8), [p], [Va])
                    P.emit()
                with ExitStack() as st:
                    QT = P.sb(st, "QTm", [128, L], BF16)
                    Wa = P.sb(st, "Wuqa", [128, 3, 768], BF16)
                    Wb = P.sb(st, "Wuqb", [128, 3, 768], BF16)
                    Wk = P.sb(st, "Wukn", [128, 2, 768], BF16)
                    t1 = P.sb(st, "t1q", [128, 512], F32)
                    t2 = P.sb(st, "t2q", [128, 512], F32)
                    rcp = P.sb(st, "rcp", [128, 1], F32)
                    PTm = [P.sb(st, "PTm%d" % j, [128, 512], BF16) for j in range(5)]
                    load_w(Wa, uqa[l], 3, 0, 768)
                    load_w(Wb, uqb[l], 3, 0, 768)
                    load_w(Wk, ukn[l], 2, 0, 768)
                    for h in range(8):
                        for cidx, (c0, n) in enumerate(CHUNKS):
                            pa, pb = fb[4 + 2 * (cidx % 2)], fb[5 + 2 * (cidx % 2)]
                            proj_feat(pa, c0, n, Wa, h * 96, 96, kchunks=3, src=cnT)
                            proj_feat(pb, c0, n, Wb, h * 96, 96, kchunks=3, src=cnT)
                            CP("scalar", QT[0:32, c0:c0 + n], pa[0:32, 0:n], [pa], [QT])
                            CP("scalar", QT[64:96, c0:c0 + n], pa[64:96, 0:n], [pa], [QT])
                            TT("vector", t1[32:64, 0:n], pa[32:64, 0:n], ropec[32:64, c0:c0 + n], ALU.mult, [pa, ropec], [t1])
                            TT("vector", t2[32:64, 0:n], pb[32:64, 0:n], ropes[32:64, c0:c0 + n], ALU.mult, [pb, ropes], [t2])
                            TT("vector", QT[32:64, c0:c0 + n], t1[32:64, 0:n], t2[32:64, 0:n], ALU.add, [t1, t2], [QT])
                            pk = pb
                            proj_feat(pk, c0, n, Wk, h * 96, 96, kchunks=2, src=cnT, src_k0=3)
                            CP("scalar", KT[0:32, c0:c0 + n], pk[0:32, 0:n], [pk], [KT])
                            CP("vector", KT[64:96, c0:c0 + n], pk[64:96, 0:n], [pk], [KT])

                        def fin(i, Os, h=h):
                            O = Os[0]
                            P.op("vector", lambda e: e.reciprocal(out=rcp[:], in_=O[:, 64:65]), [O], [rcp])
                            TS("vector", og[:, i, h * 64:(h + 1) * 64], O[:, 0:64], rcp[:, 0:1], None, ALU.mult, None, [O, rcp], [og])

                        softmax_attn(st, [(QT, KT, 0, 96, (lambda kb, h=h: Va[:, kb, h, 0:65]))], 65, 1.0 / math.sqrt(96.0), fin, PT=PTm)
                    P.emit()
            epilogue(l, 1, O_MZ)

        def branch_diff(l):
            lam_init = 0.8 - 0.6 * math.exp(-0.3 * l)
            with ExitStack() as bst:
                Vd = P.sb(bst, "Vd", [128, NT, 4, 132], BF16)
                lam = P.sb(bst, "lam", [128, 1], F32)
                gd = P.sb(bst, "gd", [128, 128], F32)
                Wq2 = [P.sb(bst, "Wdq%d" % j, [128, KC, 512], BF16) for j in range(2)]

                def load_head_w(h):
                    Wq_ = Wq2[h % 2]
                    load_w(Wq_, w_in[l], KC, O_DQ + h * 128, O_DQ + (h + 1) * 128, dst_c0=0)
                    load_w(Wq_, w_x[l], KC, h * 128, (h + 1) * 128, dst_c0=128)
                    load_w(Wq_, w_in[l], KC, O_DK + h * 128, O_DK + (h + 1) * 128, dst_c0=256)
                    load_w(Wq_, w_x[l], KC, 512 + h * 128, 512 + (h + 1) * 128, dst_c0=384)
                with ExitStack() as st:
                    Wv = P.sb(st, "Wdv", [128, KC, 512], BF16)
                    dl = P.sb(st, "dl", [128, 256], F32)
                    pr_ = P.sb(st, "prd", [128, 128], F32)
                    sm = P.sb(st, "smd", [128, 2], F32)
                    load_w(Wv, w_in[l], KC, O_DV, O_DV + 512)
                    load_head_w(0)
                    DMA("sync", dl[:], dlam[l:l + 1, :].to_broadcast([128, 256]), writes=[dl])
                    DMA("sync", gd[:], dng[l:l + 1, :].to_broadcast([128, 128]), writes=[gd])
                    dl3 = dl[:].rearrange("p (a b) -> p a b", a=2)
                    TT("vector", pr_[:].rearrange("p (a b) -> p a b", a=2), dl3[:, :, 0:64], dl3[:, :, 64:128], ALU.mult, [dl], [pr_])
                    P.op("vector", lambda e: e.reduce_sum(out=sm[:], in_=pr_[:].rearrange("p (a b) -> p a b", a=2), axis=mybir.AxisListType.X), [pr_], [sm])
                    ACT(sm[:], sm[:], AF.Exp, [sm], [sm])
                    TT("vector", lam[:], sm[:, 0:1], sm[:, 1:2], ALU.subtract, [sm], [lam])
                    TS("vector", lam[:], lam[:], lam_init, None, ALU.add, None, [lam], [lam])
                    MEMSET("gpsimd", Vd[:].rearrange("p a b c -> p (a b c)"), 1.0, [Vd])
                    for i in range(NT):
                        p = proj_tok(i, Wv, 0, 512)
                        CP("scalar" if i % 2 else "vector", Vd[:, i, :, 0:128], p[:, :].rearrange("p (h d) -> p h d", h=4), [p], [Vd])
                    P.emit()
                with ExitStack() as st:
                    PTd = [P.sb(st, "PTd%d" % j, [128, 1024], BF16) for j in range(3)]
                    QT = P.sb(st, "QTd", [128, L], BF16)
                    KT = P.sb(st, "KTd", [128, L], BF16)
                    t1 = P.sb(st, "t1d", [128, 512], F32)
                    t2 = P.sb(st, "t2d", [128, 512], F32)
                    rc = P.sb(st, "rcd", [128, 2], F32)
                    tm = P.sb(st, "tmd", [128, 128], F32)
                    oc_ = P.sb(st, "ocd", [128, 128], F32)
                    jk = P.sb(st, "jkd", [128, 128], BF16)
                    ssd = P.sb(st, "ssd", [128, 1], F32)
                    rsd = P.sb(st, "rsd", [128, 1], F32)
                    for h in range(4):
                        cc = 0
                        Wq = Wq2[h % 2]
                        if h + 1 < 4:
                            load_head_w(h + 1)
                        for (dst, wc) in ((QT, 0), (KT, 256)):
                            for (c0, n) in CHUNKS:
                                pa, pb = fb[4 + 2 * (cc % 2)], fb[5 + 2 * (cc % 2)]
                                cc += 1
                                proj_feat(pa, c0, n, Wq, wc, 128)
                                proj_feat(pb, c0, n, Wq, wc + 128, 128)
                                TT("vector", t1[:, 0:n], pa[:, 0:n], ropec[:, c0:c0 + n], ALU.mult, [pa, ropec], [t1])
                                TT("vector", t2[:, 0:n], pb[:, 0:n], ropes[:, c0:c0 + n], ALU.mult, [pb, ropes], [t2])
                                TT("vector", dst[:, c0:c0 + n], t1[:, 0:n], t2[:, 0:n], ALU.add, [t1, t2], [dst])
                                CP("scalar", dst[32:64, c0:c0 + n], pa[32:64, 0:n], [pa], [dst])

                        def fin(i, Os, h=h):
                            O0, O1 = Os
                            P.op("vector", lambda e: e.reciprocal(out=rc[:, 0:1], in_=O0[:, 128:129]), [O0], [rc])
                            P.op("vector", lambda e: e.reciprocal(out=rc[:, 1:2], in_=O1[:, 128:129]), [O1], [rc])
                            TT("vector", rc[:, 1:2], rc[:, 1:2], lam[:, 0:1], ALU.mult, [rc, lam], [rc])
                            TS("vector", tm[:], O1[:, 0:128], rc[:, 1:2], None, ALU.mult, None, [O1, rc], [tm])
                            STT("vector", oc_[:], O0[:, 0:128], rc[:, 0:1], tm[:], ALU.mult, ALU.subtract, [O0, rc, tm], [oc_])
                            MEMSET("vector", ssd[:], 0.0, [ssd])
                            ACT(jk[:], oc_[:], AF.Square, [oc_], [jk, ssd], accum_out=ssd[:, 0:1])
                            RSTD(rsd[:], ssd[:], 128, [ssd], [rsd], mult=1.0 - lam_init)
                            STT("vector", og[:, i, h * 128:(h + 1) * 128], oc_[:], rsd[:, 0:1], gd[:], ALU.mult, ALU.mult, [oc_, rsd, gd], [og])

                        diff_attn(st, QT, KT, (lambda kb, h=h: Vd[:, kb, h, 0:129]), fin, PT=PTd)
                    P.emit()
            epilogue(l, 2, O_DZ)

        for l in range(nlayers):
            phase_norm(l)
            if 0 in branches:
                branch_sb(l)
            if 1 in branches:
                branch_mla(l)
            if 2 in branches:
                branch_diff(l)

        with ExitStack() as st:
            grep = P.sb(st, "grepf", [128, D], F32)
            junk = P.sb(st, "junkf", [128, D], BF16)
            ss = P.sb(st, "ssf", [128, NT], F32)
            rs = P.sb(st, "rsf", [128, NT], F32)
            yo = [P.sb(st, "yo%d" % j, [128, D], F32) for j in range(2)]
            bcast_load(grep, final_g[0:1, :], D)
            MEMSET("vector", ss[:], 0.0, [ss])
            for i in range(NT):
                o = yo[i % 2]
                if final_norm:
                    ACT(junk[:], X[:, i, :], AF.Square, [X], [junk, ss], accum_out=ss[:, i:i + 1])
                    RSTD(rs[:, i:i + 1], ss[:, i:i + 1], D, [ss], [rs])
                    STT("vector", o[:], X[:, i, :], rs[:, i:i + 1], grep[:], ALU.mult, ALU.mult, [X, rs, grep], [o])
                else:
                    CP("vector", o[:], X[:, i, :], [X], [o])
                p_lo = NMETA if i == 0 else 0
                p_hi = NMETA if i == NT - 1 else 128
                s0 = 128 * i - NMETA + p_lo
                DMA("sync", y[s0:s0 + (p_hi - p_lo), :], o[p_lo:p_hi, :], reads=[o])
            P.wait_all("sync", yo)
            P.emit()
    return nc


_CACHE = {}


def _consts():
    if "c" not in _CACHE:
        C, Sg = _rope_tables()
        tri, m01, ident, mk = _masks()
        _CACHE["c"] = {"c_ropec": C, "c_ropes": Sg, "c_tri": tri, "c_m01": m01, "c_ident": ident, "c_mk": mk}
    return _CACHE["c"]


def make_in_maps(inp):
    x = np.asarray(inp["x"], np.float32)
    B = x.shape[0]
    meta = np.asarray(inp["meta_tokens"], np.float32)
    lay = _host_layouts({k: np.asarray(v) for k, v in inp.items()})
    shared = dict(_consts())
    shared.update(lay)
    for k in ("norm_g", "w_in", "mla_cq_g", "mla_ckv_g", "diff_norm_g", "w_o_sb", "w_o_mla", "w_o_diff", "w_out"):
        shared[k] = np.ascontiguousarray(np.asarray(inp[k], np.float32))
    shared["diff_lambda"] = np.ascontiguousarray(np.asarray(inp["diff_lambda"], np.float32).reshape(2, 256))
    shared["final_g"] = np.ascontiguousarray(np.asarray(inp["final_g"], np.float32).reshape(1, D))
    maps = []
    for b in range(B):
        h0 = np.concatenate([meta, x[b], np.zeros((L - NMETA - S, D), np.float32)], axis=0)
        m = dict(shared)
        m["h0"] = np.ascontiguousarray(h0)
        maps.append(m)
    return maps


def kernel(**inputs):
    maps = make_in_maps(inputs)
    if "nc" not in _CACHE:
        _CACHE["nc"] = build_nc()
    res = run_bass_kernel_spmd(_CACHE["nc"], maps, core_ids=list(range(len(maps))))
    return np.stack([np.asarray(r["y"], np.float32) for r in res.results], axis=0)
```

```python
import math
import numpy as np
import ml_dtypes
from contextlib import ExitStack
import concourse.bass as bass
import concourse.mybir as mybir
from concourse.bass_utils import run_bass_kernel_spmd

F32 = mybir.dt.float32
BF16 = mybir.dt.bfloat16
AF = mybir.ActivationFunctionType
ALU = mybir.AluOpType

D = 1024
S = 2048
NMETA = 16
NT = 17
L = NT * 128
KC = 8
EPS = 1e-6
THETA = 500000.0
CHUNKS = [(0, 512), (512, 512), (1024, 512), (1536, 512), (2048, 128)]


class Buf:
    __slots__ = ("name", "t", "w", "r", "dsem", "dcnt")

    def __init__(self, name, t=None):
        self.name = name
        self.t = t
        self.w = []
        self.r = []
        self.dsem = None
        self.dcnt = 0

    def __getitem__(self, idx):
        return self.t[idx]


class Prog:
    ENGS = ("tensor", "vector", "scalar", "gpsimd", "sync")

    def __init__(self, nc, stack):
        self.nc = nc
        self.stack = stack
        self.sems = {}
        self.cnt = {e: 0 for e in self.ENGS}
        self.seen = {e: {} for e in self.ENGS}
        self.q = {e: [] for e in self.ENGS}
        self.snaps = {}
        for e in self.ENGS:
            self._sem("E_" + e)
        self.nbuf = 0

    def _sem(self, key):
        if key not in self.sems:
            self.sems[key] = self.stack.enter_context(self.nc.semaphore(key))
        return key

    def sb(self, st, name, shape, dt):
        self.uid = getattr(self, "uid", 0) + 1
        name = "%s_%d" % (name, self.uid)
        t = st.enter_context(self.nc.sbuf_tensor(name, list(shape), dt))
        return Buf(name, t)

    def ps(self, st, name, shape, dt=F32):
        t = st.enter_context(self.nc.psum_tensor(name, list(shape), dt))
        return Buf(name, t)

    def _waits(self, eng, reads, writes):
        own = "E_" + eng
        need = {}
        for b in reads:
            for (k, v) in b.w:
                if k == own and (eng == "tensor" or v > self.cnt[eng]):
                    continue
                if need.get(k, 0) < v:
                    need[k] = v
        for b in writes:
            for (k, v) in b.w:
                if k == own and (eng == "tensor" or v > self.cnt[eng]):
                    continue
                if need.get(k, 0) < v:
                    need[k] = v
            for (k, v) in b.r:
                if k == own and (eng == "tensor" or v > self.cnt[eng]):
                    continue
                if need.get(k, 0) < v:
                    need[k] = v
        out = []
        seen = self.seen[eng]
        snaps = self.snaps
        for k, v in sorted(need.items(), key=lambda kv: 0 if kv[0].startswith("E_") else 1):
            if seen.get(k, 0) < v:
                seen[k] = v
                out.append((k, v))
                sn = snaps.get((k, v))
                if sn:
                    for k2, v2 in sn.items():
                        if seen.get(k2, 0) < v2:
                            seen[k2] = v2
        return out

    def op(self, eng, fn, reads=(), writes=(), inc=True):
        waits = self._waits(eng, reads, writes)
        key = "E_" + eng
        val = self.cnt[eng] + 1
        if inc:
            self.cnt[eng] = val
        ev = (key, val)
        if inc:
            self.snaps[ev] = dict(self.seen[eng])
        for b in writes:
            b.w = [ev]
            b.r = []
        for b in reads:
            b.r = [e for e in b.r if e[0] != key] + [ev]
        self.q[eng].append((waits, fn, [(key, 1)] if inc else []))

    def dma(self, eng, fn, reads=(), writes=(), sem_buf=None):
        waits = self._waits(eng, reads, writes)
        sb = sem_buf or (writes[0] if writes else reads[0])
        if sb.dsem is None:
            sb.dsem = self._sem("D_%d" % self.nbuf)
            self.nbuf += 1
        sb.dcnt += 16
        ev = (sb.dsem, sb.dcnt)
        self.snaps[ev] = dict(self.seen[eng])
        for b in writes:
            b.w = [e for e in b.w if e[0] != sb.dsem and e[0].startswith("D_")] + [ev]
            b.r = []
        for b in reads:
            b.r = [e for e in b.r if e[0] != sb.dsem] + [ev]
        self.q[eng].append((waits, fn, [(sb.dsem, 16)]))

    def wait_all(self, eng, bufs):
        waits = self._waits(eng, (), bufs)
        self.q[eng].append((waits, None, []))

    def emit(self):
        nc = self.nc
        qs = self.q
        self.q = {e: [] for e in self.ENGS}
        sems = self.sems
        with nc.Block() as block:
            def mk(ename):
                items = qs[ename]

                def body(e):
                    for waits, fn, incs in items:
                        if fn is None:
                            for (k, v) in waits:
                                e.wait_ge(sems[k], v)
                            continue
                        for (k, v) in waits[1:]:
                            e.wait_ge(sems[k], v)
                        ins = fn(e)
                        if waits:
                            ins._wait_ge(sems[waits[0][0]], waits[0][1])
                        for (k, n) in incs:
                            ins.then_inc(sems[k], n)
                return body
            block.tensor(mk("tensor"))
            block.vector(mk("vector"))
            block.scalar(mk("scalar"))
            block.gpsimd(mk("gpsimd"))
            block.sync(mk("sync"))


def _rope_tables():
    pos = np.arange(L, dtype=np.float32)
    C = np.ones((128, L), np.float32)
    Sg = np.zeros((128, L), np.float32)
    inv_d = (np.float32(THETA) ** (-np.arange(0, 16, 2, dtype=np.float32) / np.float32(16))).astype(np.float32)
    ang_d = (pos[:, None] * inv_d[None, :]).astype(np.float32)
    cd, sd = np.cos(ang_d).astype(np.float32), np.sin(ang_d).astype(np.float32)
    for base in (0, 64):
        for r in range(16):
            C[base + r] = cd[:, r % 8]
            Sg[base + r] = -sd[:, r % 8] if r < 8 else sd[:, r % 8]
    inv_m = (np.float32(THETA) ** (-np.arange(0, 32, 2, dtype=np.float32) / np.float32(32))).astype(np.float32)
    ang_m = (pos[:, None] * inv_m[None, :]).astype(np.float32)
    cm, sm = np.cos(ang_m).astype(np.float32), np.sin(ang_m).astype(np.float32)
    for r in range(32):
        C[32 + r] = cm[:, r % 16]
        Sg[32 + r] = -sm[:, r % 16] if r < 16 else sm[:, r % 16]
    return C, Sg


def _masks():
    a = np.arange(128)
    tri = (a[None, :] < a[:, None]).astype(np.float32)
    cq = (a + 48) // 64
    m0 = (cq[:, None] <= cq[None, :])
    m1 = ((a[:, None] < 16) & (a[None, :] >= 80))
    m01 = np.concatenate([m0, m1], axis=1).astype(np.float32).astype(ml_dtypes.bfloat16)
    ident = np.eye(128, dtype=np.float32).astype(ml_dtypes.bfloat16)
    BIG = 30000.0
    mk = np.zeros((8, 128), np.float32)
    mk[0] = -BIG * (cq == 1); mk[1] = -BIG * (cq == 2)
    mk[2] = (cq < 1); mk[3] = (cq < 2)
    mk[4] = -BIG; mk[5] = BIG * (a < 16)
    mk[6] = 1.0; mk[7] = (a >= 80)
    mk = mk.astype(ml_dtypes.bfloat16)
    return tri, m01, ident, mk


O_SBQ, O_SBK, O_SBV, O_SBZ = 0, 512, 1024, 1536
O_CQ, O_CKV, O_KR, O_MZ = 2048, 2432, 2688, 2720
O_DQ, O_DK, O_DV, O_DZ = 3232, 3744, 4256, 4768
O_G = 5280


def _host_layouts(inp):
    w_in = inp["w_in"]
    swap64 = np.concatenate([np.arange(8, 16), np.arange(0, 8), np.arange(16, 64)])
    idx_d = np.concatenate([m * 64 + swap64 for m in range(8)])
    kr_sw = np.concatenate([np.arange(16, 32), np.arange(0, 16)])
    w_x = np.concatenate([w_in[:, :, O_DQ + idx_d], w_in[:, :, O_DK + idx_d], w_in[:, :, O_KR + kr_sw]], axis=2)
    uq = inp["mla_w_uq"]
    ia, ib = [], []
    for h in range(8):
        b = 96 * h
        ia += list(range(b, b + 32)) + list(range(b + 64, b + 96)) + list(range(b + 32, b + 64))
        ib += list(range(b, b + 32)) + list(range(b + 80, b + 96)) + list(range(b + 64, b + 80)) + list(range(b + 32, b + 64))
    uqa = uq[:, :, np.array(ia)]
    uqb = uq[:, :, np.array(ib)]
    ukv = inp["mla_w_ukv"]
    ikn, iv = [], []
    for h in range(8):
        b = 128 * h
        ikn += list(range(b, b + 32)) + list(range(b, b + 32)) + list(range(b + 32, b + 64))
        iv += list(range(b + 64, b + 128))
    ukn = ukv[:, :, np.array(ikn)]
    ukvv = ukv[:, :, np.array(iv)]
    bg = inp["b_gate"].reshape(2, 3, 8, 128).transpose(0, 1, 3, 2)
    return {
        "w_x": np.ascontiguousarray(w_x),
        "uqa": np.ascontiguousarray(uqa), "uqb": np.ascontiguousarray(uqb),
        "ukn": np.ascontiguousarray(ukn), "ukvv": np.ascontiguousarray(ukvv),
        "bg": np.ascontiguousarray(bg),
    }


def build_nc(nlayers=2, final_norm=True, branches=(0, 1, 2)):
    nc = bass.Bass("TRN2", target_bir_lowering=False)

    def din(name, shape, dt=F32):
        return nc.dram_tensor(name, list(shape), dt, kind="ExternalInput").ap()

    h0 = din("h0", [L, D])
    norm_g = din("norm_g", [2, D])
    w_in = din("w_in", [2, D, 8352])
    w_x = din("w_x", [2, D, 1056])
    bg = din("bg", [2, 3, 128, 8])
    cq_g = din("mla_cq_g", [2, 384])
    ckv_g = din("mla_ckv_g", [2, 256])
    uqa = din("uqa", [2, 384, 768])
    uqb = din("uqb", [2, 384, 768])
    ukn = din("ukn", [2, 256, 768])
    ukvv = din("ukvv", [2, 256, 512])
    dlam = din("diff_lambda", [2, 256])
    dng = din("diff_norm_g", [2, 128])
    w_o = [din("w_o_sb", [2, 512, D]), din("w_o_mla", [2, 512, D]), din("w_o_diff", [2, 512, D])]
    w_out = din("w_out", [2, D, D])
    final_g = din("final_g", [1, D])
    c_ropec = din("c_ropec", [128, L])
    c_ropes = din("c_ropes", [128, L])
    c_tri = din("c_tri", [128, 128])
    c_m01 = din("c_m01", [128, 256], BF16)
    c_ident = din("c_ident", [128, 128], BF16)
    c_mk = din("c_mk", [8, 128], BF16)
    y = nc.dram_tensor("y", [S, D], F32, kind="ExternalOutput").ap()

    with ExitStack() as top:
        P = Prog(nc, top)
        X = P.sb(top, "X", [128, NT, D], F32)
        hT = P.sb(top, "hT", [128, KC, L], BF16)
        og = P.sb(top, "og", [128, NT, 512], BF16)
        ropec = P.sb(top, "ropec", [128, L], F32)
        ropes = P.sb(top, "ropes", [128, L], F32)
        tri = P.sb(top, "tri", [128, 128], F32)
        m01 = P.sb(top, "m01", [128, 256], BF16)
        ident = P.sb(top, "ident", [128, 128], BF16)
        mkt = [P.sb(top, "mk%d" % j, [2, 128], BF16) for j in range(4)]
        z2 = [P.ps(top, "z2_%d" % i, [128, 1024], F32) for i in range(2)]
        fb = [Buf("fb0", z2[0][:, 0:512]), Buf("fb1", z2[0][:, 512:1024]), Buf("fb2", z2[1][:, 0:512]), Buf("fb3", z2[1][:, 512:1024])]
        fb += [P.ps(top, "fb%d" % i, [128, 512], F32) for i in range(4, 8)]
        tb = [Buf("tb%d" % i, fb[6 + i][:].bitcast(BF16)) for i in range(2)]
        for i in range(2):
            tb[i].w, tb[i].r = fb[6 + i].w, fb[6 + i].r

        def MM(out, lhsT, rhs, start, stop, reads, writes, inc=True):
            P.op("tensor", lambda e: e.matmul(out, lhsT=lhsT, rhs=rhs, start=start, stop=stop), reads, writes, inc)

        def TR(out, in_, reads, writes, inc=True):
            P.op("tensor", lambda e: e.transpose(out=out, in_=in_, identity=ident[:]), list(reads) + [ident], writes, inc)

        def ACT(out, in_, func, reads, writes, bias=None, scale=None, accum_out=None):
            kw = {}
            if bias is not None:
                kw["bias"] = bias
            if scale is not None:
                kw["scale"] = scale
            if accum_out is not None:
                kw["accum_out"] = accum_out
            P.op("scalar", lambda e: e.activation(out=out, in_=in_, func=func, **kw), reads, writes)

        def TT(eng, out, in0, in1, op, reads, writes):
            P.op(eng, lambda e: e.tensor_tensor(out=out, in0=in0, in1=in1, op=op), reads, writes)

        def TS(eng, out, in0, s1, s2, op0, op1, reads, writes):
            if op1 is None:
                P.op(eng, lambda e: e.tensor_scalar(out=out, in0=in0, scalar1=s1, scalar2=None, op0=op0), reads, writes)
            else:
                P.op(eng, lambda e: e.tensor_scalar(out=out, in0=in0, scalar1=s1, scalar2=s2, op0=op0, op1=op1), reads, writes)

        def STT(eng, out, in0, scalar, in1, op0, op1, reads, writes):
            P.op(eng, lambda e: e.scalar_tensor_tensor(out=out, in0=in0, scalar=scalar, in1=in1, op0=op0, op1=op1), reads, writes)

        def RSTD(out, ss_ap, n, reads, writes, mult=1.0):
            ACT(out, ss_ap, AF.Ln, reads, writes, bias=EPS, scale=1.0 / n)
            ACT(out, out, AF.Exp, writes, writes, bias=(math.log(mult) if mult != 1.0 else None), scale=-0.5)

        def CP(eng, out, in_, reads, writes):
            if eng == "scalar":
                P.op(eng, lambda e: e.copy(out=out, in_=in_), reads, writes)
            else:
                P.op(eng, lambda e: e.tensor_copy(out=out, in_=in_), reads, writes)

        def MEMSET(eng, ap, val, writes):
            P.op(eng, lambda e: e.memset(ap, val), (), writes)

        def DMA(eng, out, in_, reads=(), writes=()):
            P.dma(eng, lambda e: e.dma_start(out=out, in_=in_), reads, writes)

        def load_w(buf, dram2d, k_chunks, c0, c1, dst_c0=0):
            v = dram2d.rearrange("(k p) c -> p k c", p=128)
            DMA("gpsimd", buf[:, 0:k_chunks, dst_c0:dst_c0 + (c1 - c0)], v[:, :, c0:c1], writes=[buf])

        def bcast_load(buf, row_ap, n):
            DMA("sync", buf[:], row_ap.to_broadcast([128, n]), writes=[buf])

        fctr = [0]

        def next_f(lo=0, hi=4):
            b = fb[lo + fctr[0] % (hi - lo)]
            fctr[0] += 1
            return b

        DMA("scalar", ropec[:], c_ropec[:, :], writes=[ropec])
        DMA("scalar", ropes[:], c_ropes[:, :], writes=[ropes])
        DMA("sync", tri[:], c_tri[:, :], writes=[tri])
        DMA("sync", m01[:], c_m01[:, :], writes=[m01])
        DMA("sync", ident[:], c_ident[:, :], writes=[ident])
        for j in range(4):
            DMA("sync", mkt[j][:], c_mk[2 * j:2 * j + 2, :], writes=[mkt[j]])
        h0v = h0.rearrange("(t p) d -> p t d", p=128)
        Xr = [Buf("Xr%d" % j, X.t) for j in range(3)]
        DMA("sync", X[:, 0:6, :], h0v[:, 0:6, :], writes=[Xr[0]])
        DMA("scalar", X[:, 6:12, :], h0v[:, 6:12, :], writes=[Xr[1]])
        DMA("sync", X[:, 12:NT, :], h0v[:, 12:NT, :], writes=[Xr[2]])

        def phase_norm(l):
            with ExitStack() as st:
                grep = P.sb(st, "grep", [128, D], F32)
                junk = [P.sb(st, "junk%d" % j, [128, D], BF16) for j in range(2)]
                ss = P.sb(st, "ss", [128, NT], F32)
                rs = P.sb(st, "rs", [128, NT], F32)
                ssb = [Buf("ss_%d" % i, ss.t) for i in range(NT)]
                rsb = [Buf("rs_%d" % i, rs.t) for i in range(NT)]
                hn = [P.sb(st, "hn%d" % j, [128, D], BF16) for j in range(3)]
                bcast_load(grep, norm_g[l:l + 1, :], D)

                def sa(i):
                    Xd = Xr[i // 6] if l == 0 else X
                    ACT(junk[i % 2][:], X[:, i, :], AF.Square, [Xd], [junk[i % 2], ssb[i]], accum_out=ss[:, i:i + 1])
                    RSTD(rs[:, i:i + 1], ss[:, i:i + 1], D, [ssb[i]], [rsb[i]])
                    h = hn[i % 3]
                    STT("vector", h[:], X[:, i, :], rs[:, i:i + 1], grep[:], ALU.mult, ALU.mult, [Xd, rsb[i], grep], [h])

                def sb_(i):
                    h, t = hn[i % 3], tb[i % 2]
                    for k in range(KC):
                        TR(t[:, k * 128:(k + 1) * 128], h[:, k * 128:(k + 1) * 128], [h], [t], inc=(k == KC - 1))

                def sc(i):
                    t = tb[i % 2]
                    CP("vector", hT[:, :, i * 128:(i + 1) * 128], t[:].rearrange("p (k c) -> p k c", k=KC), [t], [hT])

                for step in range(NT + 2):
                    if step < NT:
                        sa(step)
                    if 0 <= step - 1 < NT:
                        sb_(step - 1)
                    if 0 <= step - 2 < NT:
                        sc(step - 2)
                P.emit()

        def proj_tok(i, W, c0, n, kchunks=KC, src=None, src_k0=0):
            src = src or hT
            p = next_f()
            for k in range(kchunks):
                MM(p[:, 0:n], src[:, src_k0 + k, i * 128:(i + 1) * 128], W[:, k, c0:c0 + n], k == 0, k == kchunks - 1,
                   [src, W], [p], inc=(k == kchunks - 1))
            return p

        def proj_feat(p, c0, n, W, wc0, M, kchunks=KC, src=None, src_k0=0):
            src = src or hT
            for k in range(kchunks):
                MM(p[0:M, 0:n], W[:, k, wc0:wc0 + M], src[:, src_k0 + k, c0:c0 + n], k == 0, k == kchunks - 1,
                   [src, W], [p], inc=(k == kchunks - 1))

        def epilogue(l, b, zoff):
            with ExitStack() as st:
                Wz = P.sb(st, "Wz", [128, KC, 512], BF16)
                Wo = P.sb(st, "Wo", [128, 4, D], BF16)
                Wg = P.sb(st, "Wg", [128, KC, D], BF16)
                Wout = P.sb(st, "Wout", [128, KC, D], BF16)
                bgt = P.sb(st, "bgt", [128, 8], F32)
                G = [P.sb(st, "G%d" % j, [128, 512], BF16) for j in range(2)]
                ogg = [P.sb(st, "ogg%d" % j, [128, 512], BF16) for j in range(3)]
                oggT = P.sb(st, "oggT", [128, 4, 512], BF16)
                sg = [P.sb(st, "sg%d" % j, [128, 512], F32) for j in range(2)]
                mT = P.sb(st, "mT", [128, KC, 512], BF16)
                DMA("sync", bgt[:], bg[l, b, :, :], writes=[bgt])
                load_w(Wz, w_in[l], KC, zoff, zoff + 512)
                load_w(Wg, w_in[l], KC, O_G + b * D, O_G + (b + 1) * D)
                load_w(Wo, w_o[b][l], 4, 0, D)
                load_w(Wout, w_out[l], KC, 0, D)
                cnt = 0
                for (c0, n) in CHUNKS:
                    tiles = list(range(c0 // 128, (c0 + n) // 128))
                    pzs = [proj_tok(i, Wz, 0, 512) for i in tiles]
                    for oc in range(2):
                        proj_feat(fb[4 + oc], c0, n, Wg, oc * 128, 128)
                    for j, i in enumerate(tiles):
                        pz = pzs[j]
                        g_, o_ = G[cnt % 2], ogg[cnt % 3]
                        ACT(g_[:], pz[:, :], AF.Silu, [pz], [g_])
                        TT("vector", o_[:], og[:, i, :], g_[:], ALU.mult, [og, g_], [o_])
                        cnt += 1
                        t = tb[j % 2]
                        for c in range(4):
                            TR(t[:, c * 128:(c + 1) * 128], o_[:, c * 128:(c + 1) * 128], [o_], [t], inc=(c == 3))
                        CP("vector", oggT[:, :, j * 128:(j + 1) * 128], t[:, 0:512].rearrange("p (c q) -> p c q", c=4), [t], [oggT])
                    for oc in range(8):
                        pg = fb[4 + oc % 2]
                        if oc >= 2:
                            proj_feat(pg, c0, n, Wg, oc * 128, 128)
                        py = next_f()
                        for c in range(4):
                            MM(py[:, 0:n], Wo[:, c, oc * 128:(oc + 1) * 128], oggT[:, c, 0:n], c == 0, c == 3, [Wo, oggT], [py], inc=(c == 3))
                        s_ = sg[oc % 2]
                        ACT(s_[:, 0:n], pg[:, 0:n], AF.Sigmoid, [pg, bgt], [s_], bias=bgt[:, oc:oc + 1])
                        TT("vector", mT[:, oc, 0:n], s_[:, 0:n], py[:, 0:n], ALU.mult, [s_, py], [mT])
                    for j, i in enumerate(tiles):
                        for half in range(2):
                            po = next_f()
                            for k in range(KC):
                                MM(po[:, :], mT[:, k, j * 128:(j + 1) * 128], Wout[:, k, half * 512:(half + 1) * 512], k == 0, k == KC - 1,
                                   [mT, Wout], [po], inc=(k == KC - 1))
                            TT("vector", X[:, i, half * 512:(half + 1) * 512], X[:, i, half * 512:(half + 1) * 512], po[:, :], ALU.add, [X, po], [X])
                P.emit()

        def branch_sb(l):
            with ExitStack() as bst:
                V = P.sb(bst, "Vsb", [128, NT, 512], BF16)
                Wqk2 = [P.sb(bst, "Wqk%d" % j, [128, KC, 256], BF16) for j in range(2)]

                def load_pair_w(pr_):
                    load_w(Wqk2[pr_ % 2], w_in[l], KC, O_SBQ + pr_ * 128, O_SBQ + (pr_ + 1) * 128, dst_c0=0)
                    load_w(Wqk2[pr_ % 2], w_in[l], KC, O_SBK + pr_ * 128, O_SBK + (pr_ + 1) * 128, dst_c0=128)
                with ExitStack() as st:
                    Wv = P.sb(st, "Wv", [128, KC, 512], BF16)
                    load_w(Wv, w_in[l], KC, O_SBV, O_SBV + 512)
                    load_pair_w(0)
                    load_pair_w(1)
                    for i in range(NT):
                        p = proj_tok(i, Wv, 0, 512)
                        CP("scalar" if i % 2 else "vector", V[:, i, :], p[:, :], [p], [V])
                    P.emit()
                with ExitStack() as st:
                    qT2 = [P.sb(st, "qT%d" % j, [128, L], BF16) for j in range(2)]
                    kT2 = [P.sb(st, "kT%d" % j, [128, L], BF16) for j in range(2)]
                    NE, NSP, NC = 4, 3, 3
                    ez = [P.sb(st, "ez%d" % j, [128, 512], F32) for j in range(NE)]
                    sp = [P.sb(st, "sp%d" % j, [128, 516], F32) for j in range(NSP)]
                    Cb = [P.sb(st, "Cb%d" % j, [128, 512], F32) for j in range(NC)]
                    ctot = [P.sb(st, "ctot%d" % j, [128, 1], F32) for j in range(3)]
                    for j in range(NSP):
                        MEMSET("vector", sp[j][:], 0.0, [sp[j]])
                    wb = [P.sb(st, "wb%d" % j, [128, 512], BF16) for j in range(2)]
                    wT = [P.sb(st, "wT%d" % j, [128, 512], BF16) for j in range(2)]
                    cn = [P.sb(st, "cn%d" % j, [128, 1], F32) for j in range(3)]
                    def proj_piece(pr_, idx):
                        c0, n = CHUNKS[idx // 2]
                        p = next_f(4, 6)
                        if idx % 2 == 0:
                            proj_feat(p, c0, n, Wqk2[pr_ % 2], 0, 128)
                            CP("scalar", qT2[pr_ % 2][:, c0:c0 + n], p[:, 0:n], [p], [qT2[pr_ % 2]])
                        else:
                            proj_feat(p, c0, n, Wqk2[pr_ % 2], 128, 128)
                            CP("vector", kT2[pr_ % 2][:, c0:c0 + n], p[:, 0:n], [p], [kT2[pr_ % 2]])

                    for idx in range(10):
                        proj_piece(0, idx)
                    for _once in (0,):
                        items = []
                        for pr in range(4):
                            for hh in range(2):
                                for i in range(NT):
                                    nk = (i + 1) * 128
                                    chs = [(k0, min(512, nk - k0)) for k0 in range(0, nk, 512)][::-1]
                                    for ci, (k0, n) in enumerate(chs):
                                        items.append((pr, hh, i, ci, k0, n, ci == len(chs) - 1))
                        N = len(items)
                        NPP = N // 4
                        obank = {}
                        ocnt = [0]

                        def st_mm(j):
                            pr, hh, i, ci, k0, n, last = items[j]
                            r0 = 64 * hh
                            z = fb[j % 2]
                            q_, k_ = qT2[pr % 2], kT2[pr % 2]
                            MM(z[:, 0:n], q_[r0:r0 + 64, i * 128:(i + 1) * 128], k_[r0:r0 + 64, k0:k0 + n], True, True, [q_, k_], [z])

                        def st_expz(j):
                            pr, hh, i, ci, k0, n, last = items[j]
                            e_ = ez[j % NE]
                            ACT(e_[:, 0:n], fb[j % 2][:, 0:n], AF.Exp, [fb[j % 2]], [e_], scale=0.125)
                            if ci == 0:
                                TT("gpsimd", e_[:, n - 128:n], e_[:, n - 128:n], tri[:], ALU.mult, [e_, tri], [e_])

                        def st_ln(j):
                            pr, hh, i, ci, k0, n, last = items[j]
                            ACT(sp[j % NSP][:, 1:n + 1], ez[j % NE][:, 0:n], AF.Ln, [ez[j % NE]], [sp[j % NSP], ctot[j % 3]], bias=1.0,
                                accum_out=ctot[j % 3][:])

                        def st_scan(j):
                            pr, hh, i, ci, k0, n, last = items[j]
                            s_, c_ = sp[j % NSP], Cb[j % NC]
                            P.op("vector", lambda e: e.tensor_tensor_scan(out=c_[:, 0:n], data0=s_[:, 0:n], data1=s_[:, 0:n],
                                                                           initial=0.0, op0=ALU.add, op1=ALU.max), [s_], [c_])

                        def st_cn(j):
                            pr, hh, i, ci, k0, n, last = items[j]
                            if ci == 0:
                                TS("vector", cn[j % 3][:], ctot[j % 3][:], -1.0, None, ALU.mult, None, [ctot[j % 3]], [cn[j % 3]])
                            else:
                                TT("vector", cn[j % 3][:], cn[(j - 1) % 3][:], ctot[j % 3][:], ALU.subtract, [cn[(j - 1) % 3], ctot[j % 3]], [cn[j % 3]])

                        def st_expt(j):
                            pr, hh, i, ci, k0, n, last = items[j]
                            ACT(Cb[j % NC][:, 0:n], Cb[j % NC][:, 0:n], AF.Exp, [Cb[j % NC], cn[j % 3]], [Cb[j % NC]], bias=cn[j % 3][:, 0:1])

                        def st_mult(j):
                            pr, hh, i, ci, k0, n, last = items[j]
                            TT("gpsimd", wb[j % 2][:, 0:n], ez[j % NE][:, 0:n], Cb[j % NC][:, 0:n], ALU.mult, [ez[j % NE], Cb[j % NC]], [wb[j % 2]])

                        def st_tr(j):
                            pr, hh, i, ci, k0, n, last = items[j]
                            t = tb[j % 2]
                            nb = n // 128
                            for jb in range(nb):
                                TR(t[:, jb * 128:(jb + 1) * 128], wb[j % 2][:, jb * 128:(jb + 1) * 128], [wb[j % 2]], [t], inc=(jb == nb - 1))

                        def st_evac(j):
                            pr, hh, i, ci, k0, n, last = items[j]
                            CP("scalar" if j % 2 == 0 else "vector", wT[j % 2][:, 0:n], tb[j % 2][:, 0:n], [tb[j % 2]], [wT[j % 2]])

                        def st_pv(j):
                            pr, hh, i, ci, k0, n, last = items[j]
                            h = 2 * pr + hh
                            nb = n // 128
                            if ci == 0:
                                obank[(pr, hh, i)] = fb[2 + ocnt[0] % 2]
                                ocnt[0] += 1
                            O = obank[(pr, hh, i)]
                            for jb in range(nb):
                                kb = k0 // 128 + jb
                                MM(O[:, 0:64], wT[j % 2][:, jb * 128:(jb + 1) * 128], V[:, kb, h * 64:(h + 1) * 64],
                                   ci == 0 and jb == 0, last and jb == nb - 1, [wT[j % 2], V], [O], inc=(jb == nb - 1))
                            if last:
                                CP("vector", og[:, i, h * 64:(h + 1) * 64], O[:, 0:64], [O], [og])

                        sched = [(st_mm, 0), (st_expz, 1), (st_expt, 3), (st_ln, 1), (st_evac, 6), (st_cn, 2), (st_scan, 2),
                                 (st_mult, 4), (st_tr, 5), (st_pv, 7)]
                        for step in range(N + 7):
                            for fn, off in sched:
                                if 0 <= step - off < N:
                                    fn(step - off)
                            pcur, rel = step // NPP, step % NPP
                            if pcur + 1 < 4 and rel >= 8 and (rel - 8) % 8 == 0 and (rel - 8) // 8 < 10:
                                proj_piece(pcur + 1, (rel - 8) // 8)
                            if pcur + 2 < 4 and rel == 8 + 8 * 10:
                                load_pair_w(pcur + 2)
                    P.emit()
            epilogue(l, 0, O_SBZ)

        def softmax_attn(st, units, dv1, scale, finalize, PT=None):
            NS = 5
            zbanks = [fb[0], fb[1], fb[6], fb[7]]
            if PT is None:
                PT = [P.sb(st, "PT%d" % j, [128, 512], BF16) for j in range(NS)]
            items = []
            for i in range(NT):
                kbs = list(range(0, min(i + 2, NT)))
                groups = [kbs[a:a + 4] for a in range(0, len(kbs), 4)]
                for u in range(len(units)):
                    for gi, g in enumerate(groups):
                        items.append((i, u, g, gi == 0, gi == len(groups) - 1))
            N = len(items)
            nu = len(units)

            def s1(j):
                i, u, g, first, last = items[j]
                QTb, KTb, r0, nr, vfn = units[u]
                z = zbanks[j % 4]
                for a, kb in enumerate(g):
                    msk = kb >= i
                    MM(z[:, a * 128:(a + 1) * 128], KTb[r0:r0 + nr, kb * 128:(kb + 1) * 128], QTb[r0:r0 + nr, i * 128:(i + 1) * 128],
                       True, not msk, [QTb, KTb], [z], inc=(a == len(g) - 1 and not msk))
                    if msk:
                        mo = 0 if kb == i else 2
                        MM(z[:, a * 128:(a + 1) * 128], mkt[mo][:, :], mkt[mo + 1][:, :], False, True, [mkt[mo], mkt[mo + 1]], [z],
                           inc=(a == len(g) - 1))

            def s2(j):
                i, u, g, first, last = items[j]
                z = zbanks[j % 4]
                s = j % NS
                n = len(g) * 128
                ACT(PT[s][:, 0:n], z[:, 0:n], AF.Exp, [z], [PT[s]], scale=scale)

            def s3(j):
                i, u, g, first, last = items[j]
                QTb, KTb, r0, nr, vfn = units[u]
                s = j % NS
                O = fb[2 + u] if nu > 1 else fb[2 + i % 2]
                for a, kb in enumerate(g):
                    MM(O[:, 0:dv1], PT[s][:, a * 128:(a + 1) * 128], vfn(kb), first and a == 0, last and a == len(g) - 1,
                       [PT[s]], [O], inc=(a == len(g) - 1))
                if last and u == nu - 1:
                    finalize(i, [fb[2 + uu] for uu in range(nu)] if nu > 1 else [O])

            for step in range(N + 2):
                if step < N:
                    s1(step)
                if 0 <= step - 1 < N:
                    s2(step - 1)
                if 0 <= step - 2 < N:
                    s3(step - 2)

        def diff_attn(st, QTb, KTb, vfn, finalize, PT=None):
            NS = 3
            if PT is None:
                PT = [P.sb(st, "PTd%d" % j, [128, 1024], BF16) for j in range(NS)]
            items = []
            for i in range(NT):
                kbs = list(range(0, min(i + 2, NT)))
                groups = [kbs[a:a + 4] for a in range(0, len(kbs), 4)]
                for gi, g in enumerate(groups):
                    items.append((i, g, gi == 0, gi == len(groups) - 1))
            N = len(items)

            def s1(j):
                i, g, first, last = items[j]
                zt = z2[j % 2]
                for a, kb in enumerate(g):
                    msk = kb >= i
                    for u in range(2):
                        r0 = 64 * u
                        MM(zt[:, u * 512 + a * 128:u * 512 + (a + 1) * 128], KTb[r0:r0 + 64, kb * 128:(kb + 1) * 128],
                           QTb[r0:r0 + 64, i * 128:(i + 1) * 128], True, not msk, [QTb, KTb], [zt],
                           inc=(a == len(g) - 1 and u == 1 and not msk))
                    if msk:
                        mo = 0 if kb == i else 2
                        for u in range(2):
                            MM(zt[:, u * 512 + a * 128:u * 512 + (a + 1) * 128], mkt[mo][:, :], mkt[mo + 1][:, :], False, True,
                               [mkt[mo], mkt[mo + 1]], [zt], inc=(a == len(g) - 1 and u == 1))

            def s2(j):
                i, g, first, last = items[j]
                zt = z2[j % 2]
                p_ = PT[j % NS]
                n = len(g) * 128
                ACT(p_[:].rearrange("p (u c) -> p u c", u=2)[:, :, 0:n], zt[:].rearrange("p (u c) -> p u c", u=2)[:, :, 0:n],
                    AF.Exp, [zt], [p_], scale=0.125)

            def s3(j):
                i, g, first, last = items[j]
                p_ = PT[j % NS]
                Os = [fb[4 + 2 * (i % 2)], fb[5 + 2 * (i % 2)]]
                for a, kb in enumerate(g):
                    for u in range(2):
                        MM(Os[u][:, 0:129], p_[:, u * 512 + a * 128:u * 512 + (a + 1) * 128], vfn(kb),
                           first and a == 0, last and a == len(g) - 1, [p_], [Os[u]], inc=(a == len(g) - 1))
                if last:
                    finalize(i, Os)

            for step in range(N + 2):
                if step < N:
                    s1(step)
                if 0 <= step - 1 < N:
                    s2(step - 1)
                if 0 <= step - 2 < N:
                    s3(step - 2)

        def branch_mla(l):
            with ExitStack() as bst:
                cnT = P.sb(bst, "cnT", [128, 5, L], BF16)
                Va = P.sb(bst, "Va", [128, NT, 8, 68], BF16)
                KT = P.sb(bst, "KTm", [128, L], BF16)
                with ExitStack() as st:
                    W = P.sb(st, "Wm", [128, KC, 704], BF16)
                    Wv = P.sb(st, "Wukvv", [128, 2, 512], BF16)
                    gq = P.sb(st, "gq", [128, 640], F32)
                    junk = P.sb(st, "junkm", [128, 384], BF16)
                    ss = P.sb(st, "ssm", [128, 2 * NT], F32)
                    rs = P.sb(st, "rsm", [128, 2 * NT], F32)
                    cb = [P.sb(st, "cb%d" % j, [128, 640], BF16) for j in range(2)]
                    t1 = P.sb(st, "t1m", [128, 512], F32)
                    t2 = P.sb(st, "t2m", [128, 512], F32)
                    load_w(W, w_in[l], KC, O_CQ, O_CQ + 672)
                    load_w(W, w_x[l], KC, 1024, 1056, dst_c0=672)
                    load_w(Wv, ukvv[l], 2, 0, 512)
                    DMA("sync", gq[:, 0:384], cq_g[l:l + 1, :].to_broadcast([128, 384]), writes=[gq])
                    DMA("sync", gq[:, 384:640], ckv_g[l:l + 1, :].to_broadcast([128, 256]), writes=[gq])
                    MEMSET("gpsimd", Va[:].rearrange("p a b c -> p (a b c)"), 1.0, [Va])
                    ssb = [Buf("ssm_%d" % i, ss.t) for i in range(2 * NT)]
                    rsb = [Buf("rsm_%d" % i, rs.t) for i in range(2 * NT)]
                    cb3 = cb + [P.sb(st, "cb2", [128, 640], BF16)]
                    junk2 = [junk, P.sb(st, "junkm2", [128, 384], BF16)]

                    def sa(i):
                        c = cb3[i % 3]
                        for part, (wc0, n, dc0) in enumerate(((0, 384, 0), (384, 256, 384))):
                            p = proj_tok(i, W, wc0, n)
                            col = 2 * i + part
                            ACT(junk2[part][:, 0:n], p[:, 0:n], AF.Square, [p], [junk2[part], ssb[col]], accum_out=ss[:, col:col + 1])
                            RSTD(rs[:, col:col + 1], ss[:, col:col + 1], n, [ssb[col]], [rsb[col]])
                            STT("vector", c[:, dc0:dc0 + n], p[:, 0:n], rs[:, col:col + 1], gq[:, dc0:dc0 + n], ALU.mult, ALU.mult, [p, rsb[col], gq], [c])

                    def sb_(i):
                        c, t = cb3[i % 3], tb[i % 2]
                        for k in range(5):
                            TR(t[:, k * 128:(k + 1) * 128], c[:, k * 128:(k + 1) * 128], [c], [t], inc=(k == 4))

                    def sc(i):
                        t = tb[i % 2]
                        CP("vector" if i % 2 else "scalar", cnT[:, :, i * 128:(i + 1) * 128], t[:, 0:640].rearrange("p (k c) -> p k c", k=5), [t], [cnT])

                    for step in range(NT + 2):
                        if step < NT:
                            sa(step)
                        if 0 <= step - 1 < NT:
                            sb_(step - 1)
                        if 0 <= step - 2 < NT:
                            sc(step - 2)
                    for (c0, n) in CHUNKS:
                        pa = fb[4]
                        pb = fb[5]
                        proj_feat(pa, c0, n, W, 608, 64)
                        proj_feat(pb, c0, n, W, 640, 64)
                        TT("vector", t1[32:64, 0:n], pa[32:64, 0:n], ropec[32:64, c0:c0 + n], ALU.mult, [pa, ropec], [t1])
                        TT("vector", t2[32:64, 0:n], pb[32:64, 0:n], ropes[32:64, c0:c0 + n], ALU.mult, [pb, ropes], [t2])
                        TT("vector", KT[32:64, c0:c0 + n], t1[32:64, 0:n], t2[32:64, 0:n], ALU.add, [t1, t2], [KT])
                    for i in range(NT):
                        p = proj_tok(i, Wv, 0, 512, kchunks=2, src=cnT, src_k0=3)
                        CP("scalar" if i % 2 else "vector", Va[:, i, :, 0:64], p[:, :].rearrange("p (h d) -> p h d", h=8), [p], [Va])
                    P.emit()
                with ExitStack() as st:
                    QT = P.sb(st, "QTm", [128, L], BF16)
                    Wa = P.sb(st, "Wuqa", [128, 3, 768], BF16)
                    Wb = P.sb(st, "Wuqb", [128, 3, 768], BF16)
                    Wk = P.sb(st, "Wukn", [128, 2, 768], BF16)
                    t1 = P.sb(st, "t1q", [128, 512], F32)
                    t2 = P.sb(st, "t2q", [128, 512], F32)
                    rcp = P.sb(st, "rcp", [128, 1], F32)
                    PTm = [P.sb(st, "PTm%d" % j, [128, 512], BF16) for j in range(5)]
                    load_w(Wa, uqa[l], 3, 0, 768)
                    load_w(Wb, uqb[l], 3, 0, 768)
                    load_w(Wk, ukn[l], 2, 0, 768)
                    for h in range(8):
                        for cidx, (c0, n) in enumerate(CHUNKS):
                            pa, pb = fb[4 + 2 * (cidx % 2)], fb[5 + 2 * (cidx % 2)]
                            proj_feat(pa, c0, n, Wa, h * 96, 96, kchunks=3, src=cnT)
                            proj_feat(pb, c0, n, Wb, h * 96, 96, kchunks=3, src=cnT)
                            CP("scalar", QT[0:32, c0:c0 + n], pa[0:32, 0:n], [pa], [QT])
                            CP("scalar", QT[64:96, c0:c0 + n], pa[64:96, 0:n], [pa], [QT])
                            TT("vector", t1[32:64, 0:n], pa[32:64, 0:n], ropec[32:64, c0:c0 + n], ALU.mult, [pa, ropec], [t1])
                            TT("vector", t2[32:64, 0:n], pb[32:64, 0:n], ropes[32:64, c0:c0 + n], ALU.mult, [pb, ropes], [t2])
                            TT("vector", QT[32:64, c0:c0 + n], t1[32:64, 0:n], t2[32:64, 0:n], ALU.add, [t1, t2], [QT])
                            pk = pb
                            proj_feat(pk, c0, n, Wk, h * 96, 96, kchunks=2, src=cnT, src_k0=3)
                            CP("scalar", KT[0:32, c0:c0 + n], pk[0:32, 0:n], [pk], [KT])
                            CP("vector", KT[64:96, c0:c0 + n], pk[64:96, 0:n], [pk], [KT])

                        def fin(i, Os, h=h):
                            O = Os[0]
                            P.op("vector", lambda e: e.reciprocal(out=rcp[:], in_=O[:, 64:65]), [O], [rcp])
                            TS("vector", og[:, i, h * 64:(h + 1) * 64], O[:, 0:64], rcp[:, 0:1], None, ALU.mult, None, [O, rcp], [og])

                        softmax_attn(st, [(QT, KT, 0, 96, (lambda kb, h=h: Va[:, kb, h, 0:65]))], 65, 1.0 / math.sqrt(96.0), fin, PT=PTm)
                    P.emit()
            epilogue(l, 1, O_MZ)

        def branch_diff(l):
            lam_init = 0.8 - 0.6 * math.exp(-0.3 * l)
            with ExitStack() as bst:
                Vd = P.sb(bst, "Vd", [128, NT, 4, 132], BF16)
                lam = P.sb(bst, "lam", [128, 1], F32)
                gd = P.sb(bst, "gd", [128, 128], F32)
                Wq2 = [P.sb(bst, "Wdq%d" % j, [128, KC, 512], BF16) for j in range(2)]

                def load_head_w(h):
                    Wq_ = Wq2[h % 2]
                    load_w(Wq_, w_in[l], KC, O_DQ + h * 128, O_DQ + (h + 1) * 128, dst_c0=0)
                    load_w(Wq_, w_x[l], KC, h * 128, (h + 1) * 128, dst_c0=128)
                    load_w(Wq_, w_in[l], KC, O_DK + h * 128, O_DK + (h + 1) * 128, dst_c0=256)
                    load_w(Wq_, w_x[l], KC, 512 + h * 128, 512 + (h + 1) * 128, dst_c0=384)
                with ExitStack() as st:
                    Wv = P.sb(st, "Wdv", [128, KC, 512], BF16)
                    dl = P.sb(st, "dl", [128, 256], F32)
                    pr_ = P.sb(st, "prd", [128, 128], F32)
                    sm = P.sb(st, "smd", [128, 2], F32)
                    load_w(Wv, w_in[l], KC, O_DV, O_DV + 512)
                    load_head_w(0)
                    DMA("sync", dl[:], dlam[l:l + 1, :].to_broadcast([128, 256]), writes=[dl])
                    DMA("sync", gd[:], dng[l:l + 1, :].to_broadcast([128, 128]), writes=[gd])
                    dl3 = dl[:].rearrange("p (a b) -> p a b", a=2)
                    TT("vector", pr_[:].rearrange("p (a b) -> p a b", a=2), dl3[:, :, 0:64], dl3[:, :, 64:128], ALU.mult, [dl], [pr_])
                    P.op("vector", lambda e: e.reduce_sum(out=sm[:], in_=pr_[:].rearrange("p (a b) -> p a b", a=2), axis=mybir.AxisListType.X), [pr_], [sm])
                    ACT(sm[:], sm[:], AF.Exp, [sm], [sm])
                    TT("vector", lam[:], sm[:, 0:1], sm[:, 1:2], ALU.subtract, [sm], [lam])
                    TS("vector", lam[:], lam[:], lam_init, None, ALU.add, None, [lam], [lam])
                    MEMSET("gpsimd", Vd[:].rearrange("p a b c -> p (a b c)"), 1.0, [Vd])
                    for i in range(NT):
                        p = proj_tok(i, Wv, 0, 512)
                        CP("scalar" if i % 2 else "vector", Vd[:, i, :, 0:128], p[:, :].rearrange("p (h d) -> p h d", h=4), [p], [Vd])
                    P.emit()
                with ExitStack() as st:
                    PTd = [P.sb(st, "PTd%d" % j, [128, 1024], BF16) for j in range(3)]
                    QT = P.sb(st, "QTd", [128, L], BF16)
                    KT = P.sb(st, "KTd", [128, L], BF16)
                    t1 = P.sb(st, "t1d", [128, 512], F32)
                    t2 = P.sb(st, "t2d", [128, 512], F32)
                    rc = P.sb(st, "rcd", [128, 2], F32)
                    tm = P.sb(st, "tmd", [128, 128], F32)
                    oc_ = P.sb(st, "ocd", [128, 128], F32)
                    jk = P.sb(st, "jkd", [128, 128], BF16)
                    ssd = P.sb(st, "ssd", [128, 1], F32)
                    rsd = P.sb(st, "rsd", [128, 1], F32)
                    for h in range(4):
                        cc = 0
                        Wq = Wq2[h % 2]
                        if h + 1 < 4:
                            load_head_w(h + 1)
                        for (dst, wc) in ((QT, 0), (KT, 256)):
                            for (c0, n) in CHUNKS:
                                pa, pb = fb[4 + 2 * (cc % 2)], fb[5 + 2 * (cc % 2)]
                                cc += 1
                                proj_feat(pa, c0, n, Wq, wc, 128)
                                proj_feat(pb, c0, n, Wq, wc + 128, 128)
                                TT("vector", t1[:, 0:n], pa[:, 0:n], ropec[:, c0:c0 + n], ALU.mult, [pa, ropec], [t1])
                                TT("vector", t2[:, 0:n], pb[:, 0:n], ropes[:, c0:c0 + n], ALU.mult, [pb, ropes], [t2])
                                TT("vector", dst[:, c0:c0 + n], t1[:, 0:n], t2[:, 0:n], ALU.add, [t1, t2], [dst])
                                CP("scalar", dst[32:64, c0:c0 + n], pa[32:64, 0:n], [pa], [dst])

                        def fin(i, Os, h=h):
                            O0, O1 = Os
                            P.op("vector", lambda e: e.reciprocal(out=rc[:, 0:1], in_=O0[:, 128:129]), [O0], [rc])
                            P.op("vector", lambda e: e.reciprocal(out=rc[:, 1:2], in_=O1[:, 128:129]), [O1], [rc])
                            TT("vector", rc[:, 1:2], rc[:, 1:2], lam[:, 0:1], ALU.mult, [rc, lam], [rc])
                            TS("vector", tm[:], O1[:, 0:128], rc[:, 1:2], None, ALU.mult, None, [O1, rc], [tm])
                            STT("vector", oc_[:], O0[:, 0:128], rc[:, 0:1], tm[:], ALU.mult, ALU.subtract, [O0, rc, tm], [oc_])
                            MEMSET("vector", ssd[:], 0.0, [ssd])
                            ACT(jk[:], oc_[:], AF.Square, [oc_], [jk, ssd], accum_out=ssd[:, 0:1])
                            RSTD(rsd[:], ssd[:], 128, [ssd], [rsd], mult=1.0 - lam_init)
                            STT("vector", og[:, i, h * 128:(h + 1) * 128], oc_[:], rsd[:, 0:1], gd[:], ALU.mult, ALU.mult, [oc_, rsd, gd], [og])

                        diff_attn(st, QT, KT, (lambda kb, h=h: Vd[:, kb, h, 0:129]), fin, PT=PTd)
                    P.emit()
            epilogue(l, 2, O_DZ)

        for l in range(nlayers):
            phase_norm(l)
            if 0 in branches:
                branch_sb(l)
            if 1 in branches:
                branch_mla(l)
            if 2 in branches:
                branch_diff(l)

        with ExitStack() as st:
            grep = P.sb(st, "grepf", [128, D], F32)
            junk = P.sb(st, "junkf", [128, D], BF16)
            ss = P.sb(st, "ssf", [128, NT], F32)
            rs = P.sb(st, "rsf", [128, NT], F32)
            yo = [P.sb(st, "yo%d" % j, [128, D], F32) for j in range(2)]
            bcast_load(grep, final_g[0:1, :], D)
            MEMSET("vector", ss[:], 0.0, [ss])
            for i in range(NT):
                o = yo[i % 2]
                if final_norm:
                    ACT(junk[:], X[:, i, :], AF.Square, [X], [junk, ss], accum_out=ss[:, i:i + 1])
                    RSTD(rs[:, i:i + 1], ss[:, i:i + 1], D, [ss], [rs])
                    STT("vector", o[:], X[:, i, :], rs[:, i:i + 1], grep[:], ALU.mult, ALU.mult, [X, rs, grep], [o])
                else:
                    CP("vector", o[:], X[:, i, :], [X], [o])
                p_lo = NMETA if i == 0 else 0
                p_hi = NMETA if i == NT - 1 else 128
                s0 = 128 * i - NMETA + p_lo
                DMA("sync", y[s0:s0 + (p_hi - p_lo), :], o[p_lo:p_hi, :], reads=[o])
            P.wait_all("sync", yo)
            P.emit()
    return nc


_CACHE = {}


def _consts():
    if "c" not in _CACHE:
        C, Sg = _rope_tables()
        tri, m01, ident, mk = _masks()
        _CACHE["c"] = {"c_ropec": C, "c_ropes": Sg, "c_tri": tri, "c_m01": m01, "c_ident": ident, "c_mk": mk}
    return _CACHE["c"]


def make_in_maps(inp):
    x = np.asarray(inp["x"], np.float32)
    B = x.shape[0]
    meta = np.asarray(inp["meta_tokens"], np.float32)
    lay = _host_layouts({k: np.asarray(v) for k, v in inp.items()})
    shared = dict(_consts())
    shared.update(lay)
    for k in ("norm_g", "w_in", "mla_cq_g", "mla_ckv_g", "diff_norm_g", "w_o_sb", "w_o_mla", "w_o_diff", "w_out"):
        shared[k] = np.ascontiguousarray(np.asarray(inp[k], np.float32))
    shared["diff_lambda"] = np.ascontiguousarray(np.asarray(inp["diff_lambda"], np.float32).reshape(2, 256))
    shared["final_g"] = np.ascontiguousarray(np.asarray(inp["final_g"], np.float32).reshape(1, D))
    maps = []
    for b in range(B):
        h0 = np.concatenate([meta, x[b], np.zeros((L - NMETA - S, D), np.float32)], axis=0)
        m = dict(shared)
        m["h0"] = np.ascontiguousarray(h0)
        maps.append(m)
    return maps


def kernel(**inputs):
    maps = make_in_maps(inputs)
    if "nc" not in _CACHE:
        _CACHE["nc"] = build_nc()
    res = run_bass_kernel_spmd(_CACHE["nc"], maps, core_ids=list(range(len(maps))))
    return np.stack([np.asarray(r["y"], np.float32) for r in res.results], axis=0)
```
